# Optimizing a Trainium2 kernel written in Bass

```python
import math
import jax, jax.numpy as jnp
from jax import lax
import numpy as np

D_MODEL = 1024
BATCH = 2
SEQ = 8192
DEPTH = 2

GRID_W = 64
CTX_LEN = 256
HEAD_DIM = 64
GQA_HEADS = 8
GQA_KV_HEADS = 2
GQA_GROUP = GQA_HEADS // GQA_KV_HEADS
NA_HEADS = 8
NA_KH = 8
NA_KW = 16
MLA_HEADS = 8
MLA_NOPE = 64
MLA_ROPE = 32
MLA_V = 64
MLA_KV_RANK = 256
N_BRANCH = 3
D_FF = 4 * D_MODEL
Q_BLOCK = 128
ROPE_THETA = 10000.0
EPS = 1e-6
IN_SIZES = (
    GQA_HEADS * HEAD_DIM,
    GQA_KV_HEADS * HEAD_DIM,
    GQA_KV_HEADS * HEAD_DIM,
    NA_HEADS * HEAD_DIM,
    NA_HEADS * HEAD_DIM,
    NA_HEADS * HEAD_DIM,
    MLA_HEADS * (MLA_NOPE + MLA_ROPE),
    MLA_KV_RANK,
    MLA_ROPE,
    N_BRANCH * D_MODEL,
)
W_IN_COLS = sum(IN_SIZES)

kernel_name = 'hybrid_gqa_natten_mla_dit_block'


def rmsnorm(x, g):
    xf = x.astype(jnp.float32)
    y = xf * lax.rsqrt(jnp.mean(xf * xf, axis=-1, keepdims=True) + EPS)
    return (y * g.astype(jnp.float32)).astype(x.dtype)


def modulate(h, shift, scale):
    return h * (1 + scale) + shift


def split_cols(p):
    pts = []
    acc = 0
    for sz in IN_SIZES[:-1]:
        acc += sz
        pts.append(acc)
    return jnp.split(p, pts, axis=-1)


def axial_angles(n_tokens, rot_dim):
    t = jnp.arange(n_tokens, dtype=jnp.int32)
    row = (t // GRID_W).astype(jnp.float32)
    col = (t % GRID_W).astype(jnp.float32)
    half = rot_dim // 2
    inv_freq = ROPE_THETA ** (-jnp.arange(0, half, 2, dtype=jnp.float32) / half)
    return row[:, None] * inv_freq, col[:, None] * inv_freq


def rope_1d(x, ang):
    cos = jnp.cos(ang)[:, None, :].astype(x.dtype)
    sin = jnp.sin(ang)[:, None, :].astype(x.dtype)
    x1, x2 = jnp.split(x, 2, axis=-1)
    return jnp.concatenate([x1 * cos - x2 * sin, x2 * cos + x1 * sin], axis=-1)


def rope_axial(x, angles):
    ang_row, ang_col = angles
    xr, xc = jnp.split(x, 2, axis=-1)
    return jnp.concatenate([rope_1d(xr, ang_row), rope_1d(xc, ang_col)], axis=-1)


def sdpa_blocks(q, k, v, scale):
    b, s = q.shape[:2]
    nb = s // Q_BLOCK
    qb = jnp.swapaxes(q.reshape((b, nb, Q_BLOCK) + q.shape[2:]), 0, 1)

    def one(qblk):
        sc = jnp.einsum('bqkgd,btkd->bkgqt', qblk, k).astype(jnp.float32) * scale
        p = jax.nn.softmax(sc, axis=-1).astype(v.dtype)
        return jnp.einsum('bkgqt,btke->bqkge', p, v)

    o = lax.map(one, qb)
    return jnp.swapaxes(o, 0, 1).reshape(b, s, q.shape[2] * q.shape[3] * v.shape[-1])


def neighbourhood_attention(q, k, v, k_ctx, v_ctx, rpb):
    b, s, h, d = q.shape
    rows = s // GRID_W
    kh = min(NA_KH, rows)
    kw = NA_KW
    scale = d ** -0.5
    qg = q.reshape(b, rows, GRID_W, h, d)
    kg = k.reshape(b, rows, GRID_W, h, d)
    vg = v.reshape(b, rows, GRID_W, h, d)
    cols = jnp.arange(GRID_W, dtype=jnp.int32)
    col_start = jnp.clip(cols - kw // 2, 0, GRID_W - kw)
    col_idx = col_start[:, None] + jnp.arange(kw, dtype=jnp.int32)[None, :]
    col_bias_idx = col_idx - cols[:, None] + (NA_KW - 1)
    n_win = kh * kw

    def one(args):
        r, q_row = args
        rs = jnp.clip(r - kh // 2, 0, rows - kh)
        k_rows = lax.dynamic_slice_in_dim(kg, rs, kh, axis=1)
        v_rows = lax.dynamic_slice_in_dim(vg, rs, kh, axis=1)
        k_win = k_rows[:, :, col_idx]
        v_win = v_rows[:, :, col_idx]
        row_bias_idx = rs + jnp.arange(kh, dtype=jnp.int32) - r + (NA_KH - 1)
        bias = rpb[:, row_bias_idx][:, :, col_bias_idx]
        bias = jnp.transpose(bias, (0, 2, 1, 3))[None].astype(jnp.float32)
        s_win = jnp.einsum('bqhd,biqjhd->bhqij', q_row, k_win).astype(jnp.float32) * scale + bias
        s_ctx = jnp.einsum('bqhd,bthd->bhqt', q_row, k_ctx).astype(jnp.float32) * scale
        sc = jnp.concatenate([s_win.reshape(b, h, GRID_W, n_win), s_ctx], axis=-1)
        p = jax.nn.softmax(sc, axis=-1).astype(v.dtype)
        p_win = p[..., :n_win].reshape(b, h, GRID_W, kh, kw)
        p_ctx = p[..., n_win:]
        return (jnp.einsum('bhqij,biqjhd->bqhd', p_win, v_win)
                + jnp.einsum('bhqt,bthd->bqhd', p_ctx, v_ctx))

    o = lax.map(one, (jnp.arange(rows, dtype=jnp.int32), jnp.swapaxes(qg, 0, 1)))
    return jnp.swapaxes(o, 0, 1).reshape(b, s, h * d)


def project(h, w_in, q_norm, k_norm, kv_norm, w_uk, w_uv, ang_a, ang_m):
    b, n, _ = h.shape
    (ga_q, ga_k, ga_v, na_q, na_k, na_v, ml_q, ml_ckv, ml_kr, gates) = split_cols(h @ w_in)
    ga_q = rmsnorm(ga_q.reshape(b, n, GQA_HEADS, HEAD_DIM), q_norm)
    ga_k = rmsnorm(ga_k.reshape(b, n, GQA_KV_HEADS, HEAD_DIM), k_norm)
    ga_v = ga_v.reshape(b, n, GQA_KV_HEADS, HEAD_DIM)
    na_q = na_q.reshape(b, n, NA_HEADS, HEAD_DIM)
    na_k = na_k.reshape(b, n, NA_HEADS, HEAD_DIM)
    na_v = na_v.reshape(b, n, NA_HEADS, HEAD_DIM)
    ml_q = ml_q.reshape(b, n, MLA_HEADS, MLA_NOPE + MLA_ROPE)
    q_nope, q_rope = ml_q[..., :MLA_NOPE], ml_q[..., MLA_NOPE:]
    c_kv = rmsnorm(ml_ckv, kv_norm)
    k_nope = (c_kv @ w_uk).reshape(b, n, MLA_HEADS, MLA_NOPE)
    ml_v = (c_kv @ w_uv).reshape(b, n, MLA_HEADS, MLA_V)
    k_rope = ml_kr[:, :, None, :]
    if ang_a is not None:
        ga_q = rope_axial(ga_q, ang_a)
        ga_k = rope_axial(ga_k, ang_a)
        q_rope = rope_axial(q_rope, ang_m)
        k_rope = rope_axial(k_rope, ang_m)
    ml_q = jnp.concatenate([q_nope, q_rope], axis=-1)
    ml_k = jnp.concatenate([k_nope, jnp.broadcast_to(k_rope, (b, n, MLA_HEADS, MLA_ROPE))], axis=-1)
    gates = jax.nn.sigmoid(gates.astype(jnp.float32)).astype(h.dtype).reshape(b, n, N_BRANCH, D_MODEL)
    return {'ga_q': ga_q, 'ga_k': ga_k, 'ga_v': ga_v, 'na_q': na_q, 'na_k': na_k, 'na_v': na_v,
            'ml_q': ml_q, 'ml_k': ml_k, 'ml_v': ml_v, 'gates': gates}


def merge(ya, yb, yc, gates, w_o_gqa, w_o_na, w_o_mla, w_out):
    y = (gates[..., 0, :] * (ya @ w_o_gqa)
         + gates[..., 1, :] * (yb @ w_o_na)
         + gates[..., 2, :] * (yc @ w_o_mla))
    return y @ w_out


def parallel_mixer(h_lat, h_ctx, w_in, q_norm, k_norm, rpb, kv_norm, w_uk, w_uv,
                   w_o_gqa, w_o_na, w_o_mla, w_out, ang_a, ang_m, with_ctx):
    b, s, _ = h_lat.shape
    pl = project(h_lat, w_in, q_norm, k_norm, kv_norm, w_uk, w_uv, ang_a, ang_m)
    pc = project(h_ctx, w_in, q_norm, k_norm, kv_norm, w_uk, w_uv, None, None)
    sc_a = HEAD_DIM ** -0.5
    sc_m = (MLA_NOPE + MLA_ROPE) ** -0.5
    k_all = jnp.concatenate([pl['ga_k'], pc['ga_k']], axis=1)
    v_all = jnp.concatenate([pl['ga_v'], pc['ga_v']], axis=1)
    ya = sdpa_blocks(pl['ga_q'].reshape(b, s, GQA_KV_HEADS, GQA_GROUP, HEAD_DIM), k_all, v_all, sc_a)
    yb = neighbourhood_attention(pl['na_q'], pl['na_k'], pl['na_v'], pc['na_k'], pc['na_v'], rpb)
    mk_all = jnp.concatenate([pl['ml_k'], pc['ml_k']], axis=1)
    mv_all = jnp.concatenate([pl['ml_v'], pc['ml_v']], axis=1)
    yc = sdpa_blocks(pl['ml_q'][:, :, :, None, :], mk_all, mv_all, sc_m)
    out_lat = merge(ya, yb, yc, pl['gates'], w_o_gqa, w_o_na, w_o_mla, w_out)
    if not with_ctx:
        return out_lat, None
    bc, cl, _ = h_ctx.shape
    ca = sdpa_blocks(pc['ga_q'].reshape(bc, cl, GQA_KV_HEADS, GQA_GROUP, HEAD_DIM), pc['ga_k'], pc['ga_v'], sc_a)
    cb = sdpa_blocks(pc['na_q'][:, :, :, None, :], pc['na_k'], pc['na_v'], sc_a)
    cc = sdpa_blocks(pc['ml_q'][:, :, :, None, :], pc['ml_k'], pc['ml_v'], sc_m)
    out_ctx = merge(ca, cb, cc, pc['gates'], w_o_gqa, w_o_na, w_o_mla, w_out)
    return out_lat, out_ctx


def sq_relu_mlp(h, w1, w2):
    return jnp.square(jax.nn.relu(h @ w1)) @ w2


def setup_inputs(seed: int = 0) -> dict:
    key = jax.random.key(seed)
    ks = jax.random.split(key, 24)
    d = D_MODEL

    def nrm(k, shape, scale):
        return jax.random.normal(k, shape, jnp.float32) * scale

    def gain(k, shape):
        return 1.0 + 0.01 * jax.random.normal(k, shape, jnp.float32)

    return {
        'x': nrm(ks[0], (BATCH, SEQ, d), 1.0),
        'c': nrm(ks[1], (BATCH, d), 1.0),
        'ctx': nrm(ks[2], (BATCH, CTX_LEN, d), 1.0),
        'c_ctx': nrm(ks[3], (d,), 1.0),
        'w_mod': nrm(ks[4], (DEPTH, d, 6 * d), 0.5 * d ** -0.5),
        'b_mod': nrm(ks[5], (DEPTH, 6 * d), 0.02),
        'norm1_g': gain(ks[6], (DEPTH, d)),
        'norm2_g': gain(ks[7], (DEPTH, d)),
        'w_in': nrm(ks[8], (DEPTH, d, W_IN_COLS), d ** -0.5),
        'gqa_q_norm': gain(ks[9], (DEPTH, HEAD_DIM)),
        'gqa_k_norm': gain(ks[10], (DEPTH, HEAD_DIM)),
        'na_rpb': nrm(ks[11], (DEPTH, NA_HEADS, 2 * NA_KH - 1, 2 * NA_KW - 1), 0.1),
        'mla_kv_norm': gain(ks[12], (DEPTH, MLA_KV_RANK)),
        'mla_w_uk': nrm(ks[13], (DEPTH, MLA_KV_RANK, MLA_HEADS * MLA_NOPE), MLA_KV_RANK ** -0.5),
        'mla_w_uv': nrm(ks[14], (DEPTH, MLA_KV_RANK, MLA_HEADS * MLA_V), MLA_KV_RANK ** -0.5),
        'w_o_gqa': nrm(ks[15], (DEPTH, GQA_HEADS * HEAD_DIM, d), (GQA_HEADS * HEAD_DIM) ** -0.5),
        'w_o_na': nrm(ks[16], (DEPTH, NA_HEADS * HEAD_DIM, d), (NA_HEADS * HEAD_DIM) ** -0.5),
        'w_o_mla': nrm(ks[17], (DEPTH, MLA_HEADS * MLA_V, d), (MLA_HEADS * MLA_V) ** -0.5),
        'w_out': nrm(ks[18], (DEPTH, d, d), d ** -0.5),
        'w_mlp1': nrm(ks[19], (DEPTH, d, D_FF), d ** -0.5),
        'w_mlp2': nrm(ks[20], (DEPTH, D_FF, d), D_FF ** -0.5),
        'final_norm_g': gain(ks[21], (d,)),
    }


def reference(x, c, ctx, c_ctx, w_mod, b_mod, norm1_g, norm2_g, w_in, gqa_q_norm, gqa_k_norm,
              na_rpb, mla_kv_norm, mla_w_uk, mla_w_uv, w_o_gqa, w_o_na, w_o_mla, w_out,
              w_mlp1, w_mlp2, final_norm_g):
    s = x.shape[1]
    ang_a = axial_angles(s, HEAD_DIM)
    ang_m = axial_angles(s, MLA_ROPE)
    cond_lat = jax.nn.silu(c)
    cond_ctx = jax.nn.silu(c_ctx)[None, :]
    for l in range(DEPTH):
        with_ctx = l < DEPTH - 1
        m_lat = (cond_lat @ w_mod[l] + b_mod[l])[:, None, :]
        m_ctx = (cond_ctx @ w_mod[l] + b_mod[l])[:, None, :]
        sh1, sc1, g1, sh2, sc2, g2 = jnp.split(m_lat, 6, axis=-1)
        csh1, csc1, cg1, csh2, csc2, cg2 = jnp.split(m_ctx, 6, axis=-1)
        h_lat = modulate(rmsnorm(x, norm1_g[l]), sh1, sc1)
        h_ctx = modulate(rmsnorm(ctx, norm1_g[l]), csh1, csc1)
        a_lat, a_ctx = parallel_mixer(h_lat, h_ctx, w_in[l], gqa_q_norm[l], gqa_k_norm[l], na_rpb[l],
                                      mla_kv_norm[l], mla_w_uk[l], mla_w_uv[l], w_o_gqa[l], w_o_na[l],
                                      w_o_mla[l], w_out[l], ang_a, ang_m, with_ctx)
        x = x + g1 * a_lat
        x = x + g2 * sq_relu_mlp(modulate(rmsnorm(x, norm2_g[l]), sh2, sc2), w_mlp1[l], w_mlp2[l])
        if with_ctx:
            ctx = ctx + cg1 * a_ctx
            ctx = ctx + cg2 * sq_relu_mlp(modulate(rmsnorm(ctx, norm2_g[l]), csh2, csc2), w_mlp1[l], w_mlp2[l])
    return rmsnorm(x, final_norm_g)
```

```python
from contextlib import ExitStack
import os
import numpy as np
import ml_dtypes
import concourse.bass as bass
import concourse.mybir as mybir
from concourse.bass_utils import run_bass_kernel_spmd

F32 = mybir.dt.float32
BF16 = mybir.dt.bfloat16
AF = mybir.ActivationFunctionType
ALU = mybir.AluOpType

D = 1024
S = 8192
C = 256
SK = S + C
T = 2048
NAT = 2560
NAK = NAT + C
EPS = 1e-6
NEG = -30000.0
O_GQ, O_GK, O_GV, O_NQ, O_NK, O_NV, O_MQ, O_CKV, O_KR, O_GATE = 0, 512, 640, 768, 1280, 1792, 2304, 3072, 3328, 3360
WIN = 6432


class Sem:
    def __init__(self, nc, name):
        self.h = nc.alloc_semaphore(name)
        self.n = 0

    def inc(self, ins, k=1):
        ins.then_inc(self.h, k)
        self.n += k
        return (self, self.n)


def W(eng, tok):
    if tok is None:
        return
    if isinstance(tok, list):
        for t in tok:
            W(eng, t)
        return
    s, v = tok
    if v > 0:
        eng.wait_ge(s.h, v)


class KB:
    def __init__(self, nc):
        self.nc = nc
        self.PE, self.ACT, self.DVE, self.POOL, self.SP = nc.tensor, nc.scalar, nc.vector, nc.gpsimd, nc.sync
        self.sems = {}
        self.dram = {}

    def S(self, name):
        if name not in self.sems:
            self.sems[name] = Sem(self.nc, name)
        return self.sems[name]

    def dma(self, eng, out, in_, sem, slow=False):
        if slow:
            ins = eng.dma_start(out=out, in_=in_, allow_slow_non_contiguous=True)
        else:
            ins = eng.dma_start(out=out, in_=in_)
        return sem.inc(ins, 16)

    def barrier(self, toks):
        for e in (self.PE, self.ACT, self.DVE, self.POOL, self.SP):
            W(e, toks)


def mk_alloc(nc, es, pfx):
    def sb(name, shape, dt=F32):
        return es.enter_context(nc.sbuf_tensor(pfx + name, shape, dt))

    def ps(name, shape, dt=F32):
        return es.enter_context(nc.psum_tensor(pfx + name, shape, dt))

    return sb, ps


def phase_mod(K, L, pfx):
    nc = K.nc
    PE, ACT, DVE, POOL, SP = K.PE, K.ACT, K.DVE, K.POOL, K.SP
    with ExitStack() as es:
        sb, ps = mk_alloc(nc, es, pfx)
        cT = sb("cT", [128, 8, 2])
        sT = sb("sT", [128, 8, 2])
        wm = [sb(f"w{i}", [128, 8, 512]) for i in range(2)]
        bm = sb("b", [2, 6144])
        mrow = sb("m", [2, 6144])
        ng = sb("ng", [2, 2, 1024])
        mv = sb("mv", [2, 6, 1024])
        pm = [ps(f"p{i}", [2, 512]) for i in range(2)]
        ld = K.S("ld0")
        wl = [K.S("ld1"), K.S("ld2")]
        s_pe, s_ac, s_dv, st = K.S("pe"), K.S("ac"), K.S("dv"), K.S("st0")
        for m in range(2):
            K.dma(SP, cT[:, :, m], K.dram["cvec"][m].rearrange("(c p) -> p c", p=128), ld, slow=True)
        K.dma(SP, bm[:], L["b_mod"].partition_broadcast(2), ld)
        K.dma(SP, ng[:, 0, :], L["norm1_g"].partition_broadcast(2), ld)
        t_ld = K.dma(SP, ng[:, 1, :], L["norm2_g"].partition_broadcast(2), ld)
        W(ACT, t_ld)
        t_s = s_ac.inc(ACT.activation(out=sT[:].rearrange("p c m -> p (c m)"), in_=cT[:].rearrange("p c m -> p (c m)"), func=AF.Silu))
        W(PE, t_s)
        pe_t = [None] * 12
        dv_t = [None] * 12
        wsrc = L["w_mod"]
        for g in range(12):
            if g >= 2:
                W(SP, pe_t[g - 2])
            t_w = K.dma(SP, wm[g % 2][:], wsrc[:, g * 512:(g + 1) * 512].rearrange("(c p) n -> p c n", p=128), wl[g % 2])
            W(PE, t_w)
            if g >= 2:
                W(PE, dv_t[g - 2])
            for c in range(8):
                ins = PE.matmul(pm[g % 2][:], lhsT=sT[:, c, :], rhs=wm[g % 2][:, c, :], start=(c == 0), stop=(c == 7))
            pe_t[g] = s_pe.inc(ins)
            W(DVE, pe_t[g])
            if g == 0:
                W(DVE, t_ld)
            dv_t[g] = s_dv.inc(DVE.tensor_tensor(out=mrow[:, g * 512:(g + 1) * 512], in0=pm[g % 2][:], in1=bm[:, g * 512:(g + 1) * 512], op=ALU.add))
        W(DVE, dv_t[11])
        sl = lambda i: mrow[:, i * 1024:(i + 1) * 1024]
        DVE.scalar_tensor_tensor(out=mv[:, 0, :], in0=sl(1), scalar=1.0, in1=ng[:, 0, :], op0=ALU.add, op1=ALU.mult)
        DVE.tensor_copy(out=mv[:, 1, :], in_=sl(0))
        DVE.tensor_copy(out=mv[:, 2, :], in_=sl(2))
        DVE.scalar_tensor_tensor(out=mv[:, 3, :], in0=sl(4), scalar=1.0, in1=ng[:, 1, :], op0=ALU.add, op1=ALU.mult)
        DVE.tensor_copy(out=mv[:, 4, :], in_=sl(3))
        t_f = s_dv.inc(DVE.tensor_copy(out=mv[:, 5, :], in_=sl(5)))
        W(SP, t_f)
        t_st = K.dma(SP, K.dram["modv"], mv[:], st)
        K.barrier([t_st])


def phase_norm(K, jobs, pfx):
    nc = K.nc
    PE, ACT, DVE, POOL, SP = K.PE, K.ACT, K.DVE, K.POOL, K.SP
    tiles = []
    for ji, (src, ntok, dst, m, ia, ish) in enumerate(jobs):
        for i in range(ntok // 128):
            tiles.append((ji, i))
    NTI = len(tiles)
    with ExitStack() as es:
        sb, ps = mk_alloc(nc, es, pfx)
        xt = [sb(f"xt{i}", [128, 1024]) for i in range(2)]
        junk = sb("junk", [128, 1024])
        ss = sb("ss", [128, NTI])
        r1 = sb("r1", [128, NTI])
        r2 = sb("r2", [128, NTI])
        rstd = sb("rstd", [128, NTI])
        xn = [sb(f"xn{i}", [128, 1024]) for i in range(2)]
        hb = [sb(f"hb{i}", [128, 8, 512], BF16) for i in range(2)]
        ident = sb("ident", [128, 128])
        acol = sb("acol", [128, 2, 2, 8])
        pT = [ps(f"pT{i}", [128, 8, 128]) for i in range(2)]
        lds = [K.S("ld0"), K.S("ld1")]
        ldc = K.S("ld2")
        s_pe, s_ac, s_dv = K.S("pe"), K.S("ac"), K.S("dv")
        sts = [K.S("gs0"), K.S("gs1")]
        t_c = K.dma(SP, ident[:], K.dram["ident_f"], ldc)
        mods = sorted(set((j[3], j[4], j[5]) for j in jobs))
        assert len(set(m for m, _, _ in mods)) == len(mods)
        for (m, ia, ish) in mods:
            K.dma(SP, acol[:, m, 0, :], K.dram["modv"][m, ia].rearrange("(c p) -> p c", p=128), ldc, slow=True)
            t_c = K.dma(SP, acol[:, m, 1, :], K.dram["modv"][m, ish].rearrange("(c p) -> p c", p=128), ldc, slow=True)
        act_t = [None] * NTI
        for n, (ji, i) in enumerate(tiles):
            src = jobs[ji][0]
            if n >= 2:
                W(SP, act_t[n - 2])
            t_l = K.dma(SP, xt[n % 2][:], src(i) if callable(src) else src[i * 128:(i + 1) * 128, :], lds[n % 2])
            W(ACT, t_l)
            act_t[n] = s_ac.inc(ACT.activation(out=junk[:], in_=xt[n % 2][:], func=AF.Square, accum_out=ss[:, n:n + 1]))
        W(DVE, act_t[NTI - 1])
        t1 = s_dv.inc(DVE.tensor_scalar(out=r1[:], in0=ss[:], scalar1=1.0 / D, scalar2=EPS, op0=ALU.mult, op1=ALU.add))
        W(ACT, t1)
        t2 = s_ac.inc(ACT.activation(out=r2[:], in_=r1[:], func=AF.Sqrt))
        W(DVE, t2)
        t3 = s_dv.inc(DVE.reciprocal(out=rstd[:], in_=r2[:]))
        W(ACT, t3)
        W(SP, t2)
        W(PE, t_c)
        W(DVE, t_c)
        a_t = [None] * NTI
        p_t = [None] * NTI
        v_t = [None] * NTI
        st_t = {}
        blk = -1
        blk_of = []
        prev_key = None
        for n, (ji, i) in enumerate(tiles):
            key = (ji, i // 4)
            if key != prev_key:
                blk += 1
                prev_key = key
            blk_of.append(blk)
        for n, (ji, i) in enumerate(tiles):
            src, ntok, dst, m, ia, ish = jobs[ji]
            b = blk_of[n]
            if n >= 2:
                W(SP, a_t[n - 2])
            t_l = K.dma(SP, xt[n % 2][:], src(i) if callable(src) else src[i * 128:(i + 1) * 128, :], lds[n % 2])
            W(ACT, t_l)
            if n >= 2:
                W(ACT, p_t[n - 2])
            a_t[n] = s_ac.inc(ACT.activation(out=xn[n % 2][:], in_=xt[n % 2][:], func=AF.Copy, scale=rstd[:, n:n + 1]))
            W(PE, a_t[n])
            if n >= 2:
                W(PE, v_t[n - 2])
            for c in range(8):
                ins = PE.transpose(out=pT[n % 2][:, c, :], in_=xn[n % 2][:, c * 128:(c + 1) * 128], identity=ident[:])
            p_t[n] = s_pe.inc(ins)
            W(DVE, p_t[n])
            if (i % 4 == 0) and (b - 2) in st_t:
                W(DVE, st_t[b - 2])
            for c in range(8):
                ins = DVE.tensor_scalar(out=hb[b % 2][:, c, (i % 4) * 128:(i % 4 + 1) * 128], in0=pT[n % 2][:, c, :],
                                        scalar1=acol[:, m, 0, c:c + 1], scalar2=acol[:, m, 1, c:c + 1], op0=ALU.mult, op1=ALU.add)
            v_t[n] = s_dv.inc(ins)
            last_in_blk = (n + 1 == NTI) or (blk_of[n + 1] != b)
            if last_in_blk:
                nt = (i % 4 + 1) * 128
                t0 = (i // 4) * 512
                W(POOL, v_t[n])
                st_t[b] = K.dma(POOL, dst[:, t0:t0 + nt].rearrange("(c p) t -> p c t", p=128), hb[b % 2][:, :, 0:nt], sts[b % 2])
        K.barrier([st_t[blk], st_t.get(blk - 1)])


def phase_proj(K, L, pfx, with_ctx):
    nc = K.nc
    PE, ACT, DVE, POOL, SP = K.PE, K.ACT, K.DVE, K.POOL, K.SP
    dr = K.dram
    with ExitStack() as es:
        sb, ps = mk_alloc(nc, es, pfx)
        win = sb("win", [128, 8, O_GATE], BF16)
        wuk = sb("wuk", [128, 2, 512], BF16)
        wuv = sb("wuv", [128, 2, 512], BF16)
        onesbd = sb("onesbd", [128, 128])
        ones = sb("ones", [128, 128])
        pt128 = sb("pt128", [128, 128], BF16)
        pt96 = sb("pt96", [128, 128], BF16)
        pt32 = sb("pt32", [128, 128], BF16)
        gq = sb("gq", [128, 1])
        gk = sb("gk", [128, 1])
        kvg = sb("kvg", [128, 2])
        hblk = [sb(f"h{i}", [128, 8, 512], BF16) for i in range(2)]
        ckvn = sb("ckvn", [128, 2, 512], BF16)
        sqf = sb("sqf", [128, 512]); sqf2 = sb("sqf2", [128, 512])
        qf = sb("qf", [128, 512]); qf2 = sb("qf2", [128, 512])
        sd = sb("sd", [128, 512]); rs = sb("rs", [128, 512]); qn = sb("qn", [128, 512])
        t1b = sb("t1", [128, 512]); t2b = sb("t2", [128, 512])
        cosb = [sb(f"cos{i}", [128, 512]) for i in range(2)]
        sinb = [sb(f"sin{i}", [128, 512]) for i in range(2)]
        qb = sb("qb", [128, 512], BF16)
        outb = [sb(f"ob{i}", [128, 512], BF16) for i in range(2)]
        vout = [sb(f"vo{i}", [128, 8, 65], BF16) for i in range(2)]
        acc = [ps(f"acc{i}", [128, 512]) for i in range(2)]
        acc2 = ps("acc2", [128, 512])
        pss = ps("pss", [128, 512])
        prot = ps("prot", [128, 512])
        ptm = [ps(f"ptm{i}", [128, 512]) for i in range(2)]
        wl = K.S("ld2")
        gw = K.S("gw")
        hl = [K.S("ld0"), K.S("ld1")]
        tl = [K.S("ld3"), K.S("ld4")]
        s_pe, s_ac, s_dv, s_pl = K.S("pe"), K.S("ac"), K.S("dv"), K.S("pl")
        sto = [K.S("st0"), K.S("st1")]
        stv = [K.S("st2"), K.S("st3")]
        stc = K.S("st4")
        for c in range(8):
            K.dma(POOL, win[:, c, :], L["w_in"][c * 128:(c + 1) * 128, 0:O_GATE], gw)
        K.dma(POOL, wuk[:], L["mla_w_uk"].rearrange("(r p) n -> p r n", p=128), gw)
        t_gw = K.dma(POOL, wuv[:], L["mla_w_uv"].rearrange("(r p) n -> p r n", p=128), gw)
        K.dma(SP, onesbd[:], dr["onesbd_f"], wl)
        K.dma(SP, ones[:], dr["ones_f"], wl)
        K.dma(SP, pt128[:], dr["pt128"], wl)
        K.dma(SP, pt96[:], dr["pt96"], wl)
        K.dma(SP, pt32[:], dr["pt32"], wl)
        for hh in range(2):
            K.dma(SP, gq[hh * 64:(hh + 1) * 64, :], L["gqa_q_norm"].rearrange("(p o) -> p o", o=1), wl)
            K.dma(SP, gk[hh * 64:(hh + 1) * 64, :], L["gqa_k_norm"].rearrange("(p o) -> p o", o=1), wl)
        t_w = K.dma(SP, kvg[:], L["mla_kv_norm"].rearrange("(r p) -> p r", p=128), wl, slow=True)
        for i in range(2):
            DVE.memset(vout[i][:], 1.0)
        t_ms = s_dv.inc(DVE.memset(qn[:], 0.0))
        for e in (PE, ACT, DVE, POOL):
            W(e, t_w)
            W(e, t_gw)
        W(ACT, t_ms)

        st = {"k": 0, "rk": 0, "vk": 0, "acc_free": [None, None], "ob_free": [None, None], "tab_free": [None, None],
              "vo_free": [None, None], "ptm_free": [None, None], "hb_tok": None}

        def store(eng, dst, src, sem):
            return K.dma(eng, dst, src, sem)

        import os
        LIMIT = int(os.environ.get("PROJ_LIMIT", "1000000"))
        units = [0]

        def over():
            units[0] += 1
            return units[0] > LIMIT

        def fm_job(chunks, M, nt, norm_g, rope, dsts):
            if over():
                return
            k = st["k"]; st["k"] += 1
            a = acc[k % 2]
            W(PE, st["acc_free"][k % 2])
            W(PE, st["hb_tok"])
            for ci, (lt, rh) in enumerate(chunks):
                ins = PE.matmul(a[:M, :nt], lhsT=lt, rhs=rh, start=(ci == 0), stop=(ci == len(chunks) - 1))
            t_main = s_pe.inc(ins)
            ob = outb[k % 2]
            if norm_g is None and rope is None:
                W(ACT, t_main)
                W(ACT, st["ob_free"][k % 2])
                t_out = s_ac.inc(ACT.activation(out=ob[:M, :nt], in_=a[:M, :nt], func=AF.Copy))
                st["acc_free"][k % 2] = t_out
            else:
                if rope is not None:
                    r = st["rk"]; st["rk"] += 1
                    PT, cos_ap, sin_ap = rope
                    W(SP, st["tab_free"][r % 2])
                    K.dma(SP, cosb[r % 2][:M, :nt], cos_ap, tl[r % 2])
                    t_tab = K.dma(SP, sinb[r % 2][:M, :nt], sin_ap, tl[r % 2])
                W(DVE, t_main)
                t_qf = s_dv.inc(DVE.tensor_copy(out=qf[:M, :nt], in_=a[:M, :nt]))
                t_cur = t_qf
                cur = qf
                free_toks = [t_qf]
                if norm_g is not None:
                    W(ACT, t_qf)
                    t_sq = s_ac.inc(ACT.activation(out=sqf[:M, :nt], in_=qf[:M, :nt], func=AF.Square))
                    W(PE, t_sq)
                    t_ss = s_pe.inc(PE.matmul(pss[:M, :nt], lhsT=onesbd[:M, :M], rhs=sqf[:M, :nt], start=True, stop=True))
                    W(ACT, t_ss)
                    W(ACT, t_qf)
                    t_sd = s_ac.inc(ACT.activation(out=sd[:M, :nt], in_=pss[:M, :nt], func=AF.Sqrt, bias=EPS, scale=1.0 / 64))
                    W(DVE, t_sd)
                    t_rs = s_dv.inc(DVE.reciprocal(out=rs[:M, :nt], in_=sd[:M, :nt]))
                    W(DVE, t_rs)
                    if rope is None:
                        W(DVE, st["ob_free"][k % 2])
                        t_out = s_dv.inc(DVE.scalar_tensor_tensor(out=ob[:M, :nt], in0=qf[:M, :nt], scalar=norm_g, in1=rs[:M, :nt], op0=ALU.mult, op1=ALU.mult))
                    else:
                        t_cur = s_dv.inc(DVE.scalar_tensor_tensor(out=qn[:M, :nt], in0=qf[:M, :nt], scalar=norm_g, in1=rs[:M, :nt], op0=ALU.mult, op1=ALU.mult))
                        cur = qn
                st["acc_free"][k % 2] = free_toks
                if rope is not None:
                    W(ACT, t_cur)
                    t_qb = s_ac.inc(ACT.activation(out=qb[:M, :nt], in_=cur[:M, :nt], func=AF.Copy))
                    W(PE, t_qb)
                    t_rot = s_pe.inc(PE.matmul(prot[:M, :nt], lhsT=PT[:M, :M], rhs=qb[:M, :nt], start=True, stop=True))
                    W(POOL, t_cur)
                    W(POOL, t_tab)
                    t_t1 = s_pl.inc(POOL.tensor_tensor(out=t1b[:M, :nt], in0=cur[:M, :nt], in1=cosb[r % 2][:M, :nt], op=ALU.mult))
                    W(DVE, t_rot)
                    W(DVE, t_tab)
                    t_t2 = s_dv.inc(DVE.tensor_tensor(out=t2b[:M, :nt], in0=prot[:M, :nt], in1=sinb[r % 2][:M, :nt], op=ALU.mult))
                    W(DVE, t_t1)
                    W(DVE, t_t2)
                    W(DVE, st["ob_free"][k % 2])
                    t_out = s_dv.inc(DVE.tensor_tensor(out=ob[:M, :nt], in0=t1b[:M, :nt], in1=t2b[:M, :nt], op=ALU.add))
                    st["tab_free"][r % 2] = t_out
            W(SP, t_out)
            for (dst, r0, r1) in dsts:
                t_st = store(SP, dst, ob[r0:r1, :nt], sto[k % 2])
            st["ob_free"][k % 2] = t_st

        def ckv_job(hs, nt, dst_ckvt):
            if over():
                st["ckvn_tok"] = None
                return
            k = st["k"]; st["k"] += 1
            a = acc[k % 2]
            W(PE, st["acc_free"][k % 2])
            W(PE, st["hb_tok"])
            W(PE, st.get("acc2_free"))
            for g, aa in enumerate((a, acc2)):
                for c in range(8):
                    ins = PE.matmul(aa[:, :nt], lhsT=win[:, c, O_CKV + g * 128:O_CKV + (g + 1) * 128], rhs=hblk[hs][:, c, :nt], start=(c == 0), stop=(c == 7))
            t_main = s_pe.inc(ins)
            CUT = int(os.environ.get("CKV_CUT", "99"))
            st["ckvn_tok"] = None
            if CUT <= 1:
                return
            W(DVE, t_main)
            DVE.tensor_copy(out=qf[:, :nt], in_=a[:, :nt])
            t_qf = s_dv.inc(DVE.tensor_copy(out=qf2[:, :nt], in_=acc2[:, :nt]))
            W(ACT, t_qf)
            ACT.activation(out=sqf[:, :nt], in_=qf[:, :nt], func=AF.Square)
            t_sq = s_ac.inc(ACT.activation(out=sqf2[:, :nt], in_=qf2[:, :nt], func=AF.Square))
            st["acc_free"][k % 2] = [t_qf]
            st["acc2_free"] = [t_qf]
            if CUT <= 2:
                return
            W(PE, t_sq)
            PE.matmul(pss[:, :nt], lhsT=ones[:], rhs=sqf[:, :nt], start=True, stop=False)
            t_ss = s_pe.inc(PE.matmul(pss[:, :nt], lhsT=ones[:], rhs=sqf2[:, :nt], start=False, stop=True))
            if CUT <= 3:
                return
            W(ACT, t_ss)
            W(ACT, t_qf)
            t_sd = s_ac.inc(ACT.activation(out=sd[:, :nt], in_=pss[:, :nt], func=AF.Sqrt, bias=EPS, scale=1.0 / 256))
            W(DVE, t_sd)
            t_rs = s_dv.inc(DVE.reciprocal(out=rs[:, :nt], in_=sd[:, :nt]))
            if CUT <= 4:
                return
            W(DVE, t_rs)
            W(DVE, st.get("ckvn_free"))
            DVE.scalar_tensor_tensor(out=ckvn[:, 0, :nt], in0=qf[:, :nt], scalar=kvg[:, 0:1], in1=rs[:, :nt], op0=ALU.mult, op1=ALU.mult)
            t_out = s_dv.inc(DVE.scalar_tensor_tensor(out=ckvn[:, 1, :nt], in0=qf2[:, :nt], scalar=kvg[:, 1:2], in1=rs[:, :nt], op0=ALU.mult, op1=ALU.mult))
            if CUT <= 5:
                return
            W(SP, t_out)
            t_st = store(SP, dst_ckvt.rearrange("(r p) t -> p r t", p=128), ckvn[:, :, :nt], stc)
            st["ckvn_tok"] = t_out
            st["ckvn_st"] = t_st

        def tm_job(chunks, N, nh, dst):
            if over():
                return
            j = st["vk"]; st["vk"] += 1
            p = ptm[j % 2]
            W(PE, st["ptm_free"][j % 2])
            W(PE, st["hb_tok"])
            for ci, (lt, rh) in enumerate(chunks):
                ins = PE.matmul(p[:, :N], lhsT=lt, rhs=rh, start=(ci == 0), stop=(ci == len(chunks) - 1))
            t_main = s_pe.inc(ins)
            W(ACT, t_main)
            W(ACT, st["vo_free"][j % 2])
            t_o = s_ac.inc(ACT.activation(out=vout[j % 2][:, 0:nh, 0:64], in_=p[:, :N].rearrange("p (h d) -> p h d", d=64), func=AF.Copy))
            st["ptm_free"][j % 2] = t_o
            W(SP, t_o)
            st["vo_free"][j % 2] = store(SP, dst, vout[j % 2][:, 0:nh, :], stv[j % 2])

        nblk = [0]
        last_users = [None, None]

        def load_block(src_ap, nt):
            b = nblk[0]; nblk[0] += 1
            W(SP, last_users[b % 2])
            st["hb_tok"] = K.dma(SP, hblk[b % 2][:, :, :nt], src_ap.rearrange("(c p) t -> p c t", p=128), hl[b % 2])
            return b % 2

        def done_block(hs):
            last_users[hs] = (s_pe, s_pe.n)

        def hch(hs, c0, M, nt):
            return [(win[:, c, c0:c0 + M], hblk[hs][:, c, :nt]) for c in range(8)]

        hT_all, hT_na = dr["hT_all"], dr["hT_na"]
        for tb in range(17):
            ctxb = (tb == 16)
            t0 = tb * 512
            nt = 256 if ctxb else 512
            hs = load_block(hT_all[:, t0:t0 + nt], nt)
            rope = None if ctxb else (pt128, dr["cosA_all"][:, t0:t0 + nt], dr["sinA_all"][:, t0:t0 + nt])
            fm_job(hch(hs, O_GK, 128, nt), 128, nt, gk[:, 0:1], rope,
                   [(dr["GKT"][0, :, t0:t0 + nt], 0, 64), (dr["GKT"][1, :, t0:t0 + nt], 64, 128)])
            ckv_job(hs, nt, dr["CKVT"][:, t0:t0 + nt])
            rope = None if ctxb else (pt32, dr["cosM_all"][:, t0:t0 + nt], dr["sinM_all"][:, t0:t0 + nt])
            fm_job(hch(hs, O_KR, 32, nt), 32, nt, None, rope, [(dr["MKT"][h, 64:96, t0:t0 + nt], 0, 32) for h in range(8)])
            W(PE, st["ckvn_tok"])
            for g in range(4):
                fm_job([(wuk[:, r, g * 128:(g + 1) * 128], ckvn[:, r, :nt]) for r in range(2)], 128, nt, None, None,
                       [(dr["MKT"][2 * g, 0:64, t0:t0 + nt], 0, 64), (dr["MKT"][2 * g + 1, 0:64, t0:t0 + nt], 64, 128)])
            for ti in range(nt // 128):
                tsl = slice(ti * 128, (ti + 1) * 128)
                r0 = t0 + ti * 128
                tm_job([(ckvn[:, r, tsl], wuv[:, r, :]) for r in range(2)], 512, 8, dr["MV"][r0:r0 + 128, :, :])
                tm_job([(hblk[hs][:, c, tsl], win[:, c, O_GV:O_GV + 128]) for c in range(8)], 128, 2, dr["GV"][r0:r0 + 128, :, :])
            st["ckvn_free"] = (s_pe, s_pe.n)
            if ctxb:
                for g in range(4):
                    fm_job(hch(hs, O_NK + g * 128, 128, nt), 128, nt, None, None,
                           [(dr["NKT"][2 * g, :, NAT:NAT + nt], 0, 64), (dr["NKT"][2 * g + 1, :, NAT:NAT + nt], 64, 128)])
                for ti in range(nt // 128):
                    tsl = slice(ti * 128, (ti + 1) * 128)
                    tm_job([(hblk[hs][:, c, tsl], win[:, c, O_NV:O_NV + 512]) for c in range(8)], 512, 8, dr["NV"][NAT + ti * 128:NAT + (ti + 1) * 128, :, :])
                if with_ctx:
                    q0 = T
                    for g in range(4):
                        fm_job(hch(hs, O_GQ + g * 128, 128, nt), 128, nt, gq[:, 0:1], None,
                               [(dr["GQT"][2 * g, :, q0:q0 + nt], 0, 64), (dr["GQT"][2 * g + 1, :, q0:q0 + nt], 64, 128)])
                        fm_job(hch(hs, O_NQ + g * 128, 128, nt), 128, nt, None, None,
                               [(dr["NQT"][2 * g, :, q0:q0 + nt], 0, 64), (dr["NQT"][2 * g + 1, :, q0:q0 + nt], 64, 128)])
                    for h in range(8):
                        fm_job(hch(hs, O_MQ + h * 96, 96, nt), 96, nt, None, None, [(dr["MQT"][h, :, q0:q0 + nt], 0, 96)])
            done_block(hs)
        for tb in range(5):
            t0 = tb * 512
            nt = 512
            hs = load_block(hT_na[:, t0:t0 + nt], nt)
            for g in range(4):
                fm_job(hch(hs, O_NK + g * 128, 128, nt), 128, nt, None, None,
                       [(dr["NKT"][2 * g, :, t0:t0 + nt], 0, 64), (dr["NKT"][2 * g + 1, :, t0:t0 + nt], 64, 128)])
            for ti in range(4):
                tsl = slice(ti * 128, (ti + 1) * 128)
                tm_job([(hblk[hs][:, c, tsl], win[:, c, O_NV:O_NV + 512]) for c in range(8)], 512, 8, dr["NV"][t0 + ti * 128:t0 + (ti + 1) * 128, :, :])
            done_block(hs)
        for tb in range(4):
            q0 = tb * 512
            nt = 512
            hs = load_block(hT_na[:, 256 + q0:256 + q0 + nt], nt)
            for g in range(4):
                fm_job(hch(hs, O_GQ + g * 128, 128, nt), 128, nt, gq[:, 0:1], (pt128, dr["cosA_own"][:, q0:q0 + nt], dr["sinA_own"][:, q0:q0 + nt]),
                       [(dr["GQT"][2 * g, :, q0:q0 + nt], 0, 64), (dr["GQT"][2 * g + 1, :, q0:q0 + nt], 64, 128)])
                fm_job(hch(hs, O_NQ + g * 128, 128, nt), 128, nt, None, None,
                       [(dr["NQT"][2 * g, :, q0:q0 + nt], 0, 64), (dr["NQT"][2 * g + 1, :, q0:q0 + nt], 64, 128)])
            for h in range(8):
                fm_job(hch(hs, O_MQ + h * 96, 96, nt), 96, nt, None, (pt96, dr["cosM_own"][:, q0:q0 + nt], dr["sinM_own"][:, q0:q0 + nt]),
                       [(dr["MQT"][h, :, q0:q0 + nt], 0, 96)])
            done_block(hs)
        K.barrier([(s, s.n) for s in sto + stv + [stc]])


def phase_attn(K, heads, pfx, nkmax):
    nc = K.nc
    PE, ACT, DVE, POOL, SP = K.PE, K.ACT, K.DVE, K.POOL, K.SP
    NQ = T + C
    with ExitStack() as es:
        sb, ps = mk_alloc(nc, es, pfx)
        ktb = [sb(f"kt{i}", [128, nkmax], BF16) for i in range(2)]
        vb = [sb(f"v{i}", [128, nkmax // 128, 65], BF16) for i in range(2)]
        qb = [sb(f"q{i}", [128, NQ], BF16) for i in range(2)]
        pbuf = [sb(f"p{i}", [128, 512], BF16) for i in range(3)]
        bb = [sb(f"bias{i}", [128, 512]) for i in range(3)]
        sbs = [sb(f"sb{i}", [128, 512]) for i in range(2)]
        osb = sb("osb", [128, 512])
        rl = sb("rl", [128, 512])
        ones = sb("ones", [128, 128])
        ysb = [sb(f"y{i}", [128, 512], BF16) for i in range(2)]
        psb = [ps(f"s{i}", [128, 512]) for i in range(3)]
        po = [ps(f"o{i}", [128, 512]) for i in range(2)]
        pbc = ps("bc", [128, 512])
        hl = [K.S("ld0"), K.S("ld1")]
        bl = [K.S("ld2"), K.S("ld3"), K.S("ld4")]
        cl = K.S("ld5")
        s_pe, s_ac, s_dv = K.S("pe"), K.S("ac"), K.S("dv")
        sty = [K.S("gs0"), K.S("gs1")]
        t_c = K.dma(SP, ones[:], K.dram["ones_f"], cl)
        t_m = s_dv.inc(DVE.memset(rl[:], 1.0))
        W(PE, t_c)
        W(PE, t_m)
        steps = []
        for hi, h in enumerate(heads):
            for bi, b in enumerate(h["blocks"]):
                nt_ = len(b["tiles"])
                for si, (kti, bias) in enumerate(b["tiles"]):
                    steps.append(dict(hi=hi, b=b, kti=kti, bias=bias, first=(si == 0), last=(si == nt_ - 1),
                                      hfirst=(bi == 0 and si == 0), hlast=(bi == len(h["blocks"]) - 1 and si == nt_ - 1)))
        NS = len(steps)
        head_tok = [None] * len(heads)
        head_done = [None] * len(heads)

        def load_head(hi):
            h = heads[hi]
            s = hi % 2
            if hi >= 2:
                W(SP, head_done[hi - 2])
            dk, nk = h["dk"], h["nk"]
            K.dma(SP, ktb[s][:dk, :nk], h["kt"], hl[s])
            K.dma(SP, vb[s][:, :nk // 128, :], h["v"].rearrange("(t p) e -> p t e", p=128), hl[s])
            head_tok[hi] = K.dma(SP, qb[s][:dk, :], h["qt"], hl[s])

        tq = [None] * NS
        tb_ = [None] * NS
        te = [None] * NS
        tv = [None] * NS
        bias_ld = [None] * NS
        nbias = [0]
        bidx = [None] * NS
        blk_id = [0]
        po_free = [None, None]
        y_free = [None, None]
        pend_pe = {}
        pend_dv = {}
        state = {"bc_tok": None, "y_tok": None}

        def emit_qk(t):
            s = steps[t]
            h = heads[s["hi"]]
            hs = s["hi"] % 2
            if s["hfirst"]:
                W(PE, head_tok[s["hi"]])
            if t >= 3:
                W(PE, tb_[t - 3] if steps[t - 3]["bias"] is not None else te[t - 3])
            b = s["b"]
            dk = h["dk"]
            tq[t] = s_pe.inc(PE.matmul(psb[t % 3][:, :b["nq"]], lhsT=ktb[hs][:dk, s["kti"] * 128:(s["kti"] + 1) * 128],
                                       rhs=qb[hs][:dk, b["q0"]:b["q0"] + b["nq"]], start=True, stop=True))

        def emit_bias_load(t):
            s = steps[t]
            if s["bias"] is None:
                return
            n = nbias[0]; nbias[0] += 1
            bidx[t] = n
            W(SP, state.get(("bfree", n % 3)))
            bias_ld[t] = K.dma(SP, bb[n % 3][:, :s["b"]["nq"]], s["bias"], bl[n % 3])

        if NS > 0:
            load_head(0)
        LA = 2
        for t in range(min(LA, NS)):
            emit_bias_load(t)
            emit_qk(t)
        cur_blk = -1
        for t in range(NS):
            s = steps[t]
            h = heads[s["hi"]]
            b = s["b"]
            nq = b["nq"]
            hs = s["hi"] % 2
            if s["hfirst"] and s["hi"] + 1 < len(heads):
                load_head(s["hi"] + 1)
            if s["first"]:
                cur_blk += 1
            if t + LA < NS:
                emit_bias_load(t + LA)
                emit_qk(t + LA)
            if s["bias"] is not None:
                n = bidx[t]
                W(DVE, tq[t])
                W(DVE, bias_ld[t])
                if t >= 2:
                    W(DVE, te[t - 2])
                tb_[t] = s_dv.inc(DVE.scalar_tensor_tensor(out=sbs[t % 2][:, :nq], in0=psb[t % 3][:, :nq], scalar=float(h["scale"]),
                                                           in1=bb[n % 3][:, :nq], op0=ALU.mult, op1=ALU.add))
                state[("bfree", n % 3)] = tb_[t]
                W(ACT, tb_[t])
                if t >= 3:
                    W(ACT, tv[t - 3])
                te[t] = s_ac.inc(ACT.activation(out=pbuf[t % 3][:, :nq], in_=sbs[t % 2][:, :nq], func=AF.Exp))
            else:
                W(ACT, tq[t])
                if t >= 3:
                    W(ACT, tv[t - 3])
                te[t] = s_ac.inc(ACT.activation(out=pbuf[t % 3][:, :nq], in_=psb[t % 3][:, :nq], func=AF.Exp, scale=float(h["scale"])))
            for f in pend_pe.pop(t, []):
                f()
            W(PE, te[t])
            if s["first"]:
                W(PE, po_free[cur_blk % 2])
            tv[t] = s_pe.inc(PE.matmul(po[cur_blk % 2][:65, :nq], lhsT=vb[hs][:, s["kti"], :], rhs=pbuf[t % 3][:, :nq],
                                       start=s["first"], stop=s["last"]))
            if s["hlast"]:
                head_done[s["hi"]] = tv[t]
            for f in pend_dv.pop(t, []):
                f()
            if s["last"]:
                cb = cur_blk
                W(DVE, tv[t])
                t_o = s_dv.inc(DVE.tensor_copy(out=osb[:65, :nq], in_=po[cb % 2][:65, :nq]))
                po_free[cb % 2] = t_o
                W(DVE, t_o)
                t_rl = s_dv.inc(DVE.reciprocal(out=rl[64:65, :nq], in_=osb[64:65, :nq]))

                def pe_part(nq=nq, t_rl=t_rl):
                    W(PE, t_rl)
                    W(PE, state["y_tok"])
                    state["bc_tok"] = s_pe.inc(PE.matmul(pbc[:64, :nq], lhsT=ones[64:65, 0:64], rhs=rl[64:65, :nq], start=True, stop=True))

                def dv_part(nq=nq, cb=cb, yt=b["yt"]):
                    W(DVE, state["bc_tok"])
                    W(DVE, y_free[cb % 2])
                    state["y_tok"] = s_dv.inc(DVE.tensor_tensor(out=ysb[cb % 2][:64, :nq], in0=osb[:64, :nq], in1=pbc[:64, :nq], op=ALU.mult))
                    W(POOL, state["y_tok"])
                    y_free[cb % 2] = K.dma(POOL, yt, ysb[cb % 2][:64, :nq], sty[cb % 2])

                if t + 1 < NS:
                    nxt_len = len(steps[t + 1]["b"]["tiles"])
                    d = min(2, nxt_len - 1)
                    pend_pe.setdefault(t + max(d, 1) if nxt_len > 1 else t + 1, []).append(pe_part)
                    pend_dv.setdefault(t + max(d, 1) if nxt_len > 1 else t + 1, []).append(dv_part)
                else:
                    pe_part()
                    dv_part()
        assert not pend_pe and not pend_dv
        K.barrier([(s, s.n) for s in sty])


def phase_merge(K, L, pfx, qblocks, x_src, x_dst):
    nc = K.nc
    PE, ACT, DVE, POOL, SP = K.PE, K.ACT, K.DVE, K.POOL, K.SP
    dr = K.dram
    with ExitStack() as es:
        sb, ps = mk_alloc(nc, es, pfx)
        wg = sb("wg", [128, 8, 3072], BF16)
        wo = [sb(f"wo{i}", [128, 4, 1024], BF16) for i in range(3)]
        wout = sb("wout", [128, 8, 1024], BF16)
        g1 = sb("g1", [128, 2, 1024])
        hblk = [sb(f"h{i}", [128, 8, 512], BF16) for i in range(2)]
        yb = [[sb(f"y{r}_{i}", [128, 4, 512], BF16) for r in range(3)] for i in range(2)]
        sg = [sb(f"sg{i}", [128, 512]) for i in range(2)]
        yacc = sb("yacc", [128, 512])
        tmp = sb("tmp", [128, 512])
        yT = sb("yT", [128, 8, 512], BF16)
        xt = [sb(f"xt{i}", [128, 1024]) for i in range(2)]
        xo = [sb(f"xo{i}", [128, 1024]) for i in range(2)]
        tm2 = [sb(f"tm{i}", [128, 512]) for i in range(2)]
        pg = [ps(f"pg{i}", [128, 512]) for i in range(2)]
        pbr = [ps(f"pb{i}", [128, 512]) for i in range(2)]
        pw = [ps(f"pw{i}", [128, 512]) for i in range(2)]
        wl = K.S("ld2")
        hl = [K.S("ld0"), K.S("ld1")]
        xl = [K.S("ld3"), K.S("ld4")]
        s_pe, s_ac, s_dv, s_pl = K.S("pe"), K.S("ac"), K.S("dv"), K.S("pl")
        stx = [K.S("st0"), K.S("st1")]
        gw = K.S("gw")
        for c in range(8):
            K.dma(POOL, wg[:, c, :], L["w_in"][c * 128:(c + 1) * 128, O_GATE:WIN], gw)
        for r, nm in enumerate(("w_o_gqa", "w_o_na", "w_o_mla")):
            K.dma(POOL, wo[r][:], L[nm].rearrange("(c p) n -> p c n", p=128), gw)
        t_gw = K.dma(POOL, wout[:], L["w_out"].rearrange("(c p) n -> p c n", p=128), gw)
        K.dma(SP, g1[:, 0, :], dr["modv"][0, 2].partition_broadcast(128), wl)
        t_w = K.dma(SP, g1[:, 1, :], dr["modv"][1, 2].partition_broadcast(128), wl)
        for e in (PE, DVE, POOL):
            W(e, t_w)
            W(e, t_gw)
        ysrc = (dr["YAT"], dr["YBT"], dr["YCT"])
        blk_done = [None, None]
        k = 0
        xk = 0
        sg_free = [None, None]
        pg_free = [None, None]
        pbr_free = [None, None]
        pw_free = [None, None]
        xt_free = [None, None]
        xo_free = [None, None]
        tm_free = [None, None]
        yT_free = None
        for bi, (hT_ap, q0, nt, m) in enumerate(qblocks):
            s = bi % 2
            W(SP, blk_done[s])
            K.dma(SP, hblk[s][:, :, :nt], hT_ap.rearrange("(c p) t -> p c t", p=128), hl[s])
            for r in range(3):
                t_l = K.dma(SP, yb[s][r][:, :, :nt], ysrc[r][:, q0:q0 + nt].rearrange("(c p) t -> p c t", p=128), hl[s])
            W(PE, t_l)
            for oc in range(8):
                for r in range(3):
                    W(PE, pg_free[k % 2])
                    for c in range(8):
                        ins = PE.matmul(pg[k % 2][:, :nt], lhsT=wg[:, c, r * 1024 + oc * 128:r * 1024 + (oc + 1) * 128], rhs=hblk[s][:, c, :nt],
                                        start=(c == 0), stop=(c == 7))
                    t_g = s_pe.inc(ins)
                    W(PE, pbr_free[k % 2])
                    for c in range(4):
                        ins = PE.matmul(pbr[k % 2][:, :nt], lhsT=wo[r][:, c, oc * 128:(oc + 1) * 128], rhs=yb[s][r][:, c, :nt], start=(c == 0), stop=(c == 3))
                    t_b = s_pe.inc(ins)
                    W(ACT, t_g)
                    W(ACT, sg_free[k % 2])
                    t_s = s_ac.inc(ACT.activation(out=sg[k % 2][:, :nt], in_=pg[k % 2][:, :nt], func=AF.Sigmoid))
                    pg_free[k % 2] = t_s
                    W(DVE, t_s)
                    W(DVE, t_b)
                    if r == 0:
                        t_d = s_dv.inc(DVE.tensor_tensor(out=yacc[:, :nt], in0=sg[k % 2][:, :nt], in1=pbr[k % 2][:, :nt], op=ALU.mult))
                    else:
                        t_d = s_dv.inc(DVE.tensor_tensor(out=tmp[:, :nt], in0=sg[k % 2][:, :nt], in1=pbr[k % 2][:, :nt], op=ALU.mult))
                        W(DVE, t_d)
                        if r == 1:
                            t_d = s_dv.inc(DVE.tensor_tensor(out=yacc[:, :nt], in0=yacc[:, :nt], in1=tmp[:, :nt], op=ALU.add))
                        else:
                            if oc == 0:
                                W(DVE, yT_free)
                            t_d = s_dv.inc(DVE.tensor_tensor(out=yT[:, oc, :nt], in0=yacc[:, :nt], in1=tmp[:, :nt], op=ALU.add))
                    sg_free[k % 2] = t_d
                    pbr_free[k % 2] = t_d
                    k += 1
            blk_done[s] = (s_pe, s_pe.n)
            t_y = t_d
            W(PE, t_y)
            for ti in range(nt // 128):
                xs_ = xk % 2
                W(SP, xt_free[xs_])
                t_x = K.dma(SP, xt[xs_][:], x_src(q0 + ti * 128), xl[xs_])
                for half in range(2):
                    j = 2 * xk + half
                    W(PE, pw_free[j % 2])
                    for c in range(8):
                        ins = PE.matmul(pw[j % 2][:, :], lhsT=yT[:, c, ti * 128:(ti + 1) * 128], rhs=wout[:, c, half * 512:(half + 1) * 512],
                                        start=(c == 0), stop=(c == 7))
                    t_p = s_pe.inc(ins)
                    W(DVE, t_p)
                    W(DVE, tm_free[j % 2])
                    t_m = s_dv.inc(DVE.tensor_tensor(out=tm2[j % 2][:], in0=pw[j % 2][:], in1=g1[:, m, half * 512:(half + 1) * 512], op=ALU.mult))
                    pw_free[j % 2] = t_m
                    W(POOL, t_m)
                    W(POOL, t_x)
                    if half == 0:
                        W(POOL, xo_free[xs_])
                    t_a = s_pl.inc(POOL.tensor_tensor(out=xo[xs_][:, half * 512:(half + 1) * 512], in0=tm2[j % 2][:], in1=xt[xs_][:, half * 512:(half + 1) * 512], op=ALU.add))
                    tm_free[j % 2] = t_a
                xt_free[xs_] = t_a
                W(SP, t_a)
                xo_free[xs_] = K.dma(SP, x_dst(q0 + ti * 128), xo[xs_][:], stx[xs_])
                xk += 1
            yT_free = (s_pe, s_pe.n)
        K.barrier([(s_, s_.n) for s_ in stx])


def phase_mlp(K, L, pfx, qblocks, x_src, x_dst, final_g):
    nc = K.nc
    PE, ACT, DVE, POOL, SP = K.PE, K.ACT, K.DVE, K.POOL, K.SP
    dr = K.dram
    with ExitStack() as es:
        sb, ps = mk_alloc(nc, es, pfx)
        w1 = sb("w1", [128, 8, 4096], BF16)
        w2 = sb("w2", [128, 32, 1024], BF16)
        g2 = sb("g2", [128, 2, 1024])
        fg = sb("fg", [128, 1024])
        hblk = [sb(f"h{i}", [128, 8, 256], BF16) for i in range(2)]
        uT = sb("uT", [128, 32, 256], BF16)
        rb = [sb(f"r{i}", [128, 256]) for i in range(2)]
        xt = [sb(f"xt{i}", [128, 1024]) for i in range(2)]
        xo = [sb(f"xo{i}", [128, 1024]) for i in range(2)]
        tm2 = [sb(f"tm{i}", [128, 512]) for i in range(2)]
        junk = sb("junk", [128, 1024])
        st4 = sb("st4", [128, 4])
        pu = [ps(f"pu{i}", [128, 512]) for i in range(2)]
        pw = [ps(f"pw{i}", [128, 512]) for i in range(2)]
        wl = K.S("ld2")
        hl = [K.S("ld0"), K.S("ld1")]
        xl = [K.S("ld3"), K.S("ld4")]
        s_pe, s_ac, s_dv, s_pl = K.S("pe"), K.S("ac"), K.S("dv"), K.S("pl")
        stx = [K.S("st0"), K.S("st1")]
        gw = K.S("gw")
        for c in range(8):
            K.dma(POOL, w1[:, c, :], L["w_mlp1"][c * 128:(c + 1) * 128, :], gw)
        for c4 in range(4):
            t_gw = K.dma(POOL, w2[:, c4 * 8:(c4 + 1) * 8, :], L["w_mlp2"][c4 * 1024:(c4 + 1) * 1024, :].rearrange("(c p) n -> p c n", p=128), gw)
        K.dma(SP, g2[:, 0, :], dr["modv"][0, 5].partition_broadcast(128), wl)
        if final_g is not None:
            K.dma(SP, fg[:], final_g.partition_broadcast(128), wl)
        t_w = K.dma(SP, g2[:, 1, :], dr["modv"][1, 5].partition_broadcast(128), wl)
        for e in (PE, DVE, POOL, ACT):
            W(e, t_w)
            W(e, t_gw)
        h2T = dr["h2T"]
        blk_done = [None, None]
        k = 0
        xk = 0
        pu_free = [None, None]
        rb_free = [None, None]
        pw_free = [None, None]
        xt_free = [None, None]
        xo_free = [None, None]
        tm_free = [None, None]
        uT_free = None
        for bi, (q0, nt, m) in enumerate(qblocks):
            s = bi % 2
            W(SP, blk_done[s])
            t_l = K.dma(SP, hblk[s][:, :, :nt], h2T[:, q0:q0 + nt].rearrange("(c p) t -> p c t", p=128), hl[s])
            W(PE, t_l)
            for fc in range(32):
                W(PE, pu_free[k % 2])
                for c in range(8):
                    ins = PE.matmul(pu[k % 2][:, :nt], lhsT=w1[:, c, fc * 128:(fc + 1) * 128], rhs=hblk[s][:, c, :nt], start=(c == 0), stop=(c == 7))
                t_u = s_pe.inc(ins)
                W(ACT, t_u)
                W(ACT, rb_free[k % 2])
                t_r = s_ac.inc(ACT.activation(out=rb[k % 2][:, :nt], in_=pu[k % 2][:, :nt], func=AF.Relu))
                pu_free[k % 2] = t_r
                W(DVE, t_r)
                if fc == 0:
                    W(DVE, uT_free)
                t_q = s_dv.inc(DVE.tensor_tensor(out=uT[:, fc, :nt], in0=rb[k % 2][:, :nt], in1=rb[k % 2][:, :nt], op=ALU.mult))
                rb_free[k % 2] = t_q
                k += 1
            blk_done[s] = (s_pe, s_pe.n)
            W(PE, t_q)
            for ti in range(nt // 128):
                xs_ = xk % 2
                W(SP, xt_free[xs_])
                t_x = K.dma(SP, xt[xs_][:], x_src(q0 + ti * 128), xl[xs_])
                for half in range(2):
                    j = 2 * xk + half
                    W(PE, pw_free[j % 2])
                    for fc in range(32):
                        ins = PE.matmul(pw[j % 2][:, :], lhsT=uT[:, fc, ti * 128:(ti + 1) * 128], rhs=w2[:, fc, half * 512:(half + 1) * 512],
                                        start=(fc == 0), stop=(fc == 31))
                    t_p = s_pe.inc(ins)
                    W(DVE, t_p)
                    W(DVE, tm_free[j % 2])
                    t_m = s_dv.inc(DVE.tensor_tensor(out=tm2[j % 2][:], in0=pw[j % 2][:], in1=g2[:, m, half * 512:(half + 1) * 512], op=ALU.mult))
                    pw_free[j % 2] = t_m
                    W(POOL, t_m)
                    W(POOL, t_x)
                    if half == 0:
                        W(POOL, xo_free[xs_])
                    t_a = s_pl.inc(POOL.tensor_tensor(out=xo[xs_][:, half * 512:(half + 1) * 512], in0=tm2[j % 2][:], in1=xt[xs_][:, half * 512:(half + 1) * 512], op=ALU.add))
                    tm_free[j % 2] = t_a
                xt_free[xs_] = t_a
                t_fin = t_a
                if final_g is not None:
                    W(ACT, t_a)
                    t1 = s_ac.inc(ACT.activation(out=junk[:], in_=xo[xs_][:], func=AF.Square, accum_out=st4[:, 0:1]))
                    W(DVE, t1)
                    t2 = s_dv.inc(DVE.tensor_scalar(out=st4[:, 1:2], in0=st4[:, 0:1], scalar1=1.0 / D, scalar2=EPS, op0=ALU.mult, op1=ALU.add))
                    W(ACT, t2)
                    t3 = s_ac.inc(ACT.activation(out=st4[:, 2:3], in_=st4[:, 1:2], func=AF.Sqrt))
                    W(DVE, t3)
                    t4 = s_dv.inc(DVE.reciprocal(out=st4[:, 3:4], in_=st4[:, 2:3]))
                    W(DVE, t4)
                    t_fin = s_dv.inc(DVE.scalar_tensor_tensor(out=xo[xs_][:], in0=xo[xs_][:], scalar=st4[:, 3:4], in1=fg[:], op0=ALU.mult, op1=ALU.mult))
                W(SP, t_fin)
                xo_free[xs_] = K.dma(SP, x_dst(q0 + ti * 128), xo[xs_][:], stx[xs_])
                xk += 1
            uT_free = (s_pe, s_pe.n)
        K.barrier([(s_, s_.n) for s_ in stx])


def phase_halo(K, pfx):
    nc = K.nc
    PE, ACT, DVE, POOL, SP = K.PE, K.ACT, K.DVE, K.POOL, K.SP
    dr = K.dram
    xg, x1, xna, sel = K.xg_at, K.x1_at, dr["x_na2"], dr["sel"]
    with ExitStack() as es:
        sb, ps = mk_alloc(nc, es, pfx)
        selb = sb("sel", [128, 8])
        cand = [sb(f"c{i}", [128, 1024]) for i in range(4)]
        acc = [sb(f"a{i}", [128, 1024]) for i in range(2)]
        ld = [K.S("ld0"), K.S("ld1"), K.S("ld3"), K.S("ld4")]
        lc = K.S("ld2")
        s_dv = K.S("dv")
        st = [K.S("st0"), K.S("st1")]
        so = K.S("st2")
        t_c = K.dma(SP, selb[:], sel.partition_broadcast(128), lc)
        for q in range(0, T, 128):
            t_own = K.dma(SP, xna[256 + q:256 + q + 128, :], x1(q), so)
        W(DVE, t_c)
        jobs = []
        for u in range(2):
            jobs.append((128 * u, [xg(2048 * r + 1792 + 128 * u) for r in range(4)], 0))
        for u in range(2):
            jobs.append((256 + T + 128 * u, [xg(2048 * r + 128 * u) for r in range(4)], 4))
        dv_prev = None
        st_t = [None, None]
        for n, (row0, srcs, c0) in enumerate(jobs):
            W(SP, dv_prev)
            lts = [K.dma(SP, cand[r][:], srcs[r], ld[r]) for r in range(4)]
            W(DVE, lts)
            W(DVE, st_t[n % 2])
            t = s_dv.inc(DVE.tensor_scalar(out=acc[n % 2][:], in0=cand[0][:], scalar1=selb[:, c0:c0 + 1], scalar2=0.0, op0=ALU.mult, op1=ALU.add))
            for r in range(1, 4):
                W(DVE, t)
                t = s_dv.inc(DVE.scalar_tensor_tensor(out=acc[n % 2][:], in0=cand[r][:], scalar=selb[:, c0 + r:c0 + r + 1], in1=acc[n % 2][:],
                                                      op0=ALU.mult, op1=ALU.add))
            dv_prev = t
            W(SP, t)
            st_t[n % 2] = K.dma(SP, xna[row0:row0 + 128, :], acc[n % 2][:], st[n % 2])
        K.barrier([t_own, st_t[0], st_t[1]])


W_NAMES = ["w_mod", "b_mod", "norm1_g", "norm2_g", "w_in", "gqa_q_norm", "gqa_k_norm", "mla_kv_norm", "mla_w_uk", "mla_w_uv",
           "w_o_gqa", "w_o_na", "w_o_mla", "w_out", "w_mlp1", "w_mlp2"]
W_SHAPES = {"w_mod": [D, 6 * D], "b_mod": [6 * D], "norm1_g": [D], "norm2_g": [D], "w_in": [D, WIN], "gqa_q_norm": [64], "gqa_k_norm": [64],
            "mla_kv_norm": [256], "mla_w_uk": [256, 512], "mla_w_uv": [256, 512], "w_o_gqa": [512, D], "w_o_na": [512, D], "w_o_mla": [512, D],
            "w_out": [D, D], "w_mlp1": [D, 4 * D], "w_mlp2": [4 * D, D]}
DEPTH = 2


def emit_layer(K, L, src, with_ctx, final, sfx):
    dr = K.dram
    NQ = T + C
    phase_mod(K, L, "md" + sfx)
    phase_norm(K, [(src["x_all"], S, dr["hT_all"][:, 0:S], 0, 0, 1), (src["ctx_in"], C, dr["hT_all"][:, S:SK], 1, 0, 1),
                   (src["x_na"], NAT, dr["hT_na"], 0, 0, 1)], "n1" + sfx)
    phase_proj(K, L, "pj" + sfx, with_ctx)
    qbl = [(512 * i, 512) for i in range(4)]
    heads = []
    for h in range(8):
        blocks = [dict(q0=q0, nq=nq, tiles=[(k, None) for k in range(66)], yt=dr["YAT"][64 * h:64 * h + 64, q0:q0 + nq]) for q0, nq in qbl]
        if with_ctx:
            blocks.append(dict(q0=T, nq=C, tiles=[(64, None), (65, None)], yt=dr["YAT"][64 * h:64 * h + 64, T:NQ]))
        heads.append(dict(kt=dr["GKT"][h // 4], v=dr["GV"][:, h // 4, :], qt=dr["GQT"][h], dk=64, scale=0.125, nk=SK, blocks=blocks))
    for h in range(8):
        blocks = [dict(q0=q0, nq=nq, tiles=[(k, None) for k in range(66)], yt=dr["YCT"][64 * h:64 * h + 64, q0:q0 + nq]) for q0, nq in qbl]
        if with_ctx:
            blocks.append(dict(q0=T, nq=C, tiles=[(64, None), (65, None)], yt=dr["YCT"][64 * h:64 * h + 64, T:NQ]))
        heads.append(dict(kt=dr["MKT"][h], v=dr["MV"][:, h, :], qt=dr["MQT"][h], dk=96, scale=96 ** -0.5, nk=SK, blocks=blocks))
    phase_attn(K, heads, "at" + sfx, SK)
    heads = []
    var = [0, 1, 1, 2]
    for h in range(8):
        blocks = []
        for i, (q0, nq) in enumerate(qbl):
            tiles = [(4 * i + m, src["nabias"][var[i], h, m]) for m in range(8)] + [(20, None), (21, None)]
            blocks.append(dict(q0=q0, nq=nq, tiles=tiles, yt=dr["YBT"][64 * h:64 * h + 64, q0:q0 + nq]))
        if with_ctx:
            blocks.append(dict(q0=T, nq=C, tiles=[(20, None), (21, None)], yt=dr["YBT"][64 * h:64 * h + 64, T:NQ]))
        heads.append(dict(kt=dr["NKT"][h], v=dr["NV"][:, h, :], qt=dr["NQT"][h], dk=64, scale=0.125, nk=NAK, blocks=blocks))
    phase_attn(K, heads, "na" + sfx, NAK)
    mblocks = [(dr["hT_na"][:, 256 + 512 * i:256 + 512 * (i + 1)], 512 * i, 512, 0) for i in range(4)]
    if with_ctx:
        mblocks.append((dr["hT_all"][:, S:SK], T, C, 1))

    def x_src(q):
        if q >= T:
            return src["ctx_in"][q - T:q - T + 128, :]
        return src["x_own"](q) if callable(src["x_own"]) else src["x_own"][q:q + 128, :]

    def xs1_at(q):
        return dr["xs1"][q:q + 128, :]

    phase_merge(K, L, "mg" + sfx, mblocks, x_src, xs1_at)
    njobs = [(dr["xs1"][0:T, :], T, dr["h2T"][:, 0:T], 0, 3, 4)]
    if with_ctx:
        njobs.append((dr["xs1"][T:NQ, :], C, dr["h2T"][:, T:NQ], 1, 3, 4))
    phase_norm(K, njobs, "n2" + sfx)
    fblocks = [(256 * i, 256, 0) for i in range(8)]
    if with_ctx:
        fblocks.append((T, C, 1))
    phase_mlp(K, L, "ml" + sfx, fblocks, xs1_at, src["x_dst"], L.get("final_norm_g") if final else None)


def build_fused():
    nc = bass.Bass("TRN2", target_bir_lowering=False)
    K = KB(nc)
    NQ = T + C
    dr = K.dram

    def inp(name, shape, dt=F32):
        dr[name] = nc.dram_tensor(name, shape, dt, kind="ExternalInput").ap()

    def internal(name, shape, dt=BF16):
        dr[name] = nc.dram_tensor(name, shape, dt).ap()

    inp("x_all", [S, D]); inp("x_own", [T, D]); inp("x_na", [NAT, D]); inp("ctx_in", [C, D]); inp("cvec", [2, D])
    Wst = {}
    for n in W_NAMES:
        Wst[n] = nc.dram_tensor(n, [DEPTH] + W_SHAPES[n], F32, kind="ExternalInput").ap()
    fng = nc.dram_tensor("final_norm_g", [D], F32, kind="ExternalInput").ap()
    for n in ("ident_f", "onesbd_f", "ones_f"):
        inp(n, [128, 128])
    for n in ("pt128", "pt96", "pt32"):
        inp(n, [128, 128], BF16)
    inp("cosA_all", [128, S]); inp("sinA_all", [128, S]); inp("cosA_own", [128, T]); inp("sinA_own", [128, T])
    inp("cosM_all", [32, S]); inp("sinM_all", [32, S]); inp("cosM_own", [96, T]); inp("sinM_own", [96, T])
    inp("nabias", [DEPTH, 3, 8, 8, 128, 512])
    inp("sel", [8])
    dr["xout"] = nc.dram_tensor("xout", [T, D], F32, kind="ExternalOutput").ap()
    internal("modv", [2, 6, D], F32)
    internal("hT_all", [D, SK]); internal("hT_na", [D, NAT])
    internal("GKT", [2, 64, SK]); internal("CKVT", [256, SK]); internal("MKT", [8, 96, SK])
    internal("MV", [SK, 8, 65]); internal("GV", [SK, 2, 65])
    internal("NKT", [8, 64, NAK]); internal("NV", [NAK, 8, 65])
    internal("GQT", [8, 64, NQ]); internal("NQT", [8, 64, NQ]); internal("MQT", [8, 96, NQ])
    internal("YAT", [512, NQ]); internal("YBT", [512, NQ]); internal("YCT", [512, NQ])
    internal("xs1", [NQ, D], F32); internal("h2T", [D, NQ])
    NCH = 8
    x1c = [nc.dram_tensor(f"x1c{k}", [256, D], F32) for k in range(NCH)]
    xgc = [nc.dram_tensor(f"xgc{k}", [4 * 256, D], F32) for k in range(NCH)]
    internal("c1loc", [C, D], F32); internal("x_na2", [NAT, D], F32)

    def x1_at(q):
        return x1c[q // 256].ap()[q % 256:q % 256 + 128, :]

    def xg_at(t0):
        r, k, off = t0 // 2048, (t0 % 2048) // 256, t0 % 256
        return xgc[k].ap()[r * 256 + off:r * 256 + off + 128, :]

    K.x1_at, K.xg_at = x1_at, xg_at

    for l in range(DEPTH):
        L = {n: Wst[n][l] for n in W_NAMES}
        final = (l == DEPTH - 1)
        with_ctx = not final
        if final:
            L["final_norm_g"] = fng
        if l == 0:
            src = dict(x_all=dr["x_all"], x_own=dr["x_own"], x_na=dr["x_na"], ctx_in=dr["ctx_in"])
        else:
            src = dict(x_all=(lambda i: xg_at(128 * i)), x_own=x1_at, x_na=dr["x_na2"], ctx_in=dr["c1loc"])
        src["nabias"] = dr["nabias"][l]
        if final:
            src["x_dst"] = lambda q: dr["xout"][q:q + 128, :]
        else:
            src["x_dst"] = lambda q: (x1_at(q) if q < T else dr["c1loc"][q - T:q - T + 128, :])
        emit_layer(K, L, src, with_ctx, final, f"{l}_")
        if not final:
            cc = K.S("cc")
            for k in range(NCH):
                ins = K.POOL.collective_compute("AllGather", mybir.AluOpType.bypass, replica_groups=[[0, 1, 2, 3], [4, 5, 6, 7]],
                                                ins=[x1c[k].ap().opt()], outs=[xgc[k].ap().opt()])
                t_cc = cc.inc(ins)
            K.barrier([t_cc])
            phase_halo(K, f"hl{l}_")
    K.semcounts = {n: s.n for n, s in K.sems.items()}
    return nc, K


def _rope_tables():
    t = np.arange(S, dtype=np.int32)
    row = (t // 64).astype(np.float32)
    col = (t % 64).astype(np.float32)

    def tabs(rot_dim):
        half = rot_dim // 2
        inv = (10000.0 ** (-np.arange(0, half, 2, dtype=np.float32) / np.float32(half))).astype(np.float32)
        ar = (row[:, None] * inv).astype(np.float32)
        ac = (col[:, None] * inv).astype(np.float32)
        cos = np.concatenate([np.cos(ar), np.cos(ar), np.cos(ac), np.cos(ac)], axis=1).T.astype(np.float32)
        sin = np.concatenate([np.sin(ar), np.sin(ar), np.sin(ac), np.sin(ac)], axis=1).T.astype(np.float32)
        return np.ascontiguousarray(cos), np.ascontiguousarray(sin)

    return tabs(64), tabs(32)


def _rot_matrix(n):
    q = n // 4
    P = np.zeros((n, n), np.float32)
    for base in (0, 2 * q):
        for i in range(q):
            P[base + i, base + q + i] = -1.0
            P[base + q + i, base + i] = 1.0
    return P


def _consts():
    c = {}
    c["ident_f"] = np.eye(128, dtype=np.float32)
    c["ones_f"] = np.ones((128, 128), np.float32)
    bd = np.zeros((128, 128), np.float32)
    bd[:64, :64] = 1.0
    bd[64:, 64:] = 1.0
    c["onesbd_f"] = bd
    P64 = _rot_matrix(64)
    P32 = _rot_matrix(32)
    pt128 = np.zeros((128, 128), np.float32)
    pt128[:64, :64] = P64.T
    pt128[64:, 64:] = P64.T
    pt96 = np.zeros((128, 128), np.float32)
    pt96[64:96, 64:96] = P32.T
    pt32 = np.zeros((128, 128), np.float32)
    pt32[:32, :32] = P32.T
    c["pt128"] = pt128.astype(ml_dtypes.bfloat16)
    c["pt96"] = pt96.astype(ml_dtypes.bfloat16)
    c["pt32"] = pt32.astype(ml_dtypes.bfloat16)
    return c


def _na_bias_tables(rpb, j):
    out = np.empty((3, 8, 8, 128, 512), np.float32)
    kcol = np.arange(64)[None, :, None, None]
    qcol = np.arange(64)[None, None, None, :]
    m = np.arange(16)[:, None, None, None]
    a = np.arange(8)[None, None, :, None]
    cs = np.clip(qcol - 8, 0, 48)
    colok = (kcol >= cs) & (kcol < cs + 16)
    cidx = np.clip(kcol - qcol + 15, 0, 30)
    for v, i in enumerate((0, 1, 3)):
        r = 32 * j + 8 * i + a
        k = 32 * j + 8 * i - 4 + m
        rs = np.clip(r - 4, 0, 120)
        ok = (k >= 0) & (k < 128) & (k >= rs) & (k < rs + 8) & colok
        ridx = np.clip(k - r + 7, 0, 14)
        ridx_b = np.broadcast_to(ridx, ok.shape)
        cidx_b = np.broadcast_to(cidx, ok.shape)
        vals = rpb[:, ridx_b, cidx_b]
        tab = np.where(ok[None], vals, np.float32(NEG)).astype(np.float32)
        out[v] = tab.reshape(8, 8, 128, 512)
    return out


def _core_inputs(x_full, ctx_full, c, c_ctx, Wd, consts, ropes):
    (cosA, sinA), (cosM, sinM) = ropes
    maps = []
    cosA2 = np.ascontiguousarray(np.concatenate([cosA, cosA], 0))
    sinA2 = np.ascontiguousarray(np.concatenate([sinA, sinA], 0))
    shared = dict(consts)
    for n in W_NAMES:
        shared[n] = np.ascontiguousarray(Wd[n])
    shared["final_norm_g"] = np.ascontiguousarray(Wd["final_norm_g"])
    shared["cosA_all"] = cosA2
    shared["sinA_all"] = sinA2
    shared["cosM_all"] = cosM
    shared["sinM_all"] = sinM
    rpb = np.asarray(Wd["na_rpb"], np.float32)
    nab = [np.stack([_na_bias_tables(rpb[l], j) for l in range(rpb.shape[0])], 0) for j in range(4)]
    for core in range(8):
        b, j = core // 4, core % 4
        t0 = T * j
        d = dict(shared)
        d["x_all"] = np.ascontiguousarray(x_full[b])
        d["x_own"] = np.ascontiguousarray(x_full[b, t0:t0 + T])
        xna = np.zeros((NAT, D), np.float32)
        lo = (32 * j - 4) * 64
        hi = lo + NAT
        slo, shi = max(lo, 0), min(hi, S)
        xna[slo - lo:shi - lo] = x_full[b, slo:shi]
        d["x_na"] = xna
        d["ctx_in"] = np.ascontiguousarray(ctx_full[b])
        d["cvec"] = np.ascontiguousarray(np.stack([c[b], c_ctx]))
        d["cosA_own"] = np.ascontiguousarray(cosA2[:, t0:t0 + T])
        d["sinA_own"] = np.ascontiguousarray(sinA2[:, t0:t0 + T])
        d["cosM_own"] = np.ascontiguousarray(np.concatenate([np.ones((64, T), np.float32), cosM[:, t0:t0 + T]], 0))
        d["sinM_own"] = np.ascontiguousarray(np.concatenate([np.zeros((64, T), np.float32), sinM[:, t0:t0 + T]], 0))
        d["nabias"] = nab[j]
        sel = np.zeros(8, np.float32)
        if j > 0:
            sel[j - 1] = 1.0
        if j < 3:
            sel[4 + j + 1] = 1.0
        d["sel"] = sel
        maps.append(d)
    return maps


_PROG = []


def kernel(**inputs):
    Wd = {k: np.asarray(v, np.float32) for k, v in inputs.items()}
    if not _PROG:
        _PROG.append(build_fused()[0])
    nc = _PROG[0]
    maps = _core_inputs(Wd["x"], Wd["ctx"], Wd["c"], Wd["c_ctx"], Wd, _consts(), _rope_tables())
    res = run_bass_kernel_spmd(nc, maps, core_ids=list(range(8)))
    outs = [r["xout"] for r in res.results]
    x = np.stack([np.concatenate([outs[4 * b + j] for j in range(4)], 0) for b in range(2)], 0)
    return np.ascontiguousarray(x.astype(np.float32))
```

```python
from contextlib import ExitStack
import os
import numpy as np
import ml_dtypes
import concourse.bass as bass
import concourse.mybir as mybir
from concourse.bass_utils import run_bass_kernel_spmd

F32 = mybir.dt.float32
BF16 = mybir.dt.bfloat16
AF = mybir.ActivationFunctionType
ALU = mybir.AluOpType

D = 1024
S = 8192
C = 256
SK = S + C
T = 2048
NAT = 2560
NAK = NAT + C
EPS = 1e-6
NEG = -30000.0
O_GQ, O_GK, O_GV, O_NQ, O_NK, O_NV, O_MQ, O_CKV, O_KR, O_GATE = 0, 512, 640, 768, 1280, 1792, 2304, 3072, 3328, 3360
WIN = 6432


class Sem:
    def __init__(self, nc, name):
        self.h = nc.alloc_semaphore(name)
        self.n = 0

    def inc(self, ins, k=1):
        ins.then_inc(self.h, k)
        self.n += k
        return (self, self.n)


def W(eng, tok):
    if tok is None:
        return
    if isinstance(tok, list):
        for t in tok:
            W(eng, t)
        return
    s, v = tok
    if v > 0:
        eng.wait_ge(s.h, v)


class KB:
    def __init__(self, nc):
        self.nc = nc
        self.PE, self.ACT, self.DVE, self.POOL, self.SP = nc.tensor, nc.scalar, nc.vector, nc.gpsimd, nc.sync
        self.sems = {}
        self.dram = {}

    def S(self, name):
        if name not in self.sems:
            self.sems[name] = Sem(self.nc, name)
        return self.sems[name]

    def dma(self, eng, out, in_, sem, slow=False):
        if slow:
            ins = eng.dma_start(out=out, in_=in_, allow_slow_non_contiguous=True)
        else:
            ins = eng.dma_start(out=out, in_=in_)
        return sem.inc(ins, 16)

    def barrier(self, toks):
        for e in (self.PE, self.ACT, self.DVE, self.POOL, self.SP):
            W(e, toks)


def mk_alloc(nc, es, pfx):
    def sb(name, shape, dt=F32):
        return es.enter_context(nc.sbuf_tensor(pfx + name, shape, dt))

    def ps(name, shape, dt=F32):
        return es.enter_context(nc.psum_tensor(pfx + name, shape, dt))

    return sb, ps


def phase_mod(K, L, pfx):
    nc = K.nc
    PE, ACT, DVE, POOL, SP = K.PE, K.ACT, K.DVE, K.POOL, K.SP
    with ExitStack() as es:
        sb, ps = mk_alloc(nc, es, pfx)
        cT = sb("cT", [128, 8, 2])
        sT = sb("sT", [128, 8, 2])
        wm = [sb(f"w{i}", [128, 8, 512]) for i in range(2)]
        bm = sb("b", [2, 6144])
        mrow = sb("m", [2, 6144])
        ng = sb("ng", [2, 2, 1024])
        mv = sb("mv", [2, 6, 1024])
        pm = [ps(f"p{i}", [2, 512]) for i in range(2)]
        ld = K.S("ld0")
        wl = [K.S("ld1"), K.S("ld2")]
        s_pe, s_ac, s_dv, st = K.S("pe"), K.S("ac"), K.S("dv"), K.S("st0")
        for m in range(2):
            K.dma(SP, cT[:, :, m], K.dram["cvec"][m].rearrange("(c p) -> p c", p=128), ld, slow=True)
        K.dma(SP, bm[:], L["b_mod"].partition_broadcast(2), ld)
        K.dma(SP, ng[:, 0, :], L["norm1_g"].partition_broadcast(2), ld)
        t_ld = K.dma(SP, ng[:, 1, :], L["norm2_g"].partition_broadcast(2), ld)
        W(ACT, t_ld)
        t_s = s_ac.inc(ACT.activation(out=sT[:].rearrange("p c m -> p (c m)"), in_=cT[:].rearrange("p c m -> p (c m)"), func=AF.Silu))
        W(PE, t_s)
        pe_t = [None] * 12
        dv_t = [None] * 12
        wsrc = L["w_mod"]
        for g in range(12):
            if g >= 2:
                W(SP, pe_t[g - 2])
            t_w = K.dma(SP, wm[g % 2][:], wsrc[:, g * 512:(g + 1) * 512].rearrange("(c p) n -> p c n", p=128), wl[g % 2])
            W(PE, t_w)
            if g >= 2:
                W(PE, dv_t[g - 2])
            for c in range(8):
                ins = PE.matmul(pm[g % 2][:], lhsT=sT[:, c, :], rhs=wm[g % 2][:, c, :], start=(c == 0), stop=(c == 7))
            pe_t[g] = s_pe.inc(ins)
            W(DVE, pe_t[g])
            if g == 0:
                W(DVE, t_ld)
            dv_t[g] = s_dv.inc(DVE.tensor_tensor(out=mrow[:, g * 512:(g + 1) * 512], in0=pm[g % 2][:], in1=bm[:, g * 512:(g + 1) * 512], op=ALU.add))
        W(DVE, dv_t[11])
        sl = lambda i: mrow[:, i * 1024:(i + 1) * 1024]
        DVE.scalar_tensor_tensor(out=mv[:, 0, :], in0=sl(1), scalar=1.0, in1=ng[:, 0, :], op0=ALU.add, op1=ALU.mult)
        DVE.tensor_copy(out=mv[:, 1, :], in_=sl(0))
        DVE.tensor_copy(out=mv[:, 2, :], in_=sl(2))
        DVE.scalar_tensor_tensor(out=mv[:, 3, :], in0=sl(4), scalar=1.0, in1=ng[:, 1, :], op0=ALU.add, op1=ALU.mult)
        DVE.tensor_copy(out=mv[:, 4, :], in_=sl(3))
        t_f = s_dv.inc(DVE.tensor_copy(out=mv[:, 5, :], in_=sl(5)))
        W(SP, t_f)
        t_st = K.dma(SP, K.dram["modv"], mv[:], st)
        K.barrier([t_st])


def phase_norm(K, jobs, pfx):
    nc = K.nc
    PE, ACT, DVE, POOL, SP = K.PE, K.ACT, K.DVE, K.POOL, K.SP
    tiles = []
    for ji, (src, ntok, dst, m, ia, ish) in enumerate(jobs):
        for i in range(ntok // 128):
            tiles.append((ji, i))
    NTI = len(tiles)
    with ExitStack() as es:
        sb, ps = mk_alloc(nc, es, pfx)
        xt = [sb(f"xt{i}", [128, 1024]) for i in range(2)]
        junk = sb("junk", [128, 1024])
        ss = sb("ss", [128, NTI])
        r1 = sb("r1", [128, NTI])
        r2 = sb("r2", [128, NTI])
        rstd = sb("rstd", [128, NTI])
        xn = [sb(f"xn{i}", [128, 1024]) for i in range(2)]
        hb = [sb(f"hb{i}", [128, 8, 512], BF16) for i in range(2)]
        ident = sb("ident", [128, 128])
        acol = sb("acol", [128, 2, 2, 8])
        pT = [ps(f"pT{i}", [128, 8, 128]) for i in range(2)]
        lds = [K.S("ld0"), K.S("ld1")]
        ldc = K.S("ld2")
        s_pe, s_ac, s_dv = K.S("pe"), K.S("ac"), K.S("dv")
        sts = [K.S("gs0"), K.S("gs1")]
        t_c = K.dma(SP, ident[:], K.dram["ident_f"], ldc)
        mods = sorted(set((j[3], j[4], j[5]) for j in jobs))
        assert len(set(m for m, _, _ in mods)) == len(mods)
        for (m, ia, ish) in mods:
            K.dma(SP, acol[:, m, 0, :], K.dram["modv"][m, ia].rearrange("(c p) -> p c", p=128), ldc, slow=True)
            t_c = K.dma(SP, acol[:, m, 1, :], K.dram["modv"][m, ish].rearrange("(c p) -> p c", p=128), ldc, slow=True)
        act_t = [None] * NTI
        for n, (ji, i) in enumerate(tiles):
            src = jobs[ji][0]
            if n >= 2:
                W(SP, act_t[n - 2])
            t_l = K.dma(SP, xt[n % 2][:], src(i) if callable(src) else src[i * 128:(i + 1) * 128, :], lds[n % 2])
            W(ACT, t_l)
            act_t[n] = s_ac.inc(ACT.activation(out=junk[:], in_=xt[n % 2][:], func=AF.Square, accum_out=ss[:, n:n + 1]))
        W(DVE, act_t[NTI - 1])
        t1 = s_dv.inc(DVE.tensor_scalar(out=r1[:], in0=ss[:], scalar1=1.0 / D, scalar2=EPS, op0=ALU.mult, op1=ALU.add))
        W(ACT, t1)
        t2 = s_ac.inc(ACT.activation(out=r2[:], in_=r1[:], func=AF.Sqrt))
        W(DVE, t2)
        t3 = s_dv.inc(DVE.reciprocal(out=rstd[:], in_=r2[:]))
        W(ACT, t3)
        W(SP, t2)
        W(PE, t_c)
        W(DVE, t_c)
        a_t = [None] * NTI
        p_t = [None] * NTI
        v_t = [None] * NTI
        st_t = {}
        blk = -1
        blk_of = []
        prev_key = None
        for n, (ji, i) in enumerate(tiles):
            key = (ji, i // 4)
            if key != prev_key:
                blk += 1
                prev_key = key
            blk_of.append(blk)
        for n, (ji, i) in enumerate(tiles):
            src, ntok, dst, m, ia, ish = jobs[ji]
            b = blk_of[n]
            if n >= 2:
                W(SP, a_t[n - 2])
            t_l = K.dma(SP, xt[n % 2][:], src(i) if callable(src) else src[i * 128:(i + 1) * 128, :], lds[n % 2])
            W(ACT, t_l)
            if n >= 2:
                W(ACT, p_t[n - 2])
            a_t[n] = s_ac.inc(ACT.activation(out=xn[n % 2][:], in_=xt[n % 2][:], func=AF.Copy, scale=rstd[:, n:n + 1]))
            W(PE, a_t[n])
            if n >= 2:
                W(PE, v_t[n - 2])
            for c in range(8):
                ins = PE.transpose(out=pT[n % 2][:, c, :], in_=xn[n % 2][:, c * 128:(c + 1) * 128], identity=ident[:])
            p_t[n] = s_pe.inc(ins)
            W(DVE, p_t[n])
            if (i % 4 == 0) and (b - 2) in st_t:
                W(DVE, st_t[b - 2])
            for c in range(8):
                ins = DVE.tensor_scalar(out=hb[b % 2][:, c, (i % 4) * 128:(i % 4 + 1) * 128], in0=pT[n % 2][:, c, :],
                                        scalar1=acol[:, m, 0, c:c + 1], scalar2=acol[:, m, 1, c:c + 1], op0=ALU.mult, op1=ALU.add)
            v_t[n] = s_dv.inc(ins)
            last_in_blk = (n + 1 == NTI) or (blk_of[n + 1] != b)
            if last_in_blk:
                nt = (i % 4 + 1) * 128
                t0 = (i // 4) * 512
                W(POOL, v_t[n])
                st_t[b] = K.dma(POOL, dst[:, t0:t0 + nt].rearrange("(c p) t -> p c t", p=128), hb[b % 2][:, :, 0:nt], sts[b % 2])
        K.barrier([st_t[blk], st_t.get(blk - 1)])


def phase_proj(K, L, pfx, with_ctx):
    nc = K.nc
    PE, ACT, DVE, POOL, SP = K.PE, K.ACT, K.DVE, K.POOL, K.SP
    dr = K.dram
    with ExitStack() as es:
        sb, ps = mk_alloc(nc, es, pfx)
        win = sb("win", [128, 8, O_GATE], BF16)
        wuk = sb("wuk", [128, 2, 512], BF16)
        wuv = sb("wuv", [128, 2, 512], BF16)
        onesbd = sb("onesbd", [128, 128])
        ones = sb("ones", [128, 128])
        pt128 = sb("pt128", [128, 128], BF16)
        pt96 = sb("pt96", [128, 128], BF16)
        pt32 = sb("pt32", [128, 128], BF16)
        gq = sb("gq", [128, 1])
        gk = sb("gk", [128, 1])
        kvg = sb("kvg", [128, 2])
        hblk = [sb(f"h{i}", [128, 8, 512], BF16) for i in range(2)]
        ckvn = sb("ckvn", [128, 2, 512], BF16)
        sqf = sb("sqf", [128, 512]); sqf2 = sb("sqf2", [128, 512])
        qf = sb("qf", [128, 512]); qf2 = sb("qf2", [128, 512])
        sd = sb("sd", [128, 512]); rs = sb("rs", [128, 512]); qn = sb("qn", [128, 512])
        t1b = sb("t1", [128, 512]); t2b = sb("t2", [128, 512])
        cosb = [sb(f"cos{i}", [128, 512]) for i in range(2)]
        sinb = [sb(f"sin{i}", [128, 512]) for i in range(2)]
        qb = sb("qb", [128, 512], BF16)
        outb = [sb(f"ob{i}", [128, 512], BF16) for i in range(2)]
        vout = [sb(f"vo{i}", [128, 8, 65], BF16) for i in range(2)]
        acc = [ps(f"acc{i}", [128, 512]) for i in range(2)]
        acc2 = ps("acc2", [128, 512])
        pss = ps("pss", [128, 512])
        prot = ps("prot", [128, 512])
        ptm = [ps(f"ptm{i}", [128, 512]) for i in range(2)]
        wl = K.S("ld2")
        gw = K.S("gw")
        hl = [K.S("ld0"), K.S("ld1")]
        tl = [K.S("ld3"), K.S("ld4")]
        s_pe, s_ac, s_dv, s_pl = K.S("pe"), K.S("ac"), K.S("dv"), K.S("pl")
        sto = [K.S("st0"), K.S("st1")]
        stv = [K.S("st2"), K.S("st3")]
        stc = K.S("st4")
        for c in range(8):
            K.dma(POOL, win[:, c, :], L["w_in"][c * 128:(c + 1) * 128, 0:O_GATE], gw)
        K.dma(POOL, wuk[:], L["mla_w_uk"].rearrange("(r p) n -> p r n", p=128), gw)
        t_gw = K.dma(POOL, wuv[:], L["mla_w_uv"].rearrange("(r p) n -> p r n", p=128), gw)
        K.dma(SP, onesbd[:], dr["onesbd_f"], wl)
        K.dma(SP, ones[:], dr["ones_f"], wl)
        K.dma(SP, pt128[:], dr["pt128"], wl)
        K.dma(SP, pt96[:], dr["pt96"], wl)
        K.dma(SP, pt32[:], dr["pt32"], wl)
        for hh in range(2):
            K.dma(SP, gq[hh * 64:(hh + 1) * 64, :], L["gqa_q_norm"].rearrange("(p o) -> p o", o=1), wl)
            K.dma(SP, gk[hh * 64:(hh + 1) * 64, :], L["gqa_k_norm"].rearrange("(p o) -> p o", o=1), wl)
        t_w = K.dma(SP, kvg[:], L["mla_kv_norm"].rearrange("(r p) -> p r", p=128), wl, slow=True)
        for i in range(2):
            DVE.memset(vout[i][:], 1.0)
        t_ms = s_dv.inc(DVE.memset(qn[:], 0.0))
        for e in (PE, ACT, DVE, POOL):
            W(e, t_w)
            W(e, t_gw)
        W(ACT, t_ms)

        st = {"k": 0, "rk": 0, "vk": 0, "acc_free": [None, None], "ob_free": [None, None], "tab_free": [None, None],
              "vo_free": [None, None], "ptm_free": [None, None], "hb_tok": None}

        def store(eng, dst, src, sem):
            return K.dma(eng, dst, src, sem)

        import os
        LIMIT = int(os.environ.get("PROJ_LIMIT", "1000000"))
        units = [0]

        def over():
            units[0] += 1
            return units[0] > LIMIT

        def fm_job(chunks, M, nt, norm_g, rope, dsts):
            if over():
                return
            k = st["k"]; st["k"] += 1
            a = acc[k % 2]
            W(PE, st["acc_free"][k % 2])
            W(PE, st["hb_tok"])
            for ci, (lt, rh) in enumerate(chunks):
                ins = PE.matmul(a[:M, :nt], lhsT=lt, rhs=rh, start=(ci == 0), stop=(ci == len(chunks) - 1))
            t_main = s_pe.inc(ins)
            ob = outb[k % 2]
            if norm_g is None and rope is None:
                W(ACT, t_main)
                W(ACT, st["ob_free"][k % 2])
                t_out = s_ac.inc(ACT.activation(out=ob[:M, :nt], in_=a[:M, :nt], func=AF.Copy))
                st["acc_free"][k % 2] = t_out
            else:
                if rope is not None:
                    r = st["rk"]; st["rk"] += 1
                    PT, cos_ap, sin_ap = rope
                    W(SP, st["tab_free"][r % 2])
                    K.dma(SP, cosb[r % 2][:M, :nt], cos_ap, tl[r % 2])
                    t_tab = K.dma(SP, sinb[r % 2][:M, :nt], sin_ap, tl[r % 2])
                W(DVE, t_main)
                t_qf = s_dv.inc(DVE.tensor_copy(out=qf[:M, :nt], in_=a[:M, :nt]))
                t_cur = t_qf
                cur = qf
                free_toks = [t_qf]
                if norm_g is not None:
                    W(ACT, t_qf)
                    t_sq = s_ac.inc(ACT.activation(out=sqf[:M, :nt], in_=qf[:M, :nt], func=AF.Square))
                    W(PE, t_sq)
                    t_ss = s_pe.inc(PE.matmul(pss[:M, :nt], lhsT=onesbd[:M, :M], rhs=sqf[:M, :nt], start=True, stop=True))
                    W(ACT, t_ss)
                    W(ACT, t_qf)
                    t_sd = s_ac.inc(ACT.activation(out=sd[:M, :nt], in_=pss[:M, :nt], func=AF.Sqrt, bias=EPS, scale=1.0 / 64))
                    W(DVE, t_sd)
                    t_rs = s_dv.inc(DVE.reciprocal(out=rs[:M, :nt], in_=sd[:M, :nt]))
                    W(DVE, t_rs)
                    if rope is None:
                        W(DVE, st["ob_free"][k % 2])
                        t_out = s_dv.inc(DVE.scalar_tensor_tensor(out=ob[:M, :nt], in0=qf[:M, :nt], scalar=norm_g, in1=rs[:M, :nt], op0=ALU.mult, op1=ALU.mult))
                    else:
                        t_cur = s_dv.inc(DVE.scalar_tensor_tensor(out=qn[:M, :nt], in0=qf[:M, :nt], scalar=norm_g, in1=rs[:M, :nt], op0=ALU.mult, op1=ALU.mult))
                        cur = qn
                st["acc_free"][k % 2] = free_toks
                if rope is not None:
                    W(ACT, t_cur)
                    t_qb = s_ac.inc(ACT.activation(out=qb[:M, :nt], in_=cur[:M, :nt], func=AF.Copy))
                    W(PE, t_qb)
                    t_rot = s_pe.inc(PE.matmul(prot[:M, :nt], lhsT=PT[:M, :M], rhs=qb[:M, :nt], start=True, stop=True))
                    W(POOL, t_cur)
                    W(POOL, t_tab)
                    t_t1 = s_pl.inc(POOL.tensor_tensor(out=t1b[:M, :nt], in0=cur[:M, :nt], in1=cosb[r % 2][:M, :nt], op=ALU.mult))
                    W(DVE, t_rot)
                    W(DVE, t_tab)
                    t_t2 = s_dv.inc(DVE.tensor_tensor(out=t2b[:M, :nt], in0=prot[:M, :nt], in1=sinb[r % 2][:M, :nt], op=ALU.mult))
                    W(DVE, t_t1)
                    W(DVE, t_t2)
                    W(DVE, st["ob_free"][k % 2])
                    t_out = s_dv.inc(DVE.tensor_tensor(out=ob[:M, :nt], in0=t1b[:M, :nt], in1=t2b[:M, :nt], op=ALU.add))
                    st["tab_free"][r % 2] = t_out
            W(SP, t_out)
            for (dst, r0, r1) in dsts:
                t_st = store(SP, dst, ob[r0:r1, :nt], sto[k % 2])
            st["ob_free"][k % 2] = t_st

        def ckv_job(hs, nt, dst_ckvt):
            if over():
                st["ckvn_tok"] = None
                return
            k = st["k"]; st["k"] += 1
            a = acc[k % 2]
            W(PE, st["acc_free"][k % 2])
            W(PE, st["hb_tok"])
            W(PE, st.get("acc2_free"))
            for g, aa in enumerate((a, acc2)):
                for c in range(8):
                    ins = PE.matmul(aa[:, :nt], lhsT=win[:, c, O_CKV + g * 128:O_CKV + (g + 1) * 128], rhs=hblk[hs][:, c, :nt], start=(c == 0), stop=(c == 7))
            t_main = s_pe.inc(ins)
            CUT = int(os.environ.get("CKV_CUT", "99"))
            st["ckvn_tok"] = None
            if CUT <= 1:
                return
            W(DVE, t_main)
            DVE.tensor_copy(out=qf[:, :nt], in_=a[:, :nt])
            t_qf = s_dv.inc(DVE.tensor_copy(out=qf2[:, :nt], in_=acc2[:, :nt]))
            W(ACT, t_qf)
            ACT.activation(out=sqf[:, :nt], in_=qf[:, :nt], func=AF.Square)
            t_sq = s_ac.inc(ACT.activation(out=sqf2[:, :nt], in_=qf2[:, :nt], func=AF.Square))
            st["acc_free"][k % 2] = [t_qf]
            st["acc2_free"] = [t_qf]
            if CUT <= 2:
                return
            W(PE, t_sq)
            PE.matmul(pss[:, :nt], lhsT=ones[:], rhs=sqf[:, :nt], start=True, stop=False)
            t_ss = s_pe.inc(PE.matmul(pss[:, :nt], lhsT=ones[:], rhs=sqf2[:, :nt], start=False, stop=True))
            if CUT <= 3:
                return
            W(ACT, t_ss)
            W(ACT, t_qf)
            t_sd = s_ac.inc(ACT.activation(out=sd[:, :nt], in_=pss[:, :nt], func=AF.Sqrt, bias=EPS, scale=1.0 / 256))
            W(DVE, t_sd)
            t_rs = s_dv.inc(DVE.reciprocal(out=rs[:, :nt], in_=sd[:, :nt]))
            if CUT <= 4:
                return
            W(DVE, t_rs)
            W(DVE, st.get("ckvn_free"))
            DVE.scalar_tensor_tensor(out=ckvn[:, 0, :nt], in0=qf[:, :nt], scalar=kvg[:, 0:1], in1=rs[:, :nt], op0=ALU.mult, op1=ALU.mult)
            t_out = s_dv.inc(DVE.scalar_tensor_tensor(out=ckvn[:, 1, :nt], in0=qf2[:, :nt], scalar=kvg[:, 1:2], in1=rs[:, :nt], op0=ALU.mult, op1=ALU.mult))
            if CUT <= 5:
                return
            W(SP, t_out)
            t_st = store(SP, dst_ckvt.rearrange("(r p) t -> p r t", p=128), ckvn[:, :, :nt], stc)
            st["ckvn_tok"] = t_out
            st["ckvn_st"] = t_st

        def tm_job(chunks, N, nh, dst):
            if over():
                return
            j = st["vk"]; st["vk"] += 1
            p = ptm[j % 2]
            W(PE, st["ptm_free"][j % 2])
            W(PE, st["hb_tok"])
            for ci, (lt, rh) in enumerate(chunks):
                ins = PE.matmul(p[:, :N], lhsT=lt, rhs=rh, start=(ci == 0), stop=(ci == len(chunks) - 1))
            t_main = s_pe.inc(ins)
            W(ACT, t_main)
            W(ACT, st["vo_free"][j % 2])
            t_o = s_ac.inc(ACT.activation(out=vout[j % 2][:, 0:nh, 0:64], in_=p[:, :N].rearrange("p (h d) -> p h d", d=64), func=AF.Copy))
            st["ptm_free"][j % 2] = t_o
            W(SP, t_o)
            st["vo_free"][j % 2] = store(SP, dst, vout[j % 2][:, 0:nh, :], stv[j % 2])

        nblk = [0]
        last_users = [None, None]

        def load_block(src_ap, nt):
            b = nblk[0]; nblk[0] += 1
            W(SP, last_users[b % 2])
            st["hb_tok"] = K.dma(SP, hblk[b % 2][:, :, :nt], src_ap.rearrange("(c p) t -> p c t", p=128), hl[b % 2])
            return b % 2

        def done_block(hs):
            last_users[hs] = (s_pe, s_pe.n)

        def hch(hs, c0, M, nt):
            return [(win[:, c, c0:c0 + M], hblk[hs][:, c, :nt]) for c in range(8)]

        hT_all, hT_na = dr["hT_all"], dr["hT_na"]
        for tb in range(17):
            ctxb = (tb == 16)
            t0 = tb * 512
            nt = 256 if ctxb else 512
            hs = load_block(hT_all[:, t0:t0 + nt], nt)
            rope = None if ctxb else (pt128, dr["cosA_all"][:, t0:t0 + nt], dr["sinA_all"][:, t0:t0 + nt])
            fm_job(hch(hs, O_GK, 128, nt), 128, nt, gk[:, 0:1], rope,
                   [(dr["GKT"][0, :, t0:t0 + nt], 0, 64), (dr["GKT"][1, :, t0:t0 + nt], 64, 128)])
            ckv_job(hs, nt, dr["CKVT"][:, t0:t0 + nt])
            rope = None if ctxb else (pt32, dr["cosM_all"][:, t0:t0 + nt], dr["sinM_all"][:, t0:t0 + nt])
            fm_job(hch(hs, O_KR, 32, nt), 32, nt, None, rope, [(dr["MKT"][h, 64:96, t0:t0 + nt], 0, 32) for h in range(8)])
            W(PE, st["ckvn_tok"])
            for g in range(4):
                fm_job([(wuk[:, r, g * 128:(g + 1) * 128], ckvn[:, r, :nt]) for r in range(2)], 128, nt, None, None,
                       [(dr["MKT"][2 * g, 0:64, t0:t0 + nt], 0, 64), (dr["MKT"][2 * g + 1, 0:64, t0:t0 + nt], 64, 128)])
            for ti in range(nt // 128):
                tsl = slice(ti * 128, (ti + 1) * 128)
                r0 = t0 + ti * 128
                tm_job([(ckvn[:, r, tsl], wuv[:, r, :]) for r in range(2)], 512, 8, dr["MV"][r0:r0 + 128, :, :])
                tm_job([(hblk[hs][:, c, tsl], win[:, c, O_GV:O_GV + 128]) for c in range(8)], 128, 2, dr["GV"][r0:r0 + 128, :, :])
            st["ckvn_free"] = (s_pe, s_pe.n)
            if ctxb:
                for g in range(4):
                    fm_job(hch(hs, O_NK + g * 128, 128, nt), 128, nt, None, None,
                           [(dr["NKT"][2 * g, :, NAT:NAT + nt], 0, 64), (dr["NKT"][2 * g + 1, :, NAT:NAT + nt], 64, 128)])
                for ti in range(nt // 128):
                    tsl = slice(ti * 128, (ti + 1) * 128)
                    tm_job([(hblk[hs][:, c, tsl], win[:, c, O_NV:O_NV + 512]) for c in range(8)], 512, 8, dr["NV"][NAT + ti * 128:NAT + (ti + 1) * 128, :, :])
                if with_ctx:
                    q0 = T
                    for g in range(4):
                        fm_job(hch(hs, O_GQ + g * 128, 128, nt), 128, nt, gq[:, 0:1], None,
                               [(dr["GQT"][2 * g, :, q0:q0 + nt], 0, 64), (dr["GQT"][2 * g + 1, :, q0:q0 + nt], 64, 128)])
                        fm_job(hch(hs, O_NQ + g * 128, 128, nt), 128, nt, None, None,
                               [(dr["NQT"][2 * g, :, q0:q0 + nt], 0, 64), (dr["NQT"][2 * g + 1, :, q0:q0 + nt], 64, 128)])
                    for h in range(8):
                        fm_job(hch(hs, O_MQ + h * 96, 96, nt), 96, nt, None, None, [(dr["MQT"][h, :, q0:q0 + nt], 0, 96)])
            done_block(hs)
        for tb in range(5):
            t0 = tb * 512
            nt = 512
            hs = load_block(hT_na[:, t0:t0 + nt], nt)
            for g in range(4):
                fm_job(hch(hs, O_NK + g * 128, 128, nt), 128, nt, None, None,
                       [(dr["NKT"][2 * g, :, t0:t0 + nt], 0, 64), (dr["NKT"][2 * g + 1, :, t0:t0 + nt], 64, 128)])
            for ti in range(4):
                tsl = slice(ti * 128, (ti + 1) * 128)
                tm_job([(hblk[hs][:, c, tsl], win[:, c, O_NV:O_NV + 512]) for c in range(8)], 512, 8, dr["NV"][t0 + ti * 128:t0 + (ti + 1) * 128, :, :])
            done_block(hs)
        for tb in range(4):
            q0 = tb * 512
            nt = 512
            hs = load_block(hT_na[:, 256 + q0:256 + q0 + nt], nt)
            for g in range(4):
                fm_job(hch(hs, O_GQ + g * 128, 128, nt), 128, nt, gq[:, 0:1], (pt128, dr["cosA_own"][:, q0:q0 + nt], dr["sinA_own"][:, q0:q0 + nt]),
                       [(dr["GQT"][2 * g, :, q0:q0 + nt], 0, 64), (dr["GQT"][2 * g + 1, :, q0:q0 + nt], 64, 128)])
                fm_job(hch(hs, O_NQ + g * 128, 128, nt), 128, nt, None, None,
                       [(dr["NQT"][2 * g, :, q0:q0 + nt], 0, 64), (dr["NQT"][2 * g + 1, :, q0:q0 + nt], 64, 128)])
            for h in range(8):
                fm_job(hch(hs, O_MQ + h * 96, 96, nt), 96, nt, None, (pt96, dr["cosM_own"][:, q0:q0 + nt], dr["sinM_own"][:, q0:q0 + nt]),
                       [(dr["MQT"][h, :, q0:q0 + nt], 0, 96)])
            done_block(hs)
        K.barrier([(s, s.n) for s in sto + stv + [stc]])


def phase_attn(K, heads, pfx, nkmax):
    nc = K.nc
    PE, ACT, DVE, POOL, SP = K.PE, K.ACT, K.DVE, K.POOL, K.SP
    NQ = T + C
    with ExitStack() as es:
        sb, ps = mk_alloc(nc, es, pfx)
        ktb = [sb(f"kt{i}", [128, nkmax], BF16) for i in range(2)]
        vb = [sb(f"v{i}", [128, nkmax // 128, 65], BF16) for i in range(2)]
        qb = [sb(f"q{i}", [128, NQ], BF16) for i in range(2)]
        pbuf = [sb(f"p{i}", [128, 512], BF16) for i in range(3)]
        bb = [sb(f"bias{i}", [128, 512]) for i in range(3)]
        sbs = [sb(f"sb{i}", [128, 512]) for i in range(2)]
        osb = sb("osb", [128, 512])
        rl = sb("rl", [128, 512])
        ones = sb("ones", [128, 128])
        ysb = [sb(f"y{i}", [128, 512], BF16) for i in range(2)]
        psb = [ps(f"s{i}", [128, 512]) for i in range(3)]
        po = [ps(f"o{i}", [128, 512]) for i in range(2)]
        pbc = ps("bc", [128, 512])
        hl = [K.S("ld0"), K.S("ld1")]
        bl = [K.S("ld2"), K.S("ld3"), K.S("ld4")]
        cl = K.S("ld5")
        s_pe, s_ac, s_dv = K.S("pe"), K.S("ac"), K.S("dv")
        sty = [K.S("gs0"), K.S("gs1")]
        t_c = K.dma(SP, ones[:], K.dram["ones_f"], cl)
        for i in range(2):
            DVE.memset(ktb[i][:], 0.0)
            DVE.memset(qb[i][:], 0.0)
        t_m = s_dv.inc(DVE.memset(rl[:], 1.0))
        W(PE, t_c)
        W(PE, t_m)
        W(SP, t_m)
        steps = []
        for hi, h in enumerate(heads):
            for bi, b in enumerate(h["blocks"]):
                nt_ = len(b["tiles"])
                for si, (kti, bias) in enumerate(b["tiles"]):
                    steps.append(dict(hi=hi, b=b, kti=kti, bias=bias, first=(si == 0), last=(si == nt_ - 1),
                                      hfirst=(bi == 0 and si == 0), hlast=(bi == len(h["blocks"]) - 1 and si == nt_ - 1)))
        NS = len(steps)
        head_tok = [None] * len(heads)
        head_done = [None] * len(heads)

        def load_head(hi):
            h = heads[hi]
            s = hi % 2
            if hi >= 2:
                W(SP, head_done[hi - 2])
            dk, nk = h["dk"], h["nk"]
            K.dma(SP, ktb[s][:dk, :nk], h["kt"], hl[s])
            K.dma(SP, vb[s][:, :nk // 128, :], h["v"].rearrange("(t p) e -> p t e", p=128), hl[s])
            head_tok[hi] = K.dma(SP, qb[s][:dk, :], h["qt"], hl[s])

        tq = [None] * NS
        tb_ = [None] * NS
        te = [None] * NS
        tv = [None] * NS
        bias_ld = [None] * NS
        nbias = [0]
        bidx = [None] * NS
        blk_id = [0]
        po_free = [None, None]
        y_free = [None, None]
        pend_pe = {}
        pend_dv = {}
        state = {"bc_tok": None, "y_tok": None}

        def emit_qk(t):
            s = steps[t]
            h = heads[s["hi"]]
            hs = s["hi"] % 2
            if s["hfirst"]:
                W(PE, head_tok[s["hi"]])
            if t >= 3:
                W(PE, tb_[t - 3] if steps[t - 3]["bias"] is not None else te[t - 3])
            b = s["b"]
            dk = h["dk"]
            tq[t] = s_pe.inc(PE.matmul(psb[t % 3][:, :b["nq"]], lhsT=ktb[hs][:, s["kti"] * 128:(s["kti"] + 1) * 128],
                                       rhs=qb[hs][:, b["q0"]:b["q0"] + b["nq"]], start=True, stop=True))

        def emit_bias_load(t):
            s = steps[t]
            if s["bias"] is None:
                return
            n = nbias[0]; nbias[0] += 1
            bidx[t] = n
            W(SP, state.get(("bfree", n % 3)))
            bias_ld[t] = K.dma(SP, bb[n % 3][:, :s["b"]["nq"]], s["bias"], bl[n % 3])

        if NS > 0:
            load_head(0)
        LA = 2
        for t in range(min(LA, NS)):
            emit_bias_load(t)
            emit_qk(t)
        cur_blk = -1
        for t in range(NS):
            s = steps[t]
            h = heads[s["hi"]]
            b = s["b"]
            nq = b["nq"]
            hs = s["hi"] % 2
            if s["hfirst"] and s["hi"] + 1 < len(heads):
                load_head(s["hi"] + 1)
            if s["first"]:
                cur_blk += 1
            if t + LA < NS:
                emit_bias_load(t + LA)
                emit_qk(t + LA)
            if s["bias"] is not None:
                n = bidx[t]
                W(DVE, tq[t])
                W(DVE, bias_ld[t])
                if t >= 2:
                    W(DVE, te[t - 2])
                tb_[t] = s_dv.inc(DVE.scalar_tensor_tensor(out=sbs[t % 2][:, :nq], in0=psb[t % 3][:, :nq], scalar=float(h["scale"]),
                                                           in1=bb[n % 3][:, :nq], op0=ALU.mult, op1=ALU.add))
                state[("bfree", n % 3)] = tb_[t]
                W(ACT, tb_[t])
                if t >= 3:
                    W(ACT, tv[t - 3])
                te[t] = s_ac.inc(ACT.activation(out=pbuf[t % 3][:, :nq], in_=sbs[t % 2][:, :nq], func=AF.Exp))
            else:
                W(ACT, tq[t])
                if t >= 3:
                    W(ACT, tv[t - 3])
                te[t] = s_ac.inc(ACT.activation(out=pbuf[t % 3][:, :nq], in_=psb[t % 3][:, :nq], func=AF.Exp, scale=float(h["scale"])))
            for f in pend_pe.pop(t, []):
                f()
            W(PE, te[t])
            if s["first"]:
                W(PE, po_free[cur_blk % 2])
            tv[t] = s_pe.inc(PE.matmul(po[cur_blk % 2][:65, :nq], lhsT=vb[hs][:, s["kti"], :], rhs=pbuf[t % 3][:, :nq],
                                       start=s["first"], stop=s["last"]))
            if s["hlast"]:
                head_done[s["hi"]] = tv[t]
            for f in pend_dv.pop(t, []):
                f()
            if s["last"]:
                cb = cur_blk
                W(DVE, tv[t])
                t_o = s_dv.inc(DVE.tensor_copy(out=osb[:65, :nq], in_=po[cb % 2][:65, :nq]))
                po_free[cb % 2] = t_o
                W(DVE, t_o)
                t_rl = s_dv.inc(DVE.reciprocal(out=rl[64:65, :nq], in_=osb[64:65, :nq]))

                def pe_part(nq=nq, t_rl=t_rl):
                    W(PE, t_rl)
                    W(PE, state["y_tok"])
                    state["bc_tok"] = s_pe.inc(PE.matmul(pbc[:64, :nq], lhsT=ones[64:65, 0:64], rhs=rl[64:65, :nq], start=True, stop=True))

                def dv_part(nq=nq, cb=cb, yt=b["yt"]):
                    W(DVE, state["bc_tok"])
                    W(DVE, y_free[cb % 2])
                    state["y_tok"] = s_dv.inc(DVE.tensor_tensor(out=ysb[cb % 2][:64, :nq], in0=osb[:64, :nq], in1=pbc[:64, :nq], op=ALU.mult))
                    W(POOL, state["y_tok"])
                    y_free[cb % 2] = K.dma(POOL, yt, ysb[cb % 2][:64, :nq], sty[cb % 2])

                if t + 1 < NS:
                    nxt_len = len(steps[t + 1]["b"]["tiles"])
                    d = min(2, nxt_len - 1)
                    pend_pe.setdefault(t + max(d, 1) if nxt_len > 1 else t + 1, []).append(pe_part)
                    pend_dv.setdefault(t + max(d, 1) if nxt_len > 1 else t + 1, []).append(dv_part)
                else:
                    pe_part()
                    dv_part()
        assert not pend_pe and not pend_dv
        K.barrier([(s, s.n) for s in sty])


def phase_merge(K, L, pfx, qblocks, x_src, x_dst):
    nc = K.nc
    PE, ACT, DVE, POOL, SP = K.PE, K.ACT, K.DVE, K.POOL, K.SP
    dr = K.dram
    with ExitStack() as es:
        sb, ps = mk_alloc(nc, es, pfx)
        wg = sb("wg", [128, 8, 3072], BF16)
        wo = [sb(f"wo{i}", [128, 4, 1024], BF16) for i in range(3)]
        wout = sb("wout", [128, 8, 1024], BF16)
        g1 = sb("g1", [128, 2, 1024])
        hblk = [sb(f"h{i}", [128, 8, 512], BF16) for i in range(2)]
        yb = [[sb(f"y{r}_{i}", [128, 4, 512], BF16) for r in range(3)] for i in range(2)]
        sg = [sb(f"sg{i}", [128, 512]) for i in range(2)]
        yacc = sb("yacc", [128, 512])
        tmp = sb("tmp", [128, 512])
        yT = sb("yT", [128, 8, 512], BF16)
        xt = [sb(f"xt{i}", [128, 1024]) for i in range(2)]
        xo = [sb(f"xo{i}", [128, 1024]) for i in range(2)]
        tm2 = [sb(f"tm{i}", [128, 512]) for i in range(2)]
        pg = [ps(f"pg{i}", [128, 512]) for i in range(2)]
        pbr = [ps(f"pb{i}", [128, 512]) for i in range(2)]
        pw = [ps(f"pw{i}", [128, 512]) for i in range(2)]
        wl = K.S("ld2")
        hl = [K.S("ld0"), K.S("ld1")]
        xl = [K.S("ld3"), K.S("ld4")]
        s_pe, s_ac, s_dv, s_pl = K.S("pe"), K.S("ac"), K.S("dv"), K.S("pl")
        stx = [K.S("st0"), K.S("st1")]
        gw = K.S("gw")
        for c in range(8):
            K.dma(POOL, wg[:, c, :], L["w_in"][c * 128:(c + 1) * 128, O_GATE:WIN], gw)
        for r, nm in enumerate(("w_o_gqa", "w_o_na", "w_o_mla")):
            K.dma(POOL, wo[r][:], L[nm].rearrange("(c p) n -> p c n", p=128), gw)
        t_gw = K.dma(POOL, wout[:], L["w_out"].rearrange("(c p) n -> p c n", p=128), gw)
        K.dma(SP, g1[:, 0, :], dr["modv"][0, 2].partition_broadcast(128), wl)
        t_w = K.dma(SP, g1[:, 1, :], dr["modv"][1, 2].partition_broadcast(128), wl)
        for e in (PE, DVE, POOL):
            W(e, t_w)
            W(e, t_gw)
        ysrc = (dr["YAT"], dr["YBT"], dr["YCT"])
        blk_done = [None, None]
        k = 0
        xk = 0
        sg_free = [None, None]
        pg_free = [None, None]
        pbr_free = [None, None]
        pw_free = [None, None]
        xt_free = [None, None]
        xo_free = [None, None]
        tm_free = [None, None]
        yT_free = None
        for bi, (hT_ap, q0, nt, m) in enumerate(qblocks):
            s = bi % 2
            W(SP, blk_done[s])
            K.dma(SP, hblk[s][:, :, :nt], hT_ap.rearrange("(c p) t -> p c t", p=128), hl[s])
            for r in range(3):
                t_l = K.dma(SP, yb[s][r][:, :, :nt], ysrc[r][:, q0:q0 + nt].rearrange("(c p) t -> p c t", p=128), hl[s])
            W(PE, t_l)
            for oc in range(8):
                for r in range(3):
                    W(PE, pg_free[k % 2])
                    for c in range(8):
                        ins = PE.matmul(pg[k % 2][:, :nt], lhsT=wg[:, c, r * 1024 + oc * 128:r * 1024 + (oc + 1) * 128], rhs=hblk[s][:, c, :nt],
                                        start=(c == 0), stop=(c == 7))
                    t_g = s_pe.inc(ins)
                    W(PE, pbr_free[k % 2])
                    for c in range(4):
                        ins = PE.matmul(pbr[k % 2][:, :nt], lhsT=wo[r][:, c, oc * 128:(oc + 1) * 128], rhs=yb[s][r][:, c, :nt], start=(c == 0), stop=(c == 3))
                    t_b = s_pe.inc(ins)
                    W(ACT, t_g)
                    W(ACT, sg_free[k % 2])
                    t_s = s_ac.inc(ACT.activation(out=sg[k % 2][:, :nt], in_=pg[k % 2][:, :nt], func=AF.Sigmoid))
                    pg_free[k % 2] = t_s
                    W(DVE, t_s)
                    W(DVE, t_b)
                    if r == 0:
                        t_d = s_dv.inc(DVE.tensor_tensor(out=yacc[:, :nt], in0=sg[k % 2][:, :nt], in1=pbr[k % 2][:, :nt], op=ALU.mult))
                    else:
                        t_d = s_dv.inc(DVE.tensor_tensor(out=tmp[:, :nt], in0=sg[k % 2][:, :nt], in1=pbr[k % 2][:, :nt], op=ALU.mult))
                        W(DVE, t_d)
                        if r == 1:
                            t_d = s_dv.inc(DVE.tensor_tensor(out=yacc[:, :nt], in0=yacc[:, :nt], in1=tmp[:, :nt], op=ALU.add))
                        else:
                            if oc == 0:
                                W(DVE, yT_free)
                            t_d = s_dv.inc(DVE.tensor_tensor(out=yT[:, oc, :nt], in0=yacc[:, :nt], in1=tmp[:, :nt], op=ALU.add))
                    sg_free[k % 2] = t_d
                    pbr_free[k % 2] = t_d
                    k += 1
            blk_done[s] = (s_pe, s_pe.n)
            t_y = t_d
            W(PE, t_y)
            for ti in range(nt // 128):
                xs_ = xk % 2
                W(SP, xt_free[xs_])
                t_x = K.dma(SP, xt[xs_][:], x_src(q0 + ti * 128), xl[xs_])
                for half in range(2):
                    j = 2 * xk + half
                    W(PE, pw_free[j % 2])
                    for c in range(8):
                        ins = PE.matmul(pw[j % 2][:, :], lhsT=yT[:, c, ti * 128:(ti + 1) * 128], rhs=wout[:, c, half * 512:(half + 1) * 512],
                                        start=(c == 0), stop=(c == 7))
                    t_p = s_pe.inc(ins)
                    W(DVE, t_p)
                    W(DVE, tm_free[j % 2])
                    t_m = s_dv.inc(DVE.tensor_tensor(out=tm2[j % 2][:], in0=pw[j % 2][:], in1=g1[:, m, half * 512:(half + 1) * 512], op=ALU.mult))
                    pw_free[j % 2] = t_m
                    W(POOL, t_m)
                    W(POOL, t_x)
                    if half == 0:
                        W(POOL, xo_free[xs_])
                    t_a = s_pl.inc(POOL.tensor_tensor(out=xo[xs_][:, half * 512:(half + 1) * 512], in0=tm2[j % 2][:], in1=xt[xs_][:, half * 512:(half + 1) * 512], op=ALU.add))
                    tm_free[j % 2] = t_a
                xt_free[xs_] = t_a
                W(SP, t_a)
                xo_free[xs_] = K.dma(SP, x_dst(q0 + ti * 128), xo[xs_][:], stx[xs_])
                xk += 1
            yT_free = (s_pe, s_pe.n)
        K.barrier([(s_, s_.n) for s_ in stx])


def phase_mlp(K, L, pfx, qblocks, x_src, x_dst, final_g):
    nc = K.nc
    PE, ACT, DVE, POOL, SP = K.PE, K.ACT, K.DVE, K.POOL, K.SP
    dr = K.dram
    with ExitStack() as es:
        sb, ps = mk_alloc(nc, es, pfx)
        w1 = sb("w1", [128, 8, 4096], BF16)
        w2 = sb("w2", [128, 32, 1024], BF16)
        g2 = sb("g2", [128, 2, 1024])
        fg = sb("fg", [128, 1024])
        hblk = [sb(f"h{i}", [128, 8, 256], BF16) for i in range(2)]
        uT = sb("uT", [128, 32, 256], BF16)
        rb = [sb(f"r{i}", [128, 256]) for i in range(2)]
        xt = [sb(f"xt{i}", [128, 1024]) for i in range(2)]
        xo = [sb(f"xo{i}", [128, 1024]) for i in range(2)]
        tm2 = [sb(f"tm{i}", [128, 512]) for i in range(2)]
        junk = sb("junk", [128, 1024])
        st4 = sb("st4", [128, 4])
        pu = [ps(f"pu{i}", [128, 512]) for i in range(2)]
        pw = [ps(f"pw{i}", [128, 512]) for i in range(2)]
        wl = K.S("ld2")
        hl = [K.S("ld0"), K.S("ld1")]
        xl = [K.S("ld3"), K.S("ld4")]
        s_pe, s_ac, s_dv, s_pl = K.S("pe"), K.S("ac"), K.S("dv"), K.S("pl")
        stx = [K.S("st0"), K.S("st1")]
        gw = K.S("gw")
        for c in range(8):
            K.dma(POOL, w1[:, c, :], L["w_mlp1"][c * 128:(c + 1) * 128, :], gw)
        for c4 in range(4):
            t_gw = K.dma(POOL, w2[:, c4 * 8:(c4 + 1) * 8, :], L["w_mlp2"][c4 * 1024:(c4 + 1) * 1024, :].rearrange("(c p) n -> p c n", p=128), gw)
        K.dma(SP, g2[:, 0, :], dr["modv"][0, 5].partition_broadcast(128), wl)
        if final_g is not None:
            K.dma(SP, fg[:], final_g.partition_broadcast(128), wl)
        t_w = K.dma(SP, g2[:, 1, :], dr["modv"][1, 5].partition_broadcast(128), wl)
        for e in (PE, DVE, POOL, ACT):
            W(e, t_w)
            W(e, t_gw)
        h2T = dr["h2T"]
        blk_done = [None, None]
        k = 0
        xk = 0
        pu_free = [None, None]
        rb_free = [None, None]
        pw_free = [None, None]
        xt_free = [None, None]
        xo_free = [None, None]
        tm_free = [None, None]
        uT_free = None
        for bi, (q0, nt, m) in enumerate(qblocks):
            s = bi % 2
            W(SP, blk_done[s])
            t_l = K.dma(SP, hblk[s][:, :, :nt], h2T[:, q0:q0 + nt].rearrange("(c p) t -> p c t", p=128), hl[s])
            W(PE, t_l)
            for fc in range(32):
                W(PE, pu_free[k % 2])
                for c in range(8):
                    ins = PE.matmul(pu[k % 2][:, :nt], lhsT=w1[:, c, fc * 128:(fc + 1) * 128], rhs=hblk[s][:, c, :nt], start=(c == 0), stop=(c == 7))
                t_u = s_pe.inc(ins)
                W(ACT, t_u)
                W(ACT, rb_free[k % 2])
                t_r = s_ac.inc(ACT.activation(out=rb[k % 2][:, :nt], in_=pu[k % 2][:, :nt], func=AF.Relu))
                pu_free[k % 2] = t_r
                W(DVE, t_r)
                if fc == 0:
                    W(DVE, uT_free)
                t_q = s_dv.inc(DVE.tensor_tensor(out=uT[:, fc, :nt], in0=rb[k % 2][:, :nt], in1=rb[k % 2][:, :nt], op=ALU.mult))
                rb_free[k % 2] = t_q
                k += 1
            blk_done[s] = (s_pe, s_pe.n)
            W(PE, t_q)
            for ti in range(nt // 128):
                xs_ = xk % 2
                W(SP, xt_free[xs_])
                t_x = K.dma(SP, xt[xs_][:], x_src(q0 + ti * 128), xl[xs_])
                for half in range(2):
                    j = 2 * xk + half
                    W(PE, pw_free[j % 2])
                    for fc in range(32):
                        ins = PE.matmul(pw[j % 2][:, :], lhsT=uT[:, fc, ti * 128:(ti + 1) * 128], rhs=w2[:, fc, half * 512:(half + 1) * 512],
                                        start=(fc == 0), stop=(fc == 31))
                    t_p = s_pe.inc(ins)
                    W(DVE, t_p)
                    W(DVE, tm_free[j % 2])
                    t_m = s_dv.inc(DVE.tensor_tensor(out=tm2[j % 2][:], in0=pw[j % 2][:], in1=g2[:, m, half * 512:(half + 1) * 512], op=ALU.mult))
                    pw_free[j % 2] = t_m
                    W(POOL, t_m)
                    W(POOL, t_x)
                    if half == 0:
                        W(POOL, xo_free[xs_])
                    t_a = s_pl.inc(POOL.tensor_tensor(out=xo[xs_][:, half * 512:(half + 1) * 512], in0=tm2[j % 2][:], in1=xt[xs_][:, half * 512:(half + 1) * 512], op=ALU.add))
                    tm_free[j % 2] = t_a
                xt_free[xs_] = t_a
                t_fin = t_a
                if final_g is not None:
                    W(ACT, t_a)
                    t1 = s_ac.inc(ACT.activation(out=junk[:], in_=xo[xs_][:], func=AF.Square, accum_out=st4[:, 0:1]))
                    W(DVE, t1)
                    t2 = s_dv.inc(DVE.tensor_scalar(out=st4[:, 1:2], in0=st4[:, 0:1], scalar1=1.0 / D, scalar2=EPS, op0=ALU.mult, op1=ALU.add))
                    W(ACT, t2)
                    t3 = s_ac.inc(ACT.activation(out=st4[:, 2:3], in_=st4[:, 1:2], func=AF.Sqrt))
                    W(DVE, t3)
                    t4 = s_dv.inc(DVE.reciprocal(out=st4[:, 3:4], in_=st4[:, 2:3]))
                    W(DVE, t4)
                    t_fin = s_dv.inc(DVE.scalar_tensor_tensor(out=xo[xs_][:], in0=xo[xs_][:], scalar=st4[:, 3:4], in1=fg[:], op0=ALU.mult, op1=ALU.mult))
                W(SP, t_fin)
                xo_free[xs_] = K.dma(SP, x_dst(q0 + ti * 128), xo[xs_][:], stx[xs_])
                xk += 1
            uT_free = (s_pe, s_pe.n)
        K.barrier([(s_, s_.n) for s_ in stx])


def phase_halo(K, pfx):
    nc = K.nc
    PE, ACT, DVE, POOL, SP = K.PE, K.ACT, K.DVE, K.POOL, K.SP
    dr = K.dram
    xg, x1, xna, sel = K.xg_at, K.x1_at, dr["x_na2"], dr["sel"]
    with ExitStack() as es:
        sb, ps = mk_alloc(nc, es, pfx)
        selb = sb("sel", [128, 8])
        cand = [sb(f"c{i}", [128, 1024]) for i in range(4)]
        acc = [sb(f"a{i}", [128, 1024]) for i in range(2)]
        ld = [K.S("ld0"), K.S("ld1"), K.S("ld3"), K.S("ld4")]
        lc = K.S("ld2")
        s_dv = K.S("dv")
        st = [K.S("st0"), K.S("st1")]
        so = K.S("st2")
        t_c = K.dma(SP, selb[:], sel.partition_broadcast(128), lc)
        for q in range(0, T, 128):
            t_own = K.dma(SP, xna[256 + q:256 + q + 128, :], x1(q), so)
        W(DVE, t_c)
        jobs = []
        for u in range(2):
            jobs.append((128 * u, [xg(2048 * r + 1792 + 128 * u) for r in range(4)], 0))
        for u in range(2):
            jobs.append((256 + T + 128 * u, [xg(2048 * r + 128 * u) for r in range(4)], 4))
        dv_prev = None
        st_t = [None, None]
        for n, (row0, srcs, c0) in enumerate(jobs):
            W(SP, dv_prev)
            lts = [K.dma(SP, cand[r][:], srcs[r], ld[r]) for r in range(4)]
            W(DVE, lts)
            W(DVE, st_t[n % 2])
            t = s_dv.inc(DVE.tensor_scalar(out=acc[n % 2][:], in0=cand[0][:], scalar1=selb[:, c0:c0 + 1], scalar2=0.0, op0=ALU.mult, op1=ALU.add))
            for r in range(1, 4):
                W(DVE, t)
                t = s_dv.inc(DVE.scalar_tensor_tensor(out=acc[n % 2][:], in0=cand[r][:], scalar=selb[:, c0 + r:c0 + r + 1], in1=acc[n % 2][:],
                                                      op0=ALU.mult, op1=ALU.add))
            dv_prev = t
            W(SP, t)
            st_t[n % 2] = K.dma(SP, xna[row0:row0 + 128, :], acc[n % 2][:], st[n % 2])
        K.barrier([t_own, st_t[0], st_t[1]])


W_NAMES = ["w_mod", "b_mod", "norm1_g", "norm2_g", "w_in", "gqa_q_norm", "gqa_k_norm", "mla_kv_norm", "mla_w_uk", "mla_w_uv",
           "w_o_gqa", "w_o_na", "w_o_mla", "w_out", "w_mlp1", "w_mlp2"]
W_SHAPES = {"w_mod": [D, 6 * D], "b_mod": [6 * D], "norm1_g": [D], "norm2_g": [D], "w_in": [D, WIN], "gqa_q_norm": [64], "gqa_k_norm": [64],
            "mla_kv_norm": [256], "mla_w_uk": [256, 512], "mla_w_uv": [256, 512], "w_o_gqa": [512, D], "w_o_na": [512, D], "w_o_mla": [512, D],
            "w_out": [D, D], "w_mlp1": [D, 4 * D], "w_mlp2": [4 * D, D]}
DEPTH = 2


def emit_layer(K, L, src, with_ctx, final, sfx):
    dr = K.dram
    NQ = T + C
    phase_mod(K, L, "md" + sfx)
    phase_norm(K, [(src["x_all"], S, dr["hT_all"][:, 0:S], 0, 0, 1), (src["ctx_in"], C, dr["hT_all"][:, S:SK], 1, 0, 1),
                   (src["x_na"], NAT, dr["hT_na"], 0, 0, 1)], "n1" + sfx)
    phase_proj(K, L, "pj" + sfx, with_ctx)
    qbl = [(512 * i, 512) for i in range(4)]
    heads = []
    for h in range(8):
        blocks = [dict(q0=q0, nq=nq, tiles=[(k, None) for k in range(66)], yt=dr["YAT"][64 * h:64 * h + 64, q0:q0 + nq]) for q0, nq in qbl]
        if with_ctx:
            blocks.append(dict(q0=T, nq=C, tiles=[(64, None), (65, None)], yt=dr["YAT"][64 * h:64 * h + 64, T:NQ]))
        heads.append(dict(kt=dr["GKT"][h // 4], v=dr["GV"][:, h // 4, :], qt=dr["GQT"][h], dk=64, scale=0.125, nk=SK, blocks=blocks))
    for h in range(8):
        blocks = [dict(q0=q0, nq=nq, tiles=[(k, None) for k in range(66)], yt=dr["YCT"][64 * h:64 * h + 64, q0:q0 + nq]) for q0, nq in qbl]
        if with_ctx:
            blocks.append(dict(q0=T, nq=C, tiles=[(64, None), (65, None)], yt=dr["YCT"][64 * h:64 * h + 64, T:NQ]))
        heads.append(dict(kt=dr["MKT"][h], v=dr["MV"][:, h, :], qt=dr["MQT"][h], dk=96, scale=96 ** -0.5, nk=SK, blocks=blocks))
    phase_attn(K, heads, "at" + sfx, SK)
    heads = []
    var = [0, 1, 1, 2]
    for h in range(8):
        blocks = []
        for i, (q0, nq) in enumerate(qbl):
            tiles = [(4 * i + m, src["nabias"][var[i], h, m]) for m in range(8)] + [(20, None), (21, None)]
            blocks.append(dict(q0=q0, nq=nq, tiles=tiles, yt=dr["YBT"][64 * h:64 * h + 64, q0:q0 + nq]))
        if with_ctx:
            blocks.append(dict(q0=T, nq=C, tiles=[(20, None), (21, None)], yt=dr["YBT"][64 * h:64 * h + 64, T:NQ]))
        heads.append(dict(kt=dr["NKT"][h], v=dr["NV"][:, h, :], qt=dr["NQT"][h], dk=64, scale=0.125, nk=NAK, blocks=blocks))
    phase_attn(K, heads, "na" + sfx, NAK)
    mblocks = [(dr["hT_na"][:, 256 + 512 * i:256 + 512 * (i + 1)], 512 * i, 512, 0) for i in range(4)]
    if with_ctx:
        mblocks.append((dr["hT_all"][:, S:SK], T, C, 1))

    def x_src(q):
        if q >= T:
            return src["ctx_in"][q - T:q - T + 128, :]
        return src["x_own"](q) if callable(src["x_own"]) else src["x_own"][q:q + 128, :]

    def xs1_at(q):
        return dr["xs1"][q:q + 128, :]

    phase_merge(K, L, "mg" + sfx, mblocks, x_src, xs1_at)
    njobs = [(dr["xs1"][0:T, :], T, dr["h2T"][:, 0:T], 0, 3, 4)]
    if with_ctx:
        njobs.append((dr["xs1"][T:NQ, :], C, dr["h2T"][:, T:NQ], 1, 3, 4))
    phase_norm(K, njobs, "n2" + sfx)
    fblocks = [(256 * i, 256, 0) for i in range(8)]
    if with_ctx:
        fblocks.append((T, C, 1))
    phase_mlp(K, L, "ml" + sfx, fblocks, xs1_at, src["x_dst"], L.get("final_norm_g") if final else None)


def build_fused():
    nc = bass.Bass("TRN2", target_bir_lowering=False)
    K = KB(nc)
    NQ = T + C
    dr = K.dram

    def inp(name, shape, dt=F32):
        dr[name] = nc.dram_tensor(name, shape, dt, kind="ExternalInput").ap()

    def internal(name, shape, dt=BF16):
        dr[name] = nc.dram_tensor(name, shape, dt).ap()

    inp("x_all", [S, D]); inp("x_own", [T, D]); inp("x_na", [NAT, D]); inp("ctx_in", [C, D]); inp("cvec", [2, D])
    Wst = {}
    for n in W_NAMES:
        Wst[n] = nc.dram_tensor(n, [DEPTH] + W_SHAPES[n], F32, kind="ExternalInput").ap()
    fng = nc.dram_tensor("final_norm_g", [D], F32, kind="ExternalInput").ap()
    for n in ("ident_f", "onesbd_f", "ones_f"):
        inp(n, [128, 128])
    for n in ("pt128", "pt96", "pt32"):
        inp(n, [128, 128], BF16)
    inp("cosA_all", [128, S]); inp("sinA_all", [128, S]); inp("cosA_own", [128, T]); inp("sinA_own", [128, T])
    inp("cosM_all", [32, S]); inp("sinM_all", [32, S]); inp("cosM_own", [96, T]); inp("sinM_own", [96, T])
    inp("nabias", [DEPTH, 3, 8, 8, 128, 512])
    inp("sel", [8])
    dr["xout"] = nc.dram_tensor("xout", [T, D], F32, kind="ExternalOutput").ap()
    internal("modv", [2, 6, D], F32)
    internal("hT_all", [D, SK]); internal("hT_na", [D, NAT])
    internal("GKT", [2, 64, SK]); internal("CKVT", [256, SK]); internal("MKT", [8, 96, SK])
    internal("MV", [SK, 8, 65]); internal("GV", [SK, 2, 65])
    internal("NKT", [8, 64, NAK]); internal("NV", [NAK, 8, 65])
    internal("GQT", [8, 64, NQ]); internal("NQT", [8, 64, NQ]); internal("MQT", [8, 96, NQ])
    internal("YAT", [512, NQ]); internal("YBT", [512, NQ]); internal("YCT", [512, NQ])
    internal("xs1", [NQ, D], F32); internal("h2T", [D, NQ])
    NCH = 8
    x1c = [nc.dram_tensor(f"x1c{k}", [256, D], F32) for k in range(NCH)]
    xgc = [nc.dram_tensor(f"xgc{k}", [4 * 256, D], F32) for k in range(NCH)]
    internal("c1loc", [C, D], F32); internal("x_na2", [NAT, D], F32)

    def x1_at(q):
        return x1c[q // 256].ap()[q % 256:q % 256 + 128, :]

    def xg_at(t0):
        r, k, off = t0 // 2048, (t0 % 2048) // 256, t0 % 256
        return xgc[k].ap()[r * 256 + off:r * 256 + off + 128, :]

    K.x1_at, K.xg_at = x1_at, xg_at

    for l in range(DEPTH):
        L = {n: Wst[n][l] for n in W_NAMES}
        final = (l == DEPTH - 1)
        with_ctx = not final
        if final:
            L["final_norm_g"] = fng
        if l == 0:
            src = dict(x_all=dr["x_all"], x_own=dr["x_own"], x_na=dr["x_na"], ctx_in=dr["ctx_in"])
        else:
            src = dict(x_all=(lambda i: xg_at(128 * i)), x_own=x1_at, x_na=dr["x_na2"], ctx_in=dr["c1loc"])
        src["nabias"] = dr["nabias"][l]
        if final:
            src["x_dst"] = lambda q: dr["xout"][q:q + 128, :]
        else:
            src["x_dst"] = lambda q: (x1_at(q) if q < T else dr["c1loc"][q - T:q - T + 128, :])
        emit_layer(K, L, src, with_ctx, final, f"{l}_")
        if not final:
            cc = K.S("cc")
            for k in range(NCH):
                ins = K.POOL.collective_compute("AllGather", mybir.AluOpType.bypass, replica_groups=[[0, 1, 2, 3], [4, 5, 6, 7]],
                                                ins=[x1c[k].ap().opt()], outs=[xgc[k].ap().opt()])
                t_cc = cc.inc(ins)
            K.barrier([t_cc])
            phase_halo(K, f"hl{l}_")
    K.semcounts = {n: s.n for n, s in K.sems.items()}
    return nc, K


def _rope_tables():
    t = np.arange(S, dtype=np.int32)
    row = (t // 64).astype(np.float32)
    col = (t % 64).astype(np.float32)

    def tabs(rot_dim):
        half = rot_dim // 2
        inv = (10000.0 ** (-np.arange(0, half, 2, dtype=np.float32) / np.float32(half))).astype(np.float32)
        ar = (row[:, None] * inv).astype(np.float32)
        ac = (col[:, None] * inv).astype(np.float32)
        cos = np.concatenate([np.cos(ar), np.cos(ar), np.cos(ac), np.cos(ac)], axis=1).T.astype(np.float32)
        sin = np.concatenate([np.sin(ar), np.sin(ar), np.sin(ac), np.sin(ac)], axis=1).T.astype(np.float32)
        return np.ascontiguousarray(cos), np.ascontiguousarray(sin)

    return tabs(64), tabs(32)


def _rot_matrix(n):
    q = n // 4
    P = np.zeros((n, n), np.float32)
    for base in (0, 2 * q):
        for i in range(q):
            P[base + i, base + q + i] = -1.0
            P[base + q + i, base + i] = 1.0
    return P


def _consts():
    c = {}
    c["ident_f"] = np.eye(128, dtype=np.float32)
    c["ones_f"] = np.ones((128, 128), np.float32)
    bd = np.zeros((128, 128), np.float32)
    bd[:64, :64] = 1.0
    bd[64:, 64:] = 1.0
    c["onesbd_f"] = bd
    P64 = _rot_matrix(64)
    P32 = _rot_matrix(32)
    pt128 = np.zeros((128, 128), np.float32)
    pt128[:64, :64] = P64.T
    pt128[64:, 64:] = P64.T
    pt96 = np.zeros((128, 128), np.float32)
    pt96[64:96, 64:96] = P32.T
    pt32 = np.zeros((128, 128), np.float32)
    pt32[:32, :32] = P32.T
    c["pt128"] = pt128.astype(ml_dtypes.bfloat16)
    c["pt96"] = pt96.astype(ml_dtypes.bfloat16)
    c["pt32"] = pt32.astype(ml_dtypes.bfloat16)
    return c


def _na_bias_tables(rpb, j):
    out = np.empty((3, 8, 8, 128, 512), np.float32)
    kcol = np.arange(64)[None, :, None, None]
    qcol = np.arange(64)[None, None, None, :]
    m = np.arange(16)[:, None, None, None]
    a = np.arange(8)[None, None, :, None]
    cs = np.clip(qcol - 8, 0, 48)
    colok = (kcol >= cs) & (kcol < cs + 16)
    cidx = np.clip(kcol - qcol + 15, 0, 30)
    for v, i in enumerate((0, 1, 3)):
        r = 32 * j + 8 * i + a
        k = 32 * j + 8 * i - 4 + m
        rs = np.clip(r - 4, 0, 120)
        ok = (k >= 0) & (k < 128) & (k >= rs) & (k < rs + 8) & colok
        ridx = np.clip(k - r + 7, 0, 14)
        ridx_b = np.broadcast_to(ridx, ok.shape)
        cidx_b = np.broadcast_to(cidx, ok.shape)
        vals = rpb[:, ridx_b, cidx_b]
        tab = np.where(ok[None], vals, np.float32(NEG)).astype(np.float32)
        out[v] = tab.reshape(8, 8, 128, 512)
    return out


def _core_inputs(x_full, ctx_full, c, c_ctx, Wd, consts, ropes):
    (cosA, sinA), (cosM, sinM) = ropes
    maps = []
    cosA2 = np.ascontiguousarray(np.concatenate([cosA, cosA], 0))
    sinA2 = np.ascontiguousarray(np.concatenate([sinA, sinA], 0))
    shared = dict(consts)
    for n in W_NAMES:
        shared[n] = np.ascontiguousarray(Wd[n])
    shared["final_norm_g"] = np.ascontiguousarray(Wd["final_norm_g"])
    shared["cosA_all"] = cosA2
    shared["sinA_all"] = sinA2
    shared["cosM_all"] = cosM
    shared["sinM_all"] = sinM
    rpb = np.asarray(Wd["na_rpb"], np.float32)
    nab = [np.stack([_na_bias_tables(rpb[l], j) for l in range(rpb.shape[0])], 0) for j in range(4)]
    for core in range(8):
        b, j = core // 4, core % 4
        t0 = T * j
        d = dict(shared)
        d["x_all"] = np.ascontiguousarray(x_full[b])
        d["x_own"] = np.ascontiguousarray(x_full[b, t0:t0 + T])
        xna = np.zeros((NAT, D), np.float32)
        lo = (32 * j - 4) * 64
        hi = lo + NAT
        slo, shi = max(lo, 0), min(hi, S)
        xna[slo - lo:shi - lo] = x_full[b, slo:shi]
        d["x_na"] = xna
        d["ctx_in"] = np.ascontiguousarray(ctx_full[b])
        d["cvec"] = np.ascontiguousarray(np.stack([c[b], c_ctx]))
        d["cosA_own"] = np.ascontiguousarray(cosA2[:, t0:t0 + T])
        d["sinA_own"] = np.ascontiguousarray(sinA2[:, t0:t0 + T])
        d["cosM_own"] = np.ascontiguousarray(np.concatenate([np.ones((64, T), np.float32), cosM[:, t0:t0 + T]], 0))
        d["sinM_own"] = np.ascontiguousarray(np.concatenate([np.zeros((64, T), np.float32), sinM[:, t0:t0 + T]], 0))
        d["nabias"] = nab[j]
        sel = np.zeros(8, np.float32)
        if j > 0:
            sel[j - 1] = 1.0
        if j < 3:
            sel[4 + j + 1] = 1.0
        d["sel"] = sel
        maps.append(d)
    return maps


_PROG = []


def kernel(**inputs):
    Wd = {k: np.asarray(v, np.float32) for k, v in inputs.items()}
    if not _PROG:
        _PROG.append(build_fused()[0])
    nc = _PROG[0]
    maps = _core_inputs(Wd["x"], Wd["ctx"], Wd["c"], Wd["c_ctx"], Wd, _consts(), _rope_tables())
    res = run_bass_kernel_spmd(nc, maps, core_ids=list(range(8)))
    outs = [r["xout"] for r in res.results]
    x = np.stack([np.concatenate([outs[4 * b + j] for j in range(4)], 0) for b in range(2)], 0)
    return np.ascontiguousarray(x.astype(np.float32))
```

```python
from contextlib import ExitStack
import os
import numpy as np
import ml_dtypes
import concourse.bass as bass
import concourse.mybir as mybir
from concourse.bass_utils import run_bass_kernel_spmd

F32 = mybir.dt.float32
BF16 = mybir.dt.bfloat16
AF = mybir.ActivationFunctionType
ALU = mybir.AluOpType

D = 1024
S = 8192
C = 256
SK = S + C
T = 2048
NAT = 2560
NAK = NAT + C
EPS = 1e-6
NEG = -30000.0
O_GQ, O_GK, O_GV, O_NQ, O_NK, O_NV, O_MQ, O_CKV, O_KR, O_GATE = 0, 512, 640, 768, 1280, 1792, 2304, 3072, 3328, 3360
WIN = 6432


class Sem:
    def __init__(self, nc, name):
        self.h = nc.alloc_semaphore(name)
        self.n = 0

    def inc(self, ins, k=1):
        ins.then_inc(self.h, k)
        self.n += k
        return (self, self.n)


def W(eng, tok):
    if tok is None:
        return
    if isinstance(tok, list):
        for t in tok:
            W(eng, t)
        return
    s, v = tok
    if v > 0:
        eng.wait_ge(s.h, v)


class KB:
    def __init__(self, nc):
        self.nc = nc
        self.PE, self.ACT, self.DVE, self.POOL, self.SP = nc.tensor, nc.scalar, nc.vector, nc.gpsimd, nc.sync
        self.sems = {}
        self.dram = {}

    def S(self, name):
        if name not in self.sems:
            self.sems[name] = Sem(self.nc, name)
        return self.sems[name]

    def dma(self, eng, out, in_, sem, slow=False):
        if slow:
            ins = eng.dma_start(out=out, in_=in_, allow_slow_non_contiguous=True)
        else:
            ins = eng.dma_start(out=out, in_=in_)
        return sem.inc(ins, 16)

    def barrier(self, toks):
        for e in (self.PE, self.ACT, self.DVE, self.POOL, self.SP):
            W(e, toks)


def mk_alloc(nc, es, pfx):
    def sb(name, shape, dt=F32):
        return es.enter_context(nc.sbuf_tensor(pfx + name, shape, dt))

    def ps(name, shape, dt=F32):
        return es.enter_context(nc.psum_tensor(pfx + name, shape, dt))

    return sb, ps


def phase_mod(K, L, pfx):
    nc = K.nc
    PE, ACT, DVE, POOL, SP = K.PE, K.ACT, K.DVE, K.POOL, K.SP
    with ExitStack() as es:
        sb, ps = mk_alloc(nc, es, pfx)
        cT = sb("cT", [128, 8, 2])
        sT = sb("sT", [128, 8, 2])
        NWB = 4
        wm = [sb(f"w{i}", [128, 8, 512]) for i in range(NWB)]
        bm = sb("b", [2, 6144])
        mrow = sb("m", [2, 6144])
        ng = sb("ng", [2, 2, 1024])
        mv = sb("mv", [2, 6, 1024])
        pm = [ps(f"p{i}", [2, 512]) for i in range(2)]
        ld = K.S("ld0")
        wl = [K.S("ld1"), K.S("ld2"), K.S("ld3"), K.S("ld4")]
        s_pe, s_ac, s_dv, st = K.S("pe"), K.S("ac"), K.S("dv"), K.S("st0")
        for m in range(2):
            K.dma(SP, cT[:, :, m], K.dram["cvec"][m].rearrange("(c p) -> p c", p=128), ld, slow=True)
        K.dma(SP, bm[:], L["b_mod"].partition_broadcast(2), ld)
        K.dma(SP, ng[:, 0, :], L["norm1_g"].partition_broadcast(2), ld)
        t_ld = K.dma(SP, ng[:, 1, :], L["norm2_g"].partition_broadcast(2), ld)
        W(ACT, t_ld)
        t_s = s_ac.inc(ACT.activation(out=sT[:].rearrange("p c m -> p (c m)"), in_=cT[:].rearrange("p c m -> p (c m)"), func=AF.Silu))
        W(PE, t_s)
        pe_t = [None] * 12
        dv_t = [None] * 12
        wsrc = L["w_mod"]
        w_t = [None] * 12

        def load_w(g):
            if g >= NWB:
                W(SP, pe_t[g - NWB])
            w_t[g] = K.dma(SP, wm[g % NWB][:], wsrc[:, g * 512:(g + 1) * 512].rearrange("(c p) n -> p c n", p=128), wl[g % NWB])

        for g in range(min(NWB - 1, 12)):
            load_w(g)
        for g in range(12):
            if g + NWB - 1 < 12:
                load_w(g + NWB - 1)
            W(PE, w_t[g])
            if g >= 2:
                W(PE, dv_t[g - 2])
            for c in range(8):
                ins = PE.matmul(pm[g % 2][:], lhsT=sT[:, c, :], rhs=wm[g % NWB][:, c, :], start=(c == 0), stop=(c == 7))
            pe_t[g] = s_pe.inc(ins)
            W(DVE, pe_t[g])
            if g == 0:
                W(DVE, t_ld)
            dv_t[g] = s_dv.inc(DVE.tensor_tensor(out=mrow[:, g * 512:(g + 1) * 512], in0=pm[g % 2][:], in1=bm[:, g * 512:(g + 1) * 512], op=ALU.add))
        W(DVE, dv_t[11])
        sl = lambda i: mrow[:, i * 1024:(i + 1) * 1024]
        DVE.scalar_tensor_tensor(out=mv[:, 0, :], in0=sl(1), scalar=1.0, in1=ng[:, 0, :], op0=ALU.add, op1=ALU.mult)
        DVE.tensor_copy(out=mv[:, 1, :], in_=sl(0))
        DVE.tensor_copy(out=mv[:, 2, :], in_=sl(2))
        DVE.scalar_tensor_tensor(out=mv[:, 3, :], in0=sl(4), scalar=1.0, in1=ng[:, 1, :], op0=ALU.add, op1=ALU.mult)
        DVE.tensor_copy(out=mv[:, 4, :], in_=sl(3))
        t_f = s_dv.inc(DVE.tensor_copy(out=mv[:, 5, :], in_=sl(5)))
        W(SP, t_f)
        t_st = K.dma(SP, K.dram["modv"], mv[:], st)
        K.barrier([t_st])


def phase_norm(K, jobs, pfx):
    nc = K.nc
    PE, ACT, DVE, POOL, SP = K.PE, K.ACT, K.DVE, K.POOL, K.SP
    tiles = []
    for ji, (src, ntok, dst, m, ia, ish) in enumerate(jobs):
        for i in range(ntok // 128):
            tiles.append((ji, i))
    NTI = len(tiles)
    with ExitStack() as es:
        sb, ps = mk_alloc(nc, es, pfx)
        NX = 6
        xt = [sb(f"xt{i}", [128, 1024]) for i in range(NX)]
        junk = sb("junk", [128, 1024])
        ss = sb("ss", [128, NTI])
        r1 = sb("r1", [128, NTI])
        r2 = sb("r2", [128, NTI])
        rstd = sb("rstd", [128, NTI])
        NXN = 3
        xn = [sb(f"xn{i}", [128, 1024]) for i in range(NXN)]
        hb = [sb(f"hb{i}", [128, 8, 512], BF16) for i in range(2)]
        ident = sb("ident", [128, 128])
        acol = sb("acol", [128, 2, 2, 8])
        pT = [ps(f"pT{i}", [128, 8, 128]) for i in range(2)]
        lds = [K.S("ld0"), K.S("ld1"), K.S("ld3"), K.S("ld4"), K.S("ld5"), K.S("ld6")]
        ldc = K.S("ld2")
        s_pe, s_ac, s_dv = K.S("pe"), K.S("ac"), K.S("dv")
        sts = [K.S("gs0"), K.S("gs1")]
        t_c = K.dma(SP, ident[:], K.dram["ident_f"], ldc)
        mods = sorted(set((j[3], j[4], j[5]) for j in jobs))
        assert len(set(m for m, _, _ in mods)) == len(mods)
        for (m, ia, ish) in mods:
            K.dma(SP, acol[:, m, 0, :], K.dram["modv"][m, ia].rearrange("(c p) -> p c", p=128), ldc, slow=True)
            t_c = K.dma(SP, acol[:, m, 1, :], K.dram["modv"][m, ish].rearrange("(c p) -> p c", p=128), ldc, slow=True)
        act_t = [None] * NTI
        for n, (ji, i) in enumerate(tiles):
            src = jobs[ji][0]
            if n >= NX:
                W(SP, act_t[n - NX])
            t_l = K.dma(SP, xt[n % NX][:], src(i) if callable(src) else src[i * 128:(i + 1) * 128, :], lds[n % NX])
            W(ACT, t_l)
            act_t[n] = s_ac.inc(ACT.activation(out=junk[:], in_=xt[n % NX][:], func=AF.Square, accum_out=ss[:, n:n + 1]))
        W(DVE, act_t[NTI - 1])
        t1 = s_dv.inc(DVE.tensor_scalar(out=r1[:], in0=ss[:], scalar1=1.0 / D, scalar2=EPS, op0=ALU.mult, op1=ALU.add))
        W(ACT, t1)
        t2 = s_ac.inc(ACT.activation(out=r2[:], in_=r1[:], func=AF.Sqrt))
        W(DVE, t2)
        t3 = s_dv.inc(DVE.reciprocal(out=rstd[:], in_=r2[:]))
        W(ACT, t3)
        W(SP, t2)
        W(PE, t_c)
        W(DVE, t_c)
        a_t = [None] * NTI
        p_t = [None] * NTI
        v_t = [None] * NTI
        st_t = {}
        blk = -1
        blk_of = []
        prev_key = None
        for n, (ji, i) in enumerate(tiles):
            key = (ji, i // 4)
            if key != prev_key:
                blk += 1
                prev_key = key
            blk_of.append(blk)
        for n, (ji, i) in enumerate(tiles):
            src, ntok, dst, m, ia, ish = jobs[ji]
            b = blk_of[n]
            if n >= NX:
                W(SP, a_t[n - NX])
            t_l = K.dma(SP, xt[n % NX][:], src(i) if callable(src) else src[i * 128:(i + 1) * 128, :], lds[n % NX])
            W(ACT, t_l)
            if n >= NXN:
                W(ACT, p_t[n - NXN])
            a_t[n] = s_ac.inc(ACT.activation(out=xn[n % NXN][:], in_=xt[n % NXN if False else n % NX][:], func=AF.Copy, scale=rstd[:, n:n + 1]))
            W(PE, a_t[n])
            if n >= 2:
                W(PE, v_t[n - 2])
            for c in range(8):
                ins = PE.transpose(out=pT[n % 2][:, c, :], in_=xn[n % NXN][:, c * 128:(c + 1) * 128], identity=ident[:])
            p_t[n] = s_pe.inc(ins)
            W(DVE, p_t[n])
            if (i % 4 == 0) and (b - 2) in st_t:
                W(DVE, st_t[b - 2])
            for c in range(8):
                ins = DVE.tensor_scalar(out=hb[b % 2][:, c, (i % 4) * 128:(i % 4 + 1) * 128], in0=pT[n % 2][:, c, :],
                                        scalar1=acol[:, m, 0, c:c + 1], scalar2=acol[:, m, 1, c:c + 1], op0=ALU.mult, op1=ALU.add)
            v_t[n] = s_dv.inc(ins)
            last_in_blk = (n + 1 == NTI) or (blk_of[n + 1] != b)
            if last_in_blk:
                nt = (i % 4 + 1) * 128
                t0 = (i // 4) * 512
                W(POOL, v_t[n])
                st_t[b] = K.dma(POOL, dst[:, t0:t0 + nt].rearrange("(c p) t -> p c t", p=128), hb[b % 2][:, :, 0:nt], sts[b % 2])
        K.barrier([st_t[blk], st_t.get(blk - 1)])


def phase_proj(K, L, pfx, with_ctx):
    nc = K.nc
    PE, ACT, DVE, POOL, SP = K.PE, K.ACT, K.DVE, K.POOL, K.SP
    dr = K.dram
    with ExitStack() as es:
        sb, ps = mk_alloc(nc, es, pfx)
        win = sb("win", [128, 8, O_GATE], BF16)
        wuk = sb("wuk", [128, 2, 512], BF16)
        wuv = sb("wuv", [128, 2, 512], BF16)
        onesbd = sb("onesbd", [128, 128])
        ones = sb("ones", [128, 128])
        pt128 = sb("pt128", [128, 128], BF16)
        pt96 = sb("pt96", [128, 128], BF16)
        pt32 = sb("pt32", [128, 128], BF16)
        gq = sb("gq", [128, 1])
        gk = sb("gk", [128, 1])
        kvg = sb("kvg", [128, 2])
        hblk = [sb(f"h{i}", [128, 8, 512], BF16) for i in range(2)]
        ckvn = sb("ckvn", [128, 2, 512], BF16)
        sqf = sb("sqf", [128, 512]); sqf2 = sb("sqf2", [128, 512])
        qf = sb("qf", [128, 512]); qf2 = sb("qf2", [128, 512])
        sd = sb("sd", [128, 512]); rs = sb("rs", [128, 512]); qn = sb("qn", [128, 512])
        t1b = sb("t1", [128, 512]); t2b = sb("t2", [128, 512])
        cosb = [sb(f"cos{i}", [128, 512]) for i in range(2)]
        sinb = [sb(f"sin{i}", [128, 512]) for i in range(2)]
        qb = sb("qb", [128, 512], BF16)
        outb = [sb(f"ob{i}", [128, 512], BF16) for i in range(2)]
        vout = [sb(f"vo{i}", [128, 8, 65], BF16) for i in range(2)]
        acc = [ps(f"acc{i}", [128, 512]) for i in range(2)]
        acc2 = ps("acc2", [128, 512])
        pss = ps("pss", [128, 512])
        prot = ps("prot", [128, 512])
        ptm = [ps(f"ptm{i}", [128, 512]) for i in range(2)]
        wl = K.S("ld2")
        gw = K.S("gw")
        hl = [K.S("ld0"), K.S("ld1")]
        tl = [K.S("ld3"), K.S("ld4")]
        s_pe, s_ac, s_dv, s_pl = K.S("pe"), K.S("ac"), K.S("dv"), K.S("pl")
        sto = [K.S("st0"), K.S("st1")]
        stv = [K.S("st2"), K.S("st3")]
        stc = K.S("st4")
        for c in range(8):
            K.dma(POOL, win[:, c, :], L["w_in"][c * 128:(c + 1) * 128, 0:O_GATE], gw)
        K.dma(POOL, wuk[:], L["mla_w_uk"].rearrange("(r p) n -> p r n", p=128), gw)
        t_gw = K.dma(POOL, wuv[:], L["mla_w_uv"].rearrange("(r p) n -> p r n", p=128), gw)
        K.dma(SP, onesbd[:], dr["onesbd_f"], wl)
        K.dma(SP, ones[:], dr["ones_f"], wl)
        K.dma(SP, pt128[:], dr["pt128"], wl)
        K.dma(SP, pt96[:], dr["pt96"], wl)
        K.dma(SP, pt32[:], dr["pt32"], wl)
        for hh in range(2):
            K.dma(SP, gq[hh * 64:(hh + 1) * 64, :], L["gqa_q_norm"].rearrange("(p o) -> p o", o=1), wl)
            K.dma(SP, gk[hh * 64:(hh + 1) * 64, :], L["gqa_k_norm"].rearrange("(p o) -> p o", o=1), wl)
        t_w = K.dma(SP, kvg[:], L["mla_kv_norm"].rearrange("(r p) -> p r", p=128), wl, slow=True)
        for i in range(2):
            DVE.memset(vout[i][:], 1.0)
        t_ms = s_dv.inc(DVE.memset(qn[:], 0.0))
        for e in (PE, ACT, DVE, POOL):
            W(e, t_w)
            W(e, t_gw)
        W(ACT, t_ms)

        st = {"k": 0, "rk": 0, "vk": 0, "acc_free": [None, None], "ob_free": [None, None], "tab_free": [None, None],
              "vo_free": [None, None], "ptm_free": [None, None], "hb_tok": None}

        def store(eng, dst, src, sem):
            return K.dma(eng, dst, src, sem)

        import os
        LIMIT = int(os.environ.get("PROJ_LIMIT", "1000000"))
        units = [0]

        def over():
            units[0] += 1
            return units[0] > LIMIT

        def fm_job(chunks, M, nt, norm_g, rope, dsts, oscale=1.0):
            if over():
                return
            k = st["k"]; st["k"] += 1
            a = acc[k % 2]
            W(PE, st["acc_free"][k % 2])
            W(PE, st["hb_tok"])
            for ci, (lt, rh) in enumerate(chunks):
                ins = PE.matmul(a[:M, :nt], lhsT=lt, rhs=rh, start=(ci == 0), stop=(ci == len(chunks) - 1))
            t_main = s_pe.inc(ins)
            ob = outb[k % 2]
            if norm_g is None and rope is None:
                W(ACT, t_main)
                W(ACT, st["ob_free"][k % 2])
                t_out = s_ac.inc(ACT.activation(out=ob[:M, :nt], in_=a[:M, :nt], func=AF.Copy, scale=float(oscale)))
                st["acc_free"][k % 2] = t_out
            else:
                if rope is not None:
                    r = st["rk"]; st["rk"] += 1
                    PT, cos_ap, sin_ap = rope
                    W(SP, st["tab_free"][r % 2])
                    K.dma(SP, cosb[r % 2][:M, :nt], cos_ap, tl[r % 2])
                    t_tab = K.dma(SP, sinb[r % 2][:M, :nt], sin_ap, tl[r % 2])
                W(DVE, t_main)
                t_qf = s_dv.inc(DVE.tensor_copy(out=qf[:M, :nt], in_=a[:M, :nt]))
                t_cur = t_qf
                cur = qf
                free_toks = [t_qf]
                if norm_g is not None:
                    W(ACT, t_qf)
                    t_sq = s_ac.inc(ACT.activation(out=sqf[:M, :nt], in_=qf[:M, :nt], func=AF.Square))
                    W(PE, t_sq)
                    t_ss = s_pe.inc(PE.matmul(pss[:M, :nt], lhsT=onesbd[:M, :M], rhs=sqf[:M, :nt], start=True, stop=True))
                    W(ACT, t_ss)
                    W(ACT, t_qf)
                    t_sd = s_ac.inc(ACT.activation(out=sd[:M, :nt], in_=pss[:M, :nt], func=AF.Sqrt, bias=EPS, scale=1.0 / 64))
                    W(DVE, t_sd)
                    t_rs = s_dv.inc(DVE.reciprocal(out=rs[:M, :nt], in_=sd[:M, :nt]))
                    W(DVE, t_rs)
                    if rope is None:
                        W(DVE, st["ob_free"][k % 2])
                        t_out = s_dv.inc(DVE.scalar_tensor_tensor(out=ob[:M, :nt], in0=qf[:M, :nt], scalar=norm_g, in1=rs[:M, :nt], op0=ALU.mult, op1=ALU.mult))
                    else:
                        t_cur = s_dv.inc(DVE.scalar_tensor_tensor(out=qn[:M, :nt], in0=qf[:M, :nt], scalar=norm_g, in1=rs[:M, :nt], op0=ALU.mult, op1=ALU.mult))
                        cur = qn
                st["acc_free"][k % 2] = free_toks
                if rope is not None:
                    W(ACT, t_cur)
                    t_qb = s_ac.inc(ACT.activation(out=qb[:M, :nt], in_=cur[:M, :nt], func=AF.Copy))
                    W(PE, t_qb)
                    t_rot = s_pe.inc(PE.matmul(prot[:M, :nt], lhsT=PT[:M, :M], rhs=qb[:M, :nt], start=True, stop=True))
                    W(POOL, t_cur)
                    W(POOL, t_tab)
                    t_t1 = s_pl.inc(POOL.tensor_tensor(out=t1b[:M, :nt], in0=cur[:M, :nt], in1=cosb[r % 2][:M, :nt], op=ALU.mult))
                    W(DVE, t_rot)
                    W(DVE, t_tab)
                    t_t2 = s_dv.inc(DVE.tensor_tensor(out=t2b[:M, :nt], in0=prot[:M, :nt], in1=sinb[r % 2][:M, :nt], op=ALU.mult))
                    W(DVE, t_t1)
                    W(DVE, t_t2)
                    W(DVE, st["ob_free"][k % 2])
                    t_out = s_dv.inc(DVE.tensor_tensor(out=ob[:M, :nt], in0=t1b[:M, :nt], in1=t2b[:M, :nt], op=ALU.add))
                    st["tab_free"][r % 2] = t_out
            W(SP, t_out)
            for (dst, r0, r1) in dsts:
                t_st = store(SP, dst, ob[r0:r1, :nt], sto[k % 2])
            st["ob_free"][k % 2] = t_st

        def ckv_job(hs, nt, dst_ckvt):
            if over():
                st["ckvn_tok"] = None
                return
            k = st["k"]; st["k"] += 1
            a = acc[k % 2]
            W(PE, st["acc_free"][k % 2])
            W(PE, st["hb_tok"])
            W(PE, st.get("acc2_free"))
            for g, aa in enumerate((a, acc2)):
                for c in range(8):
                    ins = PE.matmul(aa[:, :nt], lhsT=win[:, c, O_CKV + g * 128:O_CKV + (g + 1) * 128], rhs=hblk[hs][:, c, :nt], start=(c == 0), stop=(c == 7))
            t_main = s_pe.inc(ins)
            CUT = int(os.environ.get("CKV_CUT", "99"))
            st["ckvn_tok"] = None
            if CUT <= 1:
                return
            W(DVE, t_main)
            DVE.tensor_copy(out=qf[:, :nt], in_=a[:, :nt])
            t_qf = s_dv.inc(DVE.tensor_copy(out=qf2[:, :nt], in_=acc2[:, :nt]))
            W(ACT, t_qf)
            ACT.activation(out=sqf[:, :nt], in_=qf[:, :nt], func=AF.Square)
            t_sq = s_ac.inc(ACT.activation(out=sqf2[:, :nt], in_=qf2[:, :nt], func=AF.Square))
            st["acc_free"][k % 2] = [t_qf]
            st["acc2_free"] = [t_qf]
            if CUT <= 2:
                return
            W(PE, t_sq)
            PE.matmul(pss[:, :nt], lhsT=ones[:], rhs=sqf[:, :nt], start=True, stop=False)
            t_ss = s_pe.inc(PE.matmul(pss[:, :nt], lhsT=ones[:], rhs=sqf2[:, :nt], start=False, stop=True))
            if CUT <= 3:
                return
            W(ACT, t_ss)
            W(ACT, t_qf)
            t_sd = s_ac.inc(ACT.activation(out=sd[:, :nt], in_=pss[:, :nt], func=AF.Sqrt, bias=EPS, scale=1.0 / 256))
            W(DVE, t_sd)
            t_rs = s_dv.inc(DVE.reciprocal(out=rs[:, :nt], in_=sd[:, :nt]))
            if CUT <= 4:
                return
            W(DVE, t_rs)
            W(DVE, st.get("ckvn_free"))
            DVE.scalar_tensor_tensor(out=ckvn[:, 0, :nt], in0=qf[:, :nt], scalar=kvg[:, 0:1], in1=rs[:, :nt], op0=ALU.mult, op1=ALU.mult)
            t_out = s_dv.inc(DVE.scalar_tensor_tensor(out=ckvn[:, 1, :nt], in0=qf2[:, :nt], scalar=kvg[:, 1:2], in1=rs[:, :nt], op0=ALU.mult, op1=ALU.mult))
            if CUT <= 5:
                return
            W(SP, t_out)
            t_st = store(SP, dst_ckvt.rearrange("(r p) t -> p r t", p=128), ckvn[:, :, :nt], stc)
            st["ckvn_tok"] = t_out
            st["ckvn_st"] = t_st

        def tm_job(chunks, N, nh, dst):
            if over():
                return
            j = st["vk"]; st["vk"] += 1
            p = ptm[j % 2]
            W(PE, st["ptm_free"][j % 2])
            W(PE, st["hb_tok"])
            for ci, (lt, rh) in enumerate(chunks):
                ins = PE.matmul(p[:, :N], lhsT=lt, rhs=rh, start=(ci == 0), stop=(ci == len(chunks) - 1))
            t_main = s_pe.inc(ins)
            W(ACT, t_main)
            W(ACT, st["vo_free"][j % 2])
            t_o = s_ac.inc(ACT.activation(out=vout[j % 2][:, 0:nh, 0:64], in_=p[:, :N].rearrange("p (h d) -> p h d", d=64), func=AF.Copy))
            st["ptm_free"][j % 2] = t_o
            W(SP, t_o)
            st["vo_free"][j % 2] = store(SP, dst, vout[j % 2][:, 0:nh, :], stv[j % 2])

        nblk = [0]
        last_users = [None, None]

        def load_block(src_ap, nt):
            b = nblk[0]; nblk[0] += 1
            W(SP, last_users[b % 2])
            st["hb_tok"] = K.dma(SP, hblk[b % 2][:, :, :nt], src_ap.rearrange("(c p) t -> p c t", p=128), hl[b % 2])
            return b % 2

        def done_block(hs):
            last_users[hs] = (s_pe, s_pe.n)

        def hch(hs, c0, M, nt):
            return [(win[:, c, c0:c0 + M], hblk[hs][:, c, :nt]) for c in range(8)]

        hT_all, hT_na = dr["hT_all"], dr["hT_na"]
        for tb in range(17):
            ctxb = (tb == 16)
            t0 = tb * 512
            nt = 256 if ctxb else 512
            hs = load_block(hT_all[:, t0:t0 + nt], nt)
            rope = None if ctxb else (pt128, dr["cosA_all"][:, t0:t0 + nt], dr["sinA_all"][:, t0:t0 + nt])
            fm_job(hch(hs, O_GK, 128, nt), 128, nt, gk[:, 0:1], rope,
                   [(dr["GKT"][0, :, t0:t0 + nt], 0, 64), (dr["GKT"][1, :, t0:t0 + nt], 64, 128)])
            ckv_job(hs, nt, dr["CKVT"][:, t0:t0 + nt])
            rope = None if ctxb else (pt32, dr["cosM_all"][:, t0:t0 + nt], dr["sinM_all"][:, t0:t0 + nt])
            fm_job(hch(hs, O_KR, 32, nt), 32, nt, None, rope, [(dr["MKT"][h, 64:96, t0:t0 + nt], 0, 32) for h in range(8)])
            W(PE, st["ckvn_tok"])
            for g in range(4):
                fm_job([(wuk[:, r, g * 128:(g + 1) * 128], ckvn[:, r, :nt]) for r in range(2)], 128, nt, None, None,
                       [(dr["MKT"][2 * g, 0:64, t0:t0 + nt], 0, 64), (dr["MKT"][2 * g + 1, 0:64, t0:t0 + nt], 64, 128)])
            for ti in range(nt // 128):
                tsl = slice(ti * 128, (ti + 1) * 128)
                r0 = t0 + ti * 128
                tm_job([(ckvn[:, r, tsl], wuv[:, r, :]) for r in range(2)], 512, 8, dr["MV"][r0:r0 + 128, :, :])
                tm_job([(hblk[hs][:, c, tsl], win[:, c, O_GV:O_GV + 128]) for c in range(8)], 128, 2, dr["GV"][r0:r0 + 128, :, :])
            st["ckvn_free"] = (s_pe, s_pe.n)
            if ctxb:
                for g in range(4):
                    fm_job(hch(hs, O_NK + g * 128, 128, nt), 128, nt, None, None,
                           [(dr["NKT"][2 * g, :, NAT:NAT + nt], 0, 64), (dr["NKT"][2 * g + 1, :, NAT:NAT + nt], 64, 128)])
                for ti in range(nt // 128):
                    tsl = slice(ti * 128, (ti + 1) * 128)
                    tm_job([(hblk[hs][:, c, tsl], win[:, c, O_NV:O_NV + 512]) for c in range(8)], 512, 8, dr["NV"][NAT + ti * 128:NAT + (ti + 1) * 128, :, :])
                if with_ctx:
                    q0 = T
                    for g in range(4):
                        fm_job(hch(hs, O_GQ + g * 128, 128, nt), 128, nt, gq[:, 0:1], None,
                               [(dr["GQT"][2 * g, :, q0:q0 + nt], 0, 64), (dr["GQT"][2 * g + 1, :, q0:q0 + nt], 64, 128)])
                        fm_job(hch(hs, O_NQ + g * 128, 128, nt), 128, nt, None, None,
                               [(dr["NQT"][2 * g, :, q0:q0 + nt], 0, 64), (dr["NQT"][2 * g + 1, :, q0:q0 + nt], 64, 128)], oscale=0.125)
                    for h in range(8):
                        fm_job(hch(hs, O_MQ + h * 96, 96, nt), 96, nt, None, None, [(dr["MQT"][h, :, q0:q0 + nt], 0, 96)])
            done_block(hs)
        for tb in range(5):
            t0 = tb * 512
            nt = 512
            hs = load_block(hT_na[:, t0:t0 + nt], nt)
            for g in range(4):
                fm_job(hch(hs, O_NK + g * 128, 128, nt), 128, nt, None, None,
                       [(dr["NKT"][2 * g, :, t0:t0 + nt], 0, 64), (dr["NKT"][2 * g + 1, :, t0:t0 + nt], 64, 128)])
            for ti in range(4):
                tsl = slice(ti * 128, (ti + 1) * 128)
                tm_job([(hblk[hs][:, c, tsl], win[:, c, O_NV:O_NV + 512]) for c in range(8)], 512, 8, dr["NV"][t0 + ti * 128:t0 + (ti + 1) * 128, :, :])
            done_block(hs)
        for tb in range(4):
            q0 = tb * 512
            nt = 512
            hs = load_block(hT_na[:, 256 + q0:256 + q0 + nt], nt)
            for g in range(4):
                fm_job(hch(hs, O_GQ + g * 128, 128, nt), 128, nt, gq[:, 0:1], (pt128, dr["cosA_own"][:, q0:q0 + nt], dr["sinA_own"][:, q0:q0 + nt]),
                       [(dr["GQT"][2 * g, :, q0:q0 + nt], 0, 64), (dr["GQT"][2 * g + 1, :, q0:q0 + nt], 64, 128)])
                fm_job(hch(hs, O_NQ + g * 128, 128, nt), 128, nt, None, None,
                       [(dr["NQT"][2 * g, :, q0:q0 + nt], 0, 64), (dr["NQT"][2 * g + 1, :, q0:q0 + nt], 64, 128)], oscale=0.125)
            for h in range(8):
                fm_job(hch(hs, O_MQ + h * 96, 96, nt), 96, nt, None, (pt96, dr["cosM_own"][:, q0:q0 + nt], dr["sinM_own"][:, q0:q0 + nt]),
                       [(dr["MQT"][h, :, q0:q0 + nt], 0, 96)])
            done_block(hs)
        K.barrier([(s, s.n) for s in sto + stv + [stc]])


def phase_attn(K, heads, pfx, nkmax):
    nc = K.nc
    PE, ACT, DVE, POOL, SP = K.PE, K.ACT, K.DVE, K.POOL, K.SP
    NQ = T + C
    with ExitStack() as es:
        sb, ps = mk_alloc(nc, es, pfx)
        ktb = [sb(f"kt{i}", [128, nkmax], BF16) for i in range(2)]
        vb = [sb(f"v{i}", [128, nkmax // 128, 65], BF16) for i in range(2)]
        qb = [sb(f"q{i}", [128, NQ], BF16) for i in range(2)]
        pbuf = [sb(f"p{i}", [128, 512], BF16) for i in range(3)]
        bb = [sb(f"bias{i}", [128, 512], BF16) for i in range(3)]
        identb = sb("identb", [128, 128], BF16)
        osb = sb("osb", [128, 512])
        rl = sb("rl", [128, 512])
        ones = sb("ones", [128, 128])
        ysb = [sb(f"y{i}", [128, 512], BF16) for i in range(2)]
        psb = [ps(f"s{i}", [128, 512]) for i in range(3)]
        po = [ps(f"o{i}", [128, 512]) for i in range(2)]
        pbc = ps("bc", [128, 512])
        hl = [K.S("ld0"), K.S("ld1")]
        bl = [K.S("gb0"), K.S("gb1"), K.S("gb2")]
        cl = K.S("ld5")
        s_pe, s_ac, s_dv = K.S("pe"), K.S("ac"), K.S("dv")
        sty = [K.S("st0"), K.S("st1")]
        K.dma(SP, identb[:], K.dram["ident_b"], cl)
        t_c = K.dma(SP, ones[:], K.dram["ones_f"], cl)
        for i in range(2):
            DVE.memset(ktb[i][:], 0.0)
            DVE.memset(qb[i][:], 0.0)
        t_m = s_dv.inc(DVE.memset(rl[:], 1.0))
        W(PE, t_c)
        W(PE, t_m)
        W(SP, t_m)
        steps = []
        for hi, h in enumerate(heads):
            for bi, b in enumerate(h["blocks"]):
                nt_ = len(b["tiles"])
                for si, (kti, bias) in enumerate(b["tiles"]):
                    steps.append(dict(hi=hi, b=b, kti=kti, bias=bias, first=(si == 0), last=(si == nt_ - 1),
                                      hfirst=(bi == 0 and si == 0), hlast=(bi == len(h["blocks"]) - 1 and si == nt_ - 1)))
        NS = len(steps)
        head_tok = [None] * len(heads)
        head_done = [None] * len(heads)

        def load_head(hi):
            h = heads[hi]
            s = hi % 2
            if hi >= 2:
                W(SP, head_done[hi - 2])
            dk, nk = h["dk"], h["nk"]
            K.dma(SP, ktb[s][:dk, :nk], h["kt"], hl[s])
            K.dma(SP, vb[s][:, :nk // 128, :], h["v"].rearrange("(t p) e -> p t e", p=128), hl[s])
            head_tok[hi] = K.dma(SP, qb[s][:dk, :], h["qt"], hl[s])

        tq = [None] * NS
        tb_ = [None] * NS
        te = [None] * NS
        tv = [None] * NS
        bias_ld = [None] * NS
        nbias = [0]
        bidx = [None] * NS
        blk_id = [0]
        po_free = [None, None]
        y_free = [None, None]
        pend_pe = {}
        pend_dv = {}
        state = {"bc_tok": None, "y_tok": None}

        def emit_qk(t):
            s = steps[t]
            h = heads[s["hi"]]
            hs = s["hi"] % 2
            if s["hfirst"]:
                W(PE, head_tok[s["hi"]])
            if t >= 3:
                W(PE, te[t - 3])
            b = s["b"]
            dk = h["dk"]
            hasb = s["bias"] is not None
            ins = PE.matmul(psb[t % 3][:, :b["nq"]], lhsT=ktb[hs][:, s["kti"] * 128:(s["kti"] + 1) * 128],
                            rhs=qb[hs][:, b["q0"]:b["q0"] + b["nq"]], start=True, stop=not hasb)
            if hasb:
                W(PE, bias_ld[t])
                ins = PE.matmul(psb[t % 3][:, :b["nq"]], lhsT=identb[:], rhs=bb[bidx[t] % 3][:, :b["nq"]], start=False, stop=True)
            tq[t] = s_pe.inc(ins)
            if hasb:
                state[("bfree", bidx[t] % 3)] = tq[t]

        def emit_bias_load(t):
            s = steps[t]
            if s["bias"] is None:
                return
            n = nbias[0]; nbias[0] += 1
            bidx[t] = n
            W(POOL, state.get(("bfree", n % 3)))
            bias_ld[t] = K.dma(POOL, bb[n % 3][:, :s["b"]["nq"]], s["bias"], bl[n % 3])

        if NS > 0:
            load_head(0)
        LA = 2
        for t in range(min(LA, NS)):
            emit_bias_load(t)
            emit_qk(t)
        cur_blk = -1
        for t in range(NS):
            s = steps[t]
            h = heads[s["hi"]]
            b = s["b"]
            nq = b["nq"]
            hs = s["hi"] % 2
            if s["hfirst"] and s["hi"] + 1 < len(heads):
                load_head(s["hi"] + 1)
            if s["first"]:
                cur_blk += 1
            if t + LA < NS:
                emit_bias_load(t + LA)
                emit_qk(t + LA)
            if True:
                W(ACT, tq[t])
                if t >= 3:
                    W(ACT, tv[t - 3])
                te[t] = s_ac.inc(ACT.activation(out=pbuf[t % 3][:, :nq], in_=psb[t % 3][:, :nq], func=AF.Exp, scale=float(h["scale"])))
            for f in pend_pe.pop(t, []):
                f()
            W(PE, te[t])
            if s["first"]:
                W(PE, po_free[cur_blk % 2])
            tv[t] = s_pe.inc(PE.matmul(po[cur_blk % 2][:65, :nq], lhsT=vb[hs][:, s["kti"], :], rhs=pbuf[t % 3][:, :nq],
                                       start=s["first"], stop=s["last"]))
            if s["hlast"]:
                head_done[s["hi"]] = tv[t]
            for f in pend_dv.pop(t, []):
                f()
            if s["last"]:
                cb = cur_blk
                W(DVE, tv[t])
                t_o = s_dv.inc(DVE.tensor_copy(out=osb[:65, :nq], in_=po[cb % 2][:65, :nq]))
                po_free[cb % 2] = t_o
                W(DVE, t_o)
                t_rl = s_dv.inc(DVE.reciprocal(out=rl[64:65, :nq], in_=osb[64:65, :nq]))

                def pe_part(nq=nq, t_rl=t_rl):
                    W(PE, t_rl)
                    W(PE, state["y_tok"])
                    state["bc_tok"] = s_pe.inc(PE.matmul(pbc[:64, :nq], lhsT=ones[64:65, 0:64], rhs=rl[64:65, :nq], start=True, stop=True))

                def dv_part(nq=nq, cb=cb, yt=b["yt"]):
                    W(DVE, state["bc_tok"])
                    W(DVE, y_free[cb % 2])
                    state["y_tok"] = s_dv.inc(DVE.tensor_tensor(out=ysb[cb % 2][:64, :nq], in0=osb[:64, :nq], in1=pbc[:64, :nq], op=ALU.mult))
                    W(SP, state["y_tok"])
                    y_free[cb % 2] = K.dma(SP, yt, ysb[cb % 2][:64, :nq], sty[cb % 2])

                if t + 1 < NS:
                    nxt_len = len(steps[t + 1]["b"]["tiles"])
                    d = min(2, nxt_len - 1)
                    pend_pe.setdefault(t + max(d, 1) if nxt_len > 1 else t + 1, []).append(pe_part)
                    pend_dv.setdefault(t + max(d, 1) if nxt_len > 1 else t + 1, []).append(dv_part)
                else:
                    pe_part()
                    dv_part()
        assert not pend_pe and not pend_dv
        K.barrier([(s, s.n) for s in sty])


def phase_merge(K, L, pfx, qblocks, x_src, x_dst):
    nc = K.nc
    PE, ACT, DVE, POOL, SP = K.PE, K.ACT, K.DVE, K.POOL, K.SP
    dr = K.dram
    with ExitStack() as es:
        sb, ps = mk_alloc(nc, es, pfx)
        wg = sb("wg", [128, 8, 3072], BF16)
        wo = [sb(f"wo{i}", [128, 4, 1024], BF16) for i in range(3)]
        wout = sb("wout", [128, 8, 1024], BF16)
        g1 = sb("g1", [128, 2, 1024])
        hblk = [sb(f"h{i}", [128, 8, 512], BF16) for i in range(2)]
        yb = [[sb(f"y{r}_{i}", [128, 4, 512], BF16) for r in range(3)] for i in range(2)]
        sg = [sb(f"sg{i}", [128, 512]) for i in range(2)]
        yacc = sb("yacc", [128, 512])
        tmp = sb("tmp", [128, 512])
        yT = sb("yT", [128, 8, 512], BF16)
        xt = [sb(f"xt{i}", [128, 1024]) for i in range(2)]
        xo = [sb(f"xo{i}", [128, 1024]) for i in range(2)]
        tm2 = [sb(f"tm{i}", [128, 512]) for i in range(2)]
        pg = [ps(f"pg{i}", [128, 512]) for i in range(2)]
        pbr = [ps(f"pb{i}", [128, 512]) for i in range(2)]
        pw = [ps(f"pw{i}", [128, 512]) for i in range(2)]
        wl = K.S("ld2")
        hl = [K.S("ld0"), K.S("ld1")]
        xl = [K.S("ld3"), K.S("ld4")]
        s_pe, s_ac, s_dv, s_pl = K.S("pe"), K.S("ac"), K.S("dv"), K.S("pl")
        stx = [K.S("st0"), K.S("st1")]
        gw = K.S("gw")
        for c in range(8):
            K.dma(POOL, wg[:, c, :], L["w_in"][c * 128:(c + 1) * 128, O_GATE:WIN], gw)
        for r, nm in enumerate(("w_o_gqa", "w_o_na", "w_o_mla")):
            K.dma(POOL, wo[r][:], L[nm].rearrange("(c p) n -> p c n", p=128), gw)
        t_gw = K.dma(POOL, wout[:], L["w_out"].rearrange("(c p) n -> p c n", p=128), gw)
        K.dma(SP, g1[:, 0, :], dr["modv"][0, 2].partition_broadcast(128), wl)
        t_w = K.dma(SP, g1[:, 1, :], dr["modv"][1, 2].partition_broadcast(128), wl)
        for e in (PE, DVE, POOL):
            W(e, t_w)
            W(e, t_gw)
        ysrc = (dr["YAT"], dr["YBT"], dr["YCT"])
        blk_done = [None, None]
        k = 0
        xk = 0
        sg_free = [None, None]
        pg_free = [None, None]
        pbr_free = [None, None]
        pw_free = [None, None]
        xt_free = [None, None]
        xo_free = [None, None]
        tm_free = [None, None]
        yT_free = None
        for bi, (hT_ap, q0, nt, m) in enumerate(qblocks):
            s = bi % 2
            W(SP, blk_done[s])
            K.dma(SP, hblk[s][:, :, :nt], hT_ap.rearrange("(c p) t -> p c t", p=128), hl[s])
            for r in range(3):
                t_l = K.dma(SP, yb[s][r][:, :, :nt], ysrc[r][:, q0:q0 + nt].rearrange("(c p) t -> p c t", p=128), hl[s])
            W(PE, t_l)
            for oc in range(8):
                for r in range(3):
                    W(PE, pg_free[k % 2])
                    for c in range(8):
                        ins = PE.matmul(pg[k % 2][:, :nt], lhsT=wg[:, c, r * 1024 + oc * 128:r * 1024 + (oc + 1) * 128], rhs=hblk[s][:, c, :nt],
                                        start=(c == 0), stop=(c == 7))
                    t_g = s_pe.inc(ins)
                    W(PE, pbr_free[k % 2])
                    for c in range(4):
                        ins = PE.matmul(pbr[k % 2][:, :nt], lhsT=wo[r][:, c, oc * 128:(oc + 1) * 128], rhs=yb[s][r][:, c, :nt], start=(c == 0), stop=(c == 3))
                    t_b = s_pe.inc(ins)
                    W(ACT, t_g)
                    W(ACT, sg_free[k % 2])
                    t_s = s_ac.inc(ACT.activation(out=sg[k % 2][:, :nt], in_=pg[k % 2][:, :nt], func=AF.Sigmoid))
                    pg_free[k % 2] = t_s
                    W(DVE, t_s)
                    W(DVE, t_b)
                    if r == 0:
                        t_d = s_dv.inc(DVE.tensor_tensor(out=yacc[:, :nt], in0=sg[k % 2][:, :nt], in1=pbr[k % 2][:, :nt], op=ALU.mult))
                    else:
                        t_d = s_dv.inc(DVE.tensor_tensor(out=tmp[:, :nt], in0=sg[k % 2][:, :nt], in1=pbr[k % 2][:, :nt], op=ALU.mult))
                        W(DVE, t_d)
                        if r == 1:
                            t_d = s_dv.inc(DVE.tensor_tensor(out=yacc[:, :nt], in0=yacc[:, :nt], in1=tmp[:, :nt], op=ALU.add))
                        else:
                            if oc == 0:
                                W(DVE, yT_free)
                            t_d = s_dv.inc(DVE.tensor_tensor(out=yT[:, oc, :nt], in0=yacc[:, :nt], in1=tmp[:, :nt], op=ALU.add))
                    sg_free[k % 2] = t_d
                    pbr_free[k % 2] = t_d
                    k += 1
            blk_done[s] = (s_pe, s_pe.n)
            t_y = t_d
            W(PE, t_y)
            for ti in range(nt // 128):
                xs_ = xk % 2
                W(SP, xt_free[xs_])
                t_x = K.dma(SP, xt[xs_][:], x_src(q0 + ti * 128), xl[xs_])
                for half in range(2):
                    j = 2 * xk + half
                    W(PE, pw_free[j % 2])
                    for c in range(8):
                        ins = PE.matmul(pw[j % 2][:, :], lhsT=yT[:, c, ti * 128:(ti + 1) * 128], rhs=wout[:, c, half * 512:(half + 1) * 512],
                                        start=(c == 0), stop=(c == 7))
                    t_p = s_pe.inc(ins)
                    W(DVE, t_p)
                    W(DVE, tm_free[j % 2])
                    t_m = s_dv.inc(DVE.tensor_tensor(out=tm2[j % 2][:], in0=pw[j % 2][:], in1=g1[:, m, half * 512:(half + 1) * 512], op=ALU.mult))
                    pw_free[j % 2] = t_m
                    W(POOL, t_m)
                    W(POOL, t_x)
                    if half == 0:
                        W(POOL, xo_free[xs_])
                    t_a = s_pl.inc(POOL.tensor_tensor(out=xo[xs_][:, half * 512:(half + 1) * 512], in0=tm2[j % 2][:], in1=xt[xs_][:, half * 512:(half + 1) * 512], op=ALU.add))
                    tm_free[j % 2] = t_a
                xt_free[xs_] = t_a
                W(SP, t_a)
                xo_free[xs_] = K.dma(SP, x_dst(q0 + ti * 128), xo[xs_][:], stx[xs_])
                xk += 1
            yT_free = (s_pe, s_pe.n)
        K.barrier([(s_, s_.n) for s_ in stx])


def phase_mlp(K, L, pfx, qblocks, x_src, x_dst, final_g):
    nc = K.nc
    PE, ACT, DVE, POOL, SP = K.PE, K.ACT, K.DVE, K.POOL, K.SP
    dr = K.dram
    with ExitStack() as es:
        sb, ps = mk_alloc(nc, es, pfx)
        w1 = sb("w1", [128, 8, 4096], BF16)
        w2 = sb("w2", [128, 32, 1024], BF16)
        g2 = sb("g2", [128, 2, 1024])
        fg = sb("fg", [128, 1024])
        hblk = [sb(f"h{i}", [128, 8, 256], BF16) for i in range(2)]
        uT = sb("uT", [128, 32, 256], BF16)
        rb = [sb(f"r{i}", [128, 256]) for i in range(2)]
        xt = [sb(f"xt{i}", [128, 1024]) for i in range(2)]
        xo = [sb(f"xo{i}", [128, 1024]) for i in range(2)]
        tm2 = [sb(f"tm{i}", [128, 512]) for i in range(2)]
        junk = sb("junk", [128, 1024])
        st4 = sb("st4", [128, 4])
        pu = [ps(f"pu{i}", [128, 512]) for i in range(2)]
        pw = [ps(f"pw{i}", [128, 512]) for i in range(2)]
        wl = K.S("ld2")
        hl = [K.S("ld0"), K.S("ld1")]
        xl = [K.S("ld3"), K.S("ld4")]
        s_pe, s_ac, s_dv, s_pl = K.S("pe"), K.S("ac"), K.S("dv"), K.S("pl")
        stx = [K.S("st0"), K.S("st1")]
        gw = K.S("gw")
        for c in range(8):
            K.dma(POOL, w1[:, c, :], L["w_mlp1"][c * 128:(c + 1) * 128, :], gw)
        for c4 in range(4):
            t_gw = K.dma(POOL, w2[:, c4 * 8:(c4 + 1) * 8, :], L["w_mlp2"][c4 * 1024:(c4 + 1) * 1024, :].rearrange("(c p) n -> p c n", p=128), gw)
        K.dma(SP, g2[:, 0, :], dr["modv"][0, 5].partition_broadcast(128), wl)
        if final_g is not None:
            K.dma(SP, fg[:], final_g.partition_broadcast(128), wl)
        t_w = K.dma(SP, g2[:, 1, :], dr["modv"][1, 5].partition_broadcast(128), wl)
        for e in (PE, DVE, POOL, ACT):
            W(e, t_w)
            W(e, t_gw)
        h2T = dr["h2T"]
        blk_done = [None, None]
        k = 0
        xk = 0
        pu_free = [None, None]
        rb_free = [None, None]
        pw_free = [None, None]
        xt_free = [None, None]
        xo_free = [None, None]
        tm_free = [None, None]
        uT_free = None
        for bi, (q0, nt, m) in enumerate(qblocks):
            s = bi % 2
            W(SP, blk_done[s])
            t_l = K.dma(SP, hblk[s][:, :, :nt], h2T[:, q0:q0 + nt].rearrange("(c p) t -> p c t", p=128), hl[s])
            W(PE, t_l)
            for fc in range(32):
                W(PE, pu_free[k % 2])
                for c in range(8):
                    ins = PE.matmul(pu[k % 2][:, :nt], lhsT=w1[:, c, fc * 128:(fc + 1) * 128], rhs=hblk[s][:, c, :nt], start=(c == 0), stop=(c == 7))
                t_u = s_pe.inc(ins)
                W(ACT, t_u)
                W(ACT, rb_free[k % 2])
                t_r = s_ac.inc(ACT.activation(out=rb[k % 2][:, :nt], in_=pu[k % 2][:, :nt], func=AF.Relu))
                pu_free[k % 2] = t_r
                W(DVE, t_r)
                if fc == 0:
                    W(DVE, uT_free)
                t_q = s_dv.inc(DVE.tensor_tensor(out=uT[:, fc, :nt], in0=rb[k % 2][:, :nt], in1=rb[k % 2][:, :nt], op=ALU.mult))
                rb_free[k % 2] = t_q
                k += 1
            blk_done[s] = (s_pe, s_pe.n)
            W(PE, t_q)
            for ti in range(nt // 128):
                xs_ = xk % 2
                W(SP, xt_free[xs_])
                t_x = K.dma(SP, xt[xs_][:], x_src(q0 + ti * 128), xl[xs_])
                for half in range(2):
                    j = 2 * xk + half
                    W(PE, pw_free[j % 2])
                    for fc in range(32):
                        ins = PE.matmul(pw[j % 2][:, :], lhsT=uT[:, fc, ti * 128:(ti + 1) * 128], rhs=w2[:, fc, half * 512:(half + 1) * 512],
                                        start=(fc == 0), stop=(fc == 31))
                    t_p = s_pe.inc(ins)
                    W(DVE, t_p)
                    W(DVE, tm_free[j % 2])
                    t_m = s_dv.inc(DVE.tensor_tensor(out=tm2[j % 2][:], in0=pw[j % 2][:], in1=g2[:, m, half * 512:(half + 1) * 512], op=ALU.mult))
                    pw_free[j % 2] = t_m
                    W(POOL, t_m)
                    W(POOL, t_x)
                    if half == 0:
                        W(POOL, xo_free[xs_])
                    t_a = s_pl.inc(POOL.tensor_tensor(out=xo[xs_][:, half * 512:(half + 1) * 512], in0=tm2[j % 2][:], in1=xt[xs_][:, half * 512:(half + 1) * 512], op=ALU.add))
                    tm_free[j % 2] = t_a
                xt_free[xs_] = t_a
                t_fin = t_a
                if final_g is not None:
                    W(ACT, t_a)
                    t1 = s_ac.inc(ACT.activation(out=junk[:], in_=xo[xs_][:], func=AF.Square, accum_out=st4[:, 0:1]))
                    W(DVE, t1)
                    t2 = s_dv.inc(DVE.tensor_scalar(out=st4[:, 1:2], in0=st4[:, 0:1], scalar1=1.0 / D, scalar2=EPS, op0=ALU.mult, op1=ALU.add))
                    W(ACT, t2)
                    t3 = s_ac.inc(ACT.activation(out=st4[:, 2:3], in_=st4[:, 1:2], func=AF.Sqrt))
                    W(DVE, t3)
                    t4 = s_dv.inc(DVE.reciprocal(out=st4[:, 3:4], in_=st4[:, 2:3]))
                    W(DVE, t4)
                    t_fin = s_dv.inc(DVE.scalar_tensor_tensor(out=xo[xs_][:], in0=xo[xs_][:], scalar=st4[:, 3:4], in1=fg[:], op0=ALU.mult, op1=ALU.mult))
                W(SP, t_fin)
                xo_free[xs_] = K.dma(SP, x_dst(q0 + ti * 128), xo[xs_][:], stx[xs_])
                xk += 1
            uT_free = (s_pe, s_pe.n)
        K.barrier([(s_, s_.n) for s_ in stx])


def phase_halo(K, pfx):
    nc = K.nc
    PE, ACT, DVE, POOL, SP = K.PE, K.ACT, K.DVE, K.POOL, K.SP
    dr = K.dram
    xg, x1, xna, sel = K.xg_at, K.x1_at, dr["x_na2"], dr["sel"]
    with ExitStack() as es:
        sb, ps = mk_alloc(nc, es, pfx)
        selb = sb("sel", [128, 8])
        cand = [sb(f"c{i}", [128, 1024]) for i in range(4)]
        acc = [sb(f"a{i}", [128, 1024]) for i in range(2)]
        ld = [K.S("ld0"), K.S("ld1"), K.S("ld3"), K.S("ld4")]
        lc = K.S("ld2")
        s_dv = K.S("dv")
        st = [K.S("st0"), K.S("st1")]
        so = K.S("st2")
        t_c = K.dma(SP, selb[:], sel.partition_broadcast(128), lc)
        for q in range(0, T, 128):
            t_own = K.dma(SP, xna[256 + q:256 + q + 128, :], x1(q), so)
        W(DVE, t_c)
        jobs = []
        for u in range(2):
            jobs.append((128 * u, [xg(2048 * r + 1792 + 128 * u) for r in range(4)], 0))
        for u in range(2):
            jobs.append((256 + T + 128 * u, [xg(2048 * r + 128 * u) for r in range(4)], 4))
        dv_prev = None
        st_t = [None, None]
        for n, (row0, srcs, c0) in enumerate(jobs):
            W(SP, dv_prev)
            lts = [K.dma(SP, cand[r][:], srcs[r], ld[r]) for r in range(4)]
            W(DVE, lts)
            W(DVE, st_t[n % 2])
            t = s_dv.inc(DVE.tensor_scalar(out=acc[n % 2][:], in0=cand[0][:], scalar1=selb[:, c0:c0 + 1], scalar2=0.0, op0=ALU.mult, op1=ALU.add))
            for r in range(1, 4):
                W(DVE, t)
                t = s_dv.inc(DVE.scalar_tensor_tensor(out=acc[n % 2][:], in0=cand[r][:], scalar=selb[:, c0 + r:c0 + r + 1], in1=acc[n % 2][:],
                                                      op0=ALU.mult, op1=ALU.add))
            dv_prev = t
            W(SP, t)
            st_t[n % 2] = K.dma(SP, xna[row0:row0 + 128, :], acc[n % 2][:], st[n % 2])
        K.barrier([t_own, st_t[0], st_t[1]])


W_NAMES = ["w_mod", "b_mod", "norm1_g", "norm2_g", "w_in", "gqa_q_norm", "gqa_k_norm", "mla_kv_norm", "mla_w_uk", "mla_w_uv",
           "w_o_gqa", "w_o_na", "w_o_mla", "w_out", "w_mlp1", "w_mlp2"]
W_SHAPES = {"w_mod": [D, 6 * D], "b_mod": [6 * D], "norm1_g": [D], "norm2_g": [D], "w_in": [D, WIN], "gqa_q_norm": [64], "gqa_k_norm": [64],
            "mla_kv_norm": [256], "mla_w_uk": [256, 512], "mla_w_uv": [256, 512], "w_o_gqa": [512, D], "w_o_na": [512, D], "w_o_mla": [512, D],
            "w_out": [D, D], "w_mlp1": [D, 4 * D], "w_mlp2": [4 * D, D]}
DEPTH = 2


def emit_layer(K, L, src, with_ctx, final, sfx):
    dr = K.dram
    NQ = T + C
    phase_mod(K, L, "md" + sfx)
    phase_norm(K, [(src["x_all"], S, dr["hT_all"][:, 0:S], 0, 0, 1), (src["ctx_in"], C, dr["hT_all"][:, S:SK], 1, 0, 1),
                   (src["x_na"], NAT, dr["hT_na"], 0, 0, 1)], "n1" + sfx)
    phase_proj(K, L, "pj" + sfx, with_ctx)
    qbl = [(512 * i, 512) for i in range(4)]
    heads = []
    for h in range(8):
        blocks = [dict(q0=q0, nq=nq, tiles=[(k, None) for k in range(66)], yt=dr["YAT"][64 * h:64 * h + 64, q0:q0 + nq]) for q0, nq in qbl]
        if with_ctx:
            blocks.append(dict(q0=T, nq=C, tiles=[(64, None), (65, None)], yt=dr["YAT"][64 * h:64 * h + 64, T:NQ]))
        heads.append(dict(kt=dr["GKT"][h // 4], v=dr["GV"][:, h // 4, :], qt=dr["GQT"][h], dk=64, scale=0.125, nk=SK, blocks=blocks))
    for h in range(8):
        blocks = [dict(q0=q0, nq=nq, tiles=[(k, None) for k in range(66)], yt=dr["YCT"][64 * h:64 * h + 64, q0:q0 + nq]) for q0, nq in qbl]
        if with_ctx:
            blocks.append(dict(q0=T, nq=C, tiles=[(64, None), (65, None)], yt=dr["YCT"][64 * h:64 * h + 64, T:NQ]))
        heads.append(dict(kt=dr["MKT"][h], v=dr["MV"][:, h, :], qt=dr["MQT"][h], dk=96, scale=96 ** -0.5, nk=SK, blocks=blocks))
    phase_attn(K, heads, "at" + sfx, SK)
    heads = []
    var = [0, 1, 1, 2]
    for h in range(8):
        blocks = []
        for i, (q0, nq) in enumerate(qbl):
            tiles = [(4 * i + m, src["nabias"][var[i], h, m]) for m in range(8)] + [(20, None), (21, None)]
            blocks.append(dict(q0=q0, nq=nq, tiles=tiles, yt=dr["YBT"][64 * h:64 * h + 64, q0:q0 + nq]))
        if with_ctx:
            blocks.append(dict(q0=T, nq=C, tiles=[(20, None), (21, None)], yt=dr["YBT"][64 * h:64 * h + 64, T:NQ]))
        heads.append(dict(kt=dr["NKT"][h], v=dr["NV"][:, h, :], qt=dr["NQT"][h], dk=64, scale=1.0, nk=NAK, blocks=blocks))
    phase_attn(K, heads, "na" + sfx, NAK)
    mblocks = [(dr["hT_na"][:, 256 + 512 * i:256 + 512 * (i + 1)], 512 * i, 512, 0) for i in range(4)]
    if with_ctx:
        mblocks.append((dr["hT_all"][:, S:SK], T, C, 1))

    def x_src(q):
        if q >= T:
            return src["ctx_in"][q - T:q - T + 128, :]
        return src["x_own"](q) if callable(src["x_own"]) else src["x_own"][q:q + 128, :]

    def xs1_at(q):
        return dr["xs1"][q:q + 128, :]

    phase_merge(K, L, "mg" + sfx, mblocks, x_src, xs1_at)
    njobs = [(dr["xs1"][0:T, :], T, dr["h2T"][:, 0:T], 0, 3, 4)]
    if with_ctx:
        njobs.append((dr["xs1"][T:NQ, :], C, dr["h2T"][:, T:NQ], 1, 3, 4))
    phase_norm(K, njobs, "n2" + sfx)
    fblocks = [(256 * i, 256, 0) for i in range(8)]
    if with_ctx:
        fblocks.append((T, C, 1))
    phase_mlp(K, L, "ml" + sfx, fblocks, xs1_at, src["x_dst"], L.get("final_norm_g") if final else None)


def build_fused():
    nc = bass.Bass("TRN2", target_bir_lowering=False)
    K = KB(nc)
    NQ = T + C
    dr = K.dram

    def inp(name, shape, dt=F32):
        dr[name] = nc.dram_tensor(name, shape, dt, kind="ExternalInput").ap()

    def internal(name, shape, dt=BF16):
        dr[name] = nc.dram_tensor(name, shape, dt).ap()

    inp("x_all", [S, D]); inp("x_own", [T, D]); inp("x_na", [NAT, D]); inp("ctx_in", [C, D]); inp("cvec", [2, D])
    Wst = {}
    for n in W_NAMES:
        Wst[n] = nc.dram_tensor(n, [DEPTH] + W_SHAPES[n], F32, kind="ExternalInput").ap()
    fng = nc.dram_tensor("final_norm_g", [D], F32, kind="ExternalInput").ap()
    for n in ("ident_f", "onesbd_f", "ones_f"):
        inp(n, [128, 128])
    for n in ("pt128", "pt96", "pt32", "ident_b"):
        inp(n, [128, 128], BF16)
    inp("cosA_all", [128, S]); inp("sinA_all", [128, S]); inp("cosA_own", [128, T]); inp("sinA_own", [128, T])
    inp("cosM_all", [32, S]); inp("sinM_all", [32, S]); inp("cosM_own", [96, T]); inp("sinM_own", [96, T])
    inp("nabias", [DEPTH, 3, 8, 8, 128, 512])
    inp("sel", [8])
    dr["xout"] = nc.dram_tensor("xout", [T, D], F32, kind="ExternalOutput").ap()
    internal("modv", [2, 6, D], F32)
    internal("hT_all", [D, SK]); internal("hT_na", [D, NAT])
    internal("GKT", [2, 64, SK]); internal("CKVT", [256, SK]); internal("MKT", [8, 96, SK])
    internal("MV", [SK, 8, 65]); internal("GV", [SK, 2, 65])
    internal("NKT", [8, 64, NAK]); internal("NV", [NAK, 8, 65])
    internal("GQT", [8, 64, NQ]); internal("NQT", [8, 64, NQ]); internal("MQT", [8, 96, NQ])
    internal("YAT", [512, NQ]); internal("YBT", [512, NQ]); internal("YCT", [512, NQ])
    internal("xs1", [NQ, D], F32); internal("h2T", [D, NQ])
    NCH = 8
    x1c = [nc.dram_tensor(f"x1c{k}", [256, D], F32) for k in range(NCH)]
    xgc = [nc.dram_tensor(f"xgc{k}", [4 * 256, D], F32) for k in range(NCH)]
    internal("c1loc", [C, D], F32); internal("x_na2", [NAT, D], F32)

    def x1_at(q):
        return x1c[q // 256].ap()[q % 256:q % 256 + 128, :]

    def xg_at(t0):
        r, k, off = t0 // 2048, (t0 % 2048) // 256, t0 % 256
        return xgc[k].ap()[r * 256 + off:r * 256 + off + 128, :]

    K.x1_at, K.xg_at = x1_at, xg_at

    for l in range(DEPTH):
        L = {n: Wst[n][l] for n in W_NAMES}
        final = (l == DEPTH - 1)
        with_ctx = not final
        if final:
            L["final_norm_g"] = fng
        if l == 0:
            src = dict(x_all=dr["x_all"], x_own=dr["x_own"], x_na=dr["x_na"], ctx_in=dr["ctx_in"])
        else:
            src = dict(x_all=(lambda i: xg_at(128 * i)), x_own=x1_at, x_na=dr["x_na2"], ctx_in=dr["c1loc"])
        src["nabias"] = dr["nabias"][l]
        if final:
            src["x_dst"] = lambda q: dr["xout"][q:q + 128, :]
        else:
            src["x_dst"] = lambda q: (x1_at(q) if q < T else dr["c1loc"][q - T:q - T + 128, :])
        emit_layer(K, L, src, with_ctx, final, f"{l}_")
        if not final:
            cc = K.S("cc")
            for k in range(NCH):
                ins = K.POOL.collective_compute("AllGather", mybir.AluOpType.bypass, replica_groups=[[0, 1, 2, 3], [4, 5, 6, 7]],
                                                ins=[x1c[k].ap().opt()], outs=[xgc[k].ap().opt()])
                t_cc = cc.inc(ins)
            K.barrier([t_cc])
            phase_halo(K, f"hl{l}_")
    K.semcounts = {n: s.n for n, s in K.sems.items()}
    return nc, K


def _rope_tables():
    t = np.arange(S, dtype=np.int32)
    row = (t // 64).astype(np.float32)
    col = (t % 64).astype(np.float32)

    def tabs(rot_dim):
        half = rot_dim // 2
        inv = (10000.0 ** (-np.arange(0, half, 2, dtype=np.float32) / np.float32(half))).astype(np.float32)
        ar = (row[:, None] * inv).astype(np.float32)
        ac = (col[:, None] * inv).astype(np.float32)
        cos = np.concatenate([np.cos(ar), np.cos(ar), np.cos(ac), np.cos(ac)], axis=1).T.astype(np.float32)
        sin = np.concatenate([np.sin(ar), np.sin(ar), np.sin(ac), np.sin(ac)], axis=1).T.astype(np.float32)
        return np.ascontiguousarray(cos), np.ascontiguousarray(sin)

    return tabs(64), tabs(32)


def _rot_matrix(n):
    q = n // 4
    P = np.zeros((n, n), np.float32)
    for base in (0, 2 * q):
        for i in range(q):
            P[base + i, base + q + i] = -1.0
            P[base + q + i, base + i] = 1.0
    return P


def _consts():
    c = {}
    c["ident_f"] = np.eye(128, dtype=np.float32)
    c["ones_f"] = np.ones((128, 128), np.float32)
    bd = np.zeros((128, 128), np.float32)
    bd[:64, :64] = 1.0
    bd[64:, 64:] = 1.0
    c["onesbd_f"] = bd
    P64 = _rot_matrix(64)
    P32 = _rot_matrix(32)
    pt128 = np.zeros((128, 128), np.float32)
    pt128[:64, :64] = P64.T
    pt128[64:, 64:] = P64.T
    pt96 = np.zeros((128, 128), np.float32)
    pt96[64:96, 64:96] = P32.T
    pt32 = np.zeros((128, 128), np.float32)
    pt32[:32, :32] = P32.T
    c["ident_b"] = np.eye(128, dtype=np.float32).astype(ml_dtypes.bfloat16)
    c["pt128"] = pt128.astype(ml_dtypes.bfloat16)
    c["pt96"] = pt96.astype(ml_dtypes.bfloat16)
    c["pt32"] = pt32.astype(ml_dtypes.bfloat16)
    return c


def _na_bias_tables(rpb, j):
    out = np.empty((3, 8, 8, 128, 512), np.float32)
    kcol = np.arange(64)[None, :, None, None]
    qcol = np.arange(64)[None, None, None, :]
    m = np.arange(16)[:, None, None, None]
    a = np.arange(8)[None, None, :, None]
    cs = np.clip(qcol - 8, 0, 48)
    colok = (kcol >= cs) & (kcol < cs + 16)
    cidx = np.clip(kcol - qcol + 15, 0, 30)
    for v, i in enumerate((0, 1, 3)):
        r = 32 * j + 8 * i + a
        k = 32 * j + 8 * i - 4 + m
        rs = np.clip(r - 4, 0, 120)
        ok = (k >= 0) & (k < 128) & (k >= rs) & (k < rs + 8) & colok
        ridx = np.clip(k - r + 7, 0, 14)
        ridx_b = np.broadcast_to(ridx, ok.shape)
        cidx_b = np.broadcast_to(cidx, ok.shape)
        vals = rpb[:, ridx_b, cidx_b]
        tab = np.where(ok[None], vals, np.float32(NEG)).astype(np.float32)
        out[v] = tab.reshape(8, 8, 128, 512)
    return out


def _core_inputs(x_full, ctx_full, c, c_ctx, Wd, consts, ropes):
    (cosA, sinA), (cosM, sinM) = ropes
    maps = []
    cosA2 = np.ascontiguousarray(np.concatenate([cosA, cosA], 0))
    sinA2 = np.ascontiguousarray(np.concatenate([sinA, sinA], 0))
    shared = dict(consts)
    for n in W_NAMES:
        shared[n] = np.ascontiguousarray(Wd[n])
    shared["final_norm_g"] = np.ascontiguousarray(Wd["final_norm_g"])
    shared["cosA_all"] = cosA2
    shared["sinA_all"] = sinA2
    shared["cosM_all"] = cosM
    shared["sinM_all"] = sinM
    rpb = np.asarray(Wd["na_rpb"], np.float32)
    nab = [np.stack([_na_bias_tables(rpb[l], j) for l in range(rpb.shape[0])], 0) for j in range(4)]
    for core in range(8):
        b, j = core // 4, core % 4
        t0 = T * j
        d = dict(shared)
        d["x_all"] = np.ascontiguousarray(x_full[b])
        d["x_own"] = np.ascontiguousarray(x_full[b, t0:t0 + T])
        xna = np.zeros((NAT, D), np.float32)
        lo = (32 * j - 4) * 64
        hi = lo + NAT
        slo, shi = max(lo, 0), min(hi, S)
        xna[slo - lo:shi - lo] = x_full[b, slo:shi]
        d["x_na"] = xna
        d["ctx_in"] = np.ascontiguousarray(ctx_full[b])
        d["cvec"] = np.ascontiguousarray(np.stack([c[b], c_ctx]))
        d["cosA_own"] = np.ascontiguousarray(cosA2[:, t0:t0 + T])
        d["sinA_own"] = np.ascontiguousarray(sinA2[:, t0:t0 + T])
        d["cosM_own"] = np.ascontiguousarray(np.concatenate([np.ones((64, T), np.float32), cosM[:, t0:t0 + T]], 0))
        d["sinM_own"] = np.ascontiguousarray(np.concatenate([np.zeros((64, T), np.float32), sinM[:, t0:t0 + T]], 0))
        d["nabias"] = nab[j]
        sel = np.zeros(8, np.float32)
        if j > 0:
            sel[j - 1] = 1.0
        if j < 3:
            sel[4 + j + 1] = 1.0
        d["sel"] = sel
        maps.append(d)
    return maps


_PROG = []


def kernel(**inputs):
    Wd = {k: np.asarray(v, np.float32) for k, v in inputs.items()}
    if not _PROG:
        _PROG.append(build_fused()[0])
    nc = _PROG[0]
    maps = _core_inputs(Wd["x"], Wd["ctx"], Wd["c"], Wd["c_ctx"], Wd, _consts(), _rope_tables())
    res = run_bass_kernel_spmd(nc, maps, core_ids=list(range(8)))
    outs = [r["xout"] for r in res.results]
    x = np.stack([np.concatenate([outs[4 * b + j] for j in range(4)], 0) for b in range(2)], 0)
    return np.ascontiguousarray(x.astype(np.float32))
```

```python
from contextlib import ExitStack
import os
import numpy as np
import ml_dtypes
import concourse.bass as bass
import concourse.mybir as mybir
from concourse.bass_utils import run_bass_kernel_spmd

F32 = mybir.dt.float32
BF16 = mybir.dt.bfloat16
AF = mybir.ActivationFunctionType
ALU = mybir.AluOpType

D = 1024
S = 8192
C = 256
SK = S + C
T = 2048
NAT = 2560
NAK = NAT + C
EPS = 1e-6
NEG = -30000.0
O_GQ, O_GK, O_GV, O_NQ, O_NK, O_NV, O_MQ, O_CKV, O_KR, O_GATE = 0, 512, 640, 768, 1280, 1792, 2304, 3072, 3328, 3360
WIN = 6432


class Sem:
    def __init__(self, nc, name):
        self.h = nc.alloc_semaphore(name)
        self.n = 0

    def inc(self, ins, k=1):
        ins.then_inc(self.h, k)
        self.n += k
        return (self, self.n)


def W(eng, tok):
    if tok is None:
        return
    if isinstance(tok, list):
        for t in tok:
            W(eng, t)
        return
    s, v = tok
    if v > 0:
        eng.wait_ge(s.h, v)


class KB:
    def __init__(self, nc):
        self.nc = nc
        self.PE, self.ACT, self.DVE, self.POOL, self.SP = nc.tensor, nc.scalar, nc.vector, nc.gpsimd, nc.sync
        self.sems = {}
        self.dram = {}

    def S(self, name):
        if name not in self.sems:
            self.sems[name] = Sem(self.nc, name)
        return self.sems[name]

    def dma(self, eng, out, in_, sem, slow=False):
        if slow:
            ins = eng.dma_start(out=out, in_=in_, allow_slow_non_contiguous=True)
        else:
            ins = eng.dma_start(out=out, in_=in_)
        return sem.inc(ins, 16)

    def barrier(self, toks):
        for e in (self.PE, self.ACT, self.DVE, self.POOL, self.SP):
            W(e, toks)


def mk_alloc(nc, es, pfx):
    def sb(name, shape, dt=F32):
        return es.enter_context(nc.sbuf_tensor(pfx + name, shape, dt))

    def ps(name, shape, dt=F32):
        return es.enter_context(nc.psum_tensor(pfx + name, shape, dt))

    return sb, ps


def phase_mod(K, L, pfx):
    nc = K.nc
    PE, ACT, DVE, POOL, SP = K.PE, K.ACT, K.DVE, K.POOL, K.SP
    with ExitStack() as es:
        sb, ps = mk_alloc(nc, es, pfx)
        cT = sb("cT", [128, 8, 2])
        sT = sb("sT", [128, 8, 2])
        NWB = 4
        wm = [sb(f"w{i}", [128, 8, 512]) for i in range(NWB)]
        bm = sb("b", [2, 6144])
        mrow = sb("m", [2, 6144])
        ng = sb("ng", [2, 2, 1024])
        mv = sb("mv", [2, 6, 1024])
        pm = [ps(f"p{i}", [2, 512]) for i in range(2)]
        ld = K.S("ld0")
        wl = [K.S("ld1"), K.S("ld2"), K.S("ld3"), K.S("ld4")]
        s_pe, s_ac, s_dv, st = K.S("pe"), K.S("ac"), K.S("dv"), K.S("st0")
        for m in range(2):
            K.dma(SP, cT[:, :, m], K.dram["cvec"][m].rearrange("(c p) -> p c", p=128), ld, slow=True)
        K.dma(SP, bm[:], L["b_mod"].partition_broadcast(2), ld)
        K.dma(SP, ng[:, 0, :], L["norm1_g"].partition_broadcast(2), ld)
        t_ld = K.dma(SP, ng[:, 1, :], L["norm2_g"].partition_broadcast(2), ld)
        W(ACT, t_ld)
        t_s = s_ac.inc(ACT.activation(out=sT[:].rearrange("p c m -> p (c m)"), in_=cT[:].rearrange("p c m -> p (c m)"), func=AF.Silu))
        W(PE, t_s)
        pe_t = [None] * 12
        dv_t = [None] * 12
        wsrc = L["w_mod"]
        w_t = [None] * 12

        def load_w(g):
            if g >= NWB:
                W(SP, pe_t[g - NWB])
            w_t[g] = K.dma(SP, wm[g % NWB][:], wsrc[:, g * 512:(g + 1) * 512].rearrange("(c p) n -> p c n", p=128), wl[g % NWB])

        for g in range(min(NWB - 1, 12)):
            load_w(g)
        for g in range(12):
            if g + NWB - 1 < 12:
                load_w(g + NWB - 1)
            W(PE, w_t[g])
            if g >= 2:
                W(PE, dv_t[g - 2])
            for c in range(8):
                ins = PE.matmul(pm[g % 2][:], lhsT=sT[:, c, :], rhs=wm[g % NWB][:, c, :], start=(c == 0), stop=(c == 7))
            pe_t[g] = s_pe.inc(ins)
            W(DVE, pe_t[g])
            if g == 0:
                W(DVE, t_ld)
            dv_t[g] = s_dv.inc(DVE.tensor_tensor(out=mrow[:, g * 512:(g + 1) * 512], in0=pm[g % 2][:], in1=bm[:, g * 512:(g + 1) * 512], op=ALU.add))
        W(DVE, dv_t[11])
        sl = lambda i: mrow[:, i * 1024:(i + 1) * 1024]
        DVE.scalar_tensor_tensor(out=mv[:, 0, :], in0=sl(1), scalar=1.0, in1=ng[:, 0, :], op0=ALU.add, op1=ALU.mult)
        DVE.tensor_copy(out=mv[:, 1, :], in_=sl(0))
        DVE.tensor_copy(out=mv[:, 2, :], in_=sl(2))
        DVE.scalar_tensor_tensor(out=mv[:, 3, :], in0=sl(4), scalar=1.0, in1=ng[:, 1, :], op0=ALU.add, op1=ALU.mult)
        DVE.tensor_copy(out=mv[:, 4, :], in_=sl(3))
        t_f = s_dv.inc(DVE.tensor_copy(out=mv[:, 5, :], in_=sl(5)))
        W(SP, t_f)
        t_st = K.dma(SP, K.dram["modv"], mv[:], st)
        K.barrier([t_st])


def phase_norm(K, jobs, pfx):
    nc = K.nc
    PE, ACT, DVE, POOL, SP = K.PE, K.ACT, K.DVE, K.POOL, K.SP
    tiles = []
    for ji, (src, ntok, dst, m, ia, ish) in enumerate(jobs):
        for i in range(ntok // 128):
            tiles.append((ji, i))
    NTI = len(tiles)
    with ExitStack() as es:
        sb, ps = mk_alloc(nc, es, pfx)
        NX = 6
        xt = [sb(f"xt{i}", [128, 1024]) for i in range(NX)]
        junk = sb("junk", [128, 1024])
        ss = sb("ss", [128, NTI])
        r1 = sb("r1", [128, NTI])
        r2 = sb("r2", [128, NTI])
        rstd = sb("rstd", [128, NTI])
        NXN = 3
        xn = [sb(f"xn{i}", [128, 1024]) for i in range(NXN)]
        hb = [sb(f"hb{i}", [128, 8, 512], BF16) for i in range(2)]
        ident = sb("ident", [128, 128])
        acol = sb("acol", [128, 2, 2, 8])
        pT = [ps(f"pT{i}", [128, 8, 128]) for i in range(2)]
        lds = [K.S("ld0"), K.S("ld1"), K.S("ld3"), K.S("ld4"), K.S("ld5"), K.S("ld6")]
        ldc = K.S("ld2")
        s_pe, s_ac, s_dv = K.S("pe"), K.S("ac"), K.S("dv")
        sts = [K.S("gs0"), K.S("gs1")]
        t_c = K.dma(SP, ident[:], K.dram["ident_f"], ldc)
        mods = sorted(set((j[3], j[4], j[5]) for j in jobs))
        assert len(set(m for m, _, _ in mods)) == len(mods)
        for (m, ia, ish) in mods:
            K.dma(SP, acol[:, m, 0, :], K.dram["modv"][m, ia].rearrange("(c p) -> p c", p=128), ldc, slow=True)
            t_c = K.dma(SP, acol[:, m, 1, :], K.dram["modv"][m, ish].rearrange("(c p) -> p c", p=128), ldc, slow=True)
        act_t = [None] * NTI
        for n, (ji, i) in enumerate(tiles):
            src = jobs[ji][0]
            if n >= NX:
                W(SP, act_t[n - NX])
            t_l = K.dma(SP, xt[n % NX][:], src(i) if callable(src) else src[i * 128:(i + 1) * 128, :], lds[n % NX])
            W(ACT, t_l)
            act_t[n] = s_ac.inc(ACT.activation(out=junk[:], in_=xt[n % NX][:], func=AF.Square, accum_out=ss[:, n:n + 1]))
        W(DVE, act_t[NTI - 1])
        t1 = s_dv.inc(DVE.tensor_scalar(out=r1[:], in0=ss[:], scalar1=1.0 / D, scalar2=EPS, op0=ALU.mult, op1=ALU.add))
        W(ACT, t1)
        t2 = s_ac.inc(ACT.activation(out=r2[:], in_=r1[:], func=AF.Sqrt))
        W(DVE, t2)
        t3 = s_dv.inc(DVE.reciprocal(out=rstd[:], in_=r2[:]))
        W(ACT, t3)
        W(SP, t2)
        W(PE, t_c)
        W(DVE, t_c)
        a_t = [None] * NTI
        p_t = [None] * NTI
        v_t = [None] * NTI
        st_t = {}
        blk = -1
        blk_of = []
        prev_key = None
        for n, (ji, i) in enumerate(tiles):
            key = (ji, i // 4)
            if key != prev_key:
                blk += 1
                prev_key = key
            blk_of.append(blk)
        for n, (ji, i) in enumerate(tiles):
            src, ntok, dst, m, ia, ish = jobs[ji]
            b = blk_of[n]
            if n >= NX:
                W(SP, a_t[n - NX])
            t_l = K.dma(SP, xt[n % NX][:], src(i) if callable(src) else src[i * 128:(i + 1) * 128, :], lds[n % NX])
            W(ACT, t_l)
            if n >= NXN:
                W(ACT, p_t[n - NXN])
            a_t[n] = s_ac.inc(ACT.activation(out=xn[n % NXN][:], in_=xt[n % NXN if False else n % NX][:], func=AF.Copy, scale=rstd[:, n:n + 1]))
            W(PE, a_t[n])
            if n >= 2:
                W(PE, v_t[n - 2])
            for c in range(8):
                ins = PE.transpose(out=pT[n % 2][:, c, :], in_=xn[n % NXN][:, c * 128:(c + 1) * 128], identity=ident[:])
            p_t[n] = s_pe.inc(ins)
            W(DVE, p_t[n])
            if (i % 4 == 0) and (b - 2) in st_t:
                W(DVE, st_t[b - 2])
            for c in range(8):
                ins = DVE.tensor_scalar(out=hb[b % 2][:, c, (i % 4) * 128:(i % 4 + 1) * 128], in0=pT[n % 2][:, c, :],
                                        scalar1=acol[:, m, 0, c:c + 1], scalar2=acol[:, m, 1, c:c + 1], op0=ALU.mult, op1=ALU.add)
            v_t[n] = s_dv.inc(ins)
            last_in_blk = (n + 1 == NTI) or (blk_of[n + 1] != b)
            if last_in_blk:
                nt = (i % 4 + 1) * 128
                t0 = (i // 4) * 512
                W(POOL, v_t[n])
                st_t[b] = K.dma(POOL, dst[:, t0:t0 + nt].rearrange("(c p) t -> p c t", p=128), hb[b % 2][:, :, 0:nt], sts[b % 2])
        K.barrier([st_t[blk], st_t.get(blk - 1)])


def phase_proj(K, L, pfx, with_ctx):
    nc = K.nc
    PE, ACT, DVE, POOL, SP = K.PE, K.ACT, K.DVE, K.POOL, K.SP
    dr = K.dram
    with ExitStack() as es:
        sb, ps = mk_alloc(nc, es, pfx)
        win = sb("win", [128, 8, O_GATE], BF16)
        wuk = sb("wuk", [128, 2, 512], BF16)
        wuv = sb("wuv", [128, 2, 512], BF16)
        onesbd = sb("onesbd", [128, 128])
        ones = sb("ones", [128, 128])
        pt128 = sb("pt128", [128, 128], BF16)
        pt96 = sb("pt96", [128, 128], BF16)
        pt32 = sb("pt32", [128, 128], BF16)
        gq = sb("gq", [128, 1])
        gk = sb("gk", [128, 1])
        kvg = sb("kvg", [128, 2])
        hblk = [sb(f"h{i}", [128, 8, 512], BF16) for i in range(2)]
        ckvn = sb("ckvn", [128, 2, 512], BF16)
        sqf = sb("sqf", [128, 512]); sqf2 = sb("sqf2", [128, 512])
        qf = sb("qf", [128, 512]); qf2 = sb("qf2", [128, 512])
        sd = sb("sd", [128, 512]); rs = sb("rs", [128, 512]); qn = sb("qn", [128, 512])
        t1b = sb("t1", [128, 512]); t2b = sb("t2", [128, 512])
        cosb = [sb(f"cos{i}", [128, 512]) for i in range(2)]
        sinb = [sb(f"sin{i}", [128, 512]) for i in range(2)]
        qb = sb("qb", [128, 512], BF16)
        outb = [sb(f"ob{i}", [128, 512], BF16) for i in range(2)]
        vout = [sb(f"vo{i}", [128, 8, 65], BF16) for i in range(2)]
        acc = [ps(f"acc{i}", [128, 512]) for i in range(2)]
        acc2 = ps("acc2", [128, 512])
        pss = ps("pss", [128, 512])
        prot = ps("prot", [128, 512])
        ptm = [ps(f"ptm{i}", [128, 512]) for i in range(2)]
        wl = K.S("ld2")
        gw = K.S("gw")
        hl = [K.S("ld0"), K.S("ld1")]
        tl = [K.S("ld3"), K.S("ld4")]
        s_pe, s_ac, s_dv, s_pl = K.S("pe"), K.S("ac"), K.S("dv"), K.S("pl")
        sto = [K.S("st0"), K.S("st1")]
        stv = [K.S("st2"), K.S("st3")]
        stc = K.S("st4")
        for c in range(8):
            K.dma(POOL, win[:, c, :], L["w_in"][c * 128:(c + 1) * 128, 0:O_GATE], gw)
        K.dma(POOL, wuk[:], L["mla_w_uk"].rearrange("(r p) n -> p r n", p=128), gw)
        t_gw = K.dma(POOL, wuv[:], L["mla_w_uv"].rearrange("(r p) n -> p r n", p=128), gw)
        K.dma(SP, onesbd[:], dr["onesbd_f"], wl)
        K.dma(SP, ones[:], dr["ones_f"], wl)
        K.dma(SP, pt128[:], dr["pt128"], wl)
        K.dma(SP, pt96[:], dr["pt96"], wl)
        K.dma(SP, pt32[:], dr["pt32"], wl)
        for hh in range(2):
            K.dma(SP, gq[hh * 64:(hh + 1) * 64, :], L["gqa_q_norm"].rearrange("(p o) -> p o", o=1), wl)
            K.dma(SP, gk[hh * 64:(hh + 1) * 64, :], L["gqa_k_norm"].rearrange("(p o) -> p o", o=1), wl)
        t_w = K.dma(SP, kvg[:], L["mla_kv_norm"].rearrange("(r p) -> p r", p=128), wl, slow=True)
        for i in range(2):
            DVE.memset(vout[i][:], 1.0)
        t_ms = s_dv.inc(DVE.memset(qn[:], 0.0))
        for e in (PE, ACT, DVE, POOL):
            W(e, t_w)
            W(e, t_gw)
        W(ACT, t_ms)

        st = {"k": 0, "rk": 0, "vk": 0, "acc_free": [None, None], "ob_free": [None, None], "tab_free": [None, None],
              "vo_free": [None, None], "ptm_free": [None, None], "hb_tok": None}

        def store(eng, dst, src, sem):
            return K.dma(eng, dst, src, sem)

        import os
        LIMIT = int(os.environ.get("PROJ_LIMIT", "1000000"))
        units = [0]

        def over():
            units[0] += 1
            return units[0] > LIMIT

        def fm_job(chunks, M, nt, norm_g, rope, dsts, oscale=1.0):
            if over():
                return
            k = st["k"]; st["k"] += 1
            a = acc[k % 2]
            W(PE, st["acc_free"][k % 2])
            W(PE, st["hb_tok"])
            for ci, (lt, rh) in enumerate(chunks):
                ins = PE.matmul(a[:M, :nt], lhsT=lt, rhs=rh, start=(ci == 0), stop=(ci == len(chunks) - 1))
            t_main = s_pe.inc(ins)
            ob = outb[k % 2]
            if norm_g is None and rope is None:
                W(ACT, t_main)
                W(ACT, st["ob_free"][k % 2])
                t_out = s_ac.inc(ACT.activation(out=ob[:M, :nt], in_=a[:M, :nt], func=AF.Copy, scale=float(oscale)))
                st["acc_free"][k % 2] = t_out
            else:
                if rope is not None:
                    r = st["rk"]; st["rk"] += 1
                    PT, cos_ap, sin_ap = rope
                    W(SP, st["tab_free"][r % 2])
                    K.dma(SP, cosb[r % 2][:M, :nt], cos_ap, tl[r % 2])
                    t_tab = K.dma(SP, sinb[r % 2][:M, :nt], sin_ap, tl[r % 2])
                W(DVE, t_main)
                t_qf = s_dv.inc(DVE.tensor_copy(out=qf[:M, :nt], in_=a[:M, :nt]))
                t_cur = t_qf
                cur = qf
                free_toks = [t_qf]
                if norm_g is not None:
                    W(ACT, t_qf)
                    t_sq = s_ac.inc(ACT.activation(out=sqf[:M, :nt], in_=qf[:M, :nt], func=AF.Square))
                    W(PE, t_sq)
                    t_ss = s_pe.inc(PE.matmul(pss[:M, :nt], lhsT=onesbd[:M, :M], rhs=sqf[:M, :nt], start=True, stop=True))
                    W(ACT, t_ss)
                    W(ACT, t_qf)
                    t_sd = s_ac.inc(ACT.activation(out=sd[:M, :nt], in_=pss[:M, :nt], func=AF.Sqrt, bias=EPS, scale=1.0 / 64))
                    W(DVE, t_sd)
                    t_rs = s_dv.inc(DVE.reciprocal(out=rs[:M, :nt], in_=sd[:M, :nt]))
                    W(DVE, t_rs)
                    if rope is None:
                        W(DVE, st["ob_free"][k % 2])
                        t_out = s_dv.inc(DVE.scalar_tensor_tensor(out=ob[:M, :nt], in0=qf[:M, :nt], scalar=norm_g, in1=rs[:M, :nt], op0=ALU.mult, op1=ALU.mult))
                    else:
                        t_cur = s_dv.inc(DVE.scalar_tensor_tensor(out=qn[:M, :nt], in0=qf[:M, :nt], scalar=norm_g, in1=rs[:M, :nt], op0=ALU.mult, op1=ALU.mult))
                        cur = qn
                st["acc_free"][k % 2] = free_toks
                if rope is not None:
                    W(ACT, t_cur)
                    t_qb = s_ac.inc(ACT.activation(out=qb[:M, :nt], in_=cur[:M, :nt], func=AF.Copy))
                    W(PE, t_qb)
                    t_rot = s_pe.inc(PE.matmul(prot[:M, :nt], lhsT=PT[:M, :M], rhs=qb[:M, :nt], start=True, stop=True))
                    W(POOL, t_cur)
                    W(POOL, t_tab)
                    t_t1 = s_pl.inc(POOL.tensor_tensor(out=t1b[:M, :nt], in0=cur[:M, :nt], in1=cosb[r % 2][:M, :nt], op=ALU.mult))
                    W(DVE, t_rot)
                    W(DVE, t_tab)
                    t_t2 = s_dv.inc(DVE.tensor_tensor(out=t2b[:M, :nt], in0=prot[:M, :nt], in1=sinb[r % 2][:M, :nt], op=ALU.mult))
                    W(DVE, t_t1)
                    W(DVE, t_t2)
                    W(DVE, st["ob_free"][k % 2])
                    t_out = s_dv.inc(DVE.tensor_tensor(out=ob[:M, :nt], in0=t1b[:M, :nt], in1=t2b[:M, :nt], op=ALU.add))
                    st["tab_free"][r % 2] = t_out
            W(SP, t_out)
            for (dst, r0, r1) in dsts:
                t_st = store(SP, dst, ob[r0:r1, :nt], sto[k % 2])
            st["ob_free"][k % 2] = t_st

        def ckv_job(hs, nt, dst_ckvt):
            if over():
                st["ckvn_tok"] = None
                return
            k = st["k"]; st["k"] += 1
            a = acc[k % 2]
            W(PE, st["acc_free"][k % 2])
            W(PE, st["hb_tok"])
            W(PE, st.get("acc2_free"))
            for g, aa in enumerate((a, acc2)):
                for c in range(8):
                    ins = PE.matmul(aa[:, :nt], lhsT=win[:, c, O_CKV + g * 128:O_CKV + (g + 1) * 128], rhs=hblk[hs][:, c, :nt], start=(c == 0), stop=(c == 7))
            t_main = s_pe.inc(ins)
            CUT = int(os.environ.get("CKV_CUT", "99"))
            st["ckvn_tok"] = None
            if CUT <= 1:
                return
            W(DVE, t_main)
            DVE.tensor_copy(out=qf[:, :nt], in_=a[:, :nt])
            t_qf = s_dv.inc(DVE.tensor_copy(out=qf2[:, :nt], in_=acc2[:, :nt]))
            W(ACT, t_qf)
            ACT.activation(out=sqf[:, :nt], in_=qf[:, :nt], func=AF.Square)
            t_sq = s_ac.inc(ACT.activation(out=sqf2[:, :nt], in_=qf2[:, :nt], func=AF.Square))
            st["acc_free"][k % 2] = [t_qf]
            st["acc2_free"] = [t_qf]
            if CUT <= 2:
                return
            W(PE, t_sq)
            PE.matmul(pss[:, :nt], lhsT=ones[:], rhs=sqf[:, :nt], start=True, stop=False)
            t_ss = s_pe.inc(PE.matmul(pss[:, :nt], lhsT=ones[:], rhs=sqf2[:, :nt], start=False, stop=True))
            if CUT <= 3:
                return
            W(ACT, t_ss)
            W(ACT, t_qf)
            t_sd = s_ac.inc(ACT.activation(out=sd[:, :nt], in_=pss[:, :nt], func=AF.Sqrt, bias=EPS, scale=1.0 / 256))
            W(DVE, t_sd)
            t_rs = s_dv.inc(DVE.reciprocal(out=rs[:, :nt], in_=sd[:, :nt]))
            if CUT <= 4:
                return
            W(DVE, t_rs)
            W(DVE, st.get("ckvn_free"))
            DVE.scalar_tensor_tensor(out=ckvn[:, 0, :nt], in0=qf[:, :nt], scalar=kvg[:, 0:1], in1=rs[:, :nt], op0=ALU.mult, op1=ALU.mult)
            t_out = s_dv.inc(DVE.scalar_tensor_tensor(out=ckvn[:, 1, :nt], in0=qf2[:, :nt], scalar=kvg[:, 1:2], in1=rs[:, :nt], op0=ALU.mult, op1=ALU.mult))
            if CUT <= 5:
                return
            W(SP, t_out)
            t_st = store(SP, dst_ckvt.rearrange("(r p) t -> p r t", p=128), ckvn[:, :, :nt], stc)
            st["ckvn_tok"] = t_out
            st["ckvn_st"] = t_st

        def tm_job(chunks, N, nh, dst):
            if over():
                return
            j = st["vk"]; st["vk"] += 1
            p = ptm[j % 2]
            W(PE, st["ptm_free"][j % 2])
            W(PE, st["hb_tok"])
            for ci, (lt, rh) in enumerate(chunks):
                ins = PE.matmul(p[:, :N], lhsT=lt, rhs=rh, start=(ci == 0), stop=(ci == len(chunks) - 1))
            t_main = s_pe.inc(ins)
            W(ACT, t_main)
            W(ACT, st["vo_free"][j % 2])
            t_o = s_ac.inc(ACT.activation(out=vout[j % 2][:, 0:nh, 0:64], in_=p[:, :N].rearrange("p (h d) -> p h d", d=64), func=AF.Copy))
            st["ptm_free"][j % 2] = t_o
            W(SP, t_o)
            st["vo_free"][j % 2] = store(SP, dst, vout[j % 2][:, 0:nh, :], stv[j % 2])

        nblk = [0]
        last_users = [None, None]

        def load_block(src_ap, nt):
            b = nblk[0]; nblk[0] += 1
            W(SP, last_users[b % 2])
            st["hb_tok"] = K.dma(SP, hblk[b % 2][:, :, :nt], src_ap.rearrange("(c p) t -> p c t", p=128), hl[b % 2])
            return b % 2

        def done_block(hs):
            last_users[hs] = (s_pe, s_pe.n)

        def hch(hs, c0, M, nt):
            return [(win[:, c, c0:c0 + M], hblk[hs][:, c, :nt]) for c in range(8)]

        hT_all, hT_na = dr["hT_all"], dr["hT_na"]
        for tb in range(17):
            ctxb = (tb == 16)
            t0 = tb * 512
            nt = 256 if ctxb else 512
            hs = load_block(hT_all[:, t0:t0 + nt], nt)
            rope = None if ctxb else (pt128, dr["cosA_all"][:, t0:t0 + nt], dr["sinA_all"][:, t0:t0 + nt])
            fm_job(hch(hs, O_GK, 128, nt), 128, nt, gk[:, 0:1], rope,
                   [(dr["GKT"][0, :, t0:t0 + nt], 0, 64), (dr["GKT"][1, :, t0:t0 + nt], 64, 128)])
            ckv_job(hs, nt, dr["CKVT"][:, t0:t0 + nt])
            rope = None if ctxb else (pt32, dr["cosM_all"][:, t0:t0 + nt], dr["sinM_all"][:, t0:t0 + nt])
            fm_job(hch(hs, O_KR, 32, nt), 32, nt, None, rope, [(dr["MKT"][h, 64:96, t0:t0 + nt], 0, 32) for h in range(8)])
            W(PE, st["ckvn_tok"])
            for g in range(4):
                fm_job([(wuk[:, r, g * 128:(g + 1) * 128], ckvn[:, r, :nt]) for r in range(2)], 128, nt, None, None,
                       [(dr["MKT"][2 * g, 0:64, t0:t0 + nt], 0, 64), (dr["MKT"][2 * g + 1, 0:64, t0:t0 + nt], 64, 128)])
            for ti in range(nt // 128):
                tsl = slice(ti * 128, (ti + 1) * 128)
                r0 = t0 + ti * 128
                tm_job([(ckvn[:, r, tsl], wuv[:, r, :]) for r in range(2)], 512, 8, dr["MV"][r0:r0 + 128, :, :])
                tm_job([(hblk[hs][:, c, tsl], win[:, c, O_GV:O_GV + 128]) for c in range(8)], 128, 2, dr["GV"][r0:r0 + 128, :, :])
            st["ckvn_free"] = (s_pe, s_pe.n)
            if ctxb:
                for g in range(4):
                    fm_job(hch(hs, O_NK + g * 128, 128, nt), 128, nt, None, None,
                           [(dr["NKT"][2 * g, :, NAT:NAT + nt], 0, 64), (dr["NKT"][2 * g + 1, :, NAT:NAT + nt], 64, 128)])
                for ti in range(nt // 128):
                    tsl = slice(ti * 128, (ti + 1) * 128)
                    tm_job([(hblk[hs][:, c, tsl], win[:, c, O_NV:O_NV + 512]) for c in range(8)], 512, 8, dr["NV"][NAT + ti * 128:NAT + (ti + 1) * 128, :, :])
                if with_ctx:
                    q0 = T
                    for g in range(4):
                        fm_job(hch(hs, O_GQ + g * 128, 128, nt), 128, nt, gq[:, 0:1], None,
                               [(dr["GQT"][2 * g, :, q0:q0 + nt], 0, 64), (dr["GQT"][2 * g + 1, :, q0:q0 + nt], 64, 128)])
                        fm_job(hch(hs, O_NQ + g * 128, 128, nt), 128, nt, None, None,
                               [(dr["NQT"][2 * g, :, q0:q0 + nt], 0, 64), (dr["NQT"][2 * g + 1, :, q0:q0 + nt], 64, 128)], oscale=0.125)
                    for h in range(8):
                        fm_job(hch(hs, O_MQ + h * 96, 96, nt), 96, nt, None, None, [(dr["MQT"][h, :, q0:q0 + nt], 0, 96)])
            done_block(hs)
        for tb in range(5):
            t0 = tb * 512
            nt = 512
            hs = load_block(hT_na[:, t0:t0 + nt], nt)
            for g in range(4):
                fm_job(hch(hs, O_NK + g * 128, 128, nt), 128, nt, None, None,
                       [(dr["NKT"][2 * g, :, t0:t0 + nt], 0, 64), (dr["NKT"][2 * g + 1, :, t0:t0 + nt], 64, 128)])
            for ti in range(4):
                tsl = slice(ti * 128, (ti + 1) * 128)
                tm_job([(hblk[hs][:, c, tsl], win[:, c, O_NV:O_NV + 512]) for c in range(8)], 512, 8, dr["NV"][t0 + ti * 128:t0 + (ti + 1) * 128, :, :])
            done_block(hs)
        for tb in range(4):
            q0 = tb * 512
            nt = 512
            hs = load_block(hT_na[:, 256 + q0:256 + q0 + nt], nt)
            for g in range(4):
                fm_job(hch(hs, O_GQ + g * 128, 128, nt), 128, nt, gq[:, 0:1], (pt128, dr["cosA_own"][:, q0:q0 + nt], dr["sinA_own"][:, q0:q0 + nt]),
                       [(dr["GQT"][2 * g, :, q0:q0 + nt], 0, 64), (dr["GQT"][2 * g + 1, :, q0:q0 + nt], 64, 128)])
                fm_job(hch(hs, O_NQ + g * 128, 128, nt), 128, nt, None, None,
                       [(dr["NQT"][2 * g, :, q0:q0 + nt], 0, 64), (dr["NQT"][2 * g + 1, :, q0:q0 + nt], 64, 128)], oscale=0.125)
            for h in range(8):
                fm_job(hch(hs, O_MQ + h * 96, 96, nt), 96, nt, None, (pt96, dr["cosM_own"][:, q0:q0 + nt], dr["sinM_own"][:, q0:q0 + nt]),
                       [(dr["MQT"][h, :, q0:q0 + nt], 0, 96)])
            done_block(hs)
        K.barrier([(s, s.n) for s in sto + stv + [stc]])


def phase_attn(K, heads, pfx, nkmax):
    nc = K.nc
    PE, ACT, DVE, POOL, SP = K.PE, K.ACT, K.DVE, K.POOL, K.SP
    NQ = T + C
    rls = K.dram["rls"]
    with ExitStack() as es:
        sb, ps = mk_alloc(nc, es, pfx)
        ktb = [sb(f"kt{i}", [128, nkmax], BF16) for i in range(2)]
        vb = [sb(f"v{i}", [128, nkmax // 128, 65], BF16) for i in range(2)]
        qb = [sb(f"q{i}", [128, NQ], BF16) for i in range(2)]
        pbuf = [sb(f"p{i}", [128, 1024], BF16) for i in range(3)]
        bb = [sb(f"bias{i}", [128, 512], BF16) for i in range(3)]
        identb = sb("identb", [128, 128], BF16)
        osb = [sb(f"osb{i}", [128, 512]) for i in range(2)]
        rl = [sb(f"rl{i}", [128, 512]) for i in range(2)]
        rbc = [sb(f"rbc{i}", [64, 512]) for i in range(2)]
        ysb = [[sb(f"y{i}_{w}", [64, 512], BF16) for w in range(2)] for i in range(2)]
        psb = [ps(f"s{i}", [128, 1024]) for i in range(3)]
        po = [ps(f"o{i}", [128, 512]) for i in range(2)]
        hl = [K.S("ld0"), K.S("ld1")]
        bl = [K.S("gb0"), K.S("gb1"), K.S("gb2")]
        cl = K.S("ld5")
        rld = [K.S("ld3"), K.S("ld4")]
        rst = [K.S("st2"), K.S("st3")]
        s_pe, s_ac, s_dv = K.S("pe"), K.S("ac"), K.S("dv")
        sty = [K.S("st0"), K.S("st1")]
        t_c = K.dma(SP, identb[:], K.dram["ident_b"], cl)
        for i in range(2):
            DVE.memset(ktb[i][:], 0.0)
            DVE.memset(qb[i][:], 0.0)
            DVE.memset(rl[i][:], 1.0)
        t_m = s_dv.inc(DVE.memset(osb[0][:], 0.0))
        W(PE, t_c)
        W(PE, t_m)
        W(SP, t_m)
        steps = []
        for hi, h in enumerate(heads):
            for bi, b in enumerate(h["blocks"]):
                nt_ = len(b["tiles"])
                b["chunks"] = [(c0, min(512, b["nq"] - c0)) for c0 in range(0, b["nq"], 512)]
                for si, (kti, bias) in enumerate(b["tiles"]):
                    assert bias is None or len(b["chunks"]) == 1
                    steps.append(dict(hi=hi, b=b, kti=kti, bias=bias, first=(si == 0), last=(si == nt_ - 1),
                                      hfirst=(bi == 0 and si == 0), hlast=(bi == len(h["blocks"]) - 1 and si == nt_ - 1)))
        NS = len(steps)
        head_tok = [None] * len(heads)
        head_done = [None] * len(heads)

        def load_head(hi):
            h = heads[hi]
            s = hi % 2
            if hi >= 2:
                W(SP, head_done[hi - 2])
            dk, nk = h["dk"], h["nk"]
            K.dma(SP, ktb[s][:dk, :nk], h["kt"], hl[s])
            K.dma(SP, vb[s][:, :nk // 128, :], h["v"].rearrange("(t p) e -> p t e", p=128), hl[s])
            head_tok[hi] = K.dma(SP, qb[s][:dk, :], h["qt"], hl[s])

        tq = [None] * NS
        te = [None] * NS
        tv = [None] * NS
        bias_ld = [None] * NS
        nbias = [0]
        bidx = [None] * NS
        po_free = [None, None]
        y_free = [[None, None], [None, None]]
        pend_dv = {}
        state = {}

        def emit_qk(t):
            s = steps[t]
            h = heads[s["hi"]]
            hs = s["hi"] % 2
            if s["hfirst"]:
                W(PE, head_tok[s["hi"]])
            if t >= 3:
                W(PE, te[t - 3])
            b = s["b"]
            hasb = s["bias"] is not None
            for (c0, cn) in b["chunks"]:
                ins = PE.matmul(psb[t % 3][:, c0:c0 + cn], lhsT=ktb[hs][:, s["kti"] * 128:(s["kti"] + 1) * 128],
                                rhs=qb[hs][:, b["q0"] + c0:b["q0"] + c0 + cn], start=True, stop=not hasb)
            if hasb:
                W(PE, bias_ld[t])
                ins = PE.matmul(psb[t % 3][:, :b["nq"]], lhsT=identb[:], rhs=bb[bidx[t] % 3][:, :b["nq"]], start=False, stop=True)
            tq[t] = s_pe.inc(ins)
            if hasb:
                state[("bfree", bidx[t] % 3)] = tq[t]

        def emit_bias_load(t):
            s = steps[t]
            if s["bias"] is None:
                return
            n = nbias[0]; nbias[0] += 1
            bidx[t] = n
            W(POOL, state.get(("bfree", n % 3)))
            bias_ld[t] = K.dma(POOL, bb[n % 3][:, :s["b"]["nq"]], s["bias"], bl[n % 3])

        if NS > 0:
            load_head(0)
        LA = 2
        for t in range(min(LA, NS)):
            emit_bias_load(t)
            emit_qk(t)
        cur_blk = -1
        for t in range(NS):
            s = steps[t]
            h = heads[s["hi"]]
            b = s["b"]
            nq = b["nq"]
            hs = s["hi"] % 2
            if s["hfirst"] and s["hi"] + 1 < len(heads):
                load_head(s["hi"] + 1)
            if s["first"]:
                cur_blk += 1
            if t + LA < NS:
                emit_bias_load(t + LA)
                emit_qk(t + LA)
            W(ACT, tq[t])
            if t >= 3:
                W(ACT, tv[t - 3])
            te[t] = s_ac.inc(ACT.activation(out=pbuf[t % 3][:, :nq], in_=psb[t % 3][:, :nq], func=AF.Exp, scale=float(h["scale"])))
            W(PE, te[t])
            for w, (c0, cn) in enumerate(b["chunks"]):
                if s["first"]:
                    W(PE, po_free[w])
                ins = PE.matmul(po[w][:65, :cn], lhsT=vb[hs][:, s["kti"], :], rhs=pbuf[t % 3][:, c0:c0 + cn], start=s["first"], stop=s["last"])
            tv[t] = s_pe.inc(ins)
            if s["hlast"]:
                head_done[s["hi"]] = tv[t]
            for f in pend_dv.pop(t, []):
                f()
            if s["last"]:
                cb = cur_blk
                parts = []
                for w, (c0, cn) in enumerate(b["chunks"]):
                    W(DVE, tv[t])
                    t_o = s_dv.inc(DVE.tensor_copy(out=osb[w][:65, :cn], in_=po[w][:65, :cn]))
                    po_free[w] = t_o
                    W(DVE, t_o)
                    t_rl = s_dv.inc(DVE.reciprocal(out=rl[w][64:65, :cn], in_=osb[w][64:65, :cn]))
                    slot = (cb % 2) * 2 + w
                    W(SP, t_rl)
                    t_s = K.dma(SP, rls[slot:slot + 1, :cn], rl[w][64:65, :cn], rst[w])
                    W(SP, t_s)
                    t_b = K.dma(SP, rbc[w][:, :cn], rls[slot, :cn].partition_broadcast(64), rld[w])
                    parts.append((w, cn, t_b))

                def dv_part(parts=parts, cb=cb, yts=b["yts"]):
                    for (w, cn, t_b) in parts:
                        W(DVE, t_b)
                        W(DVE, y_free[cb % 2][w])
                        t_y = s_dv.inc(DVE.tensor_tensor(out=ysb[cb % 2][w][:, :cn], in0=osb[w][:64, :cn], in1=rbc[w][:, :cn], op=ALU.mult))
                        W(SP, t_y)
                        y_free[cb % 2][w] = K.dma(SP, yts[w], ysb[cb % 2][w][:, :cn], sty[w])

                if t + 1 < NS:
                    nxt_len = len(steps[t + 1]["b"]["tiles"])
                    d = max(1, min(2, nxt_len - 1))
                    pend_dv.setdefault(t + d, []).append(dv_part)
                else:
                    dv_part()
        assert not pend_dv
        K.barrier([(s_, s_.n) for s_ in sty])


def phase_merge(K, L, pfx, qblocks, x_src, x_dst):
    nc = K.nc
    PE, ACT, DVE, POOL, SP = K.PE, K.ACT, K.DVE, K.POOL, K.SP
    dr = K.dram
    with ExitStack() as es:
        sb, ps = mk_alloc(nc, es, pfx)
        wg = sb("wg", [128, 8, 3072], BF16)
        wo = [sb(f"wo{i}", [128, 4, 1024], BF16) for i in range(3)]
        wout = sb("wout", [128, 8, 1024], BF16)
        g1 = sb("g1", [128, 2, 1024])
        hblk = [sb(f"h{i}", [128, 8, 512], BF16) for i in range(2)]
        yb = [[sb(f"y{r}_{i}", [128, 4, 512], BF16) for r in range(3)] for i in range(2)]
        sg = [sb(f"sg{i}", [128, 512]) for i in range(2)]
        yacc = sb("yacc", [128, 512])
        tmp = sb("tmp", [128, 512])
        yT = sb("yT", [128, 8, 512], BF16)
        xt = [sb(f"xt{i}", [128, 1024]) for i in range(2)]
        xo = [sb(f"xo{i}", [128, 1024]) for i in range(2)]
        tm2 = [sb(f"tm{i}", [128, 512]) for i in range(2)]
        pg = [ps(f"pg{i}", [128, 512]) for i in range(2)]
        pbr = [ps(f"pb{i}", [128, 512]) for i in range(2)]
        pw = [ps(f"pw{i}", [128, 512]) for i in range(2)]
        wl = K.S("ld2")
        hl = [K.S("ld0"), K.S("ld1")]
        xl = [K.S("ld3"), K.S("ld4")]
        s_pe, s_ac, s_dv, s_pl = K.S("pe"), K.S("ac"), K.S("dv"), K.S("pl")
        stx = [K.S("st0"), K.S("st1")]
        gw = K.S("gw")
        for c in range(8):
            K.dma(POOL, wg[:, c, :], L["w_in"][c * 128:(c + 1) * 128, O_GATE:WIN], gw)
        for r, nm in enumerate(("w_o_gqa", "w_o_na", "w_o_mla")):
            K.dma(POOL, wo[r][:], L[nm].rearrange("(c p) n -> p c n", p=128), gw)
        t_gw = K.dma(POOL, wout[:], L["w_out"].rearrange("(c p) n -> p c n", p=128), gw)
        K.dma(SP, g1[:, 0, :], dr["modv"][0, 2].partition_broadcast(128), wl)
        t_w = K.dma(SP, g1[:, 1, :], dr["modv"][1, 2].partition_broadcast(128), wl)
        for e in (PE, DVE, POOL):
            W(e, t_w)
            W(e, t_gw)
        ysrc = (dr["YAT"], dr["YBT"], dr["YCT"])
        blk_done = [None, None]
        k = 0
        xk = 0
        sg_free = [None, None]
        pg_free = [None, None]
        pbr_free = [None, None]
        pw_free = [None, None]
        xt_free = [None, None]
        xo_free = [None, None]
        tm_free = [None, None]
        yT_free = None
        for bi, (hT_ap, q0, nt, m) in enumerate(qblocks):
            s = bi % 2
            W(SP, blk_done[s])
            K.dma(SP, hblk[s][:, :, :nt], hT_ap.rearrange("(c p) t -> p c t", p=128), hl[s])
            for r in range(3):
                t_l = K.dma(SP, yb[s][r][:, :, :nt], ysrc[r][:, q0:q0 + nt].rearrange("(c p) t -> p c t", p=128), hl[s])
            W(PE, t_l)
            for oc in range(8):
                for r in range(3):
                    W(PE, pg_free[k % 2])
                    for c in range(8):
                        ins = PE.matmul(pg[k % 2][:, :nt], lhsT=wg[:, c, r * 1024 + oc * 128:r * 1024 + (oc + 1) * 128], rhs=hblk[s][:, c, :nt],
                                        start=(c == 0), stop=(c == 7))
                    t_g = s_pe.inc(ins)
                    W(PE, pbr_free[k % 2])
                    for c in range(4):
                        ins = PE.matmul(pbr[k % 2][:, :nt], lhsT=wo[r][:, c, oc * 128:(oc + 1) * 128], rhs=yb[s][r][:, c, :nt], start=(c == 0), stop=(c == 3))
                    t_b = s_pe.inc(ins)
                    W(ACT, t_g)
                    W(ACT, sg_free[k % 2])
                    t_s = s_ac.inc(ACT.activation(out=sg[k % 2][:, :nt], in_=pg[k % 2][:, :nt], func=AF.Sigmoid))
                    pg_free[k % 2] = t_s
                    W(DVE, t_s)
                    W(DVE, t_b)
                    if r == 0:
                        t_d = s_dv.inc(DVE.tensor_tensor(out=yacc[:, :nt], in0=sg[k % 2][:, :nt], in1=pbr[k % 2][:, :nt], op=ALU.mult))
                    else:
                        t_d = s_dv.inc(DVE.tensor_tensor(out=tmp[:, :nt], in0=sg[k % 2][:, :nt], in1=pbr[k % 2][:, :nt], op=ALU.mult))
                        W(DVE, t_d)
                        if r == 1:
                            t_d = s_dv.inc(DVE.tensor_tensor(out=yacc[:, :nt], in0=yacc[:, :nt], in1=tmp[:, :nt], op=ALU.add))
                        else:
                            if oc == 0:
                                W(DVE, yT_free)
                            t_d = s_dv.inc(DVE.tensor_tensor(out=yT[:, oc, :nt], in0=yacc[:, :nt], in1=tmp[:, :nt], op=ALU.add))
                    sg_free[k % 2] = t_d
                    pbr_free[k % 2] = t_d
                    k += 1
            blk_done[s] = (s_pe, s_pe.n)
            t_y = t_d
            W(PE, t_y)
            for ti in range(nt // 128):
                xs_ = xk % 2
                W(SP, xt_free[xs_])
                t_x = K.dma(SP, xt[xs_][:], x_src(q0 + ti * 128), xl[xs_])
                for half in range(2):
                    j = 2 * xk + half
                    W(PE, pw_free[j % 2])
                    for c in range(8):
                        ins = PE.matmul(pw[j % 2][:, :], lhsT=yT[:, c, ti * 128:(ti + 1) * 128], rhs=wout[:, c, half * 512:(half + 1) * 512],
                                        start=(c == 0), stop=(c == 7))
                    t_p = s_pe.inc(ins)
                    W(DVE, t_p)
                    W(DVE, tm_free[j % 2])
                    t_m = s_dv.inc(DVE.tensor_tensor(out=tm2[j % 2][:], in0=pw[j % 2][:], in1=g1[:, m, half * 512:(half + 1) * 512], op=ALU.mult))
                    pw_free[j % 2] = t_m
                    W(POOL, t_m)
                    W(POOL, t_x)
                    if half == 0:
                        W(POOL, xo_free[xs_])
                    t_a = s_pl.inc(POOL.tensor_tensor(out=xo[xs_][:, half * 512:(half + 1) * 512], in0=tm2[j % 2][:], in1=xt[xs_][:, half * 512:(half + 1) * 512], op=ALU.add))
                    tm_free[j % 2] = t_a
                xt_free[xs_] = t_a
                W(SP, t_a)
                xo_free[xs_] = K.dma(SP, x_dst(q0 + ti * 128), xo[xs_][:], stx[xs_])
                xk += 1
            yT_free = (s_pe, s_pe.n)
        K.barrier([(s_, s_.n) for s_ in stx])


def phase_mlp(K, L, pfx, qblocks, x_src, x_dst, final_g):
    nc = K.nc
    PE, ACT, DVE, POOL, SP = K.PE, K.ACT, K.DVE, K.POOL, K.SP
    dr = K.dram
    with ExitStack() as es:
        sb, ps = mk_alloc(nc, es, pfx)
        w1 = sb("w1", [128, 8, 4096], BF16)
        w2 = sb("w2", [128, 32, 1024], BF16)
        g2 = sb("g2", [128, 2, 1024])
        fg = sb("fg", [128, 1024])
        hblk = [sb(f"h{i}", [128, 8, 256], BF16) for i in range(2)]
        uT = sb("uT", [128, 32, 256], BF16)
        rb = [sb(f"r{i}", [128, 256]) for i in range(2)]
        xt = [sb(f"xt{i}", [128, 1024]) for i in range(2)]
        xo = [sb(f"xo{i}", [128, 1024]) for i in range(2)]
        tm2 = [sb(f"tm{i}", [128, 512]) for i in range(2)]
        junk = sb("junk", [128, 1024])
        st4 = sb("st4", [128, 4])
        pu = [ps(f"pu{i}", [128, 512]) for i in range(2)]
        pw = [ps(f"pw{i}", [128, 512]) for i in range(2)]
        wl = K.S("ld2")
        hl = [K.S("ld0"), K.S("ld1")]
        xl = [K.S("ld3"), K.S("ld4")]
        s_pe, s_ac, s_dv, s_pl = K.S("pe"), K.S("ac"), K.S("dv"), K.S("pl")
        stx = [K.S("st0"), K.S("st1")]
        gw = K.S("gw")
        for c in range(8):
            K.dma(POOL, w1[:, c, :], L["w_mlp1"][c * 128:(c + 1) * 128, :], gw)
        for c4 in range(4):
            t_gw = K.dma(POOL, w2[:, c4 * 8:(c4 + 1) * 8, :], L["w_mlp2"][c4 * 1024:(c4 + 1) * 1024, :].rearrange("(c p) n -> p c n", p=128), gw)
        K.dma(SP, g2[:, 0, :], dr["modv"][0, 5].partition_broadcast(128), wl)
        if final_g is not None:
            K.dma(SP, fg[:], final_g.partition_broadcast(128), wl)
        t_w = K.dma(SP, g2[:, 1, :], dr["modv"][1, 5].partition_broadcast(128), wl)
        for e in (PE, DVE, POOL, ACT):
            W(e, t_w)
            W(e, t_gw)
        h2T = dr["h2T"]
        blk_done = [None, None]
        k = 0
        xk = 0
        pu_free = [None, None]
        rb_free = [None, None]
        pw_free = [None, None]
        xt_free = [None, None]
        xo_free = [None, None]
        tm_free = [None, None]
        uT_free = None
        for bi, (q0, nt, m) in enumerate(qblocks):
            s = bi % 2
            W(SP, blk_done[s])
            t_l = K.dma(SP, hblk[s][:, :, :nt], h2T[:, q0:q0 + nt].rearrange("(c p) t -> p c t", p=128), hl[s])
            W(PE, t_l)
            for fc in range(32):
                W(PE, pu_free[k % 2])
                for c in range(8):
                    ins = PE.matmul(pu[k % 2][:, :nt], lhsT=w1[:, c, fc * 128:(fc + 1) * 128], rhs=hblk[s][:, c, :nt], start=(c == 0), stop=(c == 7))
                t_u = s_pe.inc(ins)
                W(ACT, t_u)
                W(ACT, rb_free[k % 2])
                t_r = s_ac.inc(ACT.activation(out=rb[k % 2][:, :nt], in_=pu[k % 2][:, :nt], func=AF.Relu))
                pu_free[k % 2] = t_r
                W(DVE, t_r)
                if fc == 0:
                    W(DVE, uT_free)
                t_q = s_dv.inc(DVE.tensor_tensor(out=uT[:, fc, :nt], in0=rb[k % 2][:, :nt], in1=rb[k % 2][:, :nt], op=ALU.mult))
                rb_free[k % 2] = t_q
                k += 1
            blk_done[s] = (s_pe, s_pe.n)
            W(PE, t_q)
            for ti in range(nt // 128):
                xs_ = xk % 2
                W(SP, xt_free[xs_])
                t_x = K.dma(SP, xt[xs_][:], x_src(q0 + ti * 128), xl[xs_])
                for half in range(2):
                    j = 2 * xk + half
                    W(PE, pw_free[j % 2])
                    for fc in range(32):
                        ins = PE.matmul(pw[j % 2][:, :], lhsT=uT[:, fc, ti * 128:(ti + 1) * 128], rhs=w2[:, fc, half * 512:(half + 1) * 512],
                                        start=(fc == 0), stop=(fc == 31))
                    t_p = s_pe.inc(ins)
                    W(DVE, t_p)
                    W(DVE, tm_free[j % 2])
                    t_m = s_dv.inc(DVE.tensor_tensor(out=tm2[j % 2][:], in0=pw[j % 2][:], in1=g2[:, m, half * 512:(half + 1) * 512], op=ALU.mult))
                    pw_free[j % 2] = t_m
                    W(POOL, t_m)
                    W(POOL, t_x)
                    if half == 0:
                        W(POOL, xo_free[xs_])
                    t_a = s_pl.inc(POOL.tensor_tensor(out=xo[xs_][:, half * 512:(half + 1) * 512], in0=tm2[j % 2][:], in1=xt[xs_][:, half * 512:(half + 1) * 512], op=ALU.add))
                    tm_free[j % 2] = t_a
                xt_free[xs_] = t_a
                t_fin = t_a
                if final_g is not None:
                    W(ACT, t_a)
                    t1 = s_ac.inc(ACT.activation(out=junk[:], in_=xo[xs_][:], func=AF.Square, accum_out=st4[:, 0:1]))
                    W(DVE, t1)
                    t2 = s_dv.inc(DVE.tensor_scalar(out=st4[:, 1:2], in0=st4[:, 0:1], scalar1=1.0 / D, scalar2=EPS, op0=ALU.mult, op1=ALU.add))
                    W(ACT, t2)
                    t3 = s_ac.inc(ACT.activation(out=st4[:, 2:3], in_=st4[:, 1:2], func=AF.Sqrt))
                    W(DVE, t3)
                    t4 = s_dv.inc(DVE.reciprocal(out=st4[:, 3:4], in_=st4[:, 2:3]))
                    W(DVE, t4)
                    t_fin = s_dv.inc(DVE.scalar_tensor_tensor(out=xo[xs_][:], in0=xo[xs_][:], scalar=st4[:, 3:4], in1=fg[:], op0=ALU.mult, op1=ALU.mult))
                W(SP, t_fin)
                xo_free[xs_] = K.dma(SP, x_dst(q0 + ti * 128), xo[xs_][:], stx[xs_])
                xk += 1
            uT_free = (s_pe, s_pe.n)
        K.barrier([(s_, s_.n) for s_ in stx])


def phase_halo(K, pfx):
    nc = K.nc
    PE, ACT, DVE, POOL, SP = K.PE, K.ACT, K.DVE, K.POOL, K.SP
    dr = K.dram
    xg, x1, xna, sel = K.xg_at, K.x1_at, dr["x_na2"], dr["sel"]
    with ExitStack() as es:
        sb, ps = mk_alloc(nc, es, pfx)
        selb = sb("sel", [128, 8])
        cand = [sb(f"c{i}", [128, 1024]) for i in range(4)]
        acc = [sb(f"a{i}", [128, 1024]) for i in range(2)]
        ld = [K.S("ld0"), K.S("ld1"), K.S("ld3"), K.S("ld4")]
        lc = K.S("ld2")
        s_dv = K.S("dv")
        st = [K.S("st0"), K.S("st1")]
        so = K.S("st2")
        t_c = K.dma(SP, selb[:], sel.partition_broadcast(128), lc)
        for q in range(0, T, 128):
            t_own = K.dma(SP, xna[256 + q:256 + q + 128, :], x1(q), so)
        W(DVE, t_c)
        jobs = []
        for u in range(2):
            jobs.append((128 * u, [xg(2048 * r + 1792 + 128 * u) for r in range(4)], 0))
        for u in range(2):
            jobs.append((256 + T + 128 * u, [xg(2048 * r + 128 * u) for r in range(4)], 4))
        dv_prev = None
        st_t = [None, None]
        for n, (row0, srcs, c0) in enumerate(jobs):
            W(SP, dv_prev)
            lts = [K.dma(SP, cand[r][:], srcs[r], ld[r]) for r in range(4)]
            W(DVE, lts)
            W(DVE, st_t[n % 2])
            t = s_dv.inc(DVE.tensor_scalar(out=acc[n % 2][:], in0=cand[0][:], scalar1=selb[:, c0:c0 + 1], scalar2=0.0, op0=ALU.mult, op1=ALU.add))
            for r in range(1, 4):
                W(DVE, t)
                t = s_dv.inc(DVE.scalar_tensor_tensor(out=acc[n % 2][:], in0=cand[r][:], scalar=selb[:, c0 + r:c0 + r + 1], in1=acc[n % 2][:],
                                                      op0=ALU.mult, op1=ALU.add))
            dv_prev = t
            W(SP, t)
            st_t[n % 2] = K.dma(SP, xna[row0:row0 + 128, :], acc[n % 2][:], st[n % 2])
        K.barrier([t_own, st_t[0], st_t[1]])


W_NAMES = ["w_mod", "b_mod", "norm1_g", "norm2_g", "w_in", "gqa_q_norm", "gqa_k_norm", "mla_kv_norm", "mla_w_uk", "mla_w_uv",
           "w_o_gqa", "w_o_na", "w_o_mla", "w_out", "w_mlp1", "w_mlp2"]
W_SHAPES = {"w_mod": [D, 6 * D], "b_mod": [6 * D], "norm1_g": [D], "norm2_g": [D], "w_in": [D, WIN], "gqa_q_norm": [64], "gqa_k_norm": [64],
            "mla_kv_norm": [256], "mla_w_uk": [256, 512], "mla_w_uv": [256, 512], "w_o_gqa": [512, D], "w_o_na": [512, D], "w_o_mla": [512, D],
            "w_out": [D, D], "w_mlp1": [D, 4 * D], "w_mlp2": [4 * D, D]}
DEPTH = 2


def emit_layer(K, L, src, with_ctx, final, sfx):
    dr = K.dram
    NQ = T + C
    phase_mod(K, L, "md" + sfx)
    phase_norm(K, [(src["x_all"], S, dr["hT_all"][:, 0:S], 0, 0, 1), (src["ctx_in"], C, dr["hT_all"][:, S:SK], 1, 0, 1),
                   (src["x_na"], NAT, dr["hT_na"], 0, 0, 1)], "n1" + sfx)
    phase_proj(K, L, "pj" + sfx, with_ctx)
    qbl = [(512 * i, 512) for i in range(4)]
    heads = []

    def wide_blocks(Y, h):
        blocks = [dict(q0=q0, nq=1024, tiles=[(k, None) for k in range(66)],
                       yts=[Y[64 * h:64 * h + 64, q0:q0 + 512], Y[64 * h:64 * h + 64, q0 + 512:q0 + 1024]]) for q0 in (0, 1024)]
        if with_ctx:
            blocks.append(dict(q0=T, nq=C, tiles=[(64, None), (65, None)], yts=[Y[64 * h:64 * h + 64, T:NQ]]))
        return blocks

    for h in range(8):
        heads.append(dict(kt=dr["GKT"][h // 4], v=dr["GV"][:, h // 4, :], qt=dr["GQT"][h], dk=64, scale=0.125, nk=SK, blocks=wide_blocks(dr["YAT"], h)))
    for h in range(8):
        heads.append(dict(kt=dr["MKT"][h], v=dr["MV"][:, h, :], qt=dr["MQT"][h], dk=96, scale=96 ** -0.5, nk=SK, blocks=wide_blocks(dr["YCT"], h)))
    phase_attn(K, heads, "at" + sfx, SK)
    heads = []
    var = [0, 1, 1, 2]
    for h in range(8):
        blocks = []
        for i, (q0, nq) in enumerate(qbl):
            tiles = [(4 * i + m, src["nabias"][var[i], h, m]) for m in range(8)] + [(20, None), (21, None)]
            blocks.append(dict(q0=q0, nq=nq, tiles=tiles, yts=[dr["YBT"][64 * h:64 * h + 64, q0:q0 + nq]]))
        if with_ctx:
            blocks.append(dict(q0=T, nq=C, tiles=[(20, None), (21, None)], yts=[dr["YBT"][64 * h:64 * h + 64, T:NQ]]))
        heads.append(dict(kt=dr["NKT"][h], v=dr["NV"][:, h, :], qt=dr["NQT"][h], dk=64, scale=1.0, nk=NAK, blocks=blocks))
    phase_attn(K, heads, "na" + sfx, NAK)
    mblocks = [(dr["hT_na"][:, 256 + 512 * i:256 + 512 * (i + 1)], 512 * i, 512, 0) for i in range(4)]
    if with_ctx:
        mblocks.append((dr["hT_all"][:, S:SK], T, C, 1))

    def x_src(q):
        if q >= T:
            return src["ctx_in"][q - T:q - T + 128, :]
        return src["x_own"](q) if callable(src["x_own"]) else src["x_own"][q:q + 128, :]

    def xs1_at(q):
        return dr["xs1"][q:q + 128, :]

    phase_merge(K, L, "mg" + sfx, mblocks, x_src, xs1_at)
    njobs = [(dr["xs1"][0:T, :], T, dr["h2T"][:, 0:T], 0, 3, 4)]
    if with_ctx:
        njobs.append((dr["xs1"][T:NQ, :], C, dr["h2T"][:, T:NQ], 1, 3, 4))
    phase_norm(K, njobs, "n2" + sfx)
    fblocks = [(256 * i, 256, 0) for i in range(8)]
    if with_ctx:
        fblocks.append((T, C, 1))
    phase_mlp(K, L, "ml" + sfx, fblocks, xs1_at, src["x_dst"], L.get("final_norm_g") if final else None)


def build_fused():
    nc = bass.Bass("TRN2", target_bir_lowering=False)
    K = KB(nc)
    NQ = T + C
    dr = K.dram

    def inp(name, shape, dt=F32):
        dr[name] = nc.dram_tensor(name, shape, dt, kind="ExternalInput").ap()

    def internal(name, shape, dt=BF16):
        dr[name] = nc.dram_tensor(name, shape, dt).ap()

    inp("x_all", [S, D]); inp("x_own", [T, D]); inp("x_na", [NAT, D]); inp("ctx_in", [C, D]); inp("cvec", [2, D])
    Wst = {}
    for n in W_NAMES:
        Wst[n] = nc.dram_tensor(n, [DEPTH] + W_SHAPES[n], F32, kind="ExternalInput").ap()
    fng = nc.dram_tensor("final_norm_g", [D], F32, kind="ExternalInput").ap()
    for n in ("ident_f", "onesbd_f", "ones_f"):
        inp(n, [128, 128])
    for n in ("pt128", "pt96", "pt32", "ident_b"):
        inp(n, [128, 128], BF16)
    inp("cosA_all", [128, S]); inp("sinA_all", [128, S]); inp("cosA_own", [128, T]); inp("sinA_own", [128, T])
    inp("cosM_all", [32, S]); inp("sinM_all", [32, S]); inp("cosM_own", [96, T]); inp("sinM_own", [96, T])
    inp("nabias", [DEPTH, 3, 8, 8, 128, 512])
    inp("sel", [8])
    dr["xout"] = nc.dram_tensor("xout", [T, D], F32, kind="ExternalOutput").ap()
    internal("modv", [2, 6, D], F32)
    internal("hT_all", [D, SK]); internal("hT_na", [D, NAT])
    internal("GKT", [2, 64, SK]); internal("CKVT", [256, SK]); internal("MKT", [8, 96, SK])
    internal("MV", [SK, 8, 65]); internal("GV", [SK, 2, 65])
    internal("NKT", [8, 64, NAK]); internal("NV", [NAK, 8, 65])
    internal("GQT", [8, 64, NQ]); internal("NQT", [8, 64, NQ]); internal("MQT", [8, 96, NQ])
    internal("YAT", [512, NQ]); internal("YBT", [512, NQ]); internal("YCT", [512, NQ])
    internal("xs1", [NQ, D], F32); internal("h2T", [D, NQ])
    NCH = 8
    x1c = [nc.dram_tensor(f"x1c{k}", [256, D], F32) for k in range(NCH)]
    xgc = [nc.dram_tensor(f"xgc{k}", [4 * 256, D], F32) for k in range(NCH)]
    internal("c1loc", [C, D], F32); internal("x_na2", [NAT, D], F32)
    internal("rls", [4, 512], F32)

    def x1_at(q):
        return x1c[q // 256].ap()[q % 256:q % 256 + 128, :]

    def xg_at(t0):
        r, k, off = t0 // 2048, (t0 % 2048) // 256, t0 % 256
        return xgc[k].ap()[r * 256 + off:r * 256 + off + 128, :]

    K.x1_at, K.xg_at = x1_at, xg_at

    for l in range(DEPTH):
        L = {n: Wst[n][l] for n in W_NAMES}
        final = (l == DEPTH - 1)
        with_ctx = not final
        if final:
            L["final_norm_g"] = fng
        if l == 0:
            src = dict(x_all=dr["x_all"], x_own=dr["x_own"], x_na=dr["x_na"], ctx_in=dr["ctx_in"])
        else:
            src = dict(x_all=(lambda i: xg_at(128 * i)), x_own=x1_at, x_na=dr["x_na2"], ctx_in=dr["c1loc"])
        src["nabias"] = dr["nabias"][l]
        if final:
            src["x_dst"] = lambda q: dr["xout"][q:q + 128, :]
        else:
            src["x_dst"] = lambda q: (x1_at(q) if q < T else dr["c1loc"][q - T:q - T + 128, :])
        emit_layer(K, L, src, with_ctx, final, f"{l}_")
        if not final:
            cc = K.S("cc")
            for k in range(NCH):
                ins = K.POOL.collective_compute("AllGather", mybir.AluOpType.bypass, replica_groups=[[0, 1, 2, 3], [4, 5, 6, 7]],
                                                ins=[x1c[k].ap().opt()], outs=[xgc[k].ap().opt()])
                t_cc = cc.inc(ins)
            K.barrier([t_cc])
            phase_halo(K, f"hl{l}_")
    K.semcounts = {n: s.n for n, s in K.sems.items()}
    return nc, K


def _rope_tables():
    t = np.arange(S, dtype=np.int32)
    row = (t // 64).astype(np.float32)
    col = (t % 64).astype(np.float32)

    def tabs(rot_dim):
        half = rot_dim // 2
        inv = (10000.0 ** (-np.arange(0, half, 2, dtype=np.float32) / np.float32(half))).astype(np.float32)
        ar = (row[:, None] * inv).astype(np.float32)
        ac = (col[:, None] * inv).astype(np.float32)
        cos = np.concatenate([np.cos(ar), np.cos(ar), np.cos(ac), np.cos(ac)], axis=1).T.astype(np.float32)
        sin = np.concatenate([np.sin(ar), np.sin(ar), np.sin(ac), np.sin(ac)], axis=1).T.astype(np.float32)
        return np.ascontiguousarray(cos), np.ascontiguousarray(sin)

    return tabs(64), tabs(32)


def _rot_matrix(n):
    q = n // 4
    P = np.zeros((n, n), np.float32)
    for base in (0, 2 * q):
        for i in range(q):
            P[base + i, base + q + i] = -1.0
            P[base + q + i, base + i] = 1.0
    return P


def _consts():
    c = {}
    c["ident_f"] = np.eye(128, dtype=np.float32)
    c["ones_f"] = np.ones((128, 128), np.float32)
    bd = np.zeros((128, 128), np.float32)
    bd[:64, :64] = 1.0
    bd[64:, 64:] = 1.0
    c["onesbd_f"] = bd
    P64 = _rot_matrix(64)
    P32 = _rot_matrix(32)
    pt128 = np.zeros((128, 128), np.float32)
    pt128[:64, :64] = P64.T
    pt128[64:, 64:] = P64.T
    pt96 = np.zeros((128, 128), np.float32)
    pt96[64:96, 64:96] = P32.T
    pt32 = np.zeros((128, 128), np.float32)
    pt32[:32, :32] = P32.T
    c["ident_b"] = np.eye(128, dtype=np.float32).astype(ml_dtypes.bfloat16)
    c["pt128"] = pt128.astype(ml_dtypes.bfloat16)
    c["pt96"] = pt96.astype(ml_dtypes.bfloat16)
    c["pt32"] = pt32.astype(ml_dtypes.bfloat16)
    return c


def _na_bias_tables(rpb, j):
    out = np.empty((3, 8, 8, 128, 512), np.float32)
    kcol = np.arange(64)[None, :, None, None]
    qcol = np.arange(64)[None, None, None, :]
    m = np.arange(16)[:, None, None, None]
    a = np.arange(8)[None, None, :, None]
    cs = np.clip(qcol - 8, 0, 48)
    colok = (kcol >= cs) & (kcol < cs + 16)
    cidx = np.clip(kcol - qcol + 15, 0, 30)
    for v, i in enumerate((0, 1, 3)):
        r = 32 * j + 8 * i + a
        k = 32 * j + 8 * i - 4 + m
        rs = np.clip(r - 4, 0, 120)
        ok = (k >= 0) & (k < 128) & (k >= rs) & (k < rs + 8) & colok
        ridx = np.clip(k - r + 7, 0, 14)
        ridx_b = np.broadcast_to(ridx, ok.shape)
        cidx_b = np.broadcast_to(cidx, ok.shape)
        vals = rpb[:, ridx_b, cidx_b]
        tab = np.where(ok[None], vals, np.float32(NEG)).astype(np.float32)
        out[v] = tab.reshape(8, 8, 128, 512)
    return out


def _core_inputs(x_full, ctx_full, c, c_ctx, Wd, consts, ropes):
    (cosA, sinA), (cosM, sinM) = ropes
    maps = []
    cosA2 = np.ascontiguousarray(np.concatenate([cosA, cosA], 0))
    sinA2 = np.ascontiguousarray(np.concatenate([sinA, sinA], 0))
    shared = dict(consts)
    for n in W_NAMES:
        shared[n] = np.ascontiguousarray(Wd[n])
    shared["final_norm_g"] = np.ascontiguousarray(Wd["final_norm_g"])
    shared["cosA_all"] = cosA2
    shared["sinA_all"] = sinA2
    shared["cosM_all"] = cosM
    shared["sinM_all"] = sinM
    rpb = np.asarray(Wd["na_rpb"], np.float32)
    nab = [np.stack([_na_bias_tables(rpb[l], j) for l in range(rpb.shape[0])], 0) for j in range(4)]
    for core in range(8):
        b, j = core // 4, core % 4
        t0 = T * j
        d = dict(shared)
        d["x_all"] = np.ascontiguousarray(x_full[b])
        d["x_own"] = np.ascontiguousarray(x_full[b, t0:t0 + T])
        xna = np.zeros((NAT, D), np.float32)
        lo = (32 * j - 4) * 64
        hi = lo + NAT
        slo, shi = max(lo, 0), min(hi, S)
        xna[slo - lo:shi - lo] = x_full[b, slo:shi]
        d["x_na"] = xna
        d["ctx_in"] = np.ascontiguousarray(ctx_full[b])
        d["cvec"] = np.ascontiguousarray(np.stack([c[b], c_ctx]))
        d["cosA_own"] = np.ascontiguousarray(cosA2[:, t0:t0 + T])
        d["sinA_own"] = np.ascontiguousarray(sinA2[:, t0:t0 + T])
        d["cosM_own"] = np.ascontiguousarray(np.concatenate([np.ones((64, T), np.float32), cosM[:, t0:t0 + T]], 0))
        d["sinM_own"] = np.ascontiguousarray(np.concatenate([np.zeros((64, T), np.float32), sinM[:, t0:t0 + T]], 0))
        d["nabias"] = nab[j]
        sel = np.zeros(8, np.float32)
        if j > 0:
            sel[j - 1] = 1.0
        if j < 3:
            sel[4 + j + 1] = 1.0
        d["sel"] = sel
        maps.append(d)
    return maps


_PROG = []


def kernel(**inputs):
    Wd = {k: np.asarray(v, np.float32) for k, v in inputs.items()}
    if not _PROG:
        _PROG.append(build_fused()[0])
    nc = _PROG[0]
    maps = _core_inputs(Wd["x"], Wd["ctx"], Wd["c"], Wd["c_ctx"], Wd, _consts(), _rope_tables())
    res = run_bass_kernel_spmd(nc, maps, core_ids=list(range(8)))
    outs = [r["xout"] for r in res.results]
    x = np.stack([np.concatenate([outs[4 * b + j] for j in range(4)], 0) for b in range(2)], 0)
    return np.ascontiguousarray(x.astype(np.float32))
```

```python
from contextlib import ExitStack
import os
import numpy as np
import ml_dtypes
import concourse.bass as bass
import concourse.mybir as mybir
from concourse.bass_utils import run_bass_kernel_spmd

F32 = mybir.dt.float32
BF16 = mybir.dt.bfloat16
AF = mybir.ActivationFunctionType
ALU = mybir.AluOpType

D = 1024
S = 8192
C = 256
SK = S + C
T = 2048
NAT = 2560
NAK = NAT + C
EPS = 1e-6
NEG = -30000.0
O_GQ, O_GK, O_GV, O_NQ, O_NK, O_NV, O_MQ, O_CKV, O_KR, O_GATE = 0, 512, 640, 768, 1280, 1792, 2304, 3072, 3328, 3360
WIN = 6432


class Sem:
    def __init__(self, nc, name):
        self.h = nc.alloc_semaphore(name)
        self.n = 0

    def inc(self, ins, k=1):
        ins.then_inc(self.h, k)
        self.n += k
        return (self, self.n)


def W(eng, tok):
    if tok is None:
        return
    if isinstance(tok, list):
        for t in tok:
            W(eng, t)
        return
    s, v = tok
    if v > 0:
        eng.wait_ge(s.h, v)


class KB:
    def __init__(self, nc):
        self.nc = nc
        self.PE, self.ACT, self.DVE, self.POOL, self.SP = nc.tensor, nc.scalar, nc.vector, nc.gpsimd, nc.sync
        self.sems = {}
        self.dram = {}

    def S(self, name):
        if name not in self.sems:
            self.sems[name] = Sem(self.nc, name)
        return self.sems[name]

    def dma(self, eng, out, in_, sem, slow=False):
        if slow:
            ins = eng.dma_start(out=out, in_=in_, allow_slow_non_contiguous=True)
        else:
            ins = eng.dma_start(out=out, in_=in_)
        return sem.inc(ins, 16)

    def barrier(self, toks):
        for e in (self.PE, self.ACT, self.DVE, self.POOL, self.SP):
            W(e, toks)


def mk_alloc(nc, es, pfx):
    def sb(name, shape, dt=F32):
        return es.enter_context(nc.sbuf_tensor(pfx + name, shape, dt))

    def ps(name, shape, dt=F32):
        return es.enter_context(nc.psum_tensor(pfx + name, shape, dt))

    return sb, ps


def phase_mod(K, L, pfx):
    nc = K.nc
    PE, ACT, DVE, POOL, SP = K.PE, K.ACT, K.DVE, K.POOL, K.SP
    with ExitStack() as es:
        sb, ps = mk_alloc(nc, es, pfx)
        cT = sb("cT", [128, 8, 2])
        sT = sb("sT", [128, 8, 2])
        NWB = 4
        wm = [sb(f"w{i}", [128, 8, 512]) for i in range(NWB)]
        bm = sb("b", [2, 6144])
        mrow = sb("m", [2, 6144])
        ng = sb("ng", [2, 2, 1024])
        mv = sb("mv", [2, 6, 1024])
        pm = [ps(f"p{i}", [2, 512]) for i in range(2)]
        ld = K.S("ld0")
        wl = [K.S("ld1"), K.S("ld2"), K.S("ld3"), K.S("ld4")]
        s_pe, s_ac, s_dv, st = K.S("pe"), K.S("ac"), K.S("dv"), K.S("st0")
        for m in range(2):
            K.dma(SP, cT[:, :, m], K.dram["cvec"][m].rearrange("(c p) -> p c", p=128), ld, slow=True)
        K.dma(SP, bm[:], L["b_mod"].partition_broadcast(2), ld)
        K.dma(SP, ng[:, 0, :], L["norm1_g"].partition_broadcast(2), ld)
        t_ld = K.dma(SP, ng[:, 1, :], L["norm2_g"].partition_broadcast(2), ld)
        W(ACT, t_ld)
        t_s = s_ac.inc(ACT.activation(out=sT[:].rearrange("p c m -> p (c m)"), in_=cT[:].rearrange("p c m -> p (c m)"), func=AF.Silu))
        W(PE, t_s)
        pe_t = [None] * 12
        dv_t = [None] * 12
        wsrc = L["w_mod"]
        w_t = [None] * 12

        def load_w(g):
            if g >= NWB:
                W(SP, pe_t[g - NWB])
            w_t[g] = K.dma(SP, wm[g % NWB][:], wsrc[:, g * 512:(g + 1) * 512].rearrange("(c p) n -> p c n", p=128), wl[g % NWB])

        for g in range(min(NWB - 1, 12)):
            load_w(g)
        for g in range(12):
            if g + NWB - 1 < 12:
                load_w(g + NWB - 1)
            W(PE, w_t[g])
            if g >= 2:
                W(PE, dv_t[g - 2])
            for c in range(8):
                ins = PE.matmul(pm[g % 2][:], lhsT=sT[:, c, :], rhs=wm[g % NWB][:, c, :], start=(c == 0), stop=(c == 7))
            pe_t[g] = s_pe.inc(ins)
            W(DVE, pe_t[g])
            if g == 0:
                W(DVE, t_ld)
            dv_t[g] = s_dv.inc(DVE.tensor_tensor(out=mrow[:, g * 512:(g + 1) * 512], in0=pm[g % 2][:], in1=bm[:, g * 512:(g + 1) * 512], op=ALU.add))
        W(DVE, dv_t[11])
        sl = lambda i: mrow[:, i * 1024:(i + 1) * 1024]
        DVE.scalar_tensor_tensor(out=mv[:, 0, :], in0=sl(1), scalar=1.0, in1=ng[:, 0, :], op0=ALU.add, op1=ALU.mult)
        DVE.tensor_copy(out=mv[:, 1, :], in_=sl(0))
        DVE.tensor_copy(out=mv[:, 2, :], in_=sl(2))
        DVE.scalar_tensor_tensor(out=mv[:, 3, :], in0=sl(4), scalar=1.0, in1=ng[:, 1, :], op0=ALU.add, op1=ALU.mult)
        DVE.tensor_copy(out=mv[:, 4, :], in_=sl(3))
        t_f = s_dv.inc(DVE.tensor_copy(out=mv[:, 5, :], in_=sl(5)))
        W(SP, t_f)
        t_st = K.dma(SP, K.dram["modv"], mv[:], st)
        K.barrier([t_st])


def phase_norm(K, jobs, pfx):
    nc = K.nc
    PE, ACT, DVE, POOL, SP = K.PE, K.ACT, K.DVE, K.POOL, K.SP
    tiles = []
    for ji, (src, ntok, dst, m, ia, ish) in enumerate(jobs):
        for i in range(ntok // 128):
            tiles.append((ji, i))
    NTI = len(tiles)
    with ExitStack() as es:
        sb, ps = mk_alloc(nc, es, pfx)
        NX = 6
        xt = [sb(f"xt{i}", [128, 1024]) for i in range(NX)]
        junk = sb("junk", [128, 1024])
        ss = sb("ss", [128, NTI])
        r1 = sb("r1", [128, NTI])
        r2 = sb("r2", [128, NTI])
        rstd = sb("rstd", [128, NTI])
        NXN = 3
        xn = [sb(f"xn{i}", [128, 1024]) for i in range(NXN)]
        hb = [sb(f"hb{i}", [128, 8, 512], BF16) for i in range(2)]
        ident = sb("ident", [128, 128])
        acol = sb("acol", [128, 2, 2, 8])
        pT = [ps(f"pT{i}", [128, 8, 128]) for i in range(2)]
        lds = [K.S("ld0"), K.S("ld1"), K.S("ld3"), K.S("ld4"), K.S("ld5"), K.S("ld6")]
        ldc = K.S("ld2")
        s_pe, s_ac, s_dv = K.S("pe"), K.S("ac"), K.S("dv")
        sts = [K.S("gs0"), K.S("gs1")]
        t_c = K.dma(SP, ident[:], K.dram["ident_f"], ldc)
        mods = sorted(set((j[3], j[4], j[5]) for j in jobs))
        assert len(set(m for m, _, _ in mods)) == len(mods)
        for (m, ia, ish) in mods:
            K.dma(SP, acol[:, m, 0, :], K.dram["modv"][m, ia].rearrange("(c p) -> p c", p=128), ldc, slow=True)
            t_c = K.dma(SP, acol[:, m, 1, :], K.dram["modv"][m, ish].rearrange("(c p) -> p c", p=128), ldc, slow=True)
        act_t = [None] * NTI
        for n, (ji, i) in enumerate(tiles):
            src = jobs[ji][0]
            if n >= NX:
                W(SP, act_t[n - NX])
            t_l = K.dma(SP, xt[n % NX][:], src(i) if callable(src) else src[i * 128:(i + 1) * 128, :], lds[n % NX])
            W(ACT, t_l)
            act_t[n] = s_ac.inc(ACT.activation(out=junk[:], in_=xt[n % NX][:], func=AF.Square, accum_out=ss[:, n:n + 1]))
        W(DVE, act_t[NTI - 1])
        t1 = s_dv.inc(DVE.tensor_scalar(out=r1[:], in0=ss[:], scalar1=1.0 / D, scalar2=EPS, op0=ALU.mult, op1=ALU.add))
        W(ACT, t1)
        t2 = s_ac.inc(ACT.activation(out=r2[:], in_=r1[:], func=AF.Sqrt))
        W(DVE, t2)
        t3 = s_dv.inc(DVE.reciprocal(out=rstd[:], in_=r2[:]))
        W(ACT, t3)
        W(SP, t2)
        W(PE, t_c)
        W(DVE, t_c)
        a_t = [None] * NTI
        p_t = [None] * NTI
        v_t = [None] * NTI
        st_t = {}
        blk = -1
        blk_of = []
        prev_key = None
        for n, (ji, i) in enumerate(tiles):
            key = (ji, i // 4)
            if key != prev_key:
                blk += 1
                prev_key = key
            blk_of.append(blk)
        for n, (ji, i) in enumerate(tiles):
            src, ntok, dst, m, ia, ish = jobs[ji]
            b = blk_of[n]
            if n >= NX:
                W(SP, a_t[n - NX])
            t_l = K.dma(SP, xt[n % NX][:], src(i) if callable(src) else src[i * 128:(i + 1) * 128, :], lds[n % NX])
            W(ACT, t_l)
            if n >= NXN:
                W(ACT, p_t[n - NXN])
            a_t[n] = s_ac.inc(ACT.activation(out=xn[n % NXN][:], in_=xt[n % NXN if False else n % NX][:], func=AF.Copy, scale=rstd[:, n:n + 1]))
            W(PE, a_t[n])
            if n >= 2:
                W(PE, v_t[n - 2])
            for c in range(8):
                ins = PE.transpose(out=pT[n % 2][:, c, :], in_=xn[n % NXN][:, c * 128:(c + 1) * 128], identity=ident[:])
            p_t[n] = s_pe.inc(ins)
            W(DVE, p_t[n])
            if (i % 4 == 0) and (b - 2) in st_t:
                W(DVE, st_t[b - 2])
            for c in range(8):
                ins = DVE.tensor_scalar(out=hb[b % 2][:, c, (i % 4) * 128:(i % 4 + 1) * 128], in0=pT[n % 2][:, c, :],
                                        scalar1=acol[:, m, 0, c:c + 1], scalar2=acol[:, m, 1, c:c + 1], op0=ALU.mult, op1=ALU.add)
            v_t[n] = s_dv.inc(ins)
            last_in_blk = (n + 1 == NTI) or (blk_of[n + 1] != b)
            if last_in_blk:
                nt = (i % 4 + 1) * 128
                t0 = (i // 4) * 512
                W(POOL, v_t[n])
                st_t[b] = K.dma(POOL, dst[:, t0:t0 + nt].rearrange("(c p) t -> p c t", p=128), hb[b % 2][:, :, 0:nt], sts[b % 2])
        K.barrier([st_t[blk], st_t.get(blk - 1)])


def phase_proj(K, L, pfx, with_ctx):
    nc = K.nc
    PE, ACT, DVE, POOL, SP = K.PE, K.ACT, K.DVE, K.POOL, K.SP
    dr = K.dram
    with ExitStack() as es:
        sb, ps = mk_alloc(nc, es, pfx)
        win = sb("win", [128, 8, O_GATE], BF16)
        wuk = sb("wuk", [128, 2, 512], BF16)
        wuv = sb("wuv", [128, 2, 512], BF16)
        onesbd = sb("onesbd", [128, 128])
        ones = sb("ones", [128, 128])
        pt128 = sb("pt128", [128, 128], BF16)
        pt96 = sb("pt96", [128, 128], BF16)
        pt32 = sb("pt32", [128, 128], BF16)
        gq = sb("gq", [128, 1])
        gk = sb("gk", [128, 1])
        kvg = sb("kvg", [128, 2])
        hblk = [sb(f"h{i}", [128, 8, 512], BF16) for i in range(2)]
        ckvn = sb("ckvn", [128, 2, 512], BF16)
        sqf = sb("sqf", [128, 512]); sqf2 = sb("sqf2", [128, 512])
        qf = sb("qf", [128, 512]); qf2 = sb("qf2", [128, 512])
        sd = sb("sd", [128, 512]); rs = sb("rs", [128, 512]); qn = sb("qn", [128, 512])
        t1b = sb("t1", [128, 512]); t2b = sb("t2", [128, 512])
        cosb = [sb(f"cos{i}", [128, 512]) for i in range(2)]
        sinb = [sb(f"sin{i}", [128, 512]) for i in range(2)]
        qb = sb("qb", [128, 512], BF16)
        outb = [sb(f"ob{i}", [128, 512], BF16) for i in range(2)]
        vout = [sb(f"vo{i}", [128, 8, 65], BF16) for i in range(2)]
        acc = [ps(f"acc{i}", [128, 512]) for i in range(2)]
        acc2 = ps("acc2", [128, 512])
        pss = ps("pss", [128, 512])
        prot = ps("prot", [128, 512])
        ptm = [ps(f"ptm{i}", [128, 512]) for i in range(2)]
        wl = K.S("ld2")
        gw = K.S("gw")
        hl = [K.S("ld0"), K.S("ld1")]
        tl = [K.S("ld3"), K.S("ld4")]
        s_pe, s_ac, s_dv, s_pl = K.S("pe"), K.S("ac"), K.S("dv"), K.S("pl")
        sto = [K.S("st0"), K.S("st1")]
        stv = [K.S("st2"), K.S("st3")]
        stc = K.S("st4")
        for c in range(8):
            K.dma(POOL, win[:, c, :], L["w_in"][c * 128:(c + 1) * 128, 0:O_GATE], gw)
        K.dma(POOL, wuk[:], L["mla_w_uk"].rearrange("(r p) n -> p r n", p=128), gw)
        t_gw = K.dma(POOL, wuv[:], L["mla_w_uv"].rearrange("(r p) n -> p r n", p=128), gw)
        K.dma(SP, onesbd[:], dr["onesbd_f"], wl)
        K.dma(SP, ones[:], dr["ones_f"], wl)
        K.dma(SP, pt128[:], dr["pt128"], wl)
        K.dma(SP, pt96[:], dr["pt96"], wl)
        K.dma(SP, pt32[:], dr["pt32"], wl)
        for hh in range(2):
            K.dma(SP, gq[hh * 64:(hh + 1) * 64, :], L["gqa_q_norm"].rearrange("(p o) -> p o", o=1), wl)
            K.dma(SP, gk[hh * 64:(hh + 1) * 64, :], L["gqa_k_norm"].rearrange("(p o) -> p o", o=1), wl)
        t_w = K.dma(SP, kvg[:], L["mla_kv_norm"].rearrange("(r p) -> p r", p=128), wl, slow=True)
        for i in range(2):
            DVE.memset(vout[i][:], 1.0)
        t_ms = s_dv.inc(DVE.memset(qn[:], 0.0))
        for e in (PE, ACT, DVE, POOL):
            W(e, t_w)
            W(e, t_gw)
        W(ACT, t_ms)

        st = {"k": 0, "rk": 0, "vk": 0, "acc_free": [None, None], "ob_free": [None, None], "tab_free": [None, None],
              "vo_free": [None, None], "ptm_free": [None, None], "hb_tok": None}

        def store(eng, dst, src, sem):
            return K.dma(eng, dst, src, sem)

        import os
        LIMIT = int(os.environ.get("PROJ_LIMIT", "1000000"))
        units = [0]

        def over():
            units[0] += 1
            return units[0] > LIMIT

        def fm_job(chunks, M, nt, norm_g, rope, dsts, oscale=1.0):
            if over():
                return
            k = st["k"]; st["k"] += 1
            a = acc[k % 2]
            W(PE, st["acc_free"][k % 2])
            W(PE, st["hb_tok"])
            for ci, (lt, rh) in enumerate(chunks):
                ins = PE.matmul(a[:M, :nt], lhsT=lt, rhs=rh, start=(ci == 0), stop=(ci == len(chunks) - 1))
            t_main = s_pe.inc(ins)
            ob = outb[k % 2]
            if norm_g is None and rope is None:
                W(ACT, t_main)
                W(ACT, st["ob_free"][k % 2])
                t_out = s_ac.inc(ACT.activation(out=ob[:M, :nt], in_=a[:M, :nt], func=AF.Copy, scale=float(oscale)))
                st["acc_free"][k % 2] = t_out
            else:
                if rope is not None:
                    r = st["rk"]; st["rk"] += 1
                    PT, cos_ap, sin_ap = rope
                    W(SP, st["tab_free"][r % 2])
                    K.dma(SP, cosb[r % 2][:M, :nt], cos_ap, tl[r % 2])
                    t_tab = K.dma(SP, sinb[r % 2][:M, :nt], sin_ap, tl[r % 2])
                W(DVE, t_main)
                t_qf = s_dv.inc(DVE.tensor_copy(out=qf[:M, :nt], in_=a[:M, :nt]))
                t_cur = t_qf
                cur = qf
                free_toks = [t_qf]
                if norm_g is not None:
                    W(ACT, t_qf)
                    t_sq = s_ac.inc(ACT.activation(out=sqf[:M, :nt], in_=qf[:M, :nt], func=AF.Square))
                    W(PE, t_sq)
                    t_ss = s_pe.inc(PE.matmul(pss[:M, :nt], lhsT=onesbd[:M, :M], rhs=sqf[:M, :nt], start=True, stop=True))
                    W(ACT, t_ss)
                    W(ACT, t_qf)
                    t_sd = s_ac.inc(ACT.activation(out=sd[:M, :nt], in_=pss[:M, :nt], func=AF.Sqrt, bias=EPS, scale=1.0 / 64))
                    W(DVE, t_sd)
                    t_rs = s_dv.inc(DVE.reciprocal(out=rs[:M, :nt], in_=sd[:M, :nt]))
                    W(DVE, t_rs)
                    if rope is None:
                        W(DVE, st["ob_free"][k % 2])
                        t_out = s_dv.inc(DVE.scalar_tensor_tensor(out=ob[:M, :nt], in0=qf[:M, :nt], scalar=norm_g, in1=rs[:M, :nt], op0=ALU.mult, op1=ALU.mult))
                    else:
                        t_cur = s_dv.inc(DVE.scalar_tensor_tensor(out=qn[:M, :nt], in0=qf[:M, :nt], scalar=norm_g, in1=rs[:M, :nt], op0=ALU.mult, op1=ALU.mult))
                        cur = qn
                st["acc_free"][k % 2] = free_toks
                if rope is not None:
                    W(ACT, t_cur)
                    t_qb = s_ac.inc(ACT.activation(out=qb[:M, :nt], in_=cur[:M, :nt], func=AF.Copy))
                    W(PE, t_qb)
                    t_rot = s_pe.inc(PE.matmul(prot[:M, :nt], lhsT=PT[:M, :M], rhs=qb[:M, :nt], start=True, stop=True))
                    W(POOL, t_cur)
                    W(POOL, t_tab)
                    t_t1 = s_pl.inc(POOL.tensor_tensor(out=t1b[:M, :nt], in0=cur[:M, :nt], in1=cosb[r % 2][:M, :nt], op=ALU.mult))
                    W(DVE, t_rot)
                    W(DVE, t_tab)
                    t_t2 = s_dv.inc(DVE.tensor_tensor(out=t2b[:M, :nt], in0=prot[:M, :nt], in1=sinb[r % 2][:M, :nt], op=ALU.mult))
                    W(DVE, t_t1)
                    W(DVE, t_t2)
                    W(DVE, st["ob_free"][k % 2])
                    t_out = s_dv.inc(DVE.tensor_tensor(out=ob[:M, :nt], in0=t1b[:M, :nt], in1=t2b[:M, :nt], op=ALU.add))
                    st["tab_free"][r % 2] = t_out
            W(SP, t_out)
            for (dst, r0, r1) in dsts:
                t_st = store(SP, dst, ob[r0:r1, :nt], sto[k % 2])
            st["ob_free"][k % 2] = t_st

        def ckv_job(hs, nt, dst_ckvt):
            if over():
                st["ckvn_tok"] = None
                return
            k = st["k"]; st["k"] += 1
            a = acc[k % 2]
            W(PE, st["acc_free"][k % 2])
            W(PE, st["hb_tok"])
            W(PE, st.get("acc2_free"))
            for g, aa in enumerate((a, acc2)):
                for c in range(8):
                    ins = PE.matmul(aa[:, :nt], lhsT=win[:, c, O_CKV + g * 128:O_CKV + (g + 1) * 128], rhs=hblk[hs][:, c, :nt], start=(c == 0), stop=(c == 7))
            t_main = s_pe.inc(ins)
            CUT = int(os.environ.get("CKV_CUT", "99"))
            st["ckvn_tok"] = None
            if CUT <= 1:
                return
            W(DVE, t_main)
            DVE.tensor_copy(out=qf[:, :nt], in_=a[:, :nt])
            t_qf = s_dv.inc(DVE.tensor_copy(out=qf2[:, :nt], in_=acc2[:, :nt]))
            W(ACT, t_qf)
            ACT.activation(out=sqf[:, :nt], in_=qf[:, :nt], func=AF.Square)
            t_sq = s_ac.inc(ACT.activation(out=sqf2[:, :nt], in_=qf2[:, :nt], func=AF.Square))
            st["acc_free"][k % 2] = [t_qf]
            st["acc2_free"] = [t_qf]
            if CUT <= 2:
                return
            W(PE, t_sq)
            PE.matmul(pss[:, :nt], lhsT=ones[:], rhs=sqf[:, :nt], start=True, stop=False)
            t_ss = s_pe.inc(PE.matmul(pss[:, :nt], lhsT=ones[:], rhs=sqf2[:, :nt], start=False, stop=True))
            if CUT <= 3:
                return
            W(ACT, t_ss)
            W(ACT, t_qf)
            t_sd = s_ac.inc(ACT.activation(out=sd[:, :nt], in_=pss[:, :nt], func=AF.Sqrt, bias=EPS, scale=1.0 / 256))
            W(DVE, t_sd)
            t_rs = s_dv.inc(DVE.reciprocal(out=rs[:, :nt], in_=sd[:, :nt]))
            if CUT <= 4:
                return
            W(DVE, t_rs)
            W(DVE, st.get("ckvn_free"))
            DVE.scalar_tensor_tensor(out=ckvn[:, 0, :nt], in0=qf[:, :nt], scalar=kvg[:, 0:1], in1=rs[:, :nt], op0=ALU.mult, op1=ALU.mult)
            t_out = s_dv.inc(DVE.scalar_tensor_tensor(out=ckvn[:, 1, :nt], in0=qf2[:, :nt], scalar=kvg[:, 1:2], in1=rs[:, :nt], op0=ALU.mult, op1=ALU.mult))
            if CUT <= 5:
                return
            W(SP, t_out)
            t_st = store(SP, dst_ckvt.rearrange("(r p) t -> p r t", p=128), ckvn[:, :, :nt], stc)
            st["ckvn_tok"] = t_out
            st["ckvn_st"] = t_st

        def tm_job(chunks, N, nh, dst):
            if over():
                return
            j = st["vk"]; st["vk"] += 1
            p = ptm[j % 2]
            W(PE, st["ptm_free"][j % 2])
            W(PE, st["hb_tok"])
            for ci, (lt, rh) in enumerate(chunks):
                ins = PE.matmul(p[:, :N], lhsT=lt, rhs=rh, start=(ci == 0), stop=(ci == len(chunks) - 1))
            t_main = s_pe.inc(ins)
            W(ACT, t_main)
            W(ACT, st["vo_free"][j % 2])
            t_o = s_ac.inc(ACT.activation(out=vout[j % 2][:, 0:nh, 0:64], in_=p[:, :N].rearrange("p (h d) -> p h d", d=64), func=AF.Copy))
            st["ptm_free"][j % 2] = t_o
            W(SP, t_o)
            st["vo_free"][j % 2] = store(SP, dst, vout[j % 2][:, 0:nh, :], stv[j % 2])

        nblk = [0]
        last_users = [None, None]

        def load_block(src_ap, nt):
            b = nblk[0]; nblk[0] += 1
            W(SP, last_users[b % 2])
            st["hb_tok"] = K.dma(SP, hblk[b % 2][:, :, :nt], src_ap.rearrange("(c p) t -> p c t", p=128), hl[b % 2])
            return b % 2

        def done_block(hs):
            last_users[hs] = (s_pe, s_pe.n)

        def hch(hs, c0, M, nt):
            return [(win[:, c, c0:c0 + M], hblk[hs][:, c, :nt]) for c in range(8)]

        hT_all, hT_na = dr["hT_all"], dr["hT_na"]
        for tb in range(17):
            ctxb = (tb == 16)
            t0 = tb * 512
            nt = 256 if ctxb else 512
            hs = load_block(hT_all[:, t0:t0 + nt], nt)
            rope = None if ctxb else (pt128, dr["cosA_all"][:, t0:t0 + nt], dr["sinA_all"][:, t0:t0 + nt])
            fm_job(hch(hs, O_GK, 128, nt), 128, nt, gk[:, 0:1], rope,
                   [(dr["GKT"][0, :, t0:t0 + nt], 0, 64), (dr["GKT"][1, :, t0:t0 + nt], 64, 128)])
            ckv_job(hs, nt, dr["CKVT"][:, t0:t0 + nt])
            rope = None if ctxb else (pt32, dr["cosM_all"][:, t0:t0 + nt], dr["sinM_all"][:, t0:t0 + nt])
            fm_job(hch(hs, O_KR, 32, nt), 32, nt, None, rope, [(dr["MKT"][h, 64:96, t0:t0 + nt], 0, 32) for h in range(8)])
            W(PE, st["ckvn_tok"])
            for g in range(4):
                fm_job([(wuk[:, r, g * 128:(g + 1) * 128], ckvn[:, r, :nt]) for r in range(2)], 128, nt, None, None,
                       [(dr["MKT"][2 * g, 0:64, t0:t0 + nt], 0, 64), (dr["MKT"][2 * g + 1, 0:64, t0:t0 + nt], 64, 128)])
            for ti in range(nt // 128):
                tsl = slice(ti * 128, (ti + 1) * 128)
                r0 = t0 + ti * 128
                tm_job([(ckvn[:, r, tsl], wuv[:, r, :]) for r in range(2)], 512, 8, dr["MV"][r0:r0 + 128, :, :])
                tm_job([(hblk[hs][:, c, tsl], win[:, c, O_GV:O_GV + 128]) for c in range(8)], 128, 2, dr["GV"][r0:r0 + 128, :, :])
            st["ckvn_free"] = (s_pe, s_pe.n)
            if ctxb:
                for g in range(4):
                    fm_job(hch(hs, O_NK + g * 128, 128, nt), 128, nt, None, None,
                           [(dr["NKT"][2 * g, :, NAT:NAT + nt], 0, 64), (dr["NKT"][2 * g + 1, :, NAT:NAT + nt], 64, 128)])
                for ti in range(nt // 128):
                    tsl = slice(ti * 128, (ti + 1) * 128)
                    tm_job([(hblk[hs][:, c, tsl], win[:, c, O_NV:O_NV + 512]) for c in range(8)], 512, 8, dr["NV"][NAT + ti * 128:NAT + (ti + 1) * 128, :, :])
                if with_ctx:
                    q0 = T
                    for g in range(4):
                        fm_job(hch(hs, O_GQ + g * 128, 128, nt), 128, nt, gq[:, 0:1], None,
                               [(dr["GQT"][2 * g, :, q0:q0 + nt], 0, 64), (dr["GQT"][2 * g + 1, :, q0:q0 + nt], 64, 128)])
                        fm_job(hch(hs, O_NQ + g * 128, 128, nt), 128, nt, None, None,
                               [(dr["NQT"][2 * g, :, q0:q0 + nt], 0, 64), (dr["NQT"][2 * g + 1, :, q0:q0 + nt], 64, 128)], oscale=0.125)
                    for h in range(8):
                        fm_job(hch(hs, O_MQ + h * 96, 96, nt), 96, nt, None, None, [(dr["MQT"][h, :, q0:q0 + nt], 0, 96)])
            done_block(hs)
        for tb in range(5):
            t0 = tb * 512
            nt = 512
            hs = load_block(hT_na[:, t0:t0 + nt], nt)
            for g in range(4):
                fm_job(hch(hs, O_NK + g * 128, 128, nt), 128, nt, None, None,
                       [(dr["NKT"][2 * g, :, t0:t0 + nt], 0, 64), (dr["NKT"][2 * g + 1, :, t0:t0 + nt], 64, 128)])
            for ti in range(4):
                tsl = slice(ti * 128, (ti + 1) * 128)
                tm_job([(hblk[hs][:, c, tsl], win[:, c, O_NV:O_NV + 512]) for c in range(8)], 512, 8, dr["NV"][t0 + ti * 128:t0 + (ti + 1) * 128, :, :])
            done_block(hs)
        for tb in range(4):
            q0 = tb * 512
            nt = 512
            hs = load_block(hT_na[:, 256 + q0:256 + q0 + nt], nt)
            for g in range(4):
                fm_job(hch(hs, O_GQ + g * 128, 128, nt), 128, nt, gq[:, 0:1], (pt128, dr["cosA_own"][:, q0:q0 + nt], dr["sinA_own"][:, q0:q0 + nt]),
                       [(dr["GQT"][2 * g, :, q0:q0 + nt], 0, 64), (dr["GQT"][2 * g + 1, :, q0:q0 + nt], 64, 128)])
                fm_job(hch(hs, O_NQ + g * 128, 128, nt), 128, nt, None, None,
                       [(dr["NQT"][2 * g, :, q0:q0 + nt], 0, 64), (dr["NQT"][2 * g + 1, :, q0:q0 + nt], 64, 128)], oscale=0.125)
            for h in range(8):
                fm_job(hch(hs, O_MQ + h * 96, 96, nt), 96, nt, None, (pt96, dr["cosM_own"][:, q0:q0 + nt], dr["sinM_own"][:, q0:q0 + nt]),
                       [(dr["MQT"][h, :, q0:q0 + nt], 0, 96)])
            done_block(hs)
        K.barrier([(s, s.n) for s in sto + stv + [stc]])


def phase_attn(K, heads, pfx, nkmax):
    nc = K.nc
    PE, ACT, DVE, POOL, SP = K.PE, K.ACT, K.DVE, K.POOL, K.SP
    NQ = T + C
    rls = K.dram["rls"]
    with ExitStack() as es:
        sb, ps = mk_alloc(nc, es, pfx)
        ktb = [sb(f"kt{i}", [128, nkmax], BF16) for i in range(2)]
        vb = [sb(f"v{i}", [128, nkmax // 128, 65], BF16) for i in range(2)]
        qb = [sb(f"q{i}", [128, NQ], BF16) for i in range(2)]
        pbuf = [sb(f"p{i}", [128, 1024], BF16) for i in range(3)]
        NBB, LB = 8, 6
        bb = [sb(f"bias{i}", [128, 512], BF16) for i in range(NBB)]
        identb = sb("identb", [128, 128], BF16)
        osb = [sb(f"osb{i}", [128, 512]) for i in range(2)]
        rl = [sb(f"rl{i}", [128, 512]) for i in range(2)]
        rbc = [sb(f"rbc{i}", [64, 512]) for i in range(2)]
        ysb = [[sb(f"y{i}_{w}", [64, 512], BF16) for w in range(2)] for i in range(2)]
        psb = [ps(f"s{i}", [128, 1024]) for i in range(3)]
        po = [ps(f"o{i}", [128, 512]) for i in range(2)]
        hl = [K.S("ld0"), K.S("ld1")]
        bl = [K.S(f"gb{i}") for i in range(NBB)]
        cl = K.S("ld5")
        rld = [K.S("ld3"), K.S("ld4")]
        rst = [K.S("st2"), K.S("st3")]
        s_pe, s_ac, s_dv = K.S("pe"), K.S("ac"), K.S("dv")
        sty = [K.S("st0"), K.S("st1")]
        t_c = K.dma(SP, identb[:], K.dram["ident_b"], cl)
        for i in range(2):
            DVE.memset(ktb[i][:], 0.0)
            DVE.memset(qb[i][:], 0.0)
            DVE.memset(rl[i][:], 1.0)
        t_m = s_dv.inc(DVE.memset(osb[0][:], 0.0))
        W(PE, t_c)
        W(PE, t_m)
        W(SP, t_m)
        steps = []
        for hi, h in enumerate(heads):
            for bi, b in enumerate(h["blocks"]):
                nt_ = len(b["tiles"])
                b["chunks"] = [(c0, min(512, b["nq"] - c0)) for c0 in range(0, b["nq"], 512)]
                for si, (kti, bias) in enumerate(b["tiles"]):
                    assert bias is None or len(b["chunks"]) == 1
                    steps.append(dict(hi=hi, b=b, kti=kti, bias=bias, first=(si == 0), last=(si == nt_ - 1),
                                      hfirst=(bi == 0 and si == 0), hlast=(bi == len(h["blocks"]) - 1 and si == nt_ - 1)))
        NS = len(steps)
        head_tok = [None] * len(heads)
        head_done = [None] * len(heads)

        def load_head(hi):
            h = heads[hi]
            s = hi % 2
            if hi >= 2:
                W(SP, head_done[hi - 2])
            dk, nk = h["dk"], h["nk"]
            K.dma(SP, ktb[s][:dk, :nk], h["kt"], hl[s])
            K.dma(SP, vb[s][:, :nk // 128, :], h["v"].rearrange("(t p) e -> p t e", p=128), hl[s])
            head_tok[hi] = K.dma(SP, qb[s][:dk, :], h["qt"], hl[s])

        tq = [None] * NS
        te = [None] * NS
        tv = [None] * NS
        bias_ld = [None] * NS
        nbias = [0]
        bidx = [None] * NS
        po_free = [None, None]
        y_free = [[None, None], [None, None]]
        pend_dv = {}
        state = {}

        def emit_qk(t):
            s = steps[t]
            h = heads[s["hi"]]
            hs = s["hi"] % 2
            if s["hfirst"]:
                W(PE, head_tok[s["hi"]])
            if t >= 3:
                W(PE, te[t - 3])
            b = s["b"]
            hasb = s["bias"] is not None
            for (c0, cn) in b["chunks"]:
                ins = PE.matmul(psb[t % 3][:, c0:c0 + cn], lhsT=ktb[hs][:, s["kti"] * 128:(s["kti"] + 1) * 128],
                                rhs=qb[hs][:, b["q0"] + c0:b["q0"] + c0 + cn], start=True, stop=not hasb)
            if hasb:
                W(PE, bias_ld[t])
                ins = PE.matmul(psb[t % 3][:, :b["nq"]], lhsT=identb[:], rhs=bb[bidx[t] % NBB][:, :b["nq"]], start=False, stop=True)
            tq[t] = s_pe.inc(ins)
            if hasb:
                state[("bfree", bidx[t] % NBB)] = tq[t]

        def emit_bias_load(t):
            s = steps[t]
            if s["bias"] is None:
                return
            n = nbias[0]; nbias[0] += 1
            bidx[t] = n
            W(POOL, state.get(("bfree", n % NBB)))
            bias_ld[t] = K.dma(POOL, bb[n % NBB][:, :s["b"]["nq"]], s["bias"], bl[n % NBB])

        if NS > 0:
            load_head(0)
        LA = 2
        for t in range(min(LB, NS)):
            emit_bias_load(t)
        for t in range(min(LA, NS)):
            emit_qk(t)
        cur_blk = -1
        for t in range(NS):
            s = steps[t]
            h = heads[s["hi"]]
            b = s["b"]
            nq = b["nq"]
            hs = s["hi"] % 2
            if s["hfirst"] and s["hi"] + 1 < len(heads):
                load_head(s["hi"] + 1)
            if s["first"]:
                cur_blk += 1
            if t + LB < NS:
                emit_bias_load(t + LB)
            if t + LA < NS:
                emit_qk(t + LA)
            W(ACT, tq[t])
            if t >= 3:
                W(ACT, tv[t - 3])
            te[t] = s_ac.inc(ACT.activation(out=pbuf[t % 3][:, :nq], in_=psb[t % 3][:, :nq], func=AF.Exp, scale=float(h["scale"])))
            W(PE, te[t])
            for w, (c0, cn) in enumerate(b["chunks"]):
                if s["first"]:
                    W(PE, po_free[w])
                ins = PE.matmul(po[w][:65, :cn], lhsT=vb[hs][:, s["kti"], :], rhs=pbuf[t % 3][:, c0:c0 + cn], start=s["first"], stop=s["last"])
            tv[t] = s_pe.inc(ins)
            if s["hlast"]:
                head_done[s["hi"]] = tv[t]
            for f in pend_dv.pop(t, []):
                f()
            if s["last"]:
                cb = cur_blk
                parts = []
                for w, (c0, cn) in enumerate(b["chunks"]):
                    W(DVE, tv[t])
                    t_o = s_dv.inc(DVE.tensor_copy(out=osb[w][:65, :cn], in_=po[w][:65, :cn]))
                    po_free[w] = t_o
                    W(DVE, t_o)
                    t_rl = s_dv.inc(DVE.reciprocal(out=rl[w][64:65, :cn], in_=osb[w][64:65, :cn]))
                    slot = (cb % 2) * 2 + w
                    W(SP, t_rl)
                    t_s = K.dma(SP, rls[slot:slot + 1, :cn], rl[w][64:65, :cn], rst[w])
                    W(SP, t_s)
                    t_b = K.dma(SP, rbc[w][:, :cn], rls[slot, :cn].partition_broadcast(64), rld[w])
                    parts.append((w, cn, t_b))

                def dv_part(parts=parts, cb=cb, yts=b["yts"]):
                    for (w, cn, t_b) in parts:
                        W(DVE, t_b)
                        W(DVE, y_free[cb % 2][w])
                        t_y = s_dv.inc(DVE.tensor_tensor(out=ysb[cb % 2][w][:, :cn], in0=osb[w][:64, :cn], in1=rbc[w][:, :cn], op=ALU.mult))
                        W(SP, t_y)
                        y_free[cb % 2][w] = K.dma(SP, yts[w], ysb[cb % 2][w][:, :cn], sty[w])

                if t + 1 < NS:
                    nxt_len = len(steps[t + 1]["b"]["tiles"])
                    d = max(1, min(2, nxt_len - 1))
                    pend_dv.setdefault(t + d, []).append(dv_part)
                else:
                    dv_part()
        assert not pend_dv
        K.barrier([(s_, s_.n) for s_ in sty])


def phase_merge(K, L, pfx, qblocks, x_src, x_dst):
    nc = K.nc
    PE, ACT, DVE, POOL, SP = K.PE, K.ACT, K.DVE, K.POOL, K.SP
    dr = K.dram
    with ExitStack() as es:
        sb, ps = mk_alloc(nc, es, pfx)
        wg = sb("wg", [128, 8, 3072], BF16)
        wo = [sb(f"wo{i}", [128, 4, 1024], BF16) for i in range(3)]
        wout = sb("wout", [128, 8, 1024], BF16)
        g1 = sb("g1", [128, 2, 1024])
        hblk = [sb(f"h{i}", [128, 8, 512], BF16) for i in range(2)]
        yb = [[sb(f"y{r}_{i}", [128, 4, 512], BF16) for r in range(3)] for i in range(2)]
        sg = [sb(f"sg{i}", [128, 512]) for i in range(2)]
        yacc = sb("yacc", [128, 512])
        tmp = sb("tmp", [128, 512])
        yT = sb("yT", [128, 8, 512], BF16)
        xt = [sb(f"xt{i}", [128, 1024]) for i in range(2)]
        xo = [sb(f"xo{i}", [128, 1024]) for i in range(2)]
        tm2 = [sb(f"tm{i}", [128, 512]) for i in range(2)]
        pg = [ps(f"pg{i}", [128, 512]) for i in range(2)]
        pbr = [ps(f"pb{i}", [128, 512]) for i in range(2)]
        pw = [ps(f"pw{i}", [128, 512]) for i in range(2)]
        wl = K.S("ld2")
        hl = [K.S("ld0"), K.S("ld1")]
        xl = [K.S("ld3"), K.S("ld4")]
        s_pe, s_ac, s_dv, s_pl = K.S("pe"), K.S("ac"), K.S("dv"), K.S("pl")
        stx = [K.S("st0"), K.S("st1")]
        gw = K.S("gw")
        for c in range(8):
            K.dma(POOL, wg[:, c, :], L["w_in"][c * 128:(c + 1) * 128, O_GATE:WIN], gw)
        for r, nm in enumerate(("w_o_gqa", "w_o_na", "w_o_mla")):
            K.dma(POOL, wo[r][:], L[nm].rearrange("(c p) n -> p c n", p=128), gw)
        t_gw = K.dma(POOL, wout[:], L["w_out"].rearrange("(c p) n -> p c n", p=128), gw)
        K.dma(SP, g1[:, 0, :], dr["modv"][0, 2].partition_broadcast(128), wl)
        t_w = K.dma(SP, g1[:, 1, :], dr["modv"][1, 2].partition_broadcast(128), wl)
        for e in (PE, DVE, POOL):
            W(e, t_w)
            W(e, t_gw)
        ysrc = (dr["YAT"], dr["YBT"], dr["YCT"])
        blk_done = [None, None]
        k = 0
        xk = 0
        sg_free = [None, None]
        pg_free = [None, None]
        pbr_free = [None, None]
        pw_free = [None, None]
        xt_free = [None, None]
        xo_free = [None, None]
        tm_free = [None, None]
        yT_free = None
        for bi, (hT_ap, q0, nt, m) in enumerate(qblocks):
            s = bi % 2
            W(SP, blk_done[s])
            K.dma(SP, hblk[s][:, :, :nt], hT_ap.rearrange("(c p) t -> p c t", p=128), hl[s])
            for r in range(3):
                t_l = K.dma(SP, yb[s][r][:, :, :nt], ysrc[r][:, q0:q0 + nt].rearrange("(c p) t -> p c t", p=128), hl[s])
            W(PE, t_l)
            for oc in range(8):
                for r in range(3):
                    W(PE, pg_free[k % 2])
                    for c in range(8):
                        ins = PE.matmul(pg[k % 2][:, :nt], lhsT=wg[:, c, r * 1024 + oc * 128:r * 1024 + (oc + 1) * 128], rhs=hblk[s][:, c, :nt],
                                        start=(c == 0), stop=(c == 7))
                    t_g = s_pe.inc(ins)
                    W(PE, pbr_free[k % 2])
                    for c in range(4):
                        ins = PE.matmul(pbr[k % 2][:, :nt], lhsT=wo[r][:, c, oc * 128:(oc + 1) * 128], rhs=yb[s][r][:, c, :nt], start=(c == 0), stop=(c == 3))
                    t_b = s_pe.inc(ins)
                    W(ACT, t_g)
                    W(ACT, sg_free[k % 2])
                    t_s = s_ac.inc(ACT.activation(out=sg[k % 2][:, :nt], in_=pg[k % 2][:, :nt], func=AF.Sigmoid))
                    pg_free[k % 2] = t_s
                    W(DVE, t_s)
                    W(DVE, t_b)
                    if r == 0:
                        t_d = s_dv.inc(DVE.tensor_tensor(out=yacc[:, :nt], in0=sg[k % 2][:, :nt], in1=pbr[k % 2][:, :nt], op=ALU.mult))
                    else:
                        t_d = s_dv.inc(DVE.tensor_tensor(out=tmp[:, :nt], in0=sg[k % 2][:, :nt], in1=pbr[k % 2][:, :nt], op=ALU.mult))
                        W(DVE, t_d)
                        if r == 1:
                            t_d = s_dv.inc(DVE.tensor_tensor(out=yacc[:, :nt], in0=yacc[:, :nt], in1=tmp[:, :nt], op=ALU.add))
                        else:
                            if oc == 0:
                                W(DVE, yT_free)
                            t_d = s_dv.inc(DVE.tensor_tensor(out=yT[:, oc, :nt], in0=yacc[:, :nt], in1=tmp[:, :nt], op=ALU.add))
                    sg_free[k % 2] = t_d
                    pbr_free[k % 2] = t_d
                    k += 1
            blk_done[s] = (s_pe, s_pe.n)
            t_y = t_d
            W(PE, t_y)
            for ti in range(nt // 128):
                xs_ = xk % 2
                W(SP, xt_free[xs_])
                t_x = K.dma(SP, xt[xs_][:], x_src(q0 + ti * 128), xl[xs_])
                for half in range(2):
                    j = 2 * xk + half
                    W(PE, pw_free[j % 2])
                    for c in range(8):
                        ins = PE.matmul(pw[j % 2][:, :], lhsT=yT[:, c, ti * 128:(ti + 1) * 128], rhs=wout[:, c, half * 512:(half + 1) * 512],
                                        start=(c == 0), stop=(c == 7))
                    t_p = s_pe.inc(ins)
                    W(DVE, t_p)
                    W(DVE, tm_free[j % 2])
                    t_m = s_dv.inc(DVE.tensor_tensor(out=tm2[j % 2][:], in0=pw[j % 2][:], in1=g1[:, m, half * 512:(half + 1) * 512], op=ALU.mult))
                    pw_free[j % 2] = t_m
                    W(POOL, t_m)
                    W(POOL, t_x)
                    if half == 0:
                        W(POOL, xo_free[xs_])
                    t_a = s_pl.inc(POOL.tensor_tensor(out=xo[xs_][:, half * 512:(half + 1) * 512], in0=tm2[j % 2][:], in1=xt[xs_][:, half * 512:(half + 1) * 512], op=ALU.add))
                    tm_free[j % 2] = t_a
                xt_free[xs_] = t_a
                W(SP, t_a)
                xo_free[xs_] = K.dma(SP, x_dst(q0 + ti * 128), xo[xs_][:], stx[xs_])
                xk += 1
            yT_free = (s_pe, s_pe.n)
        K.barrier([(s_, s_.n) for s_ in stx])


def phase_mlp(K, L, pfx, qblocks, x_src, x_dst, final_g, after_block=None):
    nc = K.nc
    PE, ACT, DVE, POOL, SP = K.PE, K.ACT, K.DVE, K.POOL, K.SP
    dr = K.dram
    with ExitStack() as es:
        sb, ps = mk_alloc(nc, es, pfx)
        w1 = sb("w1", [128, 8, 4096], BF16)
        w2 = sb("w2", [128, 32, 1024], BF16)
        g2 = sb("g2", [128, 2, 1024])
        fg = sb("fg", [128, 1024])
        hblk = [sb(f"h{i}", [128, 8, 256], BF16) for i in range(2)]
        uT = sb("uT", [128, 32, 256], BF16)
        rb = [sb(f"r{i}", [128, 256]) for i in range(2)]
        xt = [sb(f"xt{i}", [128, 1024]) for i in range(2)]
        xo = [sb(f"xo{i}", [128, 1024]) for i in range(2)]
        tm2 = [sb(f"tm{i}", [128, 512]) for i in range(2)]
        junk = sb("junk", [128, 1024])
        st4 = sb("st4", [128, 4])
        pu = [ps(f"pu{i}", [128, 512]) for i in range(2)]
        pw = [ps(f"pw{i}", [128, 512]) for i in range(2)]
        wl = K.S("ld2")
        hl = [K.S("ld0"), K.S("ld1")]
        xl = [K.S("ld3"), K.S("ld4")]
        s_pe, s_ac, s_dv, s_pl = K.S("pe"), K.S("ac"), K.S("dv"), K.S("pl")
        stx = [K.S("st0"), K.S("st1")]
        gw = K.S("gw")
        for c in range(8):
            K.dma(POOL, w1[:, c, :], L["w_mlp1"][c * 128:(c + 1) * 128, :], gw)
        for c4 in range(4):
            t_gw = K.dma(POOL, w2[:, c4 * 8:(c4 + 1) * 8, :], L["w_mlp2"][c4 * 1024:(c4 + 1) * 1024, :].rearrange("(c p) n -> p c n", p=128), gw)
        K.dma(SP, g2[:, 0, :], dr["modv"][0, 5].partition_broadcast(128), wl)
        if final_g is not None:
            K.dma(SP, fg[:], final_g.partition_broadcast(128), wl)
        t_w = K.dma(SP, g2[:, 1, :], dr["modv"][1, 5].partition_broadcast(128), wl)
        for e in (PE, DVE, POOL, ACT):
            W(e, t_w)
            W(e, t_gw)
        h2T = dr["h2T"]
        blk_done = [None, None]
        k = 0
        xk = 0
        pu_free = [None, None]
        rb_free = [None, None]
        pw_free = [None, None]
        xt_free = [None, None]
        xo_free = [None, None]
        tm_free = [None, None]
        uT_free = None
        for bi, (q0, nt, m) in enumerate(qblocks):
            s = bi % 2
            W(SP, blk_done[s])
            t_l = K.dma(SP, hblk[s][:, :, :nt], h2T[:, q0:q0 + nt].rearrange("(c p) t -> p c t", p=128), hl[s])
            W(PE, t_l)
            for fc in range(32):
                W(PE, pu_free[k % 2])
                for c in range(8):
                    ins = PE.matmul(pu[k % 2][:, :nt], lhsT=w1[:, c, fc * 128:(fc + 1) * 128], rhs=hblk[s][:, c, :nt], start=(c == 0), stop=(c == 7))
                t_u = s_pe.inc(ins)
                W(ACT, t_u)
                W(ACT, rb_free[k % 2])
                t_r = s_ac.inc(ACT.activation(out=rb[k % 2][:, :nt], in_=pu[k % 2][:, :nt], func=AF.Relu))
                pu_free[k % 2] = t_r
                W(DVE, t_r)
                if fc == 0:
                    W(DVE, uT_free)
                t_q = s_dv.inc(DVE.tensor_tensor(out=uT[:, fc, :nt], in0=rb[k % 2][:, :nt], in1=rb[k % 2][:, :nt], op=ALU.mult))
                rb_free[k % 2] = t_q
                k += 1
            blk_done[s] = (s_pe, s_pe.n)
            W(PE, t_q)
            for ti in range(nt // 128):
                xs_ = xk % 2
                W(SP, xt_free[xs_])
                t_x = K.dma(SP, xt[xs_][:], x_src(q0 + ti * 128), xl[xs_])
                for half in range(2):
                    j = 2 * xk + half
                    W(PE, pw_free[j % 2])
                    for fc in range(32):
                        ins = PE.matmul(pw[j % 2][:, :], lhsT=uT[:, fc, ti * 128:(ti + 1) * 128], rhs=w2[:, fc, half * 512:(half + 1) * 512],
                                        start=(fc == 0), stop=(fc == 31))
                    t_p = s_pe.inc(ins)
                    W(DVE, t_p)
                    W(DVE, tm_free[j % 2])
                    t_m = s_dv.inc(DVE.tensor_tensor(out=tm2[j % 2][:], in0=pw[j % 2][:], in1=g2[:, m, half * 512:(half + 1) * 512], op=ALU.mult))
                    pw_free[j % 2] = t_m
                    W(POOL, t_m)
                    W(POOL, t_x)
                    if half == 0:
                        W(POOL, xo_free[xs_])
                    t_a = s_pl.inc(POOL.tensor_tensor(out=xo[xs_][:, half * 512:(half + 1) * 512], in0=tm2[j % 2][:], in1=xt[xs_][:, half * 512:(half + 1) * 512], op=ALU.add))
                    tm_free[j % 2] = t_a
                xt_free[xs_] = t_a
                t_fin = t_a
                if final_g is not None:
                    W(ACT, t_a)
                    t1 = s_ac.inc(ACT.activation(out=junk[:], in_=xo[xs_][:], func=AF.Square, accum_out=st4[:, 0:1]))
                    W(DVE, t1)
                    t2 = s_dv.inc(DVE.tensor_scalar(out=st4[:, 1:2], in0=st4[:, 0:1], scalar1=1.0 / D, scalar2=EPS, op0=ALU.mult, op1=ALU.add))
                    W(ACT, t2)
                    t3 = s_ac.inc(ACT.activation(out=st4[:, 2:3], in_=st4[:, 1:2], func=AF.Sqrt))
                    W(DVE, t3)
                    t4 = s_dv.inc(DVE.reciprocal(out=st4[:, 3:4], in_=st4[:, 2:3]))
                    W(DVE, t4)
                    t_fin = s_dv.inc(DVE.scalar_tensor_tensor(out=xo[xs_][:], in0=xo[xs_][:], scalar=st4[:, 3:4], in1=fg[:], op0=ALU.mult, op1=ALU.mult))
                W(SP, t_fin)
                xo_free[xs_] = K.dma(SP, x_dst(q0 + ti * 128), xo[xs_][:], stx[xs_])
                xk += 1
            uT_free = (s_pe, s_pe.n)
            if after_block is not None:
                after_block(bi, [xo_free[0], xo_free[1]])
        K.barrier([(s_, s_.n) for s_ in stx])


def phase_halo(K, pfx):
    nc = K.nc
    PE, ACT, DVE, POOL, SP = K.PE, K.ACT, K.DVE, K.POOL, K.SP
    dr = K.dram
    xg, x1, xna, sel = K.xg_at, K.x1_at, dr["x_na2"], dr["sel"]
    with ExitStack() as es:
        sb, ps = mk_alloc(nc, es, pfx)
        selb = sb("sel", [128, 8])
        cand = [sb(f"c{i}", [128, 1024]) for i in range(4)]
        acc = [sb(f"a{i}", [128, 1024]) for i in range(2)]
        ld = [K.S("ld0"), K.S("ld1"), K.S("ld3"), K.S("ld4")]
        lc = K.S("ld2")
        s_dv = K.S("dv")
        st = [K.S("st0"), K.S("st1")]
        so = K.S("st2")
        t_c = K.dma(SP, selb[:], sel.partition_broadcast(128), lc)
        for q in range(0, T, 128):
            t_own = K.dma(SP, xna[256 + q:256 + q + 128, :], x1(q), so)
        W(DVE, t_c)
        jobs = []
        for u in range(2):
            jobs.append((128 * u, [xg(2048 * r + 1792 + 128 * u) for r in range(4)], 0))
        for u in range(2):
            jobs.append((256 + T + 128 * u, [xg(2048 * r + 128 * u) for r in range(4)], 4))
        dv_prev = None
        st_t = [None, None]
        for n, (row0, srcs, c0) in enumerate(jobs):
            W(SP, dv_prev)
            lts = [K.dma(SP, cand[r][:], srcs[r], ld[r]) for r in range(4)]
            W(DVE, lts)
            W(DVE, st_t[n % 2])
            t = s_dv.inc(DVE.tensor_scalar(out=acc[n % 2][:], in0=cand[0][:], scalar1=selb[:, c0:c0 + 1], scalar2=0.0, op0=ALU.mult, op1=ALU.add))
            for r in range(1, 4):
                W(DVE, t)
                t = s_dv.inc(DVE.scalar_tensor_tensor(out=acc[n % 2][:], in0=cand[r][:], scalar=selb[:, c0 + r:c0 + r + 1], in1=acc[n % 2][:],
                                                      op0=ALU.mult, op1=ALU.add))
            dv_prev = t
            W(SP, t)
            st_t[n % 2] = K.dma(SP, xna[row0:row0 + 128, :], acc[n % 2][:], st[n % 2])
        K.barrier([t_own, st_t[0], st_t[1]])


W_NAMES = ["w_mod", "b_mod", "norm1_g", "norm2_g", "w_in", "gqa_q_norm", "gqa_k_norm", "mla_kv_norm", "mla_w_uk", "mla_w_uv",
           "w_o_gqa", "w_o_na", "w_o_mla", "w_out", "w_mlp1", "w_mlp2"]
W_SHAPES = {"w_mod": [D, 6 * D], "b_mod": [6 * D], "norm1_g": [D], "norm2_g": [D], "w_in": [D, WIN], "gqa_q_norm": [64], "gqa_k_norm": [64],
            "mla_kv_norm": [256], "mla_w_uk": [256, 512], "mla_w_uv": [256, 512], "w_o_gqa": [512, D], "w_o_na": [512, D], "w_o_mla": [512, D],
            "w_out": [D, D], "w_mlp1": [D, 4 * D], "w_mlp2": [4 * D, D]}
DEPTH = 2


def emit_layer(K, L, src, with_ctx, final, sfx):
    dr = K.dram
    NQ = T + C
    phase_mod(K, L, "md" + sfx)
    phase_norm(K, [(src["x_all"], S, dr["hT_all"][:, 0:S], 0, 0, 1), (src["ctx_in"], C, dr["hT_all"][:, S:SK], 1, 0, 1),
                   (src["x_na"], NAT, dr["hT_na"], 0, 0, 1)], "n1" + sfx)
    phase_proj(K, L, "pj" + sfx, with_ctx)
    qbl = [(512 * i, 512) for i in range(4)]
    heads = []

    def wide_blocks(Y, h):
        blocks = [dict(q0=q0, nq=1024, tiles=[(k, None) for k in range(66)],
                       yts=[Y[64 * h:64 * h + 64, q0:q0 + 512], Y[64 * h:64 * h + 64, q0 + 512:q0 + 1024]]) for q0 in (0, 1024)]
        if with_ctx:
            blocks.append(dict(q0=T, nq=C, tiles=[(64, None), (65, None)], yts=[Y[64 * h:64 * h + 64, T:NQ]]))
        return blocks

    for h in range(8):
        heads.append(dict(kt=dr["GKT"][h // 4], v=dr["GV"][:, h // 4, :], qt=dr["GQT"][h], dk=64, scale=0.125, nk=SK, blocks=wide_blocks(dr["YAT"], h)))
    for h in range(8):
        heads.append(dict(kt=dr["MKT"][h], v=dr["MV"][:, h, :], qt=dr["MQT"][h], dk=96, scale=96 ** -0.5, nk=SK, blocks=wide_blocks(dr["YCT"], h)))
    phase_attn(K, heads, "at" + sfx, SK)
    heads = []
    var = [0, 1, 1, 2]
    for h in range(8):
        blocks = []
        for i, (q0, nq) in enumerate(qbl):
            tiles = [(4 * i + m, src["nabias"][var[i], h, m]) for m in range(8)] + [(20, None), (21, None)]
            blocks.append(dict(q0=q0, nq=nq, tiles=tiles, yts=[dr["YBT"][64 * h:64 * h + 64, q0:q0 + nq]]))
        if with_ctx:
            blocks.append(dict(q0=T, nq=C, tiles=[(20, None), (21, None)], yts=[dr["YBT"][64 * h:64 * h + 64, T:NQ]]))
        heads.append(dict(kt=dr["NKT"][h], v=dr["NV"][:, h, :], qt=dr["NQT"][h], dk=64, scale=1.0, nk=NAK, blocks=blocks))
    phase_attn(K, heads, "na" + sfx, NAK)
    mblocks = [(dr["hT_na"][:, 256 + 512 * i:256 + 512 * (i + 1)], 512 * i, 512, 0) for i in range(4)]
    if with_ctx:
        mblocks.append((dr["hT_all"][:, S:SK], T, C, 1))

    def x_src(q):
        if q >= T:
            return src["ctx_in"][q - T:q - T + 128, :]
        return src["x_own"](q) if callable(src["x_own"]) else src["x_own"][q:q + 128, :]

    def xs1_at(q):
        return dr["xs1"][q:q + 128, :]

    phase_merge(K, L, "mg" + sfx, mblocks, x_src, xs1_at)
    njobs = [(dr["xs1"][0:T, :], T, dr["h2T"][:, 0:T], 0, 3, 4)]
    if with_ctx:
        njobs.append((dr["xs1"][T:NQ, :], C, dr["h2T"][:, T:NQ], 1, 3, 4))
    phase_norm(K, njobs, "n2" + sfx)
    fblocks = [(256 * i, 256, 0) for i in range(8)]
    if with_ctx:
        fblocks.append((T, C, 1))
    phase_mlp(K, L, "ml" + sfx, fblocks, xs1_at, src["x_dst"], L.get("final_norm_g") if final else None, after_block=src.get("after_block"))


def build_fused():
    nc = bass.Bass("TRN2", target_bir_lowering=False)
    K = KB(nc)
    NQ = T + C
    dr = K.dram

    def inp(name, shape, dt=F32):
        dr[name] = nc.dram_tensor(name, shape, dt, kind="ExternalInput").ap()

    def internal(name, shape, dt=BF16):
        dr[name] = nc.dram_tensor(name, shape, dt).ap()

    inp("x_all", [S, D]); inp("x_own", [T, D]); inp("x_na", [NAT, D]); inp("ctx_in", [C, D]); inp("cvec", [2, D])
    Wst = {}
    for n in W_NAMES:
        Wst[n] = nc.dram_tensor(n, [DEPTH] + W_SHAPES[n], F32, kind="ExternalInput").ap()
    fng = nc.dram_tensor("final_norm_g", [D], F32, kind="ExternalInput").ap()
    for n in ("ident_f", "onesbd_f", "ones_f"):
        inp(n, [128, 128])
    for n in ("pt128", "pt96", "pt32", "ident_b"):
        inp(n, [128, 128], BF16)
    inp("cosA_all", [128, S]); inp("sinA_all", [128, S]); inp("cosA_own", [128, T]); inp("sinA_own", [128, T])
    inp("cosM_all", [32, S]); inp("sinM_all", [32, S]); inp("cosM_own", [96, T]); inp("sinM_own", [96, T])
    inp("nabias", [DEPTH, 3, 8, 8, 128, 512])
    inp("sel", [8])
    dr["xout"] = nc.dram_tensor("xout", [T, D], F32, kind="ExternalOutput").ap()
    internal("modv", [2, 6, D], F32)
    internal("hT_all", [D, SK]); internal("hT_na", [D, NAT])
    internal("GKT", [2, 64, SK]); internal("CKVT", [256, SK]); internal("MKT", [8, 96, SK])
    internal("MV", [SK, 8, 65]); internal("GV", [SK, 2, 65])
    internal("NKT", [8, 64, NAK]); internal("NV", [NAK, 8, 65])
    internal("GQT", [8, 64, NQ]); internal("NQT", [8, 64, NQ]); internal("MQT", [8, 96, NQ])
    internal("YAT", [512, NQ]); internal("YBT", [512, NQ]); internal("YCT", [512, NQ])
    internal("xs1", [NQ, D], F32); internal("h2T", [D, NQ])
    NCH = 8
    x1c = [nc.dram_tensor(f"x1c{k}", [256, D], F32) for k in range(NCH)]
    xgc = [nc.dram_tensor(f"xgc{k}", [4 * 256, D], F32) for k in range(NCH)]
    internal("c1loc", [C, D], F32); internal("x_na2", [NAT, D], F32)
    internal("rls", [4, 512], F32)

    def x1_at(q):
        return x1c[q // 256].ap()[q % 256:q % 256 + 128, :]

    def xg_at(t0):
        r, k, off = t0 // 2048, (t0 % 2048) // 256, t0 % 256
        return xgc[k].ap()[r * 256 + off:r * 256 + off + 128, :]

    K.x1_at, K.xg_at = x1_at, xg_at

    for l in range(DEPTH):
        L = {n: Wst[n][l] for n in W_NAMES}
        final = (l == DEPTH - 1)
        with_ctx = not final
        if final:
            L["final_norm_g"] = fng
        if l == 0:
            src = dict(x_all=dr["x_all"], x_own=dr["x_own"], x_na=dr["x_na"], ctx_in=dr["ctx_in"])
        else:
            src = dict(x_all=(lambda i: xg_at(128 * i)), x_own=x1_at, x_na=dr["x_na2"], ctx_in=dr["c1loc"])
        src["nabias"] = dr["nabias"][l]
        if final:
            src["x_dst"] = lambda q: dr["xout"][q:q + 128, :]
        else:
            src["x_dst"] = lambda q: (x1_at(q) if q < T else dr["c1loc"][q - T:q - T + 128, :])
        cc = K.S("cc")
        cc_tok = [None]
        if not final:
            def after_block(bi, store_toks):
                if bi >= NCH:
                    return
                W(K.POOL, store_toks)
                ins = K.POOL.collective_compute("AllGather", mybir.AluOpType.bypass, replica_groups=[[0, 1, 2, 3], [4, 5, 6, 7]],
                                                ins=[x1c[bi].ap().opt()], outs=[xgc[bi].ap().opt()])
                cc_tok[0] = cc.inc(ins)

            src["after_block"] = after_block
        emit_layer(K, L, src, with_ctx, final, f"{l}_")
        if not final:
            K.barrier([cc_tok[0]])
            phase_halo(K, f"hl{l}_")
    K.semcounts = {n: s.n for n, s in K.sems.items()}
    return nc, K


def _rope_tables():
    t = np.arange(S, dtype=np.int32)
    row = (t // 64).astype(np.float32)
    col = (t % 64).astype(np.float32)

    def tabs(rot_dim):
        half = rot_dim // 2
        inv = (10000.0 ** (-np.arange(0, half, 2, dtype=np.float32) / np.float32(half))).astype(np.float32)
        ar = (row[:, None] * inv).astype(np.float32)
        ac = (col[:, None] * inv).astype(np.float32)
        cos = np.concatenate([np.cos(ar), np.cos(ar), np.cos(ac), np.cos(ac)], axis=1).T.astype(np.float32)
        sin = np.concatenate([np.sin(ar), np.sin(ar), np.sin(ac), np.sin(ac)], axis=1).T.astype(np.float32)
        return np.ascontiguousarray(cos), np.ascontiguousarray(sin)

    return tabs(64), tabs(32)


def _rot_matrix(n):
    q = n // 4
    P = np.zeros((n, n), np.float32)
    for base in (0, 2 * q):
        for i in range(q):
            P[base + i, base + q + i] = -1.0
            P[base + q + i, base + i] = 1.0
    return P


def _consts():
    c = {}
    c["ident_f"] = np.eye(128, dtype=np.float32)
    c["ones_f"] = np.ones((128, 128), np.float32)
    bd = np.zeros((128, 128), np.float32)
    bd[:64, :64] = 1.0
    bd[64:, 64:] = 1.0
    c["onesbd_f"] = bd
    P64 = _rot_matrix(64)
    P32 = _rot_matrix(32)
    pt128 = np.zeros((128, 128), np.float32)
    pt128[:64, :64] = P64.T
    pt128[64:, 64:] = P64.T
    pt96 = np.zeros((128, 128), np.float32)
    pt96[64:96, 64:96] = P32.T
    pt32 = np.zeros((128, 128), np.float32)
    pt32[:32, :32] = P32.T
    c["ident_b"] = np.eye(128, dtype=np.float32).astype(ml_dtypes.bfloat16)
    c["pt128"] = pt128.astype(ml_dtypes.bfloat16)
    c["pt96"] = pt96.astype(ml_dtypes.bfloat16)
    c["pt32"] = pt32.astype(ml_dtypes.bfloat16)
    return c


def _na_bias_tables(rpb, j):
    out = np.empty((3, 8, 8, 128, 512), np.float32)
    kcol = np.arange(64)[None, :, None, None]
    qcol = np.arange(64)[None, None, None, :]
    m = np.arange(16)[:, None, None, None]
    a = np.arange(8)[None, None, :, None]
    cs = np.clip(qcol - 8, 0, 48)
    colok = (kcol >= cs) & (kcol < cs + 16)
    cidx = np.clip(kcol - qcol + 15, 0, 30)
    for v, i in enumerate((0, 1, 3)):
        r = 32 * j + 8 * i + a
        k = 32 * j + 8 * i - 4 + m
        rs = np.clip(r - 4, 0, 120)
        ok = (k >= 0) & (k < 128) & (k >= rs) & (k < rs + 8) & colok
        ridx = np.clip(k - r + 7, 0, 14)
        ridx_b = np.broadcast_to(ridx, ok.shape)
        cidx_b = np.broadcast_to(cidx, ok.shape)
        vals = rpb[:, ridx_b, cidx_b]
        tab = np.where(ok[None], vals, np.float32(NEG)).astype(np.float32)
        out[v] = tab.reshape(8, 8, 128, 512)
    return out


def _core_inputs(x_full, ctx_full, c, c_ctx, Wd, consts, ropes):
    (cosA, sinA), (cosM, sinM) = ropes
    maps = []
    cosA2 = np.ascontiguousarray(np.concatenate([cosA, cosA], 0))
    sinA2 = np.ascontiguousarray(np.concatenate([sinA, sinA], 0))
    shared = dict(consts)
    for n in W_NAMES:
        shared[n] = np.ascontiguousarray(Wd[n])
    shared["final_norm_g"] = np.ascontiguousarray(Wd["final_norm_g"])
    shared["cosA_all"] = cosA2
    shared["sinA_all"] = sinA2
    shared["cosM_all"] = cosM
    shared["sinM_all"] = sinM
    rpb = np.asarray(Wd["na_rpb"], np.float32)
    nab = [np.stack([_na_bias_tables(rpb[l], j) for l in range(rpb.shape[0])], 0) for j in range(4)]
    for core in range(8):
        b, j = core // 4, core % 4
        t0 = T * j
        d = dict(shared)
        d["x_all"] = np.ascontiguousarray(x_full[b])
        d["x_own"] = np.ascontiguousarray(x_full[b, t0:t0 + T])
        xna = np.zeros((NAT, D), np.float32)
        lo = (32 * j - 4) * 64
        hi = lo + NAT
        slo, shi = max(lo, 0), min(hi, S)
        xna[slo - lo:shi - lo] = x_full[b, slo:shi]
        d["x_na"] = xna
        d["ctx_in"] = np.ascontiguousarray(ctx_full[b])
        d["cvec"] = np.ascontiguousarray(np.stack([c[b], c_ctx]))
        d["cosA_own"] = np.ascontiguousarray(cosA2[:, t0:t0 + T])
        d["sinA_own"] = np.ascontiguousarray(sinA2[:, t0:t0 + T])
        d["cosM_own"] = np.ascontiguousarray(np.concatenate([np.ones((64, T), np.float32), cosM[:, t0:t0 + T]], 0))
        d["sinM_own"] = np.ascontiguousarray(np.concatenate([np.zeros((64, T), np.float32), sinM[:, t0:t0 + T]], 0))
        d["nabias"] = nab[j]
        sel = np.zeros(8, np.float32)
        if j > 0:
            sel[j - 1] = 1.0
        if j < 3:
            sel[4 + j + 1] = 1.0
        d["sel"] = sel
        maps.append(d)
    return maps


_PROG = []


def kernel(**inputs):
    Wd = {k: np.asarray(v, np.float32) for k, v in inputs.items()}
    if not _PROG:
        _PROG.append(build_fused()[0])
    nc = _PROG[0]
    maps = _core_inputs(Wd["x"], Wd["ctx"], Wd["c"], Wd["c_ctx"], Wd, _consts(), _rope_tables())
    res = run_bass_kernel_spmd(nc, maps, core_ids=list(range(8)))
    outs = [r["xout"] for r in res.results]
    x = np.stack([np.concatenate([outs[4 * b + j] for j in range(4)], 0) for b in range(2)], 0)
    return np.ascontiguousarray(x.astype(np.float32))
```

```python
from contextlib import ExitStack
import os
import numpy as np
import ml_dtypes
import concourse.bass as bass
import concourse.mybir as mybir
from concourse.bass_utils import run_bass_kernel_spmd

F32 = mybir.dt.float32
BF16 = mybir.dt.bfloat16
AF = mybir.ActivationFunctionType
ALU = mybir.AluOpType

D = 1024
S = 8192
C = 256
SK = S + C
T = 2048
NAT = 2560
NAK = NAT + C
EPS = 1e-6
NEG = -30000.0
O_GQ, O_GK, O_GV, O_NQ, O_NK, O_NV, O_MQ, O_CKV, O_KR, O_GATE = 0, 512, 640, 768, 1280, 1792, 2304, 3072, 3328, 3360
WIN = 6432


class Sem:
    def __init__(self, nc, name):
        self.h = nc.alloc_semaphore(name)
        self.n = 0

    def inc(self, ins, k=1):
        ins.then_inc(self.h, k)
        self.n += k
        return (self, self.n)


def W(eng, tok):
    if tok is None:
        return
    if isinstance(tok, list):
        for t in tok:
            W(eng, t)
        return
    s, v = tok
    if v > 0:
        eng.wait_ge(s.h, v)


class KB:
    def __init__(self, nc):
        self.nc = nc
        self.PE, self.ACT, self.DVE, self.POOL, self.SP = nc.tensor, nc.scalar, nc.vector, nc.gpsimd, nc.sync
        self.sems = {}
        self.dram = {}

    def S(self, name):
        if name not in self.sems:
            self.sems[name] = Sem(self.nc, name)
        return self.sems[name]

    def dma(self, eng, out, in_, sem, slow=False):
        if slow:
            ins = eng.dma_start(out=out, in_=in_, allow_slow_non_contiguous=True)
        else:
            ins = eng.dma_start(out=out, in_=in_)
        return sem.inc(ins, 16)

    def barrier(self, toks):
        for e in (self.PE, self.ACT, self.DVE, self.POOL, self.SP):
            W(e, toks)


def mk_alloc(nc, es, pfx):
    def sb(name, shape, dt=F32):
        return es.enter_context(nc.sbuf_tensor(pfx + name, shape, dt))

    def ps(name, shape, dt=F32):
        return es.enter_context(nc.psum_tensor(pfx + name, shape, dt))

    return sb, ps


def phase_mod(K, L, pfx):
    nc = K.nc
    PE, ACT, DVE, POOL, SP = K.PE, K.ACT, K.DVE, K.POOL, K.SP
    with ExitStack() as es:
        sb, ps = mk_alloc(nc, es, pfx)
        cT = sb("cT", [128, 8, 2])
        sT = sb("sT", [128, 8, 2])
        NWB = 4
        wm = [sb(f"w{i}", [128, 8, 512]) for i in range(NWB)]
        bm = sb("b", [2, 6144])
        mrow = sb("m", [2, 6144])
        ng = sb("ng", [2, 2, 1024])
        mv = sb("mv", [2, 6, 1024])
        pm = [ps(f"p{i}", [2, 512]) for i in range(2)]
        ld = K.S("ld0")
        wl = [K.S("ld1"), K.S("ld2"), K.S("ld3"), K.S("ld4")]
        s_pe, s_ac, s_dv, st = K.S("pe"), K.S("ac"), K.S("dv"), K.S("st0")
        for m in range(2):
            K.dma(SP, cT[:, :, m], K.dram["cvec"][m].rearrange("(c p) -> p c", p=128), ld, slow=True)
        K.dma(SP, bm[:], L["b_mod"].partition_broadcast(2), ld)
        K.dma(SP, ng[:, 0, :], L["norm1_g"].partition_broadcast(2), ld)
        t_ld = K.dma(SP, ng[:, 1, :], L["norm2_g"].partition_broadcast(2), ld)
        W(ACT, t_ld)
        t_s = s_ac.inc(ACT.activation(out=sT[:].rearrange("p c m -> p (c m)"), in_=cT[:].rearrange("p c m -> p (c m)"), func=AF.Silu))
        W(PE, t_s)
        pe_t = [None] * 12
        dv_t = [None] * 12
        wsrc = L["w_mod"]
        w_t = [None] * 12

        def load_w(g):
            if g >= NWB:
                W(SP, pe_t[g - NWB])
            w_t[g] = K.dma(SP, wm[g % NWB][:], wsrc[:, g * 512:(g + 1) * 512].rearrange("(c p) n -> p c n", p=128), wl[g % NWB])

        for g in range(min(NWB - 1, 12)):
            load_w(g)
        for g in range(12):
            if g + NWB - 1 < 12:
                load_w(g + NWB - 1)
            W(PE, w_t[g])
            if g >= 2:
                W(PE, dv_t[g - 2])
            for c in range(8):
                ins = PE.matmul(pm[g % 2][:], lhsT=sT[:, c, :], rhs=wm[g % NWB][:, c, :], start=(c == 0), stop=(c == 7))
            pe_t[g] = s_pe.inc(ins)
            W(DVE, pe_t[g])
            if g == 0:
                W(DVE, t_ld)
            dv_t[g] = s_dv.inc(DVE.tensor_tensor(out=mrow[:, g * 512:(g + 1) * 512], in0=pm[g % 2][:], in1=bm[:, g * 512:(g + 1) * 512], op=ALU.add))
        W(DVE, dv_t[11])
        sl = lambda i: mrow[:, i * 1024:(i + 1) * 1024]
        DVE.scalar_tensor_tensor(out=mv[:, 0, :], in0=sl(1), scalar=1.0, in1=ng[:, 0, :], op0=ALU.add, op1=ALU.mult)
        DVE.tensor_copy(out=mv[:, 1, :], in_=sl(0))
        DVE.tensor_copy(out=mv[:, 2, :], in_=sl(2))
        DVE.scalar_tensor_tensor(out=mv[:, 3, :], in0=sl(4), scalar=1.0, in1=ng[:, 1, :], op0=ALU.add, op1=ALU.mult)
        DVE.tensor_copy(out=mv[:, 4, :], in_=sl(3))
        t_f = s_dv.inc(DVE.tensor_copy(out=mv[:, 5, :], in_=sl(5)))
        W(SP, t_f)
        t_st = K.dma(SP, K.dram["modv"], mv[:], st)
        K.barrier([t_st])


def phase_norm(K, jobs, pfx):
    nc = K.nc
    PE, ACT, DVE, POOL, SP = K.PE, K.ACT, K.DVE, K.POOL, K.SP
    tiles = []
    for ji, (src, ntok, dst, m, ia, ish) in enumerate(jobs):
        for i in range(ntok // 128):
            tiles.append((ji, i))
    NTI = len(tiles)
    with ExitStack() as es:
        sb, ps = mk_alloc(nc, es, pfx)
        NX = 6
        xt = [sb(f"xt{i}", [128, 1024]) for i in range(NX)]
        junk = sb("junk", [128, 1024])
        ss = sb("ss", [128, NTI])
        r1 = sb("r1", [128, NTI])
        r2 = sb("r2", [128, NTI])
        rstd = sb("rstd", [128, NTI])
        NXN = 3
        xn = [sb(f"xn{i}", [128, 1024]) for i in range(NXN)]
        hb = [sb(f"hb{i}", [128, 8, 512], BF16) for i in range(2)]
        ident = sb("ident", [128, 128])
        acol = sb("acol", [128, 2, 2, 8])
        pT = [ps(f"pT{i}", [128, 8, 128]) for i in range(2)]
        lds = [K.S("ld0"), K.S("ld1"), K.S("ld3"), K.S("ld4"), K.S("ld5"), K.S("ld6")]
        ldc = K.S("ld2")
        s_pe, s_ac, s_dv = K.S("pe"), K.S("ac"), K.S("dv")
        sts = [K.S("gs0"), K.S("gs1")]
        t_c = K.dma(SP, ident[:], K.dram["ident_f"], ldc)
        mods = sorted(set((j[3], j[4], j[5]) for j in jobs))
        assert len(set(m for m, _, _ in mods)) == len(mods)
        for (m, ia, ish) in mods:
            K.dma(SP, acol[:, m, 0, :], K.dram["modv"][m, ia].rearrange("(c p) -> p c", p=128), ldc, slow=True)
            t_c = K.dma(SP, acol[:, m, 1, :], K.dram["modv"][m, ish].rearrange("(c p) -> p c", p=128), ldc, slow=True)
        act_t = [None] * NTI
        for n, (ji, i) in enumerate(tiles):
            src = jobs[ji][0]
            if n >= NX:
                W(SP, act_t[n - NX])
            t_l = K.dma(SP, xt[n % NX][:], src(i) if callable(src) else src[i * 128:(i + 1) * 128, :], lds[n % NX])
            W(ACT, t_l)
            act_t[n] = s_ac.inc(ACT.activation(out=junk[:], in_=xt[n % NX][:], func=AF.Square, accum_out=ss[:, n:n + 1]))
        W(DVE, act_t[NTI - 1])
        t1 = s_dv.inc(DVE.tensor_scalar(out=r1[:], in0=ss[:], scalar1=1.0 / D, scalar2=EPS, op0=ALU.mult, op1=ALU.add))
        W(ACT, t1)
        t2 = s_ac.inc(ACT.activation(out=r2[:], in_=r1[:], func=AF.Sqrt))
        W(DVE, t2)
        t3 = s_dv.inc(DVE.reciprocal(out=rstd[:], in_=r2[:]))
        W(ACT, t3)
        W(SP, t2)
        W(PE, t_c)
        W(DVE, t_c)
        a_t = [None] * NTI
        p_t = [None] * NTI
        v_t = [None] * NTI
        st_t = {}
        blk = -1
        blk_of = []
        prev_key = None
        for n, (ji, i) in enumerate(tiles):
            key = (ji, i // 4)
            if key != prev_key:
                blk += 1
                prev_key = key
            blk_of.append(blk)
        for n, (ji, i) in enumerate(tiles):
            src, ntok, dst, m, ia, ish = jobs[ji]
            b = blk_of[n]
            if n >= NX:
                W(SP, a_t[n - NX])
            t_l = K.dma(SP, xt[n % NX][:], src(i) if callable(src) else src[i * 128:(i + 1) * 128, :], lds[n % NX])
            W(ACT, t_l)
            if n >= NXN:
                W(ACT, p_t[n - NXN])
            a_t[n] = s_ac.inc(ACT.activation(out=xn[n % NXN][:], in_=xt[n % NXN if False else n % NX][:], func=AF.Copy, scale=rstd[:, n:n + 1]))
            W(PE, a_t[n])
            if n >= 2:
                W(PE, v_t[n - 2])
            for c in range(8):
                ins = PE.transpose(out=pT[n % 2][:, c, :], in_=xn[n % NXN][:, c * 128:(c + 1) * 128], identity=ident[:])
            p_t[n] = s_pe.inc(ins)
            W(DVE, p_t[n])
            if (i % 4 == 0) and (b - 2) in st_t:
                W(DVE, st_t[b - 2])
            for c in range(8):
                ins = DVE.tensor_scalar(out=hb[b % 2][:, c, (i % 4) * 128:(i % 4 + 1) * 128], in0=pT[n % 2][:, c, :],
                                        scalar1=acol[:, m, 0, c:c + 1], scalar2=acol[:, m, 1, c:c + 1], op0=ALU.mult, op1=ALU.add)
            v_t[n] = s_dv.inc(ins)
            last_in_blk = (n + 1 == NTI) or (blk_of[n + 1] != b)
            if last_in_blk:
                nt = (i % 4 + 1) * 128
                t0 = (i // 4) * 512
                W(POOL, v_t[n])
                st_t[b] = K.dma(POOL, dst[:, t0:t0 + nt].rearrange("(c p) t -> p c t", p=128), hb[b % 2][:, :, 0:nt], sts[b % 2])
        K.barrier([st_t[blk], st_t.get(blk - 1)])


def phase_proj(K, L, pfx, with_ctx):
    nc = K.nc
    PE, ACT, DVE, POOL, SP = K.PE, K.ACT, K.DVE, K.POOL, K.SP
    dr = K.dram
    with ExitStack() as es:
        sb, ps = mk_alloc(nc, es, pfx)
        win = sb("win", [128, 8, O_GATE], BF16)
        wuk = sb("wuk", [128, 2, 512], BF16)
        wuv = sb("wuv", [128, 2, 512], BF16)
        onesbd = sb("onesbd", [128, 128])
        ones = sb("ones", [128, 128])
        pt128 = sb("pt128", [128, 128], BF16)
        pt96 = sb("pt96", [128, 128], BF16)
        pt32 = sb("pt32", [128, 128], BF16)
        gq = sb("gq", [128, 1])
        gk = sb("gk", [128, 1])
        kvg = sb("kvg", [128, 2])
        hblk = [sb(f"h{i}", [128, 8, 512], BF16) for i in range(2)]
        ckvn = sb("ckvn", [128, 2, 512], BF16)
        sqf = sb("sqf", [128, 512]); sqf2 = sb("sqf2", [128, 512])
        qf = sb("qf", [128, 512]); qf2 = sb("qf2", [128, 512])
        sd = sb("sd", [128, 512]); rs = sb("rs", [128, 512]); qn = sb("qn", [128, 512])
        t1b = sb("t1", [128, 512]); t2b = sb("t2", [128, 512])
        cosb = [sb(f"cos{i}", [128, 512]) for i in range(2)]
        sinb = [sb(f"sin{i}", [128, 512]) for i in range(2)]
        qb = sb("qb", [128, 512], BF16)
        outb = [sb(f"ob{i}", [128, 512], BF16) for i in range(2)]
        vout = [sb(f"vo{i}", [128, 8, 65], BF16) for i in range(2)]
        acc = [ps(f"acc{i}", [128, 512]) for i in range(2)]
        acc2 = ps("acc2", [128, 512])
        pss = ps("pss", [128, 512])
        prot = ps("prot", [128, 512])
        ptm = [ps(f"ptm{i}", [128, 512]) for i in range(2)]
        wl = K.S("ld2")
        gw = K.S("gw")
        hl = [K.S("ld0"), K.S("ld1")]
        tl = [K.S("ld3"), K.S("ld4")]
        s_pe, s_ac, s_dv, s_pl = K.S("pe"), K.S("ac"), K.S("dv"), K.S("pl")
        sto = [K.S("gs0"), K.S("gs1")]
        stv = [K.S("gs2"), K.S("gs3")]
        stc = K.S("gs4")
        for c in range(8):
            K.dma(POOL, win[:, c, :], L["w_in"][c * 128:(c + 1) * 128, 0:O_GATE], gw)
        K.dma(POOL, wuk[:], L["mla_w_uk"].rearrange("(r p) n -> p r n", p=128), gw)
        t_gw = K.dma(POOL, wuv[:], L["mla_w_uv"].rearrange("(r p) n -> p r n", p=128), gw)
        K.dma(SP, onesbd[:], dr["onesbd_f"], wl)
        K.dma(SP, ones[:], dr["ones_f"], wl)
        K.dma(SP, pt128[:], dr["pt128"], wl)
        K.dma(SP, pt96[:], dr["pt96"], wl)
        K.dma(SP, pt32[:], dr["pt32"], wl)
        for hh in range(2):
            K.dma(SP, gq[hh * 64:(hh + 1) * 64, :], L["gqa_q_norm"].rearrange("(p o) -> p o", o=1), wl)
            K.dma(SP, gk[hh * 64:(hh + 1) * 64, :], L["gqa_k_norm"].rearrange("(p o) -> p o", o=1), wl)
        t_w = K.dma(SP, kvg[:], L["mla_kv_norm"].rearrange("(r p) -> p r", p=128), wl, slow=True)
        for i in range(2):
            DVE.memset(vout[i][:], 1.0)
        t_ms = s_dv.inc(DVE.memset(qn[:], 0.0))
        for e in (PE, ACT, DVE, POOL):
            W(e, t_w)
            W(e, t_gw)
        W(ACT, t_ms)

        st = {"k": 0, "rk": 0, "vk": 0, "acc_free": [None, None], "ob_free": [None, None], "tab_free": [None, None],
              "vo_free": [None, None], "ptm_free": [None, None], "hb_tok": None}

        def store(eng, dst, src, sem):
            return K.dma(eng, dst, src, sem)

        import os
        LIMIT = int(os.environ.get("PROJ_LIMIT", "1000000"))
        units = [0]

        def over():
            units[0] += 1
            return units[0] > LIMIT

        def fm_job(chunks, M, nt, norm_g, rope, dsts, oscale=1.0):
            if over():
                return
            k = st["k"]; st["k"] += 1
            a = acc[k % 2]
            W(PE, st["acc_free"][k % 2])
            W(PE, st["hb_tok"])
            for ci, (lt, rh) in enumerate(chunks):
                ins = PE.matmul(a[:M, :nt], lhsT=lt, rhs=rh, start=(ci == 0), stop=(ci == len(chunks) - 1))
            t_main = s_pe.inc(ins)
            ob = outb[k % 2]
            if norm_g is None and rope is None:
                W(ACT, t_main)
                W(ACT, st["ob_free"][k % 2])
                t_out = s_ac.inc(ACT.activation(out=ob[:M, :nt], in_=a[:M, :nt], func=AF.Copy, scale=float(oscale)))
                st["acc_free"][k % 2] = t_out
            else:
                if rope is not None:
                    r = st["rk"]; st["rk"] += 1
                    PT, cos_ap, sin_ap = rope
                    W(SP, st["tab_free"][r % 2])
                    K.dma(SP, cosb[r % 2][:M, :nt], cos_ap, tl[r % 2])
                    t_tab = K.dma(SP, sinb[r % 2][:M, :nt], sin_ap, tl[r % 2])
                W(DVE, t_main)
                t_qf = s_dv.inc(DVE.tensor_copy(out=qf[:M, :nt], in_=a[:M, :nt]))
                t_cur = t_qf
                cur = qf
                free_toks = [t_qf]
                if norm_g is not None:
                    W(ACT, t_qf)
                    t_sq = s_ac.inc(ACT.activation(out=sqf[:M, :nt], in_=qf[:M, :nt], func=AF.Square))
                    W(PE, t_sq)
                    t_ss = s_pe.inc(PE.matmul(pss[:M, :nt], lhsT=onesbd[:M, :M], rhs=sqf[:M, :nt], start=True, stop=True))
                    W(ACT, t_ss)
                    W(ACT, t_qf)
                    t_sd = s_ac.inc(ACT.activation(out=sd[:M, :nt], in_=pss[:M, :nt], func=AF.Sqrt, bias=EPS, scale=1.0 / 64))
                    W(DVE, t_sd)
                    t_rs = s_dv.inc(DVE.reciprocal(out=rs[:M, :nt], in_=sd[:M, :nt]))
                    W(DVE, t_rs)
                    if rope is None:
                        W(DVE, st["ob_free"][k % 2])
                        t_out = s_dv.inc(DVE.scalar_tensor_tensor(out=ob[:M, :nt], in0=qf[:M, :nt], scalar=norm_g, in1=rs[:M, :nt], op0=ALU.mult, op1=ALU.mult))
                    else:
                        t_cur = s_dv.inc(DVE.scalar_tensor_tensor(out=qn[:M, :nt], in0=qf[:M, :nt], scalar=norm_g, in1=rs[:M, :nt], op0=ALU.mult, op1=ALU.mult))
                        cur = qn
                st["acc_free"][k % 2] = free_toks
                if rope is not None:
                    W(ACT, t_cur)
                    t_qb = s_ac.inc(ACT.activation(out=qb[:M, :nt], in_=cur[:M, :nt], func=AF.Copy))
                    W(PE, t_qb)
                    t_rot = s_pe.inc(PE.matmul(prot[:M, :nt], lhsT=PT[:M, :M], rhs=qb[:M, :nt], start=True, stop=True))
                    W(POOL, t_cur)
                    W(POOL, t_tab)
                    t_t1 = s_pl.inc(POOL.tensor_tensor(out=t1b[:M, :nt], in0=cur[:M, :nt], in1=cosb[r % 2][:M, :nt], op=ALU.mult))
                    W(DVE, t_rot)
                    W(DVE, t_tab)
                    t_t2 = s_dv.inc(DVE.tensor_tensor(out=t2b[:M, :nt], in0=prot[:M, :nt], in1=sinb[r % 2][:M, :nt], op=ALU.mult))
                    W(DVE, t_t1)
                    W(DVE, t_t2)
                    W(DVE, st["ob_free"][k % 2])
                    t_out = s_dv.inc(DVE.tensor_tensor(out=ob[:M, :nt], in0=t1b[:M, :nt], in1=t2b[:M, :nt], op=ALU.add))
                    st["tab_free"][r % 2] = t_out
            W(POOL, t_out)
            for (dst, r0, r1) in dsts:
                t_st = store(POOL, dst, ob[r0:r1, :nt], sto[k % 2])
            st["ob_free"][k % 2] = t_st

        def ckv_job(hs, nt, dst_ckvt):
            if over():
                st["ckvn_tok"] = None
                return
            k = st["k"]; st["k"] += 1
            a = acc[k % 2]
            W(PE, st["acc_free"][k % 2])
            W(PE, st["hb_tok"])
            W(PE, st.get("acc2_free"))
            for g, aa in enumerate((a, acc2)):
                for c in range(8):
                    ins = PE.matmul(aa[:, :nt], lhsT=win[:, c, O_CKV + g * 128:O_CKV + (g + 1) * 128], rhs=hblk[hs][:, c, :nt], start=(c == 0), stop=(c == 7))
            t_main = s_pe.inc(ins)
            CUT = int(os.environ.get("CKV_CUT", "99"))
            st["ckvn_tok"] = None
            if CUT <= 1:
                return
            W(DVE, t_main)
            DVE.tensor_copy(out=qf[:, :nt], in_=a[:, :nt])
            t_qf = s_dv.inc(DVE.tensor_copy(out=qf2[:, :nt], in_=acc2[:, :nt]))
            W(ACT, t_qf)
            ACT.activation(out=sqf[:, :nt], in_=qf[:, :nt], func=AF.Square)
            t_sq = s_ac.inc(ACT.activation(out=sqf2[:, :nt], in_=qf2[:, :nt], func=AF.Square))
            st["acc_free"][k % 2] = [t_qf]
            st["acc2_free"] = [t_qf]
            if CUT <= 2:
                return
            W(PE, t_sq)
            PE.matmul(pss[:, :nt], lhsT=ones[:], rhs=sqf[:, :nt], start=True, stop=False)
            t_ss = s_pe.inc(PE.matmul(pss[:, :nt], lhsT=ones[:], rhs=sqf2[:, :nt], start=False, stop=True))
            if CUT <= 3:
                return
            W(ACT, t_ss)
            W(ACT, t_qf)
            t_sd = s_ac.inc(ACT.activation(out=sd[:, :nt], in_=pss[:, :nt], func=AF.Sqrt, bias=EPS, scale=1.0 / 256))
            W(DVE, t_sd)
            t_rs = s_dv.inc(DVE.reciprocal(out=rs[:, :nt], in_=sd[:, :nt]))
            if CUT <= 4:
                return
            W(DVE, t_rs)
            W(DVE, st.get("ckvn_free"))
            DVE.scalar_tensor_tensor(out=ckvn[:, 0, :nt], in0=qf[:, :nt], scalar=kvg[:, 0:1], in1=rs[:, :nt], op0=ALU.mult, op1=ALU.mult)
            t_out = s_dv.inc(DVE.scalar_tensor_tensor(out=ckvn[:, 1, :nt], in0=qf2[:, :nt], scalar=kvg[:, 1:2], in1=rs[:, :nt], op0=ALU.mult, op1=ALU.mult))
            if CUT <= 5:
                return
            W(POOL, t_out)
            t_st = store(POOL, dst_ckvt.rearrange("(r p) t -> p r t", p=128), ckvn[:, :, :nt], stc)
            st["ckvn_tok"] = t_out
            st["ckvn_st"] = t_st

        def tm_job(chunks, N, nh, dst):
            if over():
                return
            j = st["vk"]; st["vk"] += 1
            p = ptm[j % 2]
            W(PE, st["ptm_free"][j % 2])
            W(PE, st["hb_tok"])
            for ci, (lt, rh) in enumerate(chunks):
                ins = PE.matmul(p[:, :N], lhsT=lt, rhs=rh, start=(ci == 0), stop=(ci == len(chunks) - 1))
            t_main = s_pe.inc(ins)
            W(ACT, t_main)
            W(ACT, st["vo_free"][j % 2])
            t_o = s_ac.inc(ACT.activation(out=vout[j % 2][:, 0:nh, 0:64], in_=p[:, :N].rearrange("p (h d) -> p h d", d=64), func=AF.Copy))
            st["ptm_free"][j % 2] = t_o
            W(POOL, t_o)
            st["vo_free"][j % 2] = store(POOL, dst, vout[j % 2][:, 0:nh, :], stv[j % 2])

        nblk = [0]
        last_users = [None, None]

        hT_all, hT_na = dr["hT_all"], dr["hT_na"]
        bsrc = [(hT_all[:, tb * 512:tb * 512 + (256 if tb == 16 else 512)], 256 if tb == 16 else 512) for tb in range(17)]
        bsrc += [(hT_na[:, tb * 512:(tb + 1) * 512], 512) for tb in range(5)]
        bsrc += [(hT_na[:, 256 + tb * 512:256 + (tb + 1) * 512], 512) for tb in range(4)]
        btok = [None] * len(bsrc)
        issued = [0]

        def issue_upto(i):
            while issued[0] <= i and issued[0] < len(bsrc):
                b = issued[0]
                src_ap, nt_ = bsrc[b]
                W(SP, last_users[b % 2])
                btok[b] = K.dma(SP, hblk[b % 2][:, :, :nt_], src_ap.rearrange("(c p) t -> p c t", p=128), hl[b % 2])
                issued[0] += 1

        def load_block(src_ap, nt):
            b = nblk[0]; nblk[0] += 1
            issue_upto(b)
            st["hb_tok"] = btok[b]
            issue_upto(b + 1)
            return b % 2

        def done_block(hs):
            last_users[hs] = (s_pe, s_pe.n)

        def hch(hs, c0, M, nt):
            return [(win[:, c, c0:c0 + M], hblk[hs][:, c, :nt]) for c in range(8)]

        for tb in range(17):
            ctxb = (tb == 16)
            t0 = tb * 512
            nt = 256 if ctxb else 512
            hs = load_block(hT_all[:, t0:t0 + nt], nt)
            rope = None if ctxb else (pt128, dr["cosA_all"][:, t0:t0 + nt], dr["sinA_all"][:, t0:t0 + nt])
            fm_job(hch(hs, O_GK, 128, nt), 128, nt, gk[:, 0:1], rope,
                   [(dr["GKT"][0, :, t0:t0 + nt], 0, 64), (dr["GKT"][1, :, t0:t0 + nt], 64, 128)])
            ckv_job(hs, nt, dr["CKVT"][:, t0:t0 + nt])
            rope = None if ctxb else (pt32, dr["cosM_all"][:, t0:t0 + nt], dr["sinM_all"][:, t0:t0 + nt])
            fm_job(hch(hs, O_KR, 32, nt), 32, nt, None, rope, [(dr["MKT"][h, 64:96, t0:t0 + nt], 0, 32) for h in range(8)])
            W(PE, st["ckvn_tok"])
            for g in range(4):
                fm_job([(wuk[:, r, g * 128:(g + 1) * 128], ckvn[:, r, :nt]) for r in range(2)], 128, nt, None, None,
                       [(dr["MKT"][2 * g, 0:64, t0:t0 + nt], 0, 64), (dr["MKT"][2 * g + 1, 0:64, t0:t0 + nt], 64, 128)])
            for ti in range(nt // 128):
                tsl = slice(ti * 128, (ti + 1) * 128)
                r0 = t0 + ti * 128
                tm_job([(ckvn[:, r, tsl], wuv[:, r, :]) for r in range(2)], 512, 8, dr["MV"][r0:r0 + 128, :, :])
                tm_job([(hblk[hs][:, c, tsl], win[:, c, O_GV:O_GV + 128]) for c in range(8)], 128, 2, dr["GV"][r0:r0 + 128, :, :])
            st["ckvn_free"] = (s_pe, s_pe.n)
            if ctxb:
                for g in range(4):
                    fm_job(hch(hs, O_NK + g * 128, 128, nt), 128, nt, None, None,
                           [(dr["NKT"][2 * g, :, NAT:NAT + nt], 0, 64), (dr["NKT"][2 * g + 1, :, NAT:NAT + nt], 64, 128)])
                for ti in range(nt // 128):
                    tsl = slice(ti * 128, (ti + 1) * 128)
                    tm_job([(hblk[hs][:, c, tsl], win[:, c, O_NV:O_NV + 512]) for c in range(8)], 512, 8, dr["NV"][NAT + ti * 128:NAT + (ti + 1) * 128, :, :])
                if with_ctx:
                    q0 = T
                    for g in range(4):
                        fm_job(hch(hs, O_GQ + g * 128, 128, nt), 128, nt, gq[:, 0:1], None,
                               [(dr["GQT"][2 * g, :, q0:q0 + nt], 0, 64), (dr["GQT"][2 * g + 1, :, q0:q0 + nt], 64, 128)])
                        fm_job(hch(hs, O_NQ + g * 128, 128, nt), 128, nt, None, None,
                               [(dr["NQT"][2 * g, :, q0:q0 + nt], 0, 64), (dr["NQT"][2 * g + 1, :, q0:q0 + nt], 64, 128)], oscale=0.125)
                    for h in range(8):
                        fm_job(hch(hs, O_MQ + h * 96, 96, nt), 96, nt, None, None, [(dr["MQT"][h, :, q0:q0 + nt], 0, 96)])
            done_block(hs)
        for tb in range(5):
            t0 = tb * 512
            nt = 512
            hs = load_block(hT_na[:, t0:t0 + nt], nt)
            for g in range(4):
                fm_job(hch(hs, O_NK + g * 128, 128, nt), 128, nt, None, None,
                       [(dr["NKT"][2 * g, :, t0:t0 + nt], 0, 64), (dr["NKT"][2 * g + 1, :, t0:t0 + nt], 64, 128)])
            for ti in range(4):
                tsl = slice(ti * 128, (ti + 1) * 128)
                tm_job([(hblk[hs][:, c, tsl], win[:, c, O_NV:O_NV + 512]) for c in range(8)], 512, 8, dr["NV"][t0 + ti * 128:t0 + (ti + 1) * 128, :, :])
            done_block(hs)
        for tb in range(4):
            q0 = tb * 512
            nt = 512
            hs = load_block(hT_na[:, 256 + q0:256 + q0 + nt], nt)
            for g in range(4):
                fm_job(hch(hs, O_GQ + g * 128, 128, nt), 128, nt, gq[:, 0:1], (pt128, dr["cosA_own"][:, q0:q0 + nt], dr["sinA_own"][:, q0:q0 + nt]),
                       [(dr["GQT"][2 * g, :, q0:q0 + nt], 0, 64), (dr["GQT"][2 * g + 1, :, q0:q0 + nt], 64, 128)])
                fm_job(hch(hs, O_NQ + g * 128, 128, nt), 128, nt, None, None,
                       [(dr["NQT"][2 * g, :, q0:q0 + nt], 0, 64), (dr["NQT"][2 * g + 1, :, q0:q0 + nt], 64, 128)], oscale=0.125)
            for h in range(8):
                fm_job(hch(hs, O_MQ + h * 96, 96, nt), 96, nt, None, (pt96, dr["cosM_own"][:, q0:q0 + nt], dr["sinM_own"][:, q0:q0 + nt]),
                       [(dr["MQT"][h, :, q0:q0 + nt], 0, 96)])
            done_block(hs)
        K.barrier([(s, s.n) for s in sto + stv + [stc]])


def phase_attn(K, heads, pfx, nkmax):
    nc = K.nc
    PE, ACT, DVE, POOL, SP = K.PE, K.ACT, K.DVE, K.POOL, K.SP
    NQ = T + C
    rls = K.dram["rls"]
    with ExitStack() as es:
        sb, ps = mk_alloc(nc, es, pfx)
        ktb = [sb(f"kt{i}", [128, nkmax], BF16) for i in range(2)]
        vb = [sb(f"v{i}", [128, nkmax // 128, 65], BF16) for i in range(2)]
        qb = [sb(f"q{i}", [128, NQ], BF16) for i in range(2)]
        pbuf = [sb(f"p{i}", [128, 1024], BF16) for i in range(3)]
        NBB, LB = 8, 6
        bb = [sb(f"bias{i}", [128, 512], BF16) for i in range(NBB)]
        identb = sb("identb", [128, 128], BF16)
        osb = [sb(f"osb{i}", [128, 512]) for i in range(2)]
        rl = [sb(f"rl{i}", [128, 512]) for i in range(2)]
        rbc = [sb(f"rbc{i}", [64, 512]) for i in range(2)]
        ysb = [[sb(f"y{i}_{w}", [64, 512], BF16) for w in range(2)] for i in range(2)]
        psb = [ps(f"s{i}", [128, 1024]) for i in range(3)]
        po = [ps(f"o{i}", [128, 512]) for i in range(2)]
        hl = [K.S("ld0"), K.S("ld1")]
        bl = [K.S(f"gb{i}") for i in range(NBB)]
        cl = K.S("ld5")
        rld = [K.S("ld3"), K.S("ld4")]
        rst = [K.S("st2"), K.S("st3")]
        s_pe, s_ac, s_dv = K.S("pe"), K.S("ac"), K.S("dv")
        sty = [K.S("st0"), K.S("st1")]
        t_c = K.dma(SP, identb[:], K.dram["ident_b"], cl)
        for i in range(2):
            DVE.memset(ktb[i][:], 0.0)
            DVE.memset(qb[i][:], 0.0)
            DVE.memset(rl[i][:], 1.0)
        t_m = s_dv.inc(DVE.memset(osb[0][:], 0.0))
        W(PE, t_c)
        W(PE, t_m)
        W(SP, t_m)
        steps = []
        for hi, h in enumerate(heads):
            for bi, b in enumerate(h["blocks"]):
                nt_ = len(b["tiles"])
                b["chunks"] = [(c0, min(512, b["nq"] - c0)) for c0 in range(0, b["nq"], 512)]
                for si, (kti, bias) in enumerate(b["tiles"]):
                    assert bias is None or len(b["chunks"]) == 1
                    steps.append(dict(hi=hi, b=b, kti=kti, bias=bias, first=(si == 0), last=(si == nt_ - 1),
                                      hfirst=(bi == 0 and si == 0), hlast=(bi == len(h["blocks"]) - 1 and si == nt_ - 1)))
        NS = len(steps)
        head_tok = [None] * len(heads)
        head_done = [None] * len(heads)

        def load_head(hi):
            h = heads[hi]
            s = hi % 2
            if hi >= 2:
                W(SP, head_done[hi - 2])
            dk, nk = h["dk"], h["nk"]
            K.dma(SP, ktb[s][:dk, :nk], h["kt"], hl[s])
            K.dma(SP, vb[s][:, :nk // 128, :], h["v"].rearrange("(t p) e -> p t e", p=128), hl[s])
            head_tok[hi] = K.dma(SP, qb[s][:dk, :], h["qt"], hl[s])

        tq = [None] * NS
        te = [None] * NS
        tv = [None] * NS
        bias_ld = [None] * NS
        nbias = [0]
        bidx = [None] * NS
        po_free = [None, None]
        y_free = [[None, None], [None, None]]
        pend_dv = {}
        state = {}

        def emit_qk(t):
            s = steps[t]
            h = heads[s["hi"]]
            hs = s["hi"] % 2
            if s["hfirst"]:
                W(PE, head_tok[s["hi"]])
            if t >= 3:
                W(PE, te[t - 3])
            b = s["b"]
            hasb = s["bias"] is not None
            for (c0, cn) in b["chunks"]:
                ins = PE.matmul(psb[t % 3][:, c0:c0 + cn], lhsT=ktb[hs][:, s["kti"] * 128:(s["kti"] + 1) * 128],
                                rhs=qb[hs][:, b["q0"] + c0:b["q0"] + c0 + cn], start=True, stop=not hasb)
            if hasb:
                W(PE, bias_ld[t])
                ins = PE.matmul(psb[t % 3][:, :b["nq"]], lhsT=identb[:], rhs=bb[bidx[t] % NBB][:, :b["nq"]], start=False, stop=True)
            tq[t] = s_pe.inc(ins)
            if hasb:
                state[("bfree", bidx[t] % NBB)] = tq[t]

        def emit_bias_load(t):
            s = steps[t]
            if s["bias"] is None:
                return
            n = nbias[0]; nbias[0] += 1
            bidx[t] = n
            W(POOL, state.get(("bfree", n % NBB)))
            bias_ld[t] = K.dma(POOL, bb[n % NBB][:, :s["b"]["nq"]], s["bias"], bl[n % NBB])

        if NS > 0:
            load_head(0)
        LA = 2
        for t in range(min(LB, NS)):
            emit_bias_load(t)
        for t in range(min(LA, NS)):
            emit_qk(t)
        cur_blk = -1
        for t in range(NS):
            s = steps[t]
            h = heads[s["hi"]]
            b = s["b"]
            nq = b["nq"]
            hs = s["hi"] % 2
            if s["hfirst"] and s["hi"] + 1 < len(heads):
                load_head(s["hi"] + 1)
            if s["first"]:
                cur_blk += 1
            if t + LB < NS:
                emit_bias_load(t + LB)
            if t + LA < NS:
                emit_qk(t + LA)
            W(ACT, tq[t])
            if t >= 3:
                W(ACT, tv[t - 3])
            te[t] = s_ac.inc(ACT.activation(out=pbuf[t % 3][:, :nq], in_=psb[t % 3][:, :nq], func=AF.Exp, scale=float(h["scale"])))
            W(PE, te[t])
            for w, (c0, cn) in enumerate(b["chunks"]):
                if s["first"]:
                    W(PE, po_free[w])
                ins = PE.matmul(po[w][:65, :cn], lhsT=vb[hs][:, s["kti"], :], rhs=pbuf[t % 3][:, c0:c0 + cn], start=s["first"], stop=s["last"])
            tv[t] = s_pe.inc(ins)
            if s["hlast"]:
                head_done[s["hi"]] = tv[t]
            for f in pend_dv.pop(t, []):
                f()
            if s["last"]:
                cb = cur_blk
                parts = []
                for w, (c0, cn) in enumerate(b["chunks"]):
                    W(DVE, tv[t])
                    t_o = s_dv.inc(DVE.tensor_copy(out=osb[w][:65, :cn], in_=po[w][:65, :cn]))
                    po_free[w] = t_o
                    W(DVE, t_o)
                    t_rl = s_dv.inc(DVE.reciprocal(out=rl[w][64:65, :cn], in_=osb[w][64:65, :cn]))
                    slot = (cb % 2) * 2 + w
                    W(SP, t_rl)
                    t_s = K.dma(SP, rls[slot:slot + 1, :cn], rl[w][64:65, :cn], rst[w])
                    W(SP, t_s)
                    t_b = K.dma(SP, rbc[w][:, :cn], rls[slot, :cn].partition_broadcast(64), rld[w])
                    parts.append((w, cn, t_b))

                def dv_part(parts=parts, cb=cb, yts=b["yts"]):
                    for (w, cn, t_b) in parts:
                        W(DVE, t_b)
                        W(DVE, y_free[cb % 2][w])
                        t_y = s_dv.inc(DVE.tensor_tensor(out=ysb[cb % 2][w][:, :cn], in0=osb[w][:64, :cn], in1=rbc[w][:, :cn], op=ALU.mult))
                        W(SP, t_y)
                        y_free[cb % 2][w] = K.dma(SP, yts[w], ysb[cb % 2][w][:, :cn], sty[w])

                if t + 1 < NS:
                    nxt_len = len(steps[t + 1]["b"]["tiles"])
                    d = max(1, min(2, nxt_len - 1))
                    pend_dv.setdefault(t + d, []).append(dv_part)
                else:
                    dv_part()
        assert not pend_dv
        K.barrier([(s_, s_.n) for s_ in sty])


def phase_merge(K, L, pfx, qblocks, x_src, x_dst):
    nc = K.nc
    PE, ACT, DVE, POOL, SP = K.PE, K.ACT, K.DVE, K.POOL, K.SP
    dr = K.dram
    with ExitStack() as es:
        sb, ps = mk_alloc(nc, es, pfx)
        wg = sb("wg", [128, 8, 3072], BF16)
        wo = [sb(f"wo{i}", [128, 4, 1024], BF16) for i in range(3)]
        wout = sb("wout", [128, 8, 1024], BF16)
        g1 = sb("g1", [128, 2, 1024])
        hblk = [sb(f"h{i}", [128, 8, 512], BF16) for i in range(2)]
        yb = [[sb(f"y{r}_{i}", [128, 4, 512], BF16) for r in range(3)] for i in range(2)]
        sg = [sb(f"sg{i}", [128, 512]) for i in range(2)]
        yacc = sb("yacc", [128, 512])
        tmp = sb("tmp", [128, 512])
        yT = sb("yT", [128, 8, 512], BF16)
        xt = [sb(f"xt{i}", [128, 1024]) for i in range(2)]
        xo = [sb(f"xo{i}", [128, 1024]) for i in range(2)]
        tm2 = [sb(f"tm{i}", [128, 512]) for i in range(2)]
        pg = [ps(f"pg{i}", [128, 512]) for i in range(2)]
        pbr = [ps(f"pb{i}", [128, 512]) for i in range(2)]
        pw = [ps(f"pw{i}", [128, 512]) for i in range(2)]
        wl = K.S("ld2")
        hl = [K.S("ld0"), K.S("ld1")]
        xl = [K.S("ld3"), K.S("ld4")]
        s_pe, s_ac, s_dv, s_pl = K.S("pe"), K.S("ac"), K.S("dv"), K.S("pl")
        stx = [K.S("st0"), K.S("st1")]
        gw = K.S("gw")
        for c in range(8):
            K.dma(POOL, wg[:, c, :], L["w_in"][c * 128:(c + 1) * 128, O_GATE:WIN], gw)
        for r, nm in enumerate(("w_o_gqa", "w_o_na", "w_o_mla")):
            K.dma(POOL, wo[r][:], L[nm].rearrange("(c p) n -> p c n", p=128), gw)
        t_gw = K.dma(POOL, wout[:], L["w_out"].rearrange("(c p) n -> p c n", p=128), gw)
        K.dma(SP, g1[:, 0, :], dr["modv"][0, 2].partition_broadcast(128), wl)
        t_w = K.dma(SP, g1[:, 1, :], dr["modv"][1, 2].partition_broadcast(128), wl)
        for e in (PE, DVE, POOL):
            W(e, t_w)
            W(e, t_gw)
        ysrc = (dr["YAT"], dr["YBT"], dr["YCT"])
        blk_done = [None, None]
        k = 0
        xk = 0
        sg_free = [None, None]
        pg_free = [None, None]
        pbr_free = [None, None]
        pw_free = [None, None]
        xt_free = [None, None]
        xo_free = [None, None]
        tm_free = [None, None]
        yT_free = None
        for bi, (hT_ap, q0, nt, m) in enumerate(qblocks):
            s = bi % 2
            W(SP, blk_done[s])
            K.dma(SP, hblk[s][:, :, :nt], hT_ap.rearrange("(c p) t -> p c t", p=128), hl[s])
            for r in range(3):
                t_l = K.dma(SP, yb[s][r][:, :, :nt], ysrc[r][:, q0:q0 + nt].rearrange("(c p) t -> p c t", p=128), hl[s])
            W(PE, t_l)
            for oc in range(8):
                for r in range(3):
                    W(PE, pg_free[k % 2])
                    for c in range(8):
                        ins = PE.matmul(pg[k % 2][:, :nt], lhsT=wg[:, c, r * 1024 + oc * 128:r * 1024 + (oc + 1) * 128], rhs=hblk[s][:, c, :nt],
                                        start=(c == 0), stop=(c == 7))
                    t_g = s_pe.inc(ins)
                    W(PE, pbr_free[k % 2])
                    for c in range(4):
                        ins = PE.matmul(pbr[k % 2][:, :nt], lhsT=wo[r][:, c, oc * 128:(oc + 1) * 128], rhs=yb[s][r][:, c, :nt], start=(c == 0), stop=(c == 3))
                    t_b = s_pe.inc(ins)
                    W(ACT, t_g)
                    W(ACT, sg_free[k % 2])
                    t_s = s_ac.inc(ACT.activation(out=sg[k % 2][:, :nt], in_=pg[k % 2][:, :nt], func=AF.Sigmoid))
                    pg_free[k % 2] = t_s
                    W(DVE, t_s)
                    W(DVE, t_b)
                    if r == 0:
                        t_d = s_dv.inc(DVE.tensor_tensor(out=yacc[:, :nt], in0=sg[k % 2][:, :nt], in1=pbr[k % 2][:, :nt], op=ALU.mult))
                    else:
                        t_d = s_dv.inc(DVE.tensor_tensor(out=tmp[:, :nt], in0=sg[k % 2][:, :nt], in1=pbr[k % 2][:, :nt], op=ALU.mult))
                        W(DVE, t_d)
                        if r == 1:
                            t_d = s_dv.inc(DVE.tensor_tensor(out=yacc[:, :nt], in0=yacc[:, :nt], in1=tmp[:, :nt], op=ALU.add))
                        else:
                            if oc == 0:
                                W(DVE, yT_free)
                            t_d = s_dv.inc(DVE.tensor_tensor(out=yT[:, oc, :nt], in0=yacc[:, :nt], in1=tmp[:, :nt], op=ALU.add))
                    sg_free[k % 2] = t_d
                    pbr_free[k % 2] = t_d
                    k += 1
            blk_done[s] = (s_pe, s_pe.n)
            t_y = t_d
            W(PE, t_y)
            for ti in range(nt // 128):
                xs_ = xk % 2
                W(SP, xt_free[xs_])
                t_x = K.dma(SP, xt[xs_][:], x_src(q0 + ti * 128), xl[xs_])
                for half in range(2):
                    j = 2 * xk + half
                    W(PE, pw_free[j % 2])
                    for c in range(8):
                        ins = PE.matmul(pw[j % 2][:, :], lhsT=yT[:, c, ti * 128:(ti + 1) * 128], rhs=wout[:, c, half * 512:(half + 1) * 512],
                                        start=(c == 0), stop=(c == 7))
                    t_p = s_pe.inc(ins)
                    W(DVE, t_p)
                    W(DVE, tm_free[j % 2])
                    t_m = s_dv.inc(DVE.tensor_tensor(out=tm2[j % 2][:], in0=pw[j % 2][:], in1=g1[:, m, half * 512:(half + 1) * 512], op=ALU.mult))
                    pw_free[j % 2] = t_m
                    W(POOL, t_m)
                    W(POOL, t_x)
                    if half == 0:
                        W(POOL, xo_free[xs_])
                    t_a = s_pl.inc(POOL.tensor_tensor(out=xo[xs_][:, half * 512:(half + 1) * 512], in0=tm2[j % 2][:], in1=xt[xs_][:, half * 512:(half + 1) * 512], op=ALU.add))
                    tm_free[j % 2] = t_a
                xt_free[xs_] = t_a
                W(SP, t_a)
                xo_free[xs_] = K.dma(SP, x_dst(q0 + ti * 128), xo[xs_][:], stx[xs_])
                xk += 1
            yT_free = (s_pe, s_pe.n)
        K.barrier([(s_, s_.n) for s_ in stx])


def phase_mlp(K, L, pfx, qblocks, x_src, x_dst, final_g, after_block=None):
    nc = K.nc
    PE, ACT, DVE, POOL, SP = K.PE, K.ACT, K.DVE, K.POOL, K.SP
    dr = K.dram
    with ExitStack() as es:
        sb, ps = mk_alloc(nc, es, pfx)
        w1 = sb("w1", [128, 8, 4096], BF16)
        w2 = sb("w2", [128, 32, 1024], BF16)
        g2 = sb("g2", [128, 2, 1024])
        fg = sb("fg", [128, 1024])
        hblk = [sb(f"h{i}", [128, 8, 256], BF16) for i in range(2)]
        uT = sb("uT", [128, 32, 256], BF16)
        rb = [sb(f"r{i}", [128, 256]) for i in range(2)]
        xt = [sb(f"xt{i}", [128, 1024]) for i in range(2)]
        xo = [sb(f"xo{i}", [128, 1024]) for i in range(2)]
        tm2 = [sb(f"tm{i}", [128, 512]) for i in range(2)]
        junk = sb("junk", [128, 1024])
        st4 = sb("st4", [128, 4])
        pu = [ps(f"pu{i}", [128, 512]) for i in range(2)]
        pw = [ps(f"pw{i}", [128, 512]) for i in range(2)]
        wl = K.S("ld2")
        hl = [K.S("ld0"), K.S("ld1")]
        xl = [K.S("ld3"), K.S("ld4")]
        s_pe, s_ac, s_dv, s_pl = K.S("pe"), K.S("ac"), K.S("dv"), K.S("pl")
        stx = [K.S("st0"), K.S("st1")]
        gw = K.S("gw")
        for c in range(8):
            K.dma(POOL, w1[:, c, :], L["w_mlp1"][c * 128:(c + 1) * 128, :], gw)
        for c4 in range(4):
            t_gw = K.dma(POOL, w2[:, c4 * 8:(c4 + 1) * 8, :], L["w_mlp2"][c4 * 1024:(c4 + 1) * 1024, :].rearrange("(c p) n -> p c n", p=128), gw)
        K.dma(SP, g2[:, 0, :], dr["modv"][0, 5].partition_broadcast(128), wl)
        if final_g is not None:
            K.dma(SP, fg[:], final_g.partition_broadcast(128), wl)
        t_w = K.dma(SP, g2[:, 1, :], dr["modv"][1, 5].partition_broadcast(128), wl)
        for e in (PE, DVE, POOL, ACT):
            W(e, t_w)
            W(e, t_gw)
        h2T = dr["h2T"]
        blk_done = [None, None]
        k = 0
        xk = 0
        pu_free = [None, None]
        rb_free = [None, None]
        pw_free = [None, None]
        xt_free = [None, None]
        xo_free = [None, None]
        tm_free = [None, None]
        uT_free = None
        for bi, (q0, nt, m) in enumerate(qblocks):
            s = bi % 2
            W(SP, blk_done[s])
            t_l = K.dma(SP, hblk[s][:, :, :nt], h2T[:, q0:q0 + nt].rearrange("(c p) t -> p c t", p=128), hl[s])
            W(PE, t_l)
            for fc in range(32):
                W(PE, pu_free[k % 2])
                for c in range(8):
                    ins = PE.matmul(pu[k % 2][:, :nt], lhsT=w1[:, c, fc * 128:(fc + 1) * 128], rhs=hblk[s][:, c, :nt], start=(c == 0), stop=(c == 7))
                t_u = s_pe.inc(ins)
                W(ACT, t_u)
                W(ACT, rb_free[k % 2])
                t_r = s_ac.inc(ACT.activation(out=rb[k % 2][:, :nt], in_=pu[k % 2][:, :nt], func=AF.Relu))
                pu_free[k % 2] = t_r
                W(DVE, t_r)
                if fc == 0:
                    W(DVE, uT_free)
                t_q = s_dv.inc(DVE.tensor_tensor(out=uT[:, fc, :nt], in0=rb[k % 2][:, :nt], in1=rb[k % 2][:, :nt], op=ALU.mult))
                rb_free[k % 2] = t_q
                k += 1
            blk_done[s] = (s_pe, s_pe.n)
            W(PE, t_q)
            for ti in range(nt // 128):
                xs_ = xk % 2
                W(SP, xt_free[xs_])
                t_x = K.dma(SP, xt[xs_][:], x_src(q0 + ti * 128), xl[xs_])
                for half in range(2):
                    j = 2 * xk + half
                    W(PE, pw_free[j % 2])
                    for fc in range(32):
                        ins = PE.matmul(pw[j % 2][:, :], lhsT=uT[:, fc, ti * 128:(ti + 1) * 128], rhs=w2[:, fc, half * 512:(half + 1) * 512],
                                        start=(fc == 0), stop=(fc == 31))
                    t_p = s_pe.inc(ins)
                    W(DVE, t_p)
                    W(DVE, tm_free[j % 2])
                    t_m = s_dv.inc(DVE.tensor_tensor(out=tm2[j % 2][:], in0=pw[j % 2][:], in1=g2[:, m, half * 512:(half + 1) * 512], op=ALU.mult))
                    pw_free[j % 2] = t_m
                    W(POOL, t_m)
                    W(POOL, t_x)
                    if half == 0:
                        W(POOL, xo_free[xs_])
                    t_a = s_pl.inc(POOL.tensor_tensor(out=xo[xs_][:, half * 512:(half + 1) * 512], in0=tm2[j % 2][:], in1=xt[xs_][:, half * 512:(half + 1) * 512], op=ALU.add))
                    tm_free[j % 2] = t_a
                xt_free[xs_] = t_a
                t_fin = t_a
                if final_g is not None:
                    W(ACT, t_a)
                    t1 = s_ac.inc(ACT.activation(out=junk[:], in_=xo[xs_][:], func=AF.Square, accum_out=st4[:, 0:1]))
                    W(DVE, t1)
                    t2 = s_dv.inc(DVE.tensor_scalar(out=st4[:, 1:2], in0=st4[:, 0:1], scalar1=1.0 / D, scalar2=EPS, op0=ALU.mult, op1=ALU.add))
                    W(ACT, t2)
                    t3 = s_ac.inc(ACT.activation(out=st4[:, 2:3], in_=st4[:, 1:2], func=AF.Sqrt))
                    W(DVE, t3)
                    t4 = s_dv.inc(DVE.reciprocal(out=st4[:, 3:4], in_=st4[:, 2:3]))
                    W(DVE, t4)
                    t_fin = s_dv.inc(DVE.scalar_tensor_tensor(out=xo[xs_][:], in0=xo[xs_][:], scalar=st4[:, 3:4], in1=fg[:], op0=ALU.mult, op1=ALU.mult))
                W(SP, t_fin)
                xo_free[xs_] = K.dma(SP, x_dst(q0 + ti * 128), xo[xs_][:], stx[xs_])
                xk += 1
            uT_free = (s_pe, s_pe.n)
            if after_block is not None:
                after_block(bi, [xo_free[0], xo_free[1]])
        K.barrier([(s_, s_.n) for s_ in stx])


def phase_halo(K, pfx):
    nc = K.nc
    PE, ACT, DVE, POOL, SP = K.PE, K.ACT, K.DVE, K.POOL, K.SP
    dr = K.dram
    xg, x1, xna, sel = K.xg_at, K.x1_at, dr["x_na2"], dr["sel"]
    with ExitStack() as es:
        sb, ps = mk_alloc(nc, es, pfx)
        selb = sb("sel", [128, 8])
        cand = [sb(f"c{i}", [128, 1024]) for i in range(4)]
        acc = [sb(f"a{i}", [128, 1024]) for i in range(2)]
        ld = [K.S("ld0"), K.S("ld1"), K.S("ld3"), K.S("ld4")]
        lc = K.S("ld2")
        s_dv = K.S("dv")
        st = [K.S("st0"), K.S("st1")]
        so = K.S("st2")
        t_c = K.dma(SP, selb[:], sel.partition_broadcast(128), lc)
        for q in range(0, T, 128):
            t_own = K.dma(SP, xna[256 + q:256 + q + 128, :], x1(q), so)
        W(DVE, t_c)
        jobs = []
        for u in range(2):
            jobs.append((128 * u, [xg(2048 * r + 1792 + 128 * u) for r in range(4)], 0))
        for u in range(2):
            jobs.append((256 + T + 128 * u, [xg(2048 * r + 128 * u) for r in range(4)], 4))
        dv_prev = None
        st_t = [None, None]
        for n, (row0, srcs, c0) in enumerate(jobs):
            W(SP, dv_prev)
            lts = [K.dma(SP, cand[r][:], srcs[r], ld[r]) for r in range(4)]
            W(DVE, lts)
            W(DVE, st_t[n % 2])
            t = s_dv.inc(DVE.tensor_scalar(out=acc[n % 2][:], in0=cand[0][:], scalar1=selb[:, c0:c0 + 1], scalar2=0.0, op0=ALU.mult, op1=ALU.add))
            for r in range(1, 4):
                W(DVE, t)
                t = s_dv.inc(DVE.scalar_tensor_tensor(out=acc[n % 2][:], in0=cand[r][:], scalar=selb[:, c0 + r:c0 + r + 1], in1=acc[n % 2][:],
                                                      op0=ALU.mult, op1=ALU.add))
            dv_prev = t
            W(SP, t)
            st_t[n % 2] = K.dma(SP, xna[row0:row0 + 128, :], acc[n % 2][:], st[n % 2])
        K.barrier([t_own, st_t[0], st_t[1]])


W_NAMES = ["w_mod", "b_mod", "norm1_g", "norm2_g", "w_in", "gqa_q_norm", "gqa_k_norm", "mla_kv_norm", "mla_w_uk", "mla_w_uv",
           "w_o_gqa", "w_o_na", "w_o_mla", "w_out", "w_mlp1", "w_mlp2"]
W_SHAPES = {"w_mod": [D, 6 * D], "b_mod": [6 * D], "norm1_g": [D], "norm2_g": [D], "w_in": [D, WIN], "gqa_q_norm": [64], "gqa_k_norm": [64],
            "mla_kv_norm": [256], "mla_w_uk": [256, 512], "mla_w_uv": [256, 512], "w_o_gqa": [512, D], "w_o_na": [512, D], "w_o_mla": [512, D],
            "w_out": [D, D], "w_mlp1": [D, 4 * D], "w_mlp2": [4 * D, D]}
DEPTH = 2


def emit_layer(K, L, src, with_ctx, final, sfx):
    dr = K.dram
    NQ = T + C
    phase_mod(K, L, "md" + sfx)
    phase_norm(K, [(src["x_all"], S, dr["hT_all"][:, 0:S], 0, 0, 1), (src["ctx_in"], C, dr["hT_all"][:, S:SK], 1, 0, 1),
                   (src["x_na"], NAT, dr["hT_na"], 0, 0, 1)], "n1" + sfx)
    phase_proj(K, L, "pj" + sfx, with_ctx)
    qbl = [(512 * i, 512) for i in range(4)]
    heads = []

    def wide_blocks(Y, h):
        blocks = [dict(q0=q0, nq=1024, tiles=[(k, None) for k in range(66)],
                       yts=[Y[64 * h:64 * h + 64, q0:q0 + 512], Y[64 * h:64 * h + 64, q0 + 512:q0 + 1024]]) for q0 in (0, 1024)]
        if with_ctx:
            blocks.append(dict(q0=T, nq=C, tiles=[(64, None), (65, None)], yts=[Y[64 * h:64 * h + 64, T:NQ]]))
        return blocks

    for h in range(8):
        heads.append(dict(kt=dr["GKT"][h // 4], v=dr["GV"][:, h // 4, :], qt=dr["GQT"][h], dk=64, scale=0.125, nk=SK, blocks=wide_blocks(dr["YAT"], h)))
    for h in range(8):
        heads.append(dict(kt=dr["MKT"][h], v=dr["MV"][:, h, :], qt=dr["MQT"][h], dk=96, scale=96 ** -0.5, nk=SK, blocks=wide_blocks(dr["YCT"], h)))
    phase_attn(K, heads, "at" + sfx, SK)
    heads = []
    var = [0, 1, 1, 2]
    for h in range(8):
        blocks = []
        for i, (q0, nq) in enumerate(qbl):
            tiles = [(4 * i + m, src["nabias"][var[i], h, m]) for m in range(8)] + [(20, None), (21, None)]
            blocks.append(dict(q0=q0, nq=nq, tiles=tiles, yts=[dr["YBT"][64 * h:64 * h + 64, q0:q0 + nq]]))
        if with_ctx:
            blocks.append(dict(q0=T, nq=C, tiles=[(20, None), (21, None)], yts=[dr["YBT"][64 * h:64 * h + 64, T:NQ]]))
        heads.append(dict(kt=dr["NKT"][h], v=dr["NV"][:, h, :], qt=dr["NQT"][h], dk=64, scale=1.0, nk=NAK, blocks=blocks))
    phase_attn(K, heads, "na" + sfx, NAK)
    mblocks = [(dr["hT_na"][:, 256 + 512 * i:256 + 512 * (i + 1)], 512 * i, 512, 0) for i in range(4)]
    if with_ctx:
        mblocks.append((dr["hT_all"][:, S:SK], T, C, 1))

    def x_src(q):
        if q >= T:
            return src["ctx_in"][q - T:q - T + 128, :]
        return src["x_own"](q) if callable(src["x_own"]) else src["x_own"][q:q + 128, :]

    def xs1_at(q):
        return dr["xs1"][q:q + 128, :]

    phase_merge(K, L, "mg" + sfx, mblocks, x_src, xs1_at)
    njobs = [(dr["xs1"][0:T, :], T, dr["h2T"][:, 0:T], 0, 3, 4)]
    if with_ctx:
        njobs.append((dr["xs1"][T:NQ, :], C, dr["h2T"][:, T:NQ], 1, 3, 4))
    phase_norm(K, njobs, "n2" + sfx)
    fblocks = [(256 * i, 256, 0) for i in range(8)]
    if with_ctx:
        fblocks.append((T, C, 1))
    phase_mlp(K, L, "ml" + sfx, fblocks, xs1_at, src["x_dst"], L.get("final_norm_g") if final else None, after_block=src.get("after_block"))


def build_fused():
    nc = bass.Bass("TRN2", target_bir_lowering=False)
    K = KB(nc)
    NQ = T + C
    dr = K.dram

    def inp(name, shape, dt=F32):
        dr[name] = nc.dram_tensor(name, shape, dt, kind="ExternalInput").ap()

    def internal(name, shape, dt=BF16):
        dr[name] = nc.dram_tensor(name, shape, dt).ap()

    inp("x_all", [S, D]); inp("x_own", [T, D]); inp("x_na", [NAT, D]); inp("ctx_in", [C, D]); inp("cvec", [2, D])
    Wst = {}
    for n in W_NAMES:
        Wst[n] = nc.dram_tensor(n, [DEPTH] + W_SHAPES[n], F32, kind="ExternalInput").ap()
    fng = nc.dram_tensor("final_norm_g", [D], F32, kind="ExternalInput").ap()
    for n in ("ident_f", "onesbd_f", "ones_f"):
        inp(n, [128, 128])
    for n in ("pt128", "pt96", "pt32", "ident_b"):
        inp(n, [128, 128], BF16)
    inp("cosA_all", [128, S]); inp("sinA_all", [128, S]); inp("cosA_own", [128, T]); inp("sinA_own", [128, T])
    inp("cosM_all", [32, S]); inp("sinM_all", [32, S]); inp("cosM_own", [96, T]); inp("sinM_own", [96, T])
    inp("nabias", [DEPTH, 3, 8, 8, 128, 512])
    inp("sel", [8])
    dr["xout"] = nc.dram_tensor("xout", [T, D], F32, kind="ExternalOutput").ap()
    internal("modv", [2, 6, D], F32)
    internal("hT_all", [D, SK]); internal("hT_na", [D, NAT])
    internal("GKT", [2, 64, SK]); internal("CKVT", [256, SK]); internal("MKT", [8, 96, SK])
    internal("MV", [SK, 8, 65]); internal("GV", [SK, 2, 65])
    internal("NKT", [8, 64, NAK]); internal("NV", [NAK, 8, 65])
    internal("GQT", [8, 64, NQ]); internal("NQT", [8, 64, NQ]); internal("MQT", [8, 96, NQ])
    internal("YAT", [512, NQ]); internal("YBT", [512, NQ]); internal("YCT", [512, NQ])
    internal("xs1", [NQ, D], F32); internal("h2T", [D, NQ])
    NCH = 8
    x1c = [nc.dram_tensor(f"x1c{k}", [256, D], F32) for k in range(NCH)]
    xgc = [nc.dram_tensor(f"xgc{k}", [4 * 256, D], F32) for k in range(NCH)]
    internal("c1loc", [C, D], F32); internal("x_na2", [NAT, D], F32)
    internal("rls", [4, 512], F32)

    def x1_at(q):
        return x1c[q // 256].ap()[q % 256:q % 256 + 128, :]

    def xg_at(t0):
        r, k, off = t0 // 2048, (t0 % 2048) // 256, t0 % 256
        return xgc[k].ap()[r * 256 + off:r * 256 + off + 128, :]

    K.x1_at, K.xg_at = x1_at, xg_at

    for l in range(DEPTH):
        L = {n: Wst[n][l] for n in W_NAMES}
        final = (l == DEPTH - 1)
        with_ctx = not final
        if final:
            L["final_norm_g"] = fng
        if l == 0:
            src = dict(x_all=dr["x_all"], x_own=dr["x_own"], x_na=dr["x_na"], ctx_in=dr["ctx_in"])
        else:
            src = dict(x_all=(lambda i: xg_at(128 * i)), x_own=x1_at, x_na=dr["x_na2"], ctx_in=dr["c1loc"])
        src["nabias"] = dr["nabias"][l]
        if final:
            src["x_dst"] = lambda q: dr["xout"][q:q + 128, :]
        else:
            src["x_dst"] = lambda q: (x1_at(q) if q < T else dr["c1loc"][q - T:q - T + 128, :])
        cc = K.S("cc")
        cc_tok = [None]
        if not final:
            def after_block(bi, store_toks):
                if bi >= NCH:
                    return
                W(K.POOL, store_toks)
                ins = K.POOL.collective_compute("AllGather", mybir.AluOpType.bypass, replica_groups=[[0, 1, 2, 3], [4, 5, 6, 7]],
                                                ins=[x1c[bi].ap().opt()], outs=[xgc[bi].ap().opt()])
                cc_tok[0] = cc.inc(ins)

            src["after_block"] = after_block
        emit_layer(K, L, src, with_ctx, final, f"{l}_")
        if not final:
            K.barrier([cc_tok[0]])
            phase_halo(K, f"hl{l}_")
    K.semcounts = {n: s.n for n, s in K.sems.items()}
    return nc, K


def _rope_tables():
    t = np.arange(S, dtype=np.int32)
    row = (t // 64).astype(np.float32)
    col = (t % 64).astype(np.float32)

    def tabs(rot_dim):
        half = rot_dim // 2
        inv = (10000.0 ** (-np.arange(0, half, 2, dtype=np.float32) / np.float32(half))).astype(np.float32)
        ar = (row[:, None] * inv).astype(np.float32)
        ac = (col[:, None] * inv).astype(np.float32)
        cos = np.concatenate([np.cos(ar), np.cos(ar), np.cos(ac), np.cos(ac)], axis=1).T.astype(np.float32)
        sin = np.concatenate([np.sin(ar), np.sin(ar), np.sin(ac), np.sin(ac)], axis=1).T.astype(np.float32)
        return np.ascontiguousarray(cos), np.ascontiguousarray(sin)

    return tabs(64), tabs(32)


def _rot_matrix(n):
    q = n // 4
    P = np.zeros((n, n), np.float32)
    for base in (0, 2 * q):
        for i in range(q):
            P[base + i, base + q + i] = -1.0
            P[base + q + i, base + i] = 1.0
    return P


def _consts():
    c = {}
    c["ident_f"] = np.eye(128, dtype=np.float32)
    c["ones_f"] = np.ones((128, 128), np.float32)
    bd = np.zeros((128, 128), np.float32)
    bd[:64, :64] = 1.0
    bd[64:, 64:] = 1.0
    c["onesbd_f"] = bd
    P64 = _rot_matrix(64)
    P32 = _rot_matrix(32)
    pt128 = np.zeros((128, 128), np.float32)
    pt128[:64, :64] = P64.T
    pt128[64:, 64:] = P64.T
    pt96 = np.zeros((128, 128), np.float32)
    pt96[64:96, 64:96] = P32.T
    pt32 = np.zeros((128, 128), np.float32)
    pt32[:32, :32] = P32.T
    c["ident_b"] = np.eye(128, dtype=np.float32).astype(ml_dtypes.bfloat16)
    c["pt128"] = pt128.astype(ml_dtypes.bfloat16)
    c["pt96"] = pt96.astype(ml_dtypes.bfloat16)
    c["pt32"] = pt32.astype(ml_dtypes.bfloat16)
    return c


def _na_bias_tables(rpb, j):
    out = np.empty((3, 8, 8, 128, 512), np.float32)
    kcol = np.arange(64)[None, :, None, None]
    qcol = np.arange(64)[None, None, None, :]
    m = np.arange(16)[:, None, None, None]
    a = np.arange(8)[None, None, :, None]
    cs = np.clip(qcol - 8, 0, 48)
    colok = (kcol >= cs) & (kcol < cs + 16)
    cidx = np.clip(kcol - qcol + 15, 0, 30)
    for v, i in enumerate((0, 1, 3)):
        r = 32 * j + 8 * i + a
        k = 32 * j + 8 * i - 4 + m
        rs = np.clip(r - 4, 0, 120)
        ok = (k >= 0) & (k < 128) & (k >= rs) & (k < rs + 8) & colok
        ridx = np.clip(k - r + 7, 0, 14)
        ridx_b = np.broadcast_to(ridx, ok.shape)
        cidx_b = np.broadcast_to(cidx, ok.shape)
        vals = rpb[:, ridx_b, cidx_b]
        tab = np.where(ok[None], vals, np.float32(NEG)).astype(np.float32)
        out[v] = tab.reshape(8, 8, 128, 512)
    return out


def _core_inputs(x_full, ctx_full, c, c_ctx, Wd, consts, ropes):
    (cosA, sinA), (cosM, sinM) = ropes
    maps = []
    cosA2 = np.ascontiguousarray(np.concatenate([cosA, cosA], 0))
    sinA2 = np.ascontiguousarray(np.concatenate([sinA, sinA], 0))
    shared = dict(consts)
    for n in W_NAMES:
        shared[n] = np.ascontiguousarray(Wd[n])
    shared["final_norm_g"] = np.ascontiguousarray(Wd["final_norm_g"])
    shared["cosA_all"] = cosA2
    shared["sinA_all"] = sinA2
    shared["cosM_all"] = cosM
    shared["sinM_all"] = sinM
    rpb = np.asarray(Wd["na_rpb"], np.float32)
    nab = [np.stack([_na_bias_tables(rpb[l], j) for l in range(rpb.shape[0])], 0) for j in range(4)]
    for core in range(8):
        b, j = core // 4, core % 4
        t0 = T * j
        d = dict(shared)
        d["x_all"] = np.ascontiguousarray(x_full[b])
        d["x_own"] = np.ascontiguousarray(x_full[b, t0:t0 + T])
        xna = np.zeros((NAT, D), np.float32)
        lo = (32 * j - 4) * 64
        hi = lo + NAT
        slo, shi = max(lo, 0), min(hi, S)
        xna[slo - lo:shi - lo] = x_full[b, slo:shi]
        d["x_na"] = xna
        d["ctx_in"] = np.ascontiguousarray(ctx_full[b])
        d["cvec"] = np.ascontiguousarray(np.stack([c[b], c_ctx]))
        d["cosA_own"] = np.ascontiguousarray(cosA2[:, t0:t0 + T])
        d["sinA_own"] = np.ascontiguousarray(sinA2[:, t0:t0 + T])
        d["cosM_own"] = np.ascontiguousarray(np.concatenate([np.ones((64, T), np.float32), cosM[:, t0:t0 + T]], 0))
        d["sinM_own"] = np.ascontiguousarray(np.concatenate([np.zeros((64, T), np.float32), sinM[:, t0:t0 + T]], 0))
        d["nabias"] = nab[j]
        sel = np.zeros(8, np.float32)
        if j > 0:
            sel[j - 1] = 1.0
        if j < 3:
            sel[4 + j + 1] = 1.0
        d["sel"] = sel
        maps.append(d)
    return maps


_PROG = []


def kernel(**inputs):
    Wd = {k: np.asarray(v, np.float32) for k, v in inputs.items()}
    if not _PROG:
        _PROG.append(build_fused()[0])
    nc = _PROG[0]
    maps = _core_inputs(Wd["x"], Wd["ctx"], Wd["c"], Wd["c_ctx"], Wd, _consts(), _rope_tables())
    res = run_bass_kernel_spmd(nc, maps, core_ids=list(range(8)))
    outs = [r["xout"] for r in res.results]
    x = np.stack([np.concatenate([outs[4 * b + j] for j in range(4)], 0) for b in range(2)], 0)
    return np.ascontiguousarray(x.astype(np.float32))
```

```python
from contextlib import ExitStack
import os
import numpy as np
import ml_dtypes
import concourse.bass as bass
import concourse.mybir as mybir
from concourse.bass_utils import run_bass_kernel_spmd

F32 = mybir.dt.float32
BF16 = mybir.dt.bfloat16
AF = mybir.ActivationFunctionType
ALU = mybir.AluOpType

D = 1024
S = 8192
C = 256
SK = S + C
T = 2048
NAT = 2560
NAK = NAT + C
EPS = 1e-6
NEG = -30000.0
O_GQ, O_GK, O_GV, O_NQ, O_NK, O_NV, O_MQ, O_CKV, O_KR, O_GATE = 0, 512, 640, 768, 1280, 1792, 2304, 3072, 3328, 3360
WIN = 6432


class Sem:
    def __init__(self, nc, name):
        self.h = nc.alloc_semaphore(name)
        self.n = 0

    def inc(self, ins, k=1):
        ins.then_inc(self.h, k)
        self.n += k
        return (self, self.n)


def W(eng, tok):
    if tok is None:
        return
    if isinstance(tok, list):
        for t in tok:
            W(eng, t)
        return
    s, v = tok
    if v > 0:
        eng.wait_ge(s.h, v)


class KB:
    def __init__(self, nc):
        self.nc = nc
        self.PE, self.ACT, self.DVE, self.POOL, self.SP = nc.tensor, nc.scalar, nc.vector, nc.gpsimd, nc.sync
        self.sems = {}
        self.dram = {}

    def S(self, name):
        if name not in self.sems:
            self.sems[name] = Sem(self.nc, name)
        return self.sems[name]

    def dma(self, eng, out, in_, sem, slow=False):
        if slow:
            ins = eng.dma_start(out=out, in_=in_, allow_slow_non_contiguous=True)
        else:
            ins = eng.dma_start(out=out, in_=in_)
        return sem.inc(ins, 16)

    def barrier(self, toks):
        for e in (self.PE, self.ACT, self.DVE, self.POOL, self.SP):
            W(e, toks)


def mk_alloc(nc, es, pfx):
    def sb(name, shape, dt=F32):
        return es.enter_context(nc.sbuf_tensor(pfx + name, shape, dt))

    def ps(name, shape, dt=F32):
        return es.enter_context(nc.psum_tensor(pfx + name, shape, dt))

    return sb, ps


def phase_mod(K, L, pfx):
    nc = K.nc
    PE, ACT, DVE, POOL, SP = K.PE, K.ACT, K.DVE, K.POOL, K.SP
    with ExitStack() as es:
        sb, ps = mk_alloc(nc, es, pfx)
        cT = sb("cT", [128, 8, 2])
        sT = sb("sT", [128, 8, 2])
        NWB = 4
        wm = [sb(f"w{i}", [128, 8, 512]) for i in range(NWB)]
        bm = sb("b", [2, 6144])
        mrow = sb("m", [2, 6144])
        ng = sb("ng", [2, 2, 1024])
        mv = sb("mv", [2, 6, 1024])
        pm = [ps(f"p{i}", [2, 512]) for i in range(2)]
        ld = K.S("ld0")
        wl = [K.S("ld1"), K.S("ld2"), K.S("ld3"), K.S("ld4")]
        s_pe, s_ac, s_dv, st = K.S("pe"), K.S("ac"), K.S("dv"), K.S("st0")
        for m in range(2):
            K.dma(SP, cT[:, :, m], K.dram["cvec"][m].rearrange("(c p) -> p c", p=128), ld, slow=True)
        K.dma(SP, bm[:], L["b_mod"].partition_broadcast(2), ld)
        K.dma(SP, ng[:, 0, :], L["norm1_g"].partition_broadcast(2), ld)
        t_ld = K.dma(SP, ng[:, 1, :], L["norm2_g"].partition_broadcast(2), ld)
        W(ACT, t_ld)
        t_s = s_ac.inc(ACT.activation(out=sT[:].rearrange("p c m -> p (c m)"), in_=cT[:].rearrange("p c m -> p (c m)"), func=AF.Silu))
        W(PE, t_s)
        pe_t = [None] * 12
        dv_t = [None] * 12
        wsrc = L["w_mod"]
        w_t = [None] * 12

        def load_w(g):
            if g >= NWB:
                W(SP, pe_t[g - NWB])
            w_t[g] = K.dma(SP, wm[g % NWB][:], wsrc[:, g * 512:(g + 1) * 512].rearrange("(c p) n -> p c n", p=128), wl[g % NWB])

        for g in range(min(NWB - 1, 12)):
            load_w(g)
        for g in range(12):
            if g + NWB - 1 < 12:
                load_w(g + NWB - 1)
            W(PE, w_t[g])
            if g >= 2:
                W(PE, dv_t[g - 2])
            for c in range(8):
                ins = PE.matmul(pm[g % 2][:], lhsT=sT[:, c, :], rhs=wm[g % NWB][:, c, :], start=(c == 0), stop=(c == 7))
            pe_t[g] = s_pe.inc(ins)
            W(DVE, pe_t[g])
            if g == 0:
                W(DVE, t_ld)
            dv_t[g] = s_dv.inc(DVE.tensor_tensor(out=mrow[:, g * 512:(g + 1) * 512], in0=pm[g % 2][:], in1=bm[:, g * 512:(g + 1) * 512], op=ALU.add))
        W(DVE, dv_t[11])
        sl = lambda i: mrow[:, i * 1024:(i + 1) * 1024]
        DVE.scalar_tensor_tensor(out=mv[:, 0, :], in0=sl(1), scalar=1.0, in1=ng[:, 0, :], op0=ALU.add, op1=ALU.mult)
        DVE.tensor_copy(out=mv[:, 1, :], in_=sl(0))
        DVE.tensor_copy(out=mv[:, 2, :], in_=sl(2))
        DVE.scalar_tensor_tensor(out=mv[:, 3, :], in0=sl(4), scalar=1.0, in1=ng[:, 1, :], op0=ALU.add, op1=ALU.mult)
        DVE.tensor_copy(out=mv[:, 4, :], in_=sl(3))
        t_f = s_dv.inc(DVE.tensor_copy(out=mv[:, 5, :], in_=sl(5)))
        W(SP, t_f)
        t_st = K.dma(SP, K.dram["modv"], mv[:], st)
        K.barrier([t_st])


def phase_norm(K, jobs, pfx):
    nc = K.nc
    PE, ACT, DVE, POOL, SP = K.PE, K.ACT, K.DVE, K.POOL, K.SP
    tiles = []
    for ji, (src, ntok, dst, m, ia, ish) in enumerate(jobs):
        for i in range(ntok // 128):
            tiles.append((ji, i))
    NTI = len(tiles)
    with ExitStack() as es:
        sb, ps = mk_alloc(nc, es, pfx)
        NX = 6
        xt = [sb(f"xt{i}", [128, 1024]) for i in range(NX)]
        junk = sb("junk", [128, 1024])
        ss = sb("ss", [128, NTI])
        r1 = sb("r1", [128, NTI])
        r2 = sb("r2", [128, NTI])
        rstd = sb("rstd", [128, NTI])
        NXN = 3
        xn = [sb(f"xn{i}", [128, 1024]) for i in range(NXN)]
        hb = [sb(f"hb{i}", [128, 8, 512], BF16) for i in range(2)]
        ident = sb("ident", [128, 128])
        acol = sb("acol", [128, 2, 2, 8])
        pT = [ps(f"pT{i}", [128, 8, 128]) for i in range(2)]
        lds = [K.S("ld0"), K.S("ld1"), K.S("ld3"), K.S("ld4"), K.S("ld5"), K.S("ld6")]
        ldc = K.S("ld2")
        s_pe, s_ac, s_dv = K.S("pe"), K.S("ac"), K.S("dv")
        sts = [K.S("gs0"), K.S("gs1")]
        t_c = K.dma(SP, ident[:], K.dram["ident_f"], ldc)
        mods = sorted(set((j[3], j[4], j[5]) for j in jobs))
        assert len(set(m for m, _, _ in mods)) == len(mods)
        for (m, ia, ish) in mods:
            K.dma(SP, acol[:, m, 0, :], K.dram["modv"][m, ia].rearrange("(c p) -> p c", p=128), ldc, slow=True)
            t_c = K.dma(SP, acol[:, m, 1, :], K.dram["modv"][m, ish].rearrange("(c p) -> p c", p=128), ldc, slow=True)
        act_t = [None] * NTI
        for n, (ji, i) in enumerate(tiles):
            src = jobs[ji][0]
            if n >= NX:
                W(SP, act_t[n - NX])
            t_l = K.dma(SP, xt[n % NX][:], src(i) if callable(src) else src[i * 128:(i + 1) * 128, :], lds[n % NX])
            W(ACT, t_l)
            act_t[n] = s_ac.inc(ACT.activation(out=junk[:], in_=xt[n % NX][:], func=AF.Square, accum_out=ss[:, n:n + 1]))
        W(DVE, act_t[NTI - 1])
        t1 = s_dv.inc(DVE.tensor_scalar(out=r1[:], in0=ss[:], scalar1=1.0 / D, scalar2=EPS, op0=ALU.mult, op1=ALU.add))
        W(ACT, t1)
        t2 = s_ac.inc(ACT.activation(out=r2[:], in_=r1[:], func=AF.Sqrt))
        W(DVE, t2)
        t3 = s_dv.inc(DVE.reciprocal(out=rstd[:], in_=r2[:]))
        W(ACT, t3)
        W(SP, t2)
        W(PE, t_c)
        W(DVE, t_c)
        a_t = [None] * NTI
        p_t = [None] * NTI
        v_t = [None] * NTI
        st_t = {}
        blk = -1
        blk_of = []
        prev_key = None
        for n, (ji, i) in enumerate(tiles):
            key = (ji, i // 4)
            if key != prev_key:
                blk += 1
                prev_key = key
            blk_of.append(blk)
        for n, (ji, i) in enumerate(tiles):
            src, ntok, dst, m, ia, ish = jobs[ji]
            b = blk_of[n]
            if n >= NX:
                W(SP, a_t[n - NX])
            t_l = K.dma(SP, xt[n % NX][:], src(i) if callable(src) else src[i * 128:(i + 1) * 128, :], lds[n % NX])
            W(ACT, t_l)
            if n >= NXN:
                W(ACT, p_t[n - NXN])
            a_t[n] = s_ac.inc(ACT.activation(out=xn[n % NXN][:], in_=xt[n % NXN if False else n % NX][:], func=AF.Copy, scale=rstd[:, n:n + 1]))
            W(PE, a_t[n])
            if n >= 2:
                W(PE, v_t[n - 2])
            for c in range(8):
                ins = PE.transpose(out=pT[n % 2][:, c, :], in_=xn[n % NXN][:, c * 128:(c + 1) * 128], identity=ident[:])
            p_t[n] = s_pe.inc(ins)
            W(DVE, p_t[n])
            if (i % 4 == 0) and (b - 2) in st_t:
                W(DVE, st_t[b - 2])
            for c in range(8):
                ins = DVE.tensor_scalar(out=hb[b % 2][:, c, (i % 4) * 128:(i % 4 + 1) * 128], in0=pT[n % 2][:, c, :],
                                        scalar1=acol[:, m, 0, c:c + 1], scalar2=acol[:, m, 1, c:c + 1], op0=ALU.mult, op1=ALU.add)
            v_t[n] = s_dv.inc(ins)
            last_in_blk = (n + 1 == NTI) or (blk_of[n + 1] != b)
            if last_in_blk:
                nt = (i % 4 + 1) * 128
                t0 = (i // 4) * 512
                W(POOL, v_t[n])
                st_t[b] = K.dma(POOL, dst[:, t0:t0 + nt].rearrange("(c p) t -> p c t", p=128), hb[b % 2][:, :, 0:nt], sts[b % 2])
        K.barrier([st_t[blk], st_t.get(blk - 1)])


def phase_proj(K, L, pfx, with_ctx):
    nc = K.nc
    PE, ACT, DVE, POOL, SP = K.PE, K.ACT, K.DVE, K.POOL, K.SP
    dr = K.dram
    with ExitStack() as es:
        sb, ps = mk_alloc(nc, es, pfx)
        win = sb("win", [128, 8, O_GATE], BF16)
        wuk = sb("wuk", [128, 2, 512], BF16)
        wuv = sb("wuv", [128, 2, 512], BF16)
        onesbd = sb("onesbd", [128, 128])
        ones = sb("ones", [128, 128])
        pt128 = sb("pt128", [128, 128], BF16)
        pt96 = sb("pt96", [128, 128], BF16)
        pt32 = sb("pt32", [128, 128], BF16)
        gq = sb("gq", [128, 1])
        gk = sb("gk", [128, 1])
        kvg = sb("kvg", [128, 2])
        hblk = [sb(f"h{i}", [128, 8, 512], BF16) for i in range(2)]
        ckvn = sb("ckvn", [128, 2, 512], BF16)
        sqf = sb("sqf", [128, 512]); sqf2 = sb("sqf2", [128, 512])
        qf = sb("qf", [128, 512]); qf2 = sb("qf2", [128, 512])
        sd = sb("sd", [128, 512]); rs = sb("rs", [128, 512]); qn = sb("qn", [128, 512])
        t1b = sb("t1", [128, 512]); t2b = sb("t2", [128, 512])
        cosb = [sb(f"cos{i}", [128, 512]) for i in range(2)]
        sinb = [sb(f"sin{i}", [128, 512]) for i in range(2)]
        qb = sb("qb", [128, 512], BF16)
        outb = [sb(f"ob{i}", [128, 512], BF16) for i in range(2)]
        vout = [sb(f"vo{i}", [128, 8, 65], BF16) for i in range(2)]
        acc = [ps(f"acc{i}", [128, 512]) for i in range(2)]
        acc2 = ps("acc2", [128, 512])
        pss = ps("pss", [128, 512])
        prot = ps("prot", [128, 512])
        ptm = [ps(f"ptm{i}", [128, 512]) for i in range(2)]
        wl = K.S("ld2")
        gw = K.S("gw")
        hl = [K.S("ld0"), K.S("ld1")]
        tl = [K.S("ld3"), K.S("ld4")]
        s_pe, s_ac, s_dv, s_pl = K.S("pe"), K.S("ac"), K.S("dv"), K.S("pl")
        sto = [K.S("gs0"), K.S("gs1")]
        stv = [K.S("gs2"), K.S("gs3")]
        stc = K.S("gs4")
        for c in range(8):
            K.dma(POOL, win[:, c, :], L["w_in"][c * 128:(c + 1) * 128, 0:O_GATE], gw)
        K.dma(POOL, wuk[:], L["mla_w_uk"].rearrange("(r p) n -> p r n", p=128), gw)
        t_gw = K.dma(POOL, wuv[:], L["mla_w_uv"].rearrange("(r p) n -> p r n", p=128), gw)
        K.dma(SP, onesbd[:], dr["onesbd_f"], wl)
        K.dma(SP, ones[:], dr["ones_f"], wl)
        K.dma(SP, pt128[:], dr["pt128"], wl)
        K.dma(SP, pt96[:], dr["pt96"], wl)
        K.dma(SP, pt32[:], dr["pt32"], wl)
        for hh in range(2):
            K.dma(SP, gq[hh * 64:(hh + 1) * 64, :], L["gqa_q_norm"].rearrange("(p o) -> p o", o=1), wl)
            K.dma(SP, gk[hh * 64:(hh + 1) * 64, :], L["gqa_k_norm"].rearrange("(p o) -> p o", o=1), wl)
        t_w = K.dma(SP, kvg[:], L["mla_kv_norm"].rearrange("(r p) -> p r", p=128), wl, slow=True)
        for i in range(2):
            DVE.memset(vout[i][:], 1.0)
        t_ms = s_dv.inc(DVE.memset(qn[:], 0.0))
        for e in (PE, ACT, DVE, POOL):
            W(e, t_w)
            W(e, t_gw)
        W(ACT, t_ms)

        st = {"k": 0, "rk": 0, "vk": 0, "acc_free": [None, None], "ob_free": [None, None], "tab_free": [None, None],
              "vo_free": [None, None], "ptm_free": [None, None], "hb_tok": None}

        def store(eng, dst, src, sem):
            return K.dma(eng, dst, src, sem)

        import os
        LIMIT = int(os.environ.get("PROJ_LIMIT", "1000000"))
        units = [0]

        def over():
            units[0] += 1
            return units[0] > LIMIT

        def fm_job(chunks, M, nt, norm_g, rope, dsts, oscale=1.0):
            if over():
                return
            k = st["k"]; st["k"] += 1
            a = acc[k % 2]
            W(PE, st["acc_free"][k % 2])
            W(PE, st["hb_tok"])
            for ci, (lt, rh) in enumerate(chunks):
                ins = PE.matmul(a[:M, :nt], lhsT=lt, rhs=rh, start=(ci == 0), stop=(ci == len(chunks) - 1))
            t_main = s_pe.inc(ins)
            ob = outb[k % 2]
            if norm_g is None and rope is None:
                W(ACT, t_main)
                W(ACT, st["ob_free"][k % 2])
                t_out = s_ac.inc(ACT.activation(out=ob[:M, :nt], in_=a[:M, :nt], func=AF.Copy, scale=float(oscale)))
                st["acc_free"][k % 2] = t_out
            else:
                if rope is not None:
                    r = st["rk"]; st["rk"] += 1
                    PT, cos_ap, sin_ap = rope
                    W(SP, st["tab_free"][r % 2])
                    K.dma(SP, cosb[r % 2][:M, :nt], cos_ap, tl[r % 2])
                    t_tab = K.dma(SP, sinb[r % 2][:M, :nt], sin_ap, tl[r % 2])
                W(DVE, t_main)
                t_qf = s_dv.inc(DVE.tensor_copy(out=qf[:M, :nt], in_=a[:M, :nt]))
                t_cur = t_qf
                cur = qf
                free_toks = [t_qf]
                if norm_g is not None:
                    W(ACT, t_qf)
                    t_sq = s_ac.inc(ACT.activation(out=sqf[:M, :nt], in_=qf[:M, :nt], func=AF.Square))
                    W(PE, t_sq)
                    t_ss = s_pe.inc(PE.matmul(pss[:M, :nt], lhsT=onesbd[:M, :M], rhs=sqf[:M, :nt], start=True, stop=True))
                    W(ACT, t_ss)
                    W(ACT, t_qf)
                    t_sd = s_ac.inc(ACT.activation(out=sd[:M, :nt], in_=pss[:M, :nt], func=AF.Sqrt, bias=EPS, scale=1.0 / 64))
                    W(DVE, t_sd)
                    t_rs = s_dv.inc(DVE.reciprocal(out=rs[:M, :nt], in_=sd[:M, :nt]))
                    W(DVE, t_rs)
                    if rope is None:
                        W(DVE, st["ob_free"][k % 2])
                        t_out = s_dv.inc(DVE.scalar_tensor_tensor(out=ob[:M, :nt], in0=qf[:M, :nt], scalar=norm_g, in1=rs[:M, :nt], op0=ALU.mult, op1=ALU.mult))
                    else:
                        t_cur = s_dv.inc(DVE.scalar_tensor_tensor(out=qn[:M, :nt], in0=qf[:M, :nt], scalar=norm_g, in1=rs[:M, :nt], op0=ALU.mult, op1=ALU.mult))
                        cur = qn
                st["acc_free"][k % 2] = free_toks
                if rope is not None:
                    W(ACT, t_cur)
                    t_qb = s_ac.inc(ACT.activation(out=qb[:M, :nt], in_=cur[:M, :nt], func=AF.Copy))
                    W(PE, t_qb)
                    t_rot = s_pe.inc(PE.matmul(prot[:M, :nt], lhsT=PT[:M, :M], rhs=qb[:M, :nt], start=True, stop=True))
                    W(POOL, t_cur)
                    W(POOL, t_tab)
                    t_t1 = s_pl.inc(POOL.tensor_tensor(out=t1b[:M, :nt], in0=cur[:M, :nt], in1=cosb[r % 2][:M, :nt], op=ALU.mult))
                    W(DVE, t_rot)
                    W(DVE, t_tab)
                    t_t2 = s_dv.inc(DVE.tensor_tensor(out=t2b[:M, :nt], in0=prot[:M, :nt], in1=sinb[r % 2][:M, :nt], op=ALU.mult))
                    W(DVE, t_t1)
                    W(DVE, t_t2)
                    W(DVE, st["ob_free"][k % 2])
                    t_out = s_dv.inc(DVE.tensor_tensor(out=ob[:M, :nt], in0=t1b[:M, :nt], in1=t2b[:M, :nt], op=ALU.add))
                    st["tab_free"][r % 2] = t_out
            W(POOL, t_out)
            for (dst, r0, r1) in dsts:
                t_st = store(POOL, dst, ob[r0:r1, :nt], sto[k % 2])
            st["ob_free"][k % 2] = t_st

        def ckv_job(hs, nt, dst_ckvt):
            if over():
                st["ckvn_tok"] = None
                return
            k = st["k"]; st["k"] += 1
            a = acc[k % 2]
            W(PE, st["acc_free"][k % 2])
            W(PE, st["hb_tok"])
            W(PE, st.get("acc2_free"))
            for g, aa in enumerate((a, acc2)):
                for c in range(8):
                    ins = PE.matmul(aa[:, :nt], lhsT=win[:, c, O_CKV + g * 128:O_CKV + (g + 1) * 128], rhs=hblk[hs][:, c, :nt], start=(c == 0), stop=(c == 7))
            t_main = s_pe.inc(ins)
            CUT = int(os.environ.get("CKV_CUT", "99"))
            st["ckvn_tok"] = None
            if CUT <= 1:
                return
            W(DVE, t_main)
            DVE.tensor_copy(out=qf[:, :nt], in_=a[:, :nt])
            t_qf = s_dv.inc(DVE.tensor_copy(out=qf2[:, :nt], in_=acc2[:, :nt]))
            W(ACT, t_qf)
            ACT.activation(out=sqf[:, :nt], in_=qf[:, :nt], func=AF.Square)
            t_sq = s_ac.inc(ACT.activation(out=sqf2[:, :nt], in_=qf2[:, :nt], func=AF.Square))
            st["acc_free"][k % 2] = [t_qf]
            st["acc2_free"] = [t_qf]
            if CUT <= 2:
                return
            W(PE, t_sq)
            PE.matmul(pss[:, :nt], lhsT=ones[:], rhs=sqf[:, :nt], start=True, stop=False)
            t_ss = s_pe.inc(PE.matmul(pss[:, :nt], lhsT=ones[:], rhs=sqf2[:, :nt], start=False, stop=True))
            if CUT <= 3:
                return
            W(ACT, t_ss)
            W(ACT, t_qf)
            t_sd = s_ac.inc(ACT.activation(out=sd[:, :nt], in_=pss[:, :nt], func=AF.Sqrt, bias=EPS, scale=1.0 / 256))
            W(DVE, t_sd)
            t_rs = s_dv.inc(DVE.reciprocal(out=rs[:, :nt], in_=sd[:, :nt]))
            if CUT <= 4:
                return
            W(DVE, t_rs)
            W(DVE, st.get("ckvn_free"))
            DVE.scalar_tensor_tensor(out=ckvn[:, 0, :nt], in0=qf[:, :nt], scalar=kvg[:, 0:1], in1=rs[:, :nt], op0=ALU.mult, op1=ALU.mult)
            t_out = s_dv.inc(DVE.scalar_tensor_tensor(out=ckvn[:, 1, :nt], in0=qf2[:, :nt], scalar=kvg[:, 1:2], in1=rs[:, :nt], op0=ALU.mult, op1=ALU.mult))
            if CUT <= 5:
                return
            W(POOL, t_out)
            t_st = store(POOL, dst_ckvt.rearrange("(r p) t -> p r t", p=128), ckvn[:, :, :nt], stc)
            st["ckvn_tok"] = t_out
            st["ckvn_st"] = t_st

        def tm_job(chunks, N, nh, dst):
            if over():
                return
            j = st["vk"]; st["vk"] += 1
            p = ptm[j % 2]
            W(PE, st["ptm_free"][j % 2])
            W(PE, st["hb_tok"])
            for ci, (lt, rh) in enumerate(chunks):
                ins = PE.matmul(p[:, :N], lhsT=lt, rhs=rh, start=(ci == 0), stop=(ci == len(chunks) - 1))
            t_main = s_pe.inc(ins)
            W(ACT, t_main)
            W(ACT, st["vo_free"][j % 2])
            t_o = s_ac.inc(ACT.activation(out=vout[j % 2][:, 0:nh, 0:64], in_=p[:, :N].rearrange("p (h d) -> p h d", d=64), func=AF.Copy))
            st["ptm_free"][j % 2] = t_o
            W(POOL, t_o)
            st["vo_free"][j % 2] = store(POOL, dst, vout[j % 2][:, 0:nh, :], stv[j % 2])

        nblk = [0]
        last_users = [None, None]

        hT_all, hT_na = dr["hT_all"], dr["hT_na"]
        bsrc = [(hT_all[:, tb * 512:tb * 512 + (256 if tb == 16 else 512)], 256 if tb == 16 else 512) for tb in range(17)]
        bsrc += [(hT_na[:, tb * 512:(tb + 1) * 512], 512) for tb in range(5)]
        bsrc += [(hT_na[:, 256 + tb * 512:256 + (tb + 1) * 512], 512) for tb in range(4)]
        btok = [None] * len(bsrc)
        issued = [0]

        def issue_upto(i):
            while issued[0] <= i and issued[0] < len(bsrc):
                b = issued[0]
                src_ap, nt_ = bsrc[b]
                W(SP, last_users[b % 2])
                btok[b] = K.dma(SP, hblk[b % 2][:, :, :nt_], src_ap.rearrange("(c p) t -> p c t", p=128), hl[b % 2])
                issued[0] += 1

        def load_block(src_ap, nt):
            b = nblk[0]; nblk[0] += 1
            issue_upto(b)
            st["hb_tok"] = btok[b]
            issue_upto(b + 1)
            return b % 2

        def done_block(hs):
            last_users[hs] = (s_pe, s_pe.n)

        def hch(hs, c0, M, nt):
            return [(win[:, c, c0:c0 + M], hblk[hs][:, c, :nt]) for c in range(8)]

        for tb in range(17):
            ctxb = (tb == 16)
            t0 = tb * 512
            nt = 256 if ctxb else 512
            hs = load_block(hT_all[:, t0:t0 + nt], nt)
            rope = None if ctxb else (pt128, dr["cosA_all"][:, t0:t0 + nt], dr["sinA_all"][:, t0:t0 + nt])
            fm_job(hch(hs, O_GK, 128, nt), 128, nt, gk[:, 0:1], rope,
                   [(dr["GKT"][0, :, t0:t0 + nt], 0, 64), (dr["GKT"][1, :, t0:t0 + nt], 64, 128)])
            ckv_job(hs, nt, dr["CKVT"][:, t0:t0 + nt])
            rope = None if ctxb else (pt32, dr["cosM_all"][:, t0:t0 + nt], dr["sinM_all"][:, t0:t0 + nt])
            fm_job(hch(hs, O_KR, 32, nt), 32, nt, None, rope, [(dr["MKT"][h, 64:96, t0:t0 + nt], 0, 32) for h in range(8)])
            W(PE, st["ckvn_tok"])
            for g in range(4):
                fm_job([(wuk[:, r, g * 128:(g + 1) * 128], ckvn[:, r, :nt]) for r in range(2)], 128, nt, None, None,
                       [(dr["MKT"][2 * g, 0:64, t0:t0 + nt], 0, 64), (dr["MKT"][2 * g + 1, 0:64, t0:t0 + nt], 64, 128)])
            for ti in range(nt // 128):
                tsl = slice(ti * 128, (ti + 1) * 128)
                r0 = t0 + ti * 128
                tm_job([(ckvn[:, r, tsl], wuv[:, r, :]) for r in range(2)], 512, 8, dr["MV"][r0:r0 + 128, :, :])
                tm_job([(hblk[hs][:, c, tsl], win[:, c, O_GV:O_GV + 128]) for c in range(8)], 128, 2, dr["GV"][r0:r0 + 128, :, :])
            st["ckvn_free"] = (s_pe, s_pe.n)
            if ctxb:
                for g in range(4):
                    fm_job(hch(hs, O_NK + g * 128, 128, nt), 128, nt, None, None,
                           [(dr["NKT"][2 * g, :, NAT:NAT + nt], 0, 64), (dr["NKT"][2 * g + 1, :, NAT:NAT + nt], 64, 128)])
                for ti in range(nt // 128):
                    tsl = slice(ti * 128, (ti + 1) * 128)
                    tm_job([(hblk[hs][:, c, tsl], win[:, c, O_NV:O_NV + 512]) for c in range(8)], 512, 8, dr["NV"][NAT + ti * 128:NAT + (ti + 1) * 128, :, :])
                if with_ctx:
                    q0 = T
                    for g in range(4):
                        fm_job(hch(hs, O_GQ + g * 128, 128, nt), 128, nt, gq[:, 0:1], None,
                               [(dr["GQT"][2 * g, :, q0:q0 + nt], 0, 64), (dr["GQT"][2 * g + 1, :, q0:q0 + nt], 64, 128)])
                        fm_job(hch(hs, O_NQ + g * 128, 128, nt), 128, nt, None, None,
                               [(dr["NQT"][2 * g, :, q0:q0 + nt], 0, 64), (dr["NQT"][2 * g + 1, :, q0:q0 + nt], 64, 128)], oscale=0.125)
                    for h in range(8):
                        fm_job(hch(hs, O_MQ + h * 96, 96, nt), 96, nt, None, None, [(dr["MQT"][h, :, q0:q0 + nt], 0, 96)])
            done_block(hs)
        for tb in range(5):
            t0 = tb * 512
            nt = 512
            hs = load_block(hT_na[:, t0:t0 + nt], nt)
            for g in range(4):
                fm_job(hch(hs, O_NK + g * 128, 128, nt), 128, nt, None, None,
                       [(dr["NKT"][2 * g, :, t0:t0 + nt], 0, 64), (dr["NKT"][2 * g + 1, :, t0:t0 + nt], 64, 128)])
            for ti in range(4):
                tsl = slice(ti * 128, (ti + 1) * 128)
                tm_job([(hblk[hs][:, c, tsl], win[:, c, O_NV:O_NV + 512]) for c in range(8)], 512, 8, dr["NV"][t0 + ti * 128:t0 + (ti + 1) * 128, :, :])
            done_block(hs)
        for tb in range(4):
            q0 = tb * 512
            nt = 512
            hs = load_block(hT_na[:, 256 + q0:256 + q0 + nt], nt)
            for g in range(4):
                fm_job(hch(hs, O_GQ + g * 128, 128, nt), 128, nt, gq[:, 0:1], (pt128, dr["cosA_own"][:, q0:q0 + nt], dr["sinA_own"][:, q0:q0 + nt]),
                       [(dr["GQT"][2 * g, :, q0:q0 + nt], 0, 64), (dr["GQT"][2 * g + 1, :, q0:q0 + nt], 64, 128)])
                fm_job(hch(hs, O_NQ + g * 128, 128, nt), 128, nt, None, None,
                       [(dr["NQT"][2 * g, :, q0:q0 + nt], 0, 64), (dr["NQT"][2 * g + 1, :, q0:q0 + nt], 64, 128)], oscale=0.125)
            for h in range(8):
                fm_job(hch(hs, O_MQ + h * 96, 96, nt), 96, nt, None, (pt96, dr["cosM_own"][:, q0:q0 + nt], dr["sinM_own"][:, q0:q0 + nt]),
                       [(dr["MQT"][h, :, q0:q0 + nt], 0, 96)])
            done_block(hs)
        K.barrier([(s, s.n) for s in sto + stv + [stc]])


def phase_attn(K, heads, pfx, nkmax):
    nc = K.nc
    PE, ACT, DVE, POOL, SP = K.PE, K.ACT, K.DVE, K.POOL, K.SP
    NQ = T + C
    rls = K.dram["rls"]
    with ExitStack() as es:
        sb, ps = mk_alloc(nc, es, pfx)
        ktb = [sb(f"kt{i}", [128, nkmax], BF16) for i in range(2)]
        vb = [sb(f"v{i}", [128, nkmax // 128, 65], BF16) for i in range(2)]
        qb = [sb(f"q{i}", [128, NQ], BF16) for i in range(2)]
        pbuf = [sb(f"p{i}", [128, 1024], BF16) for i in range(3)]
        NBB, LB = 8, 6
        bb = [sb(f"bias{i}", [128, 512], BF16) for i in range(NBB)]
        identb = sb("identb", [128, 128], BF16)
        osb = [sb(f"osb{i}", [128, 512]) for i in range(2)]
        rl = [sb(f"rl{i}", [128, 512]) for i in range(2)]
        rbc = [sb(f"rbc{i}", [64, 512]) for i in range(2)]
        ysb = [[sb(f"y{i}_{w}", [64, 512], BF16) for w in range(2)] for i in range(2)]
        psb = [ps(f"s{i}", [128, 1024]) for i in range(3)]
        po = [ps(f"o{i}", [128, 512]) for i in range(2)]
        hl = [K.S("ld0"), K.S("ld1")]
        bl = [K.S(f"gb{i}") for i in range(NBB)]
        cl = K.S("ld5")
        rld = [K.S("ld3"), K.S("ld4")]
        rst = [K.S("st2"), K.S("st3")]
        s_pe, s_ac, s_dv = K.S("pe"), K.S("ac"), K.S("dv")
        sty = [K.S("st0"), K.S("st1")]
        t_c = K.dma(SP, identb[:], K.dram["ident_b"], cl)
        for i in range(2):
            DVE.memset(ktb[i][:], 0.0)
            DVE.memset(qb[i][:], 0.0)
            DVE.memset(rl[i][:], 1.0)
        t_m = s_dv.inc(DVE.memset(osb[0][:], 0.0))
        W(PE, t_c)
        W(PE, t_m)
        W(SP, t_m)
        steps = []
        for hi, h in enumerate(heads):
            for bi, b in enumerate(h["blocks"]):
                nt_ = len(b["tiles"])
                b["chunks"] = [(c0, min(512, b["nq"] - c0)) for c0 in range(0, b["nq"], 512)]
                for si, (kti, bias) in enumerate(b["tiles"]):
                    assert bias is None or len(b["chunks"]) == 1
                    steps.append(dict(hi=hi, b=b, kti=kti, bias=bias, first=(si == 0), last=(si == nt_ - 1),
                                      hfirst=(bi == 0 and si == 0), hlast=(bi == len(h["blocks"]) - 1 and si == nt_ - 1)))
        NS = len(steps)
        head_tok = [None] * len(heads)
        head_done = [None] * len(heads)

        def load_head(hi):
            h = heads[hi]
            s = hi % 2
            if hi >= 2:
                W(SP, head_done[hi - 2])
            dk, nk = h["dk"], h["nk"]
            K.dma(SP, ktb[s][:dk, :nk], h["kt"], hl[s])
            K.dma(SP, vb[s][:, :nk // 128, :], h["v"].rearrange("(t p) e -> p t e", p=128), hl[s])
            head_tok[hi] = K.dma(SP, qb[s][:dk, :], h["qt"], hl[s])

        tq = [None] * NS
        te = [None] * NS
        tv = [None] * NS
        bias_ld = [None] * NS
        nbias = [0]
        bidx = [None] * NS
        po_free = [None, None]
        y_free = [[None, None], [None, None]]
        pend_dv = {}
        state = {}

        def emit_qk(t):
            s = steps[t]
            h = heads[s["hi"]]
            hs = s["hi"] % 2
            if s["hfirst"]:
                W(PE, head_tok[s["hi"]])
            if t >= 3:
                W(PE, te[t - 3])
            b = s["b"]
            hasb = s["bias"] is not None
            for (c0, cn) in b["chunks"]:
                ins = PE.matmul(psb[t % 3][:, c0:c0 + cn], lhsT=ktb[hs][:, s["kti"] * 128:(s["kti"] + 1) * 128],
                                rhs=qb[hs][:, b["q0"] + c0:b["q0"] + c0 + cn], start=True, stop=not hasb)
            if hasb:
                W(PE, bias_ld[t])
                ins = PE.matmul(psb[t % 3][:, :b["nq"]], lhsT=identb[:], rhs=bb[bidx[t] % NBB][:, :b["nq"]], start=False, stop=True)
            tq[t] = s_pe.inc(ins)
            if hasb:
                state[("bfree", bidx[t] % NBB)] = tq[t]

        def emit_bias_load(t):
            s = steps[t]
            if s["bias"] is None:
                return
            n = nbias[0]; nbias[0] += 1
            bidx[t] = n
            W(POOL, state.get(("bfree", n % NBB)))
            bias_ld[t] = K.dma(POOL, bb[n % NBB][:, :s["b"]["nq"]], s["bias"], bl[n % NBB])

        if NS > 0:
            load_head(0)
        LA = 2
        for t in range(min(LB, NS)):
            emit_bias_load(t)
        for t in range(min(LA, NS)):
            emit_qk(t)
        cur_blk = -1
        for t in range(NS):
            s = steps[t]
            h = heads[s["hi"]]
            b = s["b"]
            nq = b["nq"]
            hs = s["hi"] % 2
            if s["hfirst"] and s["hi"] + 1 < len(heads):
                load_head(s["hi"] + 1)
            if s["first"]:
                cur_blk += 1
            if t + LB < NS:
                emit_bias_load(t + LB)
            if t + LA < NS:
                emit_qk(t + LA)
            W(ACT, tq[t])
            if t >= 3:
                W(ACT, tv[t - 3])
            te[t] = s_ac.inc(ACT.activation(out=pbuf[t % 3][:, :nq], in_=psb[t % 3][:, :nq], func=AF.Exp, scale=float(h["scale"])))
            W(PE, te[t])
            for w, (c0, cn) in enumerate(b["chunks"]):
                if s["first"]:
                    W(PE, po_free[w])
                ins = PE.matmul(po[w][:65, :cn], lhsT=vb[hs][:, s["kti"], :], rhs=pbuf[t % 3][:, c0:c0 + cn], start=s["first"], stop=s["last"])
            tv[t] = s_pe.inc(ins)
            if s["hlast"]:
                head_done[s["hi"]] = tv[t]
            for f in pend_dv.pop(t, []):
                f()
            if s["last"]:
                cb = cur_blk
                parts = []
                for w, (c0, cn) in enumerate(b["chunks"]):
                    W(DVE, tv[t])
                    t_o = s_dv.inc(DVE.tensor_copy(out=osb[w][:65, :cn], in_=po[w][:65, :cn]))
                    po_free[w] = t_o
                    W(DVE, t_o)
                    t_rl = s_dv.inc(DVE.reciprocal(out=rl[w][64:65, :cn], in_=osb[w][64:65, :cn]))
                    slot = (cb % 2) * 2 + w
                    W(SP, t_rl)
                    t_s = K.dma(SP, rls[slot:slot + 1, :cn], rl[w][64:65, :cn], rst[w])
                    W(SP, t_s)
                    t_b = K.dma(SP, rbc[w][:, :cn], rls[slot, :cn].partition_broadcast(64), rld[w])
                    parts.append((w, cn, t_b))

                def dv_part(parts=parts, cb=cb, yts=b["yts"]):
                    for (w, cn, t_b) in parts:
                        W(DVE, t_b)
                        W(DVE, y_free[cb % 2][w])
                        t_y = s_dv.inc(DVE.tensor_tensor(out=ysb[cb % 2][w][:, :cn], in0=osb[w][:64, :cn], in1=rbc[w][:, :cn], op=ALU.mult))
                        W(SP, t_y)
                        y_free[cb % 2][w] = K.dma(SP, yts[w], ysb[cb % 2][w][:, :cn], sty[w])

                if t + 1 < NS:
                    nxt_len = len(steps[t + 1]["b"]["tiles"])
                    d = max(1, min(2, nxt_len - 1))
                    pend_dv.setdefault(t + d, []).append(dv_part)
                else:
                    dv_part()
        assert not pend_dv
        K.barrier([(s_, s_.n) for s_ in sty])


def phase_merge(K, L, pfx, qblocks, x_src, x_dst):
    nc = K.nc
    PE, ACT, DVE, POOL, SP = K.PE, K.ACT, K.DVE, K.POOL, K.SP
    dr = K.dram
    with ExitStack() as es:
        sb, ps = mk_alloc(nc, es, pfx)
        wg = sb("wg", [128, 8, 3072], BF16)
        wo = [sb(f"wo{i}", [128, 4, 1024], BF16) for i in range(3)]
        wout = sb("wout", [128, 8, 1024], BF16)
        g1 = sb("g1", [128, 2, 1024])
        hblk = [sb(f"h{i}", [128, 8, 512], BF16) for i in range(2)]
        yb = [[sb(f"y{r}_{i}", [128, 4, 512], BF16) for r in range(3)] for i in range(2)]
        sg = [sb(f"sg{i}", [128, 512]) for i in range(2)]
        yacc = sb("yacc", [128, 512])
        tmp = sb("tmp", [128, 512])
        yT = sb("yT", [128, 8, 512], BF16)
        xt = [sb(f"xt{i}", [128, 1024]) for i in range(2)]
        xo = [sb(f"xo{i}", [128, 1024]) for i in range(2)]
        tm2 = [sb(f"tm{i}", [128, 512]) for i in range(2)]
        pg = [ps(f"pg{i}", [128, 512]) for i in range(2)]
        pbr = [ps(f"pb{i}", [128, 512]) for i in range(2)]
        pw = [ps(f"pw{i}", [128, 512]) for i in range(2)]
        wl = K.S("ld2")
        hl = [K.S("ld0"), K.S("ld1")]
        xl = [K.S("ld3"), K.S("ld4")]
        s_pe, s_ac, s_dv, s_pl = K.S("pe"), K.S("ac"), K.S("dv"), K.S("pl")
        stx = [K.S("gs0"), K.S("gs1")]
        gw = K.S("gw")
        for c in range(8):
            K.dma(POOL, wg[:, c, :], L["w_in"][c * 128:(c + 1) * 128, O_GATE:WIN], gw)
        for r, nm in enumerate(("w_o_gqa", "w_o_na", "w_o_mla")):
            K.dma(POOL, wo[r][:], L[nm].rearrange("(c p) n -> p c n", p=128), gw)
        t_gw = K.dma(POOL, wout[:], L["w_out"].rearrange("(c p) n -> p c n", p=128), gw)
        K.dma(SP, g1[:, 0, :], dr["modv"][0, 2].partition_broadcast(128), wl)
        t_w = K.dma(SP, g1[:, 1, :], dr["modv"][1, 2].partition_broadcast(128), wl)
        for e in (PE, DVE, POOL):
            W(e, t_w)
            W(e, t_gw)
        ysrc = (dr["YAT"], dr["YBT"], dr["YCT"])
        blk_done = [None, None]
        k = 0
        xk = 0
        sg_free = [None, None]
        pg_free = [None, None]
        pbr_free = [None, None]
        pw_free = [None, None]
        xt_free = [None, None]
        xo_free = [None, None]
        tm_free = [None, None]
        yT_free = None
        for bi, (hT_ap, q0, nt, m) in enumerate(qblocks):
            s = bi % 2
            W(SP, blk_done[s])
            K.dma(SP, hblk[s][:, :, :nt], hT_ap.rearrange("(c p) t -> p c t", p=128), hl[s])
            for r in range(3):
                t_l = K.dma(SP, yb[s][r][:, :, :nt], ysrc[r][:, q0:q0 + nt].rearrange("(c p) t -> p c t", p=128), hl[s])
            W(PE, t_l)
            for oc in range(8):
                for r in range(3):
                    W(PE, pg_free[k % 2])
                    for c in range(8):
                        ins = PE.matmul(pg[k % 2][:, :nt], lhsT=wg[:, c, r * 1024 + oc * 128:r * 1024 + (oc + 1) * 128], rhs=hblk[s][:, c, :nt],
                                        start=(c == 0), stop=(c == 7))
                    t_g = s_pe.inc(ins)
                    W(PE, pbr_free[k % 2])
                    for c in range(4):
                        ins = PE.matmul(pbr[k % 2][:, :nt], lhsT=wo[r][:, c, oc * 128:(oc + 1) * 128], rhs=yb[s][r][:, c, :nt], start=(c == 0), stop=(c == 3))
                    t_b = s_pe.inc(ins)
                    W(ACT, t_g)
                    W(ACT, sg_free[k % 2])
                    t_s = s_ac.inc(ACT.activation(out=sg[k % 2][:, :nt], in_=pg[k % 2][:, :nt], func=AF.Sigmoid))
                    pg_free[k % 2] = t_s
                    W(DVE, t_s)
                    W(DVE, t_b)
                    if r == 0:
                        t_d = s_dv.inc(DVE.tensor_tensor(out=yacc[:, :nt], in0=sg[k % 2][:, :nt], in1=pbr[k % 2][:, :nt], op=ALU.mult))
                    else:
                        t_d = s_dv.inc(DVE.tensor_tensor(out=tmp[:, :nt], in0=sg[k % 2][:, :nt], in1=pbr[k % 2][:, :nt], op=ALU.mult))
                        W(DVE, t_d)
                        if r == 1:
                            t_d = s_dv.inc(DVE.tensor_tensor(out=yacc[:, :nt], in0=yacc[:, :nt], in1=tmp[:, :nt], op=ALU.add))
                        else:
                            if oc == 0:
                                W(DVE, yT_free)
                            t_d = s_dv.inc(DVE.tensor_tensor(out=yT[:, oc, :nt], in0=yacc[:, :nt], in1=tmp[:, :nt], op=ALU.add))
                    sg_free[k % 2] = t_d
                    pbr_free[k % 2] = t_d
                    k += 1
            blk_done[s] = (s_pe, s_pe.n)
            t_y = t_d
            W(PE, t_y)
            for ti in range(nt // 128):
                xs_ = xk % 2
                W(SP, xt_free[xs_])
                t_x = K.dma(SP, xt[xs_][:], x_src(q0 + ti * 128), xl[xs_])
                for half in range(2):
                    j = 2 * xk + half
                    W(PE, pw_free[j % 2])
                    for c in range(8):
                        ins = PE.matmul(pw[j % 2][:, :], lhsT=yT[:, c, ti * 128:(ti + 1) * 128], rhs=wout[:, c, half * 512:(half + 1) * 512],
                                        start=(c == 0), stop=(c == 7))
                    t_p = s_pe.inc(ins)
                    W(DVE, t_p)
                    W(DVE, tm_free[j % 2])
                    t_m = s_dv.inc(DVE.tensor_tensor(out=tm2[j % 2][:], in0=pw[j % 2][:], in1=g1[:, m, half * 512:(half + 1) * 512], op=ALU.mult))
                    pw_free[j % 2] = t_m
                    W(POOL, t_m)
                    W(POOL, t_x)
                    if half == 0:
                        W(POOL, xo_free[xs_])
                    t_a = s_pl.inc(POOL.tensor_tensor(out=xo[xs_][:, half * 512:(half + 1) * 512], in0=tm2[j % 2][:], in1=xt[xs_][:, half * 512:(half + 1) * 512], op=ALU.add))
                    tm_free[j % 2] = t_a
                xt_free[xs_] = t_a
                W(POOL, t_a)
                xo_free[xs_] = K.dma(POOL, x_dst(q0 + ti * 128), xo[xs_][:], stx[xs_])
                xk += 1
            yT_free = (s_pe, s_pe.n)
        K.barrier([(s_, s_.n) for s_ in stx])


def phase_mlp(K, L, pfx, qblocks, x_src, x_dst, final_g, after_block=None):
    nc = K.nc
    PE, ACT, DVE, POOL, SP = K.PE, K.ACT, K.DVE, K.POOL, K.SP
    dr = K.dram
    with ExitStack() as es:
        sb, ps = mk_alloc(nc, es, pfx)
        w1 = sb("w1", [128, 8, 4096], BF16)
        w2 = sb("w2", [128, 32, 1024], BF16)
        g2 = sb("g2", [128, 2, 1024])
        fg = sb("fg", [128, 1024])
        hblk = [sb(f"h{i}", [128, 8, 256], BF16) for i in range(2)]
        uT = sb("uT", [128, 32, 256], BF16)
        rb = [sb(f"r{i}", [128, 256]) for i in range(2)]
        xt = [sb(f"xt{i}", [128, 1024]) for i in range(2)]
        xo = [sb(f"xo{i}", [128, 1024]) for i in range(2)]
        tm2 = [sb(f"tm{i}", [128, 512]) for i in range(2)]
        junk = sb("junk", [128, 1024])
        st4 = sb("st4", [128, 4])
        pu = [ps(f"pu{i}", [128, 512]) for i in range(2)]
        pw = [ps(f"pw{i}", [128, 512]) for i in range(2)]
        wl = K.S("ld2")
        hl = [K.S("ld0"), K.S("ld1")]
        xl = [K.S("ld3"), K.S("ld4")]
        s_pe, s_ac, s_dv, s_pl = K.S("pe"), K.S("ac"), K.S("dv"), K.S("pl")
        stx = [K.S("gs0"), K.S("gs1")]
        gw = K.S("gw")
        for c in range(8):
            K.dma(POOL, w1[:, c, :], L["w_mlp1"][c * 128:(c + 1) * 128, :], gw)
        for c4 in range(4):
            t_gw = K.dma(POOL, w2[:, c4 * 8:(c4 + 1) * 8, :], L["w_mlp2"][c4 * 1024:(c4 + 1) * 1024, :].rearrange("(c p) n -> p c n", p=128), gw)
        K.dma(SP, g2[:, 0, :], dr["modv"][0, 5].partition_broadcast(128), wl)
        if final_g is not None:
            K.dma(SP, fg[:], final_g.partition_broadcast(128), wl)
        t_w = K.dma(SP, g2[:, 1, :], dr["modv"][1, 5].partition_broadcast(128), wl)
        for e in (PE, DVE, POOL, ACT):
            W(e, t_w)
            W(e, t_gw)
        h2T = dr["h2T"]
        blk_done = [None, None]
        k = 0
        xk = 0
        pu_free = [None, None]
        rb_free = [None, None]
        pw_free = [None, None]
        xt_free = [None, None]
        xo_free = [None, None]
        tm_free = [None, None]
        uT_free = None
        for bi, (q0, nt, m) in enumerate(qblocks):
            s = bi % 2
            W(SP, blk_done[s])
            t_l = K.dma(SP, hblk[s][:, :, :nt], h2T[:, q0:q0 + nt].rearrange("(c p) t -> p c t", p=128), hl[s])
            W(PE, t_l)
            for fc in range(32):
                W(PE, pu_free[k % 2])
                for c in range(8):
                    ins = PE.matmul(pu[k % 2][:, :nt], lhsT=w1[:, c, fc * 128:(fc + 1) * 128], rhs=hblk[s][:, c, :nt], start=(c == 0), stop=(c == 7))
                t_u = s_pe.inc(ins)
                W(ACT, t_u)
                W(ACT, rb_free[k % 2])
                t_r = s_ac.inc(ACT.activation(out=rb[k % 2][:, :nt], in_=pu[k % 2][:, :nt], func=AF.Relu))
                pu_free[k % 2] = t_r
                W(DVE, t_r)
                if fc == 0:
                    W(DVE, uT_free)
                t_q = s_dv.inc(DVE.tensor_tensor(out=uT[:, fc, :nt], in0=rb[k % 2][:, :nt], in1=rb[k % 2][:, :nt], op=ALU.mult))
                rb_free[k % 2] = t_q
                k += 1
            blk_done[s] = (s_pe, s_pe.n)
            W(PE, t_q)
            for ti in range(nt // 128):
                xs_ = xk % 2
                W(SP, xt_free[xs_])
                t_x = K.dma(SP, xt[xs_][:], x_src(q0 + ti * 128), xl[xs_])
                for half in range(2):
                    j = 2 * xk + half
                    W(PE, pw_free[j % 2])
                    for fc in range(32):
                        ins = PE.matmul(pw[j % 2][:, :], lhsT=uT[:, fc, ti * 128:(ti + 1) * 128], rhs=w2[:, fc, half * 512:(half + 1) * 512],
                                        start=(fc == 0), stop=(fc == 31))
                    t_p = s_pe.inc(ins)
                    W(DVE, t_p)
                    W(DVE, tm_free[j % 2])
                    t_m = s_dv.inc(DVE.tensor_tensor(out=tm2[j % 2][:], in0=pw[j % 2][:], in1=g2[:, m, half * 512:(half + 1) * 512], op=ALU.mult))
                    pw_free[j % 2] = t_m
                    W(POOL, t_m)
                    W(POOL, t_x)
                    if half == 0:
                        W(POOL, xo_free[xs_])
                    t_a = s_pl.inc(POOL.tensor_tensor(out=xo[xs_][:, half * 512:(half + 1) * 512], in0=tm2[j % 2][:], in1=xt[xs_][:, half * 512:(half + 1) * 512], op=ALU.add))
                    tm_free[j % 2] = t_a
                xt_free[xs_] = t_a
                t_fin = t_a
                if final_g is not None:
                    W(ACT, t_a)
                    t1 = s_ac.inc(ACT.activation(out=junk[:], in_=xo[xs_][:], func=AF.Square, accum_out=st4[:, 0:1]))
                    W(DVE, t1)
                    t2 = s_dv.inc(DVE.tensor_scalar(out=st4[:, 1:2], in0=st4[:, 0:1], scalar1=1.0 / D, scalar2=EPS, op0=ALU.mult, op1=ALU.add))
                    W(ACT, t2)
                    t3 = s_ac.inc(ACT.activation(out=st4[:, 2:3], in_=st4[:, 1:2], func=AF.Sqrt))
                    W(DVE, t3)
                    t4 = s_dv.inc(DVE.reciprocal(out=st4[:, 3:4], in_=st4[:, 2:3]))
                    W(DVE, t4)
                    t_fin = s_dv.inc(DVE.scalar_tensor_tensor(out=xo[xs_][:], in0=xo[xs_][:], scalar=st4[:, 3:4], in1=fg[:], op0=ALU.mult, op1=ALU.mult))
                W(POOL, t_fin)
                xo_free[xs_] = K.dma(POOL, x_dst(q0 + ti * 128), xo[xs_][:], stx[xs_])
                xk += 1
            uT_free = (s_pe, s_pe.n)
            if after_block is not None:
                after_block(bi, [xo_free[0], xo_free[1]])
        K.barrier([(s_, s_.n) for s_ in stx])


def phase_halo(K, pfx):
    nc = K.nc
    PE, ACT, DVE, POOL, SP = K.PE, K.ACT, K.DVE, K.POOL, K.SP
    dr = K.dram
    xg, x1, xna, sel = K.xg_at, K.x1_at, dr["x_na2"], dr["sel"]
    with ExitStack() as es:
        sb, ps = mk_alloc(nc, es, pfx)
        selb = sb("sel", [128, 8])
        cand = [sb(f"c{i}", [128, 1024]) for i in range(4)]
        acc = [sb(f"a{i}", [128, 1024]) for i in range(2)]
        ld = [K.S("ld0"), K.S("ld1"), K.S("ld3"), K.S("ld4")]
        lc = K.S("ld2")
        s_dv = K.S("dv")
        st = [K.S("st0"), K.S("st1")]
        so = K.S("st2")
        t_c = K.dma(SP, selb[:], sel.partition_broadcast(128), lc)
        for q in range(0, T, 128):
            t_own = K.dma(SP, xna[256 + q:256 + q + 128, :], x1(q), so)
        W(DVE, t_c)
        jobs = []
        for u in range(2):
            jobs.append((128 * u, [xg(2048 * r + 1792 + 128 * u) for r in range(4)], 0))
        for u in range(2):
            jobs.append((256 + T + 128 * u, [xg(2048 * r + 128 * u) for r in range(4)], 4))
        dv_prev = None
        st_t = [None, None]
        for n, (row0, srcs, c0) in enumerate(jobs):
            W(SP, dv_prev)
            lts = [K.dma(SP, cand[r][:], srcs[r], ld[r]) for r in range(4)]
            W(DVE, lts)
            W(DVE, st_t[n % 2])
            t = s_dv.inc(DVE.tensor_scalar(out=acc[n % 2][:], in0=cand[0][:], scalar1=selb[:, c0:c0 + 1], scalar2=0.0, op0=ALU.mult, op1=ALU.add))
            for r in range(1, 4):
                W(DVE, t)
                t = s_dv.inc(DVE.scalar_tensor_tensor(out=acc[n % 2][:], in0=cand[r][:], scalar=selb[:, c0 + r:c0 + r + 1], in1=acc[n % 2][:],
                                                      op0=ALU.mult, op1=ALU.add))
            dv_prev = t
            W(SP, t)
            st_t[n % 2] = K.dma(SP, xna[row0:row0 + 128, :], acc[n % 2][:], st[n % 2])
        K.barrier([t_own, st_t[0], st_t[1]])


W_NAMES = ["w_mod", "b_mod", "norm1_g", "norm2_g", "w_in", "gqa_q_norm", "gqa_k_norm", "mla_kv_norm", "mla_w_uk", "mla_w_uv",
           "w_o_gqa", "w_o_na", "w_o_mla", "w_out", "w_mlp1", "w_mlp2"]
W_SHAPES = {"w_mod": [D, 6 * D], "b_mod": [6 * D], "norm1_g": [D], "norm2_g": [D], "w_in": [D, WIN], "gqa_q_norm": [64], "gqa_k_norm": [64],
            "mla_kv_norm": [256], "mla_w_uk": [256, 512], "mla_w_uv": [256, 512], "w_o_gqa": [512, D], "w_o_na": [512, D], "w_o_mla": [512, D],
            "w_out": [D, D], "w_mlp1": [D, 4 * D], "w_mlp2": [4 * D, D]}
DEPTH = 2


def emit_layer(K, L, src, with_ctx, final, sfx):
    dr = K.dram
    NQ = T + C
    phase_mod(K, L, "md" + sfx)
    phase_norm(K, [(src["x_all"], S, dr["hT_all"][:, 0:S], 0, 0, 1), (src["ctx_in"], C, dr["hT_all"][:, S:SK], 1, 0, 1),
                   (src["x_na"], NAT, dr["hT_na"], 0, 0, 1)], "n1" + sfx)
    phase_proj(K, L, "pj" + sfx, with_ctx)
    qbl = [(512 * i, 512) for i in range(4)]
    heads = []

    def wide_blocks(Y, h):
        blocks = [dict(q0=q0, nq=1024, tiles=[(k, None) for k in range(66)],
                       yts=[Y[64 * h:64 * h + 64, q0:q0 + 512], Y[64 * h:64 * h + 64, q0 + 512:q0 + 1024]]) for q0 in (0, 1024)]
        if with_ctx:
            blocks.append(dict(q0=T, nq=C, tiles=[(64, None), (65, None)], yts=[Y[64 * h:64 * h + 64, T:NQ]]))
        return blocks

    for h in range(8):
        heads.append(dict(kt=dr["GKT"][h // 4], v=dr["GV"][:, h // 4, :], qt=dr["GQT"][h], dk=64, scale=0.125, nk=SK, blocks=wide_blocks(dr["YAT"], h)))
    for h in range(8):
        heads.append(dict(kt=dr["MKT"][h], v=dr["MV"][:, h, :], qt=dr["MQT"][h], dk=96, scale=96 ** -0.5, nk=SK, blocks=wide_blocks(dr["YCT"], h)))
    phase_attn(K, heads, "at" + sfx, SK)
    heads = []
    var = [0, 1, 1, 2]
    for h in range(8):
        blocks = []
        for i, (q0, nq) in enumerate(qbl):
            tiles = [(4 * i + m, src["nabias"][var[i], h, m]) for m in range(8)] + [(20, None), (21, None)]
            blocks.append(dict(q0=q0, nq=nq, tiles=tiles, yts=[dr["YBT"][64 * h:64 * h + 64, q0:q0 + nq]]))
        if with_ctx:
            blocks.append(dict(q0=T, nq=C, tiles=[(20, None), (21, None)], yts=[dr["YBT"][64 * h:64 * h + 64, T:NQ]]))
        heads.append(dict(kt=dr["NKT"][h], v=dr["NV"][:, h, :], qt=dr["NQT"][h], dk=64, scale=1.0, nk=NAK, blocks=blocks))
    phase_attn(K, heads, "na" + sfx, NAK)
    mblocks = [(dr["hT_na"][:, 256 + 512 * i:256 + 512 * (i + 1)], 512 * i, 512, 0) for i in range(4)]
    if with_ctx:
        mblocks.append((dr["hT_all"][:, S:SK], T, C, 1))

    def x_src(q):
        if q >= T:
            return src["ctx_in"][q - T:q - T + 128, :]
        return src["x_own"](q) if callable(src["x_own"]) else src["x_own"][q:q + 128, :]

    def xs1_at(q):
        return dr["xs1"][q:q + 128, :]

    phase_merge(K, L, "mg" + sfx, mblocks, x_src, xs1_at)
    njobs = [(dr["xs1"][0:T, :], T, dr["h2T"][:, 0:T], 0, 3, 4)]
    if with_ctx:
        njobs.append((dr["xs1"][T:NQ, :], C, dr["h2T"][:, T:NQ], 1, 3, 4))
    phase_norm(K, njobs, "n2" + sfx)
    fblocks = [(256 * i, 256, 0) for i in range(8)]
    if with_ctx:
        fblocks.append((T, C, 1))
    phase_mlp(K, L, "ml" + sfx, fblocks, xs1_at, src["x_dst"], L.get("final_norm_g") if final else None, after_block=src.get("after_block"))


def build_fused():
    nc = bass.Bass("TRN2", target_bir_lowering=False)
    K = KB(nc)
    NQ = T + C
    dr = K.dram

    def inp(name, shape, dt=F32):
        dr[name] = nc.dram_tensor(name, shape, dt, kind="ExternalInput").ap()

    def internal(name, shape, dt=BF16):
        dr[name] = nc.dram_tensor(name, shape, dt).ap()

    inp("x_all", [S, D]); inp("x_own", [T, D]); inp("x_na", [NAT, D]); inp("ctx_in", [C, D]); inp("cvec", [2, D])
    Wst = {}
    for n in W_NAMES:
        Wst[n] = nc.dram_tensor(n, [DEPTH] + W_SHAPES[n], F32, kind="ExternalInput").ap()
    fng = nc.dram_tensor("final_norm_g", [D], F32, kind="ExternalInput").ap()
    for n in ("ident_f", "onesbd_f", "ones_f"):
        inp(n, [128, 128])
    for n in ("pt128", "pt96", "pt32", "ident_b"):
        inp(n, [128, 128], BF16)
    inp("cosA_all", [128, S]); inp("sinA_all", [128, S]); inp("cosA_own", [128, T]); inp("sinA_own", [128, T])
    inp("cosM_all", [32, S]); inp("sinM_all", [32, S]); inp("cosM_own", [96, T]); inp("sinM_own", [96, T])
    inp("nabias", [DEPTH, 3, 8, 8, 128, 512])
    inp("sel", [8])
    dr["xout"] = nc.dram_tensor("xout", [T, D], F32, kind="ExternalOutput").ap()
    internal("modv", [2, 6, D], F32)
    internal("hT_all", [D, SK]); internal("hT_na", [D, NAT])
    internal("GKT", [2, 64, SK]); internal("CKVT", [256, SK]); internal("MKT", [8, 96, SK])
    internal("MV", [SK, 8, 65]); internal("GV", [SK, 2, 65])
    internal("NKT", [8, 64, NAK]); internal("NV", [NAK, 8, 65])
    internal("GQT", [8, 64, NQ]); internal("NQT", [8, 64, NQ]); internal("MQT", [8, 96, NQ])
    internal("YAT", [512, NQ]); internal("YBT", [512, NQ]); internal("YCT", [512, NQ])
    internal("xs1", [NQ, D], F32); internal("h2T", [D, NQ])
    NCH = 8
    x1c = [nc.dram_tensor(f"x1c{k}", [256, D], F32) for k in range(NCH)]
    xgc = [nc.dram_tensor(f"xgc{k}", [4 * 256, D], F32) for k in range(NCH)]
    internal("c1loc", [C, D], F32); internal("x_na2", [NAT, D], F32)
    internal("rls", [4, 512], F32)

    def x1_at(q):
        return x1c[q // 256].ap()[q % 256:q % 256 + 128, :]

    def xg_at(t0):
        r, k, off = t0 // 2048, (t0 % 2048) // 256, t0 % 256
        return xgc[k].ap()[r * 256 + off:r * 256 + off + 128, :]

    K.x1_at, K.xg_at = x1_at, xg_at

    for l in range(DEPTH):
        L = {n: Wst[n][l] for n in W_NAMES}
        final = (l == DEPTH - 1)
        with_ctx = not final
        if final:
            L["final_norm_g"] = fng
        if l == 0:
            src = dict(x_all=dr["x_all"], x_own=dr["x_own"], x_na=dr["x_na"], ctx_in=dr["ctx_in"])
        else:
            src = dict(x_all=(lambda i: xg_at(128 * i)), x_own=x1_at, x_na=dr["x_na2"], ctx_in=dr["c1loc"])
        src["nabias"] = dr["nabias"][l]
        if final:
            src["x_dst"] = lambda q: dr["xout"][q:q + 128, :]
        else:
            src["x_dst"] = lambda q: (x1_at(q) if q < T else dr["c1loc"][q - T:q - T + 128, :])
        cc = K.S("cc")
        cc_tok = [None]
        if not final:
            def after_block(bi, store_toks):
                if bi >= NCH:
                    return
                W(K.POOL, store_toks)
                ins = K.POOL.collective_compute("AllGather", mybir.AluOpType.bypass, replica_groups=[[0, 1, 2, 3], [4, 5, 6, 7]],
                                                ins=[x1c[bi].ap().opt()], outs=[xgc[bi].ap().opt()])
                cc_tok[0] = cc.inc(ins)

            src["after_block"] = after_block
        emit_layer(K, L, src, with_ctx, final, f"{l}_")
        if not final:
            K.barrier([cc_tok[0]])
            phase_halo(K, f"hl{l}_")
    K.semcounts = {n: s.n for n, s in K.sems.items()}
    return nc, K


def _rope_tables():
    t = np.arange(S, dtype=np.int32)
    row = (t // 64).astype(np.float32)
    col = (t % 64).astype(np.float32)

    def tabs(rot_dim):
        half = rot_dim // 2
        inv = (10000.0 ** (-np.arange(0, half, 2, dtype=np.float32) / np.float32(half))).astype(np.float32)
        ar = (row[:, None] * inv).astype(np.float32)
        ac = (col[:, None] * inv).astype(np.float32)
        cos = np.concatenate([np.cos(ar), np.cos(ar), np.cos(ac), np.cos(ac)], axis=1).T.astype(np.float32)
        sin = np.concatenate([np.sin(ar), np.sin(ar), np.sin(ac), np.sin(ac)], axis=1).T.astype(np.float32)
        return np.ascontiguousarray(cos), np.ascontiguousarray(sin)

    return tabs(64), tabs(32)


def _rot_matrix(n):
    q = n // 4
    P = np.zeros((n, n), np.float32)
    for base in (0, 2 * q):
        for i in range(q):
            P[base + i, base + q + i] = -1.0
            P[base + q + i, base + i] = 1.0
    return P


def _consts():
    c = {}
    c["ident_f"] = np.eye(128, dtype=np.float32)
    c["ones_f"] = np.ones((128, 128), np.float32)
    bd = np.zeros((128, 128), np.float32)
    bd[:64, :64] = 1.0
    bd[64:, 64:] = 1.0
    c["onesbd_f"] = bd
    P64 = _rot_matrix(64)
    P32 = _rot_matrix(32)
    pt128 = np.zeros((128, 128), np.float32)
    pt128[:64, :64] = P64.T
    pt128[64:, 64:] = P64.T
    pt96 = np.zeros((128, 128), np.float32)
    pt96[64:96, 64:96] = P32.T
    pt32 = np.zeros((128, 128), np.float32)
    pt32[:32, :32] = P32.T
    c["ident_b"] = np.eye(128, dtype=np.float32).astype(ml_dtypes.bfloat16)
    c["pt128"] = pt128.astype(ml_dtypes.bfloat16)
    c["pt96"] = pt96.astype(ml_dtypes.bfloat16)
    c["pt32"] = pt32.astype(ml_dtypes.bfloat16)
    return c


def _na_bias_tables(rpb, j):
    out = np.empty((3, 8, 8, 128, 512), np.float32)
    kcol = np.arange(64)[None, :, None, None]
    qcol = np.arange(64)[None, None, None, :]
    m = np.arange(16)[:, None, None, None]
    a = np.arange(8)[None, None, :, None]
    cs = np.clip(qcol - 8, 0, 48)
    colok = (kcol >= cs) & (kcol < cs + 16)
    cidx = np.clip(kcol - qcol + 15, 0, 30)
    for v, i in enumerate((0, 1, 3)):
        r = 32 * j + 8 * i + a
        k = 32 * j + 8 * i - 4 + m
        rs = np.clip(r - 4, 0, 120)
        ok = (k >= 0) & (k < 128) & (k >= rs) & (k < rs + 8) & colok
        ridx = np.clip(k - r + 7, 0, 14)
        ridx_b = np.broadcast_to(ridx, ok.shape)
        cidx_b = np.broadcast_to(cidx, ok.shape)
        vals = rpb[:, ridx_b, cidx_b]
        tab = np.where(ok[None], vals, np.float32(NEG)).astype(np.float32)
        out[v] = tab.reshape(8, 8, 128, 512)
    return out


def _core_inputs(x_full, ctx_full, c, c_ctx, Wd, consts, ropes):
    (cosA, sinA), (cosM, sinM) = ropes
    maps = []
    cosA2 = np.ascontiguousarray(np.concatenate([cosA, cosA], 0))
    sinA2 = np.ascontiguousarray(np.concatenate([sinA, sinA], 0))
    shared = dict(consts)
    for n in W_NAMES:
        shared[n] = np.ascontiguousarray(Wd[n])
    shared["final_norm_g"] = np.ascontiguousarray(Wd["final_norm_g"])
    shared["cosA_all"] = cosA2
    shared["sinA_all"] = sinA2
    shared["cosM_all"] = cosM
    shared["sinM_all"] = sinM
    rpb = np.asarray(Wd["na_rpb"], np.float32)
    nab = [np.stack([_na_bias_tables(rpb[l], j) for l in range(rpb.shape[0])], 0) for j in range(4)]
    for core in range(8):
        b, j = core // 4, core % 4
        t0 = T * j
        d = dict(shared)
        d["x_all"] = np.ascontiguousarray(x_full[b])
        d["x_own"] = np.ascontiguousarray(x_full[b, t0:t0 + T])
        xna = np.zeros((NAT, D), np.float32)
        lo = (32 * j - 4) * 64
        hi = lo + NAT
        slo, shi = max(lo, 0), min(hi, S)
        xna[slo - lo:shi - lo] = x_full[b, slo:shi]
        d["x_na"] = xna
        d["ctx_in"] = np.ascontiguousarray(ctx_full[b])
        d["cvec"] = np.ascontiguousarray(np.stack([c[b], c_ctx]))
        d["cosA_own"] = np.ascontiguousarray(cosA2[:, t0:t0 + T])
        d["sinA_own"] = np.ascontiguousarray(sinA2[:, t0:t0 + T])
        d["cosM_own"] = np.ascontiguousarray(np.concatenate([np.ones((64, T), np.float32), cosM[:, t0:t0 + T]], 0))
        d["sinM_own"] = np.ascontiguousarray(np.concatenate([np.zeros((64, T), np.float32), sinM[:, t0:t0 + T]], 0))
        d["nabias"] = nab[j]
        sel = np.zeros(8, np.float32)
        if j > 0:
            sel[j - 1] = 1.0
        if j < 3:
            sel[4 + j + 1] = 1.0
        d["sel"] = sel
        maps.append(d)
    return maps


_PROG = []


def kernel(**inputs):
    Wd = {k: np.asarray(v, np.float32) for k, v in inputs.items()}
    if not _PROG:
        _PROG.append(build_fused()[0])
    nc = _PROG[0]
    maps = _core_inputs(Wd["x"], Wd["ctx"], Wd["c"], Wd["c_ctx"], Wd, _consts(), _rope_tables())
    res = run_bass_kernel_spmd(nc, maps, core_ids=list(range(8)))
    outs = [r["xout"] for r in res.results]
    x = np.stack([np.concatenate([outs[4 * b + j] for j in range(4)], 0) for b in range(2)], 0)
    return np.ascontiguousarray(x.astype(np.float32))
```

```python
from contextlib import ExitStack
import os
import numpy as np
import ml_dtypes
import concourse.bass as bass
import concourse.mybir as mybir
from concourse.bass_utils import run_bass_kernel_spmd

F32 = mybir.dt.float32
BF16 = mybir.dt.bfloat16
AF = mybir.ActivationFunctionType
ALU = mybir.AluOpType

D = 1024
S = 8192
C = 256
SK = S + C
T = 2048
NAT = 2560
NAK = NAT + C
EPS = 1e-6
NEG = -30000.0
O_GQ, O_GK, O_GV, O_NQ, O_NK, O_NV, O_MQ, O_CKV, O_KR, O_GATE = 0, 512, 640, 768, 1280, 1792, 2304, 3072, 3328, 3360
WIN = 6432


class Sem:
    def __init__(self, nc, name):
        self.h = nc.alloc_semaphore(name)
        self.n = 0

    def inc(self, ins, k=1):
        ins.then_inc(self.h, k)
        self.n += k
        return (self, self.n)


def W(eng, tok):
    if tok is None:
        return
    if isinstance(tok, list):
        for t in tok:
            W(eng, t)
        return
    s, v = tok
    if v > 0:
        eng.wait_ge(s.h, v)


class KB:
    def __init__(self, nc):
        self.nc = nc
        self.PE, self.ACT, self.DVE, self.POOL, self.SP = nc.tensor, nc.scalar, nc.vector, nc.gpsimd, nc.sync
        self.sems = {}
        self.dram = {}

    def S(self, name):
        if name not in self.sems:
            self.sems[name] = Sem(self.nc, name)
        return self.sems[name]

    def dma(self, eng, out, in_, sem, slow=False):
        if slow:
            ins = eng.dma_start(out=out, in_=in_, allow_slow_non_contiguous=True)
        else:
            ins = eng.dma_start(out=out, in_=in_)
        return sem.inc(ins, 16)

    def barrier(self, toks):
        for e in (self.PE, self.ACT, self.DVE, self.POOL, self.SP):
            W(e, toks)


def mk_alloc(nc, es, pfx):
    def sb(name, shape, dt=F32):
        return es.enter_context(nc.sbuf_tensor(pfx + name, shape, dt))

    def ps(name, shape, dt=F32):
        return es.enter_context(nc.psum_tensor(pfx + name, shape, dt))

    return sb, ps


def phase_mod(K, L, pfx):
    nc = K.nc
    PE, ACT, DVE, POOL, SP = K.PE, K.ACT, K.DVE, K.POOL, K.SP
    with ExitStack() as es:
        sb, ps = mk_alloc(nc, es, pfx)
        cT = sb("cT", [128, 8, 2])
        sT = sb("sT", [128, 8, 2])
        NWB = 4
        wm = [sb(f"w{i}", [128, 8, 512]) for i in range(NWB)]
        bm = sb("b", [2, 6144])
        mrow = sb("m", [2, 6144])
        ng = sb("ng", [2, 2, 1024])
        mv = sb("mv", [2, 6, 1024])
        pm = [ps(f"p{i}", [2, 512]) for i in range(2)]
        ld = K.S("ld0")
        wl = [K.S("ld1"), K.S("ld2"), K.S("ld3"), K.S("ld4")]
        s_pe, s_ac, s_dv, st = K.S("pe"), K.S("ac"), K.S("dv"), K.S("st0")
        for m in range(2):
            K.dma(SP, cT[:, :, m], K.dram["cvec"][m].rearrange("(c p) -> p c", p=128), ld, slow=True)
        K.dma(SP, bm[:], L["b_mod"].partition_broadcast(2), ld)
        K.dma(SP, ng[:, 0, :], L["norm1_g"].partition_broadcast(2), ld)
        t_ld = K.dma(SP, ng[:, 1, :], L["norm2_g"].partition_broadcast(2), ld)
        W(ACT, t_ld)
        t_s = s_ac.inc(ACT.activation(out=sT[:].rearrange("p c m -> p (c m)"), in_=cT[:].rearrange("p c m -> p (c m)"), func=AF.Silu))
        W(PE, t_s)
        pe_t = [None] * 12
        dv_t = [None] * 12
        wsrc = L["w_mod"]
        w_t = [None] * 12

        def load_w(g):
            if g >= NWB:
                W(SP, pe_t[g - NWB])
            w_t[g] = K.dma(SP, wm[g % NWB][:], wsrc[:, g * 512:(g + 1) * 512].rearrange("(c p) n -> p c n", p=128), wl[g % NWB])

        for g in range(min(NWB - 1, 12)):
            load_w(g)
        for g in range(12):
            if g + NWB - 1 < 12:
                load_w(g + NWB - 1)
            W(PE, w_t[g])
            if g >= 2:
                W(PE, dv_t[g - 2])
            for c in range(8):
                ins = PE.matmul(pm[g % 2][:], lhsT=sT[:, c, :], rhs=wm[g % NWB][:, c, :], start=(c == 0), stop=(c == 7))
            pe_t[g] = s_pe.inc(ins)
            W(DVE, pe_t[g])
            if g == 0:
                W(DVE, t_ld)
            dv_t[g] = s_dv.inc(DVE.tensor_tensor(out=mrow[:, g * 512:(g + 1) * 512], in0=pm[g % 2][:], in1=bm[:, g * 512:(g + 1) * 512], op=ALU.add))
        W(DVE, dv_t[11])
        sl = lambda i: mrow[:, i * 1024:(i + 1) * 1024]
        DVE.scalar_tensor_tensor(out=mv[:, 0, :], in0=sl(1), scalar=1.0, in1=ng[:, 0, :], op0=ALU.add, op1=ALU.mult)
        DVE.tensor_copy(out=mv[:, 1, :], in_=sl(0))
        DVE.tensor_copy(out=mv[:, 2, :], in_=sl(2))
        DVE.scalar_tensor_tensor(out=mv[:, 3, :], in0=sl(4), scalar=1.0, in1=ng[:, 1, :], op0=ALU.add, op1=ALU.mult)
        DVE.tensor_copy(out=mv[:, 4, :], in_=sl(3))
        t_f = s_dv.inc(DVE.tensor_copy(out=mv[:, 5, :], in_=sl(5)))
        W(SP, t_f)
        t_st = K.dma(SP, K.dram["modv"], mv[:], st)
        K.barrier([t_st])


def phase_norm(K, jobs, pfx):
    nc = K.nc
    PE, ACT, DVE, POOL, SP = K.PE, K.ACT, K.DVE, K.POOL, K.SP
    tiles = []
    for ji, (src, ntok, dst, m, ia, ish) in enumerate(jobs):
        for i in range(ntok // 128):
            tiles.append((ji, i))
    NTI = len(tiles)
    with ExitStack() as es:
        sb, ps = mk_alloc(nc, es, pfx)
        NX = 6
        xt = [sb(f"xt{i}", [128, 1024]) for i in range(NX)]
        junk = sb("junk", [128, 1024])
        ss = sb("ss", [128, NTI])
        r1 = sb("r1", [128, NTI])
        r2 = sb("r2", [128, NTI])
        rstd = sb("rstd", [128, NTI])
        NXN = 3
        xn = [sb(f"xn{i}", [128, 1024]) for i in range(NXN)]
        hb = [sb(f"hb{i}", [128, 8, 512], BF16) for i in range(2)]
        ident = sb("ident", [128, 128])
        acol = sb("acol", [128, 2, 2, 8])
        pT = [ps(f"pT{i}", [128, 8, 128]) for i in range(2)]
        lds = [K.S("ld0"), K.S("ld1"), K.S("ld3"), K.S("ld4"), K.S("ld5"), K.S("ld6")]
        ldc = K.S("ld2")
        s_pe, s_ac, s_dv = K.S("pe"), K.S("ac"), K.S("dv")
        sts = [K.S("gs0"), K.S("gs1")]
        t_c = K.dma(SP, ident[:], K.dram["ident_f"], ldc)
        mods = sorted(set((j[3], j[4], j[5]) for j in jobs))
        assert len(set(m for m, _, _ in mods)) == len(mods)
        for (m, ia, ish) in mods:
            K.dma(SP, acol[:, m, 0, :], K.dram["modv"][m, ia].rearrange("(c p) -> p c", p=128), ldc, slow=True)
            t_c = K.dma(SP, acol[:, m, 1, :], K.dram["modv"][m, ish].rearrange("(c p) -> p c", p=128), ldc, slow=True)
        act_t = [None] * NTI
        for n, (ji, i) in enumerate(tiles):
            src = jobs[ji][0]
            if n >= NX:
                W(SP, act_t[n - NX])
            t_l = K.dma(SP, xt[n % NX][:], src(i) if callable(src) else src[i * 128:(i + 1) * 128, :], lds[n % NX])
            W(ACT, t_l)
            act_t[n] = s_ac.inc(ACT.activation(out=junk[:], in_=xt[n % NX][:], func=AF.Square, accum_out=ss[:, n:n + 1]))
        W(DVE, act_t[NTI - 1])
        t1 = s_dv.inc(DVE.tensor_scalar(out=r1[:], in0=ss[:], scalar1=1.0 / D, scalar2=EPS, op0=ALU.mult, op1=ALU.add))
        W(ACT, t1)
        t2 = s_ac.inc(ACT.activation(out=r2[:], in_=r1[:], func=AF.Sqrt))
        W(DVE, t2)
        t3 = s_dv.inc(DVE.reciprocal(out=rstd[:], in_=r2[:]))
        W(ACT, t3)
        W(SP, t2)
        W(PE, t_c)
        W(DVE, t_c)
        a_t = [None] * NTI
        p_t = [None] * NTI
        v_t = [None] * NTI
        st_t = {}
        blk = -1
        blk_of = []
        prev_key = None
        for n, (ji, i) in enumerate(tiles):
            key = (ji, i // 4)
            if key != prev_key:
                blk += 1
                prev_key = key
            blk_of.append(blk)
        for n, (ji, i) in enumerate(tiles):
            src, ntok, dst, m, ia, ish = jobs[ji]
            b = blk_of[n]
            if n >= NX:
                W(SP, a_t[n - NX])
            t_l = K.dma(SP, xt[n % NX][:], src(i) if callable(src) else src[i * 128:(i + 1) * 128, :], lds[n % NX])
            W(ACT, t_l)
            if n >= NXN:
                W(ACT, p_t[n - NXN])
            a_t[n] = s_ac.inc(ACT.activation(out=xn[n % NXN][:], in_=xt[n % NXN if False else n % NX][:], func=AF.Copy, scale=rstd[:, n:n + 1]))
            W(PE, a_t[n])
            if n >= 2:
                W(PE, v_t[n - 2])
            for c in range(8):
                ins = PE.transpose(out=pT[n % 2][:, c, :], in_=xn[n % NXN][:, c * 128:(c + 1) * 128], identity=ident[:])
            p_t[n] = s_pe.inc(ins)
            W(DVE, p_t[n])
            if (i % 4 == 0) and (b - 2) in st_t:
                W(DVE, st_t[b - 2])
            for c in range(8):
                ins = DVE.tensor_scalar(out=hb[b % 2][:, c, (i % 4) * 128:(i % 4 + 1) * 128], in0=pT[n % 2][:, c, :],
                                        scalar1=acol[:, m, 0, c:c + 1], scalar2=acol[:, m, 1, c:c + 1], op0=ALU.mult, op1=ALU.add)
            v_t[n] = s_dv.inc(ins)
            last_in_blk = (n + 1 == NTI) or (blk_of[n + 1] != b)
            if last_in_blk:
                nt = (i % 4 + 1) * 128
                t0 = (i // 4) * 512
                W(POOL, v_t[n])
                st_t[b] = K.dma(POOL, dst[:, t0:t0 + nt].rearrange("(c p) t -> p c t", p=128), hb[b % 2][:, :, 0:nt], sts[b % 2])
        K.barrier([st_t[blk], st_t.get(blk - 1)])


def load_proj_weights(K, L, sb):
    win = sb("win", [128, 8, O_GATE], BF16)
    wuk = sb("wuk", [128, 2, 512], BF16)
    wuv = sb("wuv", [128, 2, 512], BF16)
    gw = K.S("gw")
    for c in range(8):
        K.dma(K.POOL, win[:, c, :], L["w_in"][c * 128:(c + 1) * 128, 0:O_GATE], gw)
    K.dma(K.POOL, wuk[:], L["mla_w_uk"].rearrange("(r p) n -> p r n", p=128), gw)
    t_gw = K.dma(K.POOL, wuv[:], L["mla_w_uv"].rearrange("(r p) n -> p r n", p=128), gw)
    return win, wuk, wuv, t_gw


def load_merge_weights(K, L, sb):
    wg = sb("wg", [128, 8, 3072], BF16)
    wo = [sb(f"wo{i}", [128, 4, 1024], BF16) for i in range(3)]
    wout = sb("wout", [128, 8, 1024], BF16)
    gw = K.S("gw")
    for c in range(8):
        K.dma(K.POOL, wg[:, c, :], L["w_in"][c * 128:(c + 1) * 128, O_GATE:WIN], gw)
    for r, nm in enumerate(("w_o_gqa", "w_o_na", "w_o_mla")):
        K.dma(K.POOL, wo[r][:], L[nm].rearrange("(c p) n -> p c n", p=128), gw)
    t_gw = K.dma(K.POOL, wout[:], L["w_out"].rearrange("(c p) n -> p c n", p=128), gw)
    return wg, wo, wout, t_gw


def phase_proj(K, L, pfx, with_ctx, pre=None):
    nc = K.nc
    PE, ACT, DVE, POOL, SP = K.PE, K.ACT, K.DVE, K.POOL, K.SP
    dr = K.dram
    with ExitStack() as es:
        sb, ps = mk_alloc(nc, es, pfx)
        win, wuk, wuv, t_gw = pre if pre is not None else load_proj_weights(K, L, sb)
        onesbd = sb("onesbd", [128, 128])
        ones = sb("ones", [128, 128])
        pt128 = sb("pt128", [128, 128], BF16)
        pt96 = sb("pt96", [128, 128], BF16)
        pt32 = sb("pt32", [128, 128], BF16)
        gq = sb("gq", [128, 1])
        gk = sb("gk", [128, 1])
        kvg = sb("kvg", [128, 2])
        hblk = [sb(f"h{i}", [128, 8, 512], BF16) for i in range(2)]
        ckvn = sb("ckvn", [128, 2, 512], BF16)
        sqf = sb("sqf", [128, 512]); sqf2 = sb("sqf2", [128, 512])
        qf = sb("qf", [128, 512]); qf2 = sb("qf2", [128, 512])
        sd = sb("sd", [128, 512]); rs = sb("rs", [128, 512]); qn = sb("qn", [128, 512])
        t1b = sb("t1", [128, 512]); t2b = sb("t2", [128, 512])
        cosb = [sb(f"cos{i}", [128, 512]) for i in range(2)]
        sinb = [sb(f"sin{i}", [128, 512]) for i in range(2)]
        qb = sb("qb", [128, 512], BF16)
        outb = [sb(f"ob{i}", [128, 512], BF16) for i in range(2)]
        vout = [sb(f"vo{i}", [128, 8, 65], BF16) for i in range(2)]
        acc = [ps(f"acc{i}", [128, 512]) for i in range(2)]
        acc2 = ps("acc2", [128, 512])
        pss = ps("pss", [128, 512])
        prot = ps("prot", [128, 512])
        ptm = [ps(f"ptm{i}", [128, 512]) for i in range(2)]
        wl = K.S("ld2")
        gw = K.S("gw")
        hl = [K.S("ld0"), K.S("ld1")]
        tl = [K.S("ld3"), K.S("ld4")]
        s_pe, s_ac, s_dv, s_pl = K.S("pe"), K.S("ac"), K.S("dv"), K.S("pl")
        sto = [K.S("gs0"), K.S("gs1")]
        stv = [K.S("gs2"), K.S("gs3")]
        stc = K.S("gs4")
        K.dma(SP, onesbd[:], dr["onesbd_f"], wl)
        K.dma(SP, ones[:], dr["ones_f"], wl)
        K.dma(SP, pt128[:], dr["pt128"], wl)
        K.dma(SP, pt96[:], dr["pt96"], wl)
        K.dma(SP, pt32[:], dr["pt32"], wl)
        for hh in range(2):
            K.dma(SP, gq[hh * 64:(hh + 1) * 64, :], L["gqa_q_norm"].rearrange("(p o) -> p o", o=1), wl)
            K.dma(SP, gk[hh * 64:(hh + 1) * 64, :], L["gqa_k_norm"].rearrange("(p o) -> p o", o=1), wl)
        t_w = K.dma(SP, kvg[:], L["mla_kv_norm"].rearrange("(r p) -> p r", p=128), wl, slow=True)
        for i in range(2):
            DVE.memset(vout[i][:], 1.0)
        t_ms = s_dv.inc(DVE.memset(qn[:], 0.0))
        for e in (PE, ACT, DVE, POOL):
            W(e, t_w)
            W(e, t_gw)
        W(ACT, t_ms)

        st = {"k": 0, "rk": 0, "vk": 0, "acc_free": [None, None], "ob_free": [None, None], "tab_free": [None, None],
              "vo_free": [None, None], "ptm_free": [None, None], "hb_tok": None}

        def store(eng, dst, src, sem):
            return K.dma(eng, dst, src, sem)

        import os
        LIMIT = int(os.environ.get("PROJ_LIMIT", "1000000"))
        units = [0]

        def over():
            units[0] += 1
            return units[0] > LIMIT

        def fm_job(chunks, M, nt, norm_g, rope, dsts, oscale=1.0):
            if over():
                return
            k = st["k"]; st["k"] += 1
            a = acc[k % 2]
            W(PE, st["acc_free"][k % 2])
            W(PE, st["hb_tok"])
            for ci, (lt, rh) in enumerate(chunks):
                ins = PE.matmul(a[:M, :nt], lhsT=lt, rhs=rh, start=(ci == 0), stop=(ci == len(chunks) - 1))
            t_main = s_pe.inc(ins)
            ob = outb[k % 2]
            if norm_g is None and rope is None:
                W(ACT, t_main)
                W(ACT, st["ob_free"][k % 2])
                t_out = s_ac.inc(ACT.activation(out=ob[:M, :nt], in_=a[:M, :nt], func=AF.Copy, scale=float(oscale)))
                st["acc_free"][k % 2] = t_out
            else:
                if rope is not None:
                    r = st["rk"]; st["rk"] += 1
                    PT, cos_ap, sin_ap = rope
                    W(SP, st["tab_free"][r % 2])
                    K.dma(SP, cosb[r % 2][:M, :nt], cos_ap, tl[r % 2])
                    t_tab = K.dma(SP, sinb[r % 2][:M, :nt], sin_ap, tl[r % 2])
                W(DVE, t_main)
                t_qf = s_dv.inc(DVE.tensor_copy(out=qf[:M, :nt], in_=a[:M, :nt]))
                t_cur = t_qf
                cur = qf
                free_toks = [t_qf]
                if norm_g is not None:
                    W(ACT, t_qf)
                    t_sq = s_ac.inc(ACT.activation(out=sqf[:M, :nt], in_=qf[:M, :nt], func=AF.Square))
                    W(PE, t_sq)
                    t_ss = s_pe.inc(PE.matmul(pss[:M, :nt], lhsT=onesbd[:M, :M], rhs=sqf[:M, :nt], start=True, stop=True))
                    W(ACT, t_ss)
                    W(ACT, t_qf)
                    t_sd = s_ac.inc(ACT.activation(out=sd[:M, :nt], in_=pss[:M, :nt], func=AF.Sqrt, bias=EPS, scale=1.0 / 64))
                    W(DVE, t_sd)
                    t_rs = s_dv.inc(DVE.reciprocal(out=rs[:M, :nt], in_=sd[:M, :nt]))
                    W(DVE, t_rs)
                    if rope is None:
                        W(DVE, st["ob_free"][k % 2])
                        t_out = s_dv.inc(DVE.scalar_tensor_tensor(out=ob[:M, :nt], in0=qf[:M, :nt], scalar=norm_g, in1=rs[:M, :nt], op0=ALU.mult, op1=ALU.mult))
                    else:
                        t_cur = s_dv.inc(DVE.scalar_tensor_tensor(out=qn[:M, :nt], in0=qf[:M, :nt], scalar=norm_g, in1=rs[:M, :nt], op0=ALU.mult, op1=ALU.mult))
                        cur = qn
                st["acc_free"][k % 2] = free_toks
                if rope is not None:
                    W(ACT, t_cur)
                    t_qb = s_ac.inc(ACT.activation(out=qb[:M, :nt], in_=cur[:M, :nt], func=AF.Copy))
                    W(PE, t_qb)
                    t_rot = s_pe.inc(PE.matmul(prot[:M, :nt], lhsT=PT[:M, :M], rhs=qb[:M, :nt], start=True, stop=True))
                    W(POOL, t_cur)
                    W(POOL, t_tab)
                    t_t1 = s_pl.inc(POOL.tensor_tensor(out=t1b[:M, :nt], in0=cur[:M, :nt], in1=cosb[r % 2][:M, :nt], op=ALU.mult))
                    W(DVE, t_rot)
                    W(DVE, t_tab)
                    t_t2 = s_dv.inc(DVE.tensor_tensor(out=t2b[:M, :nt], in0=prot[:M, :nt], in1=sinb[r % 2][:M, :nt], op=ALU.mult))
                    W(DVE, t_t1)
                    W(DVE, t_t2)
                    W(DVE, st["ob_free"][k % 2])
                    t_out = s_dv.inc(DVE.tensor_tensor(out=ob[:M, :nt], in0=t1b[:M, :nt], in1=t2b[:M, :nt], op=ALU.add))
                    st["tab_free"][r % 2] = t_out
            W(POOL, t_out)
            for (dst, r0, r1) in dsts:
                t_st = store(POOL, dst, ob[r0:r1, :nt], sto[k % 2])
            st["ob_free"][k % 2] = t_st

        def ckv_job(hs, nt, dst_ckvt):
            if over():
                st["ckvn_tok"] = None
                return
            k = st["k"]; st["k"] += 1
            a = acc[k % 2]
            W(PE, st["acc_free"][k % 2])
            W(PE, st["hb_tok"])
            W(PE, st.get("acc2_free"))
            for g, aa in enumerate((a, acc2)):
                for c in range(8):
                    ins = PE.matmul(aa[:, :nt], lhsT=win[:, c, O_CKV + g * 128:O_CKV + (g + 1) * 128], rhs=hblk[hs][:, c, :nt], start=(c == 0), stop=(c == 7))
            t_main = s_pe.inc(ins)
            CUT = int(os.environ.get("CKV_CUT", "99"))
            st["ckvn_tok"] = None
            if CUT <= 1:
                return
            W(DVE, t_main)
            DVE.tensor_copy(out=qf[:, :nt], in_=a[:, :nt])
            t_qf = s_dv.inc(DVE.tensor_copy(out=qf2[:, :nt], in_=acc2[:, :nt]))
            W(ACT, t_qf)
            ACT.activation(out=sqf[:, :nt], in_=qf[:, :nt], func=AF.Square)
            t_sq = s_ac.inc(ACT.activation(out=sqf2[:, :nt], in_=qf2[:, :nt], func=AF.Square))
            st["acc_free"][k % 2] = [t_qf]
            st["acc2_free"] = [t_qf]
            if CUT <= 2:
                return
            W(PE, t_sq)
            PE.matmul(pss[:, :nt], lhsT=ones[:], rhs=sqf[:, :nt], start=True, stop=False)
            t_ss = s_pe.inc(PE.matmul(pss[:, :nt], lhsT=ones[:], rhs=sqf2[:, :nt], start=False, stop=True))
            if CUT <= 3:
                return
            W(ACT, t_ss)
            W(ACT, t_qf)
            t_sd = s_ac.inc(ACT.activation(out=sd[:, :nt], in_=pss[:, :nt], func=AF.Sqrt, bias=EPS, scale=1.0 / 256))
            W(DVE, t_sd)
            t_rs = s_dv.inc(DVE.reciprocal(out=rs[:, :nt], in_=sd[:, :nt]))
            if CUT <= 4:
                return
            W(DVE, t_rs)
            W(DVE, st.get("ckvn_free"))
            DVE.scalar_tensor_tensor(out=ckvn[:, 0, :nt], in0=qf[:, :nt], scalar=kvg[:, 0:1], in1=rs[:, :nt], op0=ALU.mult, op1=ALU.mult)
            t_out = s_dv.inc(DVE.scalar_tensor_tensor(out=ckvn[:, 1, :nt], in0=qf2[:, :nt], scalar=kvg[:, 1:2], in1=rs[:, :nt], op0=ALU.mult, op1=ALU.mult))
            if CUT <= 5:
                return
            W(POOL, t_out)
            t_st = store(POOL, dst_ckvt.rearrange("(r p) t -> p r t", p=128), ckvn[:, :, :nt], stc)
            st["ckvn_tok"] = t_out
            st["ckvn_st"] = t_st

        def tm_job(chunks, N, nh, dst):
            if over():
                return
            j = st["vk"]; st["vk"] += 1
            p = ptm[j % 2]
            W(PE, st["ptm_free"][j % 2])
            W(PE, st["hb_tok"])
            for ci, (lt, rh) in enumerate(chunks):
                ins = PE.matmul(p[:, :N], lhsT=lt, rhs=rh, start=(ci == 0), stop=(ci == len(chunks) - 1))
            t_main = s_pe.inc(ins)
            W(ACT, t_main)
            W(ACT, st["vo_free"][j % 2])
            t_o = s_ac.inc(ACT.activation(out=vout[j % 2][:, 0:nh, 0:64], in_=p[:, :N].rearrange("p (h d) -> p h d", d=64), func=AF.Copy))
            st["ptm_free"][j % 2] = t_o
            W(POOL, t_o)
            st["vo_free"][j % 2] = store(POOL, dst, vout[j % 2][:, 0:nh, :], stv[j % 2])

        nblk = [0]
        last_users = [None, None]

        hT_all, hT_na = dr["hT_all"], dr["hT_na"]
        bsrc = [(hT_all[:, tb * 512:tb * 512 + (256 if tb == 16 else 512)], 256 if tb == 16 else 512) for tb in range(17)]
        bsrc += [(hT_na[:, tb * 512:(tb + 1) * 512], 512) for tb in range(5)]
        bsrc += [(hT_na[:, 256 + tb * 512:256 + (tb + 1) * 512], 512) for tb in range(4)]
        btok = [None] * len(bsrc)
        issued = [0]

        def issue_upto(i):
            while issued[0] <= i and issued[0] < len(bsrc):
                b = issued[0]
                src_ap, nt_ = bsrc[b]
                W(SP, last_users[b % 2])
                btok[b] = K.dma(SP, hblk[b % 2][:, :, :nt_], src_ap.rearrange("(c p) t -> p c t", p=128), hl[b % 2])
                issued[0] += 1

        def load_block(src_ap, nt):
            b = nblk[0]; nblk[0] += 1
            issue_upto(b)
            st["hb_tok"] = btok[b]
            issue_upto(b + 1)
            return b % 2

        def done_block(hs):
            last_users[hs] = (s_pe, s_pe.n)

        def hch(hs, c0, M, nt):
            return [(win[:, c, c0:c0 + M], hblk[hs][:, c, :nt]) for c in range(8)]

        for tb in range(17):
            ctxb = (tb == 16)
            t0 = tb * 512
            nt = 256 if ctxb else 512
            hs = load_block(hT_all[:, t0:t0 + nt], nt)
            rope = None if ctxb else (pt128, dr["cosA_all"][:, t0:t0 + nt], dr["sinA_all"][:, t0:t0 + nt])
            fm_job(hch(hs, O_GK, 128, nt), 128, nt, gk[:, 0:1], rope,
                   [(dr["GKT"][0, :, t0:t0 + nt], 0, 64), (dr["GKT"][1, :, t0:t0 + nt], 64, 128)])
            ckv_job(hs, nt, dr["CKVT"][:, t0:t0 + nt])
            rope = None if ctxb else (pt32, dr["cosM_all"][:, t0:t0 + nt], dr["sinM_all"][:, t0:t0 + nt])
            fm_job(hch(hs, O_KR, 32, nt), 32, nt, None, rope, [(dr["MKT"][h, 64:96, t0:t0 + nt], 0, 32) for h in range(8)])
            W(PE, st["ckvn_tok"])
            for g in range(4):
                fm_job([(wuk[:, r, g * 128:(g + 1) * 128], ckvn[:, r, :nt]) for r in range(2)], 128, nt, None, None,
                       [(dr["MKT"][2 * g, 0:64, t0:t0 + nt], 0, 64), (dr["MKT"][2 * g + 1, 0:64, t0:t0 + nt], 64, 128)])
            for ti in range(nt // 128):
                tsl = slice(ti * 128, (ti + 1) * 128)
                r0 = t0 + ti * 128
                tm_job([(ckvn[:, r, tsl], wuv[:, r, :]) for r in range(2)], 512, 8, dr["MV"][r0:r0 + 128, :, :])
                tm_job([(hblk[hs][:, c, tsl], win[:, c, O_GV:O_GV + 128]) for c in range(8)], 128, 2, dr["GV"][r0:r0 + 128, :, :])
            st["ckvn_free"] = (s_pe, s_pe.n)
            if ctxb:
                for g in range(4):
                    fm_job(hch(hs, O_NK + g * 128, 128, nt), 128, nt, None, None,
                           [(dr["NKT"][2 * g, :, NAT:NAT + nt], 0, 64), (dr["NKT"][2 * g + 1, :, NAT:NAT + nt], 64, 128)])
                for ti in range(nt // 128):
                    tsl = slice(ti * 128, (ti + 1) * 128)
                    tm_job([(hblk[hs][:, c, tsl], win[:, c, O_NV:O_NV + 512]) for c in range(8)], 512, 8, dr["NV"][NAT + ti * 128:NAT + (ti + 1) * 128, :, :])
                if with_ctx:
                    q0 = T
                    for g in range(4):
                        fm_job(hch(hs, O_GQ + g * 128, 128, nt), 128, nt, gq[:, 0:1], None,
                               [(dr["GQT"][2 * g, :, q0:q0 + nt], 0, 64), (dr["GQT"][2 * g + 1, :, q0:q0 + nt], 64, 128)])
                        fm_job(hch(hs, O_NQ + g * 128, 128, nt), 128, nt, None, None,
                               [(dr["NQT"][2 * g, :, q0:q0 + nt], 0, 64), (dr["NQT"][2 * g + 1, :, q0:q0 + nt], 64, 128)], oscale=0.125)
                    for h in range(8):
                        fm_job(hch(hs, O_MQ + h * 96, 96, nt), 96, nt, None, None, [(dr["MQT"][h, :, q0:q0 + nt], 0, 96)])
            done_block(hs)
        for tb in range(5):
            t0 = tb * 512
            nt = 512
            hs = load_block(hT_na[:, t0:t0 + nt], nt)
            for g in range(4):
                fm_job(hch(hs, O_NK + g * 128, 128, nt), 128, nt, None, None,
                       [(dr["NKT"][2 * g, :, t0:t0 + nt], 0, 64), (dr["NKT"][2 * g + 1, :, t0:t0 + nt], 64, 128)])
            for ti in range(4):
                tsl = slice(ti * 128, (ti + 1) * 128)
                tm_job([(hblk[hs][:, c, tsl], win[:, c, O_NV:O_NV + 512]) for c in range(8)], 512, 8, dr["NV"][t0 + ti * 128:t0 + (ti + 1) * 128, :, :])
            done_block(hs)
        for tb in range(4):
            q0 = tb * 512
            nt = 512
            hs = load_block(hT_na[:, 256 + q0:256 + q0 + nt], nt)
            for g in range(4):
                fm_job(hch(hs, O_GQ + g * 128, 128, nt), 128, nt, gq[:, 0:1], (pt128, dr["cosA_own"][:, q0:q0 + nt], dr["sinA_own"][:, q0:q0 + nt]),
                       [(dr["GQT"][2 * g, :, q0:q0 + nt], 0, 64), (dr["GQT"][2 * g + 1, :, q0:q0 + nt], 64, 128)])
                fm_job(hch(hs, O_NQ + g * 128, 128, nt), 128, nt, None, None,
                       [(dr["NQT"][2 * g, :, q0:q0 + nt], 0, 64), (dr["NQT"][2 * g + 1, :, q0:q0 + nt], 64, 128)], oscale=0.125)
            for h in range(8):
                fm_job(hch(hs, O_MQ + h * 96, 96, nt), 96, nt, None, (pt96, dr["cosM_own"][:, q0:q0 + nt], dr["sinM_own"][:, q0:q0 + nt]),
                       [(dr["MQT"][h, :, q0:q0 + nt], 0, 96)])
            done_block(hs)
        K.barrier([(s, s.n) for s in sto + stv + [stc]])


def phase_attn(K, heads, pfx, nkmax):
    nc = K.nc
    PE, ACT, DVE, POOL, SP = K.PE, K.ACT, K.DVE, K.POOL, K.SP
    NQ = T + C
    rls = K.dram["rls"]
    with ExitStack() as es:
        sb, ps = mk_alloc(nc, es, pfx)
        ktb = [sb(f"kt{i}", [128, nkmax], BF16) for i in range(2)]
        vb = [sb(f"v{i}", [128, nkmax // 128, 65], BF16) for i in range(2)]
        qb = [sb(f"q{i}", [128, NQ], BF16) for i in range(2)]
        pbuf = [sb(f"p{i}", [128, 1024], BF16) for i in range(3)]
        NBB, LB = 8, 6
        bb = [sb(f"bias{i}", [128, 512], BF16) for i in range(NBB)]
        identb = sb("identb", [128, 128], BF16)
        osb = [sb(f"osb{i}", [128, 512]) for i in range(2)]
        rl = [sb(f"rl{i}", [128, 512]) for i in range(2)]
        rbc = [sb(f"rbc{i}", [64, 512]) for i in range(2)]
        ysb = [[sb(f"y{i}_{w}", [64, 512], BF16) for w in range(2)] for i in range(2)]
        psb = [ps(f"s{i}", [128, 1024]) for i in range(3)]
        po = [ps(f"o{i}", [128, 512]) for i in range(2)]
        hl = [K.S("ld0"), K.S("ld1")]
        bl = [K.S(f"gb{i}") for i in range(NBB)]
        cl = K.S("ld5")
        rld = [K.S("ld3"), K.S("ld4")]
        rst = [K.S("st2"), K.S("st3")]
        s_pe, s_ac, s_dv = K.S("pe"), K.S("ac"), K.S("dv")
        sty = [K.S("st0"), K.S("st1")]
        t_c = K.dma(SP, identb[:], K.dram["ident_b"], cl)
        for i in range(2):
            DVE.memset(ktb[i][:], 0.0)
            DVE.memset(qb[i][:], 0.0)
            DVE.memset(rl[i][:], 1.0)
        t_m = s_dv.inc(DVE.memset(osb[0][:], 0.0))
        W(PE, t_c)
        W(PE, t_m)
        W(SP, t_m)
        steps = []
        for hi, h in enumerate(heads):
            for bi, b in enumerate(h["blocks"]):
                nt_ = len(b["tiles"])
                b["chunks"] = [(c0, min(512, b["nq"] - c0)) for c0 in range(0, b["nq"], 512)]
                for si, (kti, bias) in enumerate(b["tiles"]):
                    assert bias is None or len(b["chunks"]) == 1
                    steps.append(dict(hi=hi, b=b, kti=kti, bias=bias, first=(si == 0), last=(si == nt_ - 1),
                                      hfirst=(bi == 0 and si == 0), hlast=(bi == len(h["blocks"]) - 1 and si == nt_ - 1)))
        NS = len(steps)
        head_tok = [None] * len(heads)
        head_done = [None] * len(heads)

        def load_head(hi):
            h = heads[hi]
            s = hi % 2
            if hi >= 2:
                W(SP, head_done[hi - 2])
            dk, nk = h["dk"], h["nk"]
            K.dma(SP, ktb[s][:dk, :nk], h["kt"], hl[s])
            K.dma(SP, vb[s][:, :nk // 128, :], h["v"].rearrange("(t p) e -> p t e", p=128), hl[s])
            head_tok[hi] = K.dma(SP, qb[s][:dk, :], h["qt"], hl[s])

        tq = [None] * NS
        te = [None] * NS
        tv = [None] * NS
        bias_ld = [None] * NS
        nbias = [0]
        bidx = [None] * NS
        po_free = [None, None]
        y_free = [[None, None], [None, None]]
        pend_dv = {}
        state = {}

        def emit_qk(t):
            s = steps[t]
            h = heads[s["hi"]]
            hs = s["hi"] % 2
            if s["hfirst"]:
                W(PE, head_tok[s["hi"]])
            if t >= 3:
                W(PE, te[t - 3])
            b = s["b"]
            hasb = s["bias"] is not None
            for (c0, cn) in b["chunks"]:
                ins = PE.matmul(psb[t % 3][:, c0:c0 + cn], lhsT=ktb[hs][:, s["kti"] * 128:(s["kti"] + 1) * 128],
                                rhs=qb[hs][:, b["q0"] + c0:b["q0"] + c0 + cn], start=True, stop=not hasb)
            if hasb:
                W(PE, bias_ld[t])
                ins = PE.matmul(psb[t % 3][:, :b["nq"]], lhsT=identb[:], rhs=bb[bidx[t] % NBB][:, :b["nq"]], start=False, stop=True)
            tq[t] = s_pe.inc(ins)
            if hasb:
                state[("bfree", bidx[t] % NBB)] = tq[t]

        def emit_bias_load(t):
            s = steps[t]
            if s["bias"] is None:
                return
            n = nbias[0]; nbias[0] += 1
            bidx[t] = n
            W(POOL, state.get(("bfree", n % NBB)))
            bias_ld[t] = K.dma(POOL, bb[n % NBB][:, :s["b"]["nq"]], s["bias"], bl[n % NBB])

        if NS > 0:
            load_head(0)
        LA = 2
        for t in range(min(LB, NS)):
            emit_bias_load(t)
        for t in range(min(LA, NS)):
            emit_qk(t)
        cur_blk = -1
        for t in range(NS):
            s = steps[t]
            h = heads[s["hi"]]
            b = s["b"]
            nq = b["nq"]
            hs = s["hi"] % 2
            if s["hfirst"] and s["hi"] + 1 < len(heads):
                load_head(s["hi"] + 1)
            if s["first"]:
                cur_blk += 1
            if t + LB < NS:
                emit_bias_load(t + LB)
            if t + LA < NS:
                emit_qk(t + LA)
            W(ACT, tq[t])
            if t >= 3:
                W(ACT, tv[t - 3])
            te[t] = s_ac.inc(ACT.activation(out=pbuf[t % 3][:, :nq], in_=psb[t % 3][:, :nq], func=AF.Exp, scale=float(h["scale"])))
            W(PE, te[t])
            for w, (c0, cn) in enumerate(b["chunks"]):
                if s["first"]:
                    W(PE, po_free[w])
                ins = PE.matmul(po[w][:65, :cn], lhsT=vb[hs][:, s["kti"], :], rhs=pbuf[t % 3][:, c0:c0 + cn], start=s["first"], stop=s["last"])
            tv[t] = s_pe.inc(ins)
            if s["hlast"]:
                head_done[s["hi"]] = tv[t]
            for f in pend_dv.pop(t, []):
                f()
            if s["last"]:
                cb = cur_blk
                parts = []
                for w, (c0, cn) in enumerate(b["chunks"]):
                    W(DVE, tv[t])
                    t_o = s_dv.inc(DVE.tensor_copy(out=osb[w][:65, :cn], in_=po[w][:65, :cn]))
                    po_free[w] = t_o
                    W(DVE, t_o)
                    t_rl = s_dv.inc(DVE.reciprocal(out=rl[w][64:65, :cn], in_=osb[w][64:65, :cn]))
                    slot = (cb % 2) * 2 + w
                    W(SP, t_rl)
                    t_s = K.dma(SP, rls[slot:slot + 1, :cn], rl[w][64:65, :cn], rst[w])
                    W(SP, t_s)
                    t_b = K.dma(SP, rbc[w][:, :cn], rls[slot, :cn].partition_broadcast(64), rld[w])
                    parts.append((w, cn, t_b))

                def dv_part(parts=parts, cb=cb, yts=b["yts"]):
                    for (w, cn, t_b) in parts:
                        W(DVE, t_b)
                        W(DVE, y_free[cb % 2][w])
                        t_y = s_dv.inc(DVE.tensor_tensor(out=ysb[cb % 2][w][:, :cn], in0=osb[w][:64, :cn], in1=rbc[w][:, :cn], op=ALU.mult))
                        W(SP, t_y)
                        y_free[cb % 2][w] = K.dma(SP, yts[w], ysb[cb % 2][w][:, :cn], sty[w])

                if t + 1 < NS:
                    nxt_len = len(steps[t + 1]["b"]["tiles"])
                    d = max(1, min(2, nxt_len - 1))
                    pend_dv.setdefault(t + d, []).append(dv_part)
                else:
                    dv_part()
        assert not pend_dv
        K.barrier([(s_, s_.n) for s_ in sty])


def phase_merge(K, L, pfx, qblocks, x_src, x_dst, pre=None):
    nc = K.nc
    PE, ACT, DVE, POOL, SP = K.PE, K.ACT, K.DVE, K.POOL, K.SP
    dr = K.dram
    with ExitStack() as es:
        sb, ps = mk_alloc(nc, es, pfx)
        wg, wo, wout, t_gw = pre if pre is not None else load_merge_weights(K, L, sb)
        g1 = sb("g1", [128, 2, 1024])
        hblk = [sb(f"h{i}", [128, 8, 512], BF16) for i in range(2)]
        yb = [[sb(f"y{r}_{i}", [128, 4, 512], BF16) for r in range(3)] for i in range(2)]
        sg = [sb(f"sg{i}", [128, 512]) for i in range(2)]
        yacc = sb("yacc", [128, 512])
        tmp = sb("tmp", [128, 512])
        yT = sb("yT", [128, 8, 512], BF16)
        xt = [sb(f"xt{i}", [128, 1024]) for i in range(2)]
        xo = [sb(f"xo{i}", [128, 1024]) for i in range(2)]
        tm2 = [sb(f"tm{i}", [128, 512]) for i in range(2)]
        pg = [ps(f"pg{i}", [128, 512]) for i in range(2)]
        pbr = [ps(f"pb{i}", [128, 512]) for i in range(2)]
        pw = [ps(f"pw{i}", [128, 512]) for i in range(2)]
        wl = K.S("ld2")
        hl = [K.S("ld0"), K.S("ld1")]
        xl = [K.S("ld3"), K.S("ld4")]
        s_pe, s_ac, s_dv, s_pl = K.S("pe"), K.S("ac"), K.S("dv"), K.S("pl")
        stx = [K.S("gs0"), K.S("gs1")]
        K.dma(SP, g1[:, 0, :], dr["modv"][0, 2].partition_broadcast(128), wl)
        t_w = K.dma(SP, g1[:, 1, :], dr["modv"][1, 2].partition_broadcast(128), wl)
        for e in (PE, DVE, POOL):
            W(e, t_w)
            W(e, t_gw)
        ysrc = (dr["YAT"], dr["YBT"], dr["YCT"])
        blk_done = [None, None]
        k = 0
        xk = 0
        sg_free = [None, None]
        pg_free = [None, None]
        pbr_free = [None, None]
        pw_free = [None, None]
        xt_free = [None, None]
        xo_free = [None, None]
        tm_free = [None, None]
        yT_free = None
        for bi, (hT_ap, q0, nt, m) in enumerate(qblocks):
            s = bi % 2
            W(SP, blk_done[s])
            K.dma(SP, hblk[s][:, :, :nt], hT_ap.rearrange("(c p) t -> p c t", p=128), hl[s])
            for r in range(3):
                t_l = K.dma(SP, yb[s][r][:, :, :nt], ysrc[r][:, q0:q0 + nt].rearrange("(c p) t -> p c t", p=128), hl[s])
            W(PE, t_l)
            for oc in range(8):
                for r in range(3):
                    W(PE, pg_free[k % 2])
                    for c in range(8):
                        ins = PE.matmul(pg[k % 2][:, :nt], lhsT=wg[:, c, r * 1024 + oc * 128:r * 1024 + (oc + 1) * 128], rhs=hblk[s][:, c, :nt],
                                        start=(c == 0), stop=(c == 7))
                    t_g = s_pe.inc(ins)
                    W(PE, pbr_free[k % 2])
                    for c in range(4):
                        ins = PE.matmul(pbr[k % 2][:, :nt], lhsT=wo[r][:, c, oc * 128:(oc + 1) * 128], rhs=yb[s][r][:, c, :nt], start=(c == 0), stop=(c == 3))
                    t_b = s_pe.inc(ins)
                    W(ACT, t_g)
                    W(ACT, sg_free[k % 2])
                    t_s = s_ac.inc(ACT.activation(out=sg[k % 2][:, :nt], in_=pg[k % 2][:, :nt], func=AF.Sigmoid))
                    pg_free[k % 2] = t_s
                    W(DVE, t_s)
                    W(DVE, t_b)
                    if r == 0:
                        t_d = s_dv.inc(DVE.tensor_tensor(out=yacc[:, :nt], in0=sg[k % 2][:, :nt], in1=pbr[k % 2][:, :nt], op=ALU.mult))
                    else:
                        t_d = s_dv.inc(DVE.tensor_tensor(out=tmp[:, :nt], in0=sg[k % 2][:, :nt], in1=pbr[k % 2][:, :nt], op=ALU.mult))
                        W(DVE, t_d)
                        if r == 1:
                            t_d = s_dv.inc(DVE.tensor_tensor(out=yacc[:, :nt], in0=yacc[:, :nt], in1=tmp[:, :nt], op=ALU.add))
                        else:
                            if oc == 0:
                                W(DVE, yT_free)
                            t_d = s_dv.inc(DVE.tensor_tensor(out=yT[:, oc, :nt], in0=yacc[:, :nt], in1=tmp[:, :nt], op=ALU.add))
                    sg_free[k % 2] = t_d
                    pbr_free[k % 2] = t_d
                    k += 1
            blk_done[s] = (s_pe, s_pe.n)
            t_y = t_d
            W(PE, t_y)
            for ti in range(nt // 128):
                xs_ = xk % 2
                W(SP, xt_free[xs_])
                t_x = K.dma(SP, xt[xs_][:], x_src(q0 + ti * 128), xl[xs_])
                for half in range(2):
                    j = 2 * xk + half
                    W(PE, pw_free[j % 2])
                    for c in range(8):
                        ins = PE.matmul(pw[j % 2][:, :], lhsT=yT[:, c, ti * 128:(ti + 1) * 128], rhs=wout[:, c, half * 512:(half + 1) * 512],
                                        start=(c == 0), stop=(c == 7))
                    t_p = s_pe.inc(ins)
                    W(DVE, t_p)
                    W(DVE, tm_free[j % 2])
                    t_m = s_dv.inc(DVE.tensor_tensor(out=tm2[j % 2][:], in0=pw[j % 2][:], in1=g1[:, m, half * 512:(half + 1) * 512], op=ALU.mult))
                    pw_free[j % 2] = t_m
                    W(POOL, t_m)
                    W(POOL, t_x)
                    if half == 0:
                        W(POOL, xo_free[xs_])
                    t_a = s_pl.inc(POOL.tensor_tensor(out=xo[xs_][:, half * 512:(half + 1) * 512], in0=tm2[j % 2][:], in1=xt[xs_][:, half * 512:(half + 1) * 512], op=ALU.add))
                    tm_free[j % 2] = t_a
                xt_free[xs_] = t_a
                W(POOL, t_a)
                xo_free[xs_] = K.dma(POOL, x_dst(q0 + ti * 128), xo[xs_][:], stx[xs_])
                xk += 1
            yT_free = (s_pe, s_pe.n)
        K.barrier([(s_, s_.n) for s_ in stx])


def phase_mlp(K, L, pfx, qblocks, x_src, x_dst, final_g, after_block=None):
    nc = K.nc
    PE, ACT, DVE, POOL, SP = K.PE, K.ACT, K.DVE, K.POOL, K.SP
    dr = K.dram
    with ExitStack() as es:
        sb, ps = mk_alloc(nc, es, pfx)
        w1 = sb("w1", [128, 8, 4096], BF16)
        w2 = sb("w2", [128, 32, 1024], BF16)
        g2 = sb("g2", [128, 2, 1024])
        fg = sb("fg", [128, 1024])
        hblk = [sb(f"h{i}", [128, 8, 256], BF16) for i in range(2)]
        uT = sb("uT", [128, 32, 256], BF16)
        rb = [sb(f"r{i}", [128, 256]) for i in range(2)]
        xt = [sb(f"xt{i}", [128, 1024]) for i in range(2)]
        xo = [sb(f"xo{i}", [128, 1024]) for i in range(2)]
        tm2 = [sb(f"tm{i}", [128, 512]) for i in range(2)]
        junk = sb("junk", [128, 1024])
        st4 = sb("st4", [128, 4])
        pu = [ps(f"pu{i}", [128, 512]) for i in range(2)]
        pw = [ps(f"pw{i}", [128, 512]) for i in range(2)]
        wl = K.S("ld2")
        hl = [K.S("ld0"), K.S("ld1")]
        xl = [K.S("ld3"), K.S("ld4")]
        s_pe, s_ac, s_dv, s_pl = K.S("pe"), K.S("ac"), K.S("dv"), K.S("pl")
        stx = [K.S("gs0"), K.S("gs1")]
        gw = K.S("gw")
        for c in range(8):
            K.dma(POOL, w1[:, c, :], L["w_mlp1"][c * 128:(c + 1) * 128, :], gw)
        for c4 in range(4):
            t_gw = K.dma(POOL, w2[:, c4 * 8:(c4 + 1) * 8, :], L["w_mlp2"][c4 * 1024:(c4 + 1) * 1024, :].rearrange("(c p) n -> p c n", p=128), gw)
        K.dma(SP, g2[:, 0, :], dr["modv"][0, 5].partition_broadcast(128), wl)
        if final_g is not None:
            K.dma(SP, fg[:], final_g.partition_broadcast(128), wl)
        t_w = K.dma(SP, g2[:, 1, :], dr["modv"][1, 5].partition_broadcast(128), wl)
        for e in (PE, DVE, POOL, ACT):
            W(e, t_w)
            W(e, t_gw)
        h2T = dr["h2T"]
        blk_done = [None, None]
        k = 0
        xk = 0
        pu_free = [None, None]
        rb_free = [None, None]
        pw_free = [None, None]
        xt_free = [None, None]
        xo_free = [None, None]
        tm_free = [None, None]
        uT_free = None
        for bi, (q0, nt, m) in enumerate(qblocks):
            s = bi % 2
            W(SP, blk_done[s])
            t_l = K.dma(SP, hblk[s][:, :, :nt], h2T[:, q0:q0 + nt].rearrange("(c p) t -> p c t", p=128), hl[s])
            W(PE, t_l)
            for fc in range(32):
                W(PE, pu_free[k % 2])
                for c in range(8):
                    ins = PE.matmul(pu[k % 2][:, :nt], lhsT=w1[:, c, fc * 128:(fc + 1) * 128], rhs=hblk[s][:, c, :nt], start=(c == 0), stop=(c == 7))
                t_u = s_pe.inc(ins)
                W(ACT, t_u)
                W(ACT, rb_free[k % 2])
                t_r = s_ac.inc(ACT.activation(out=rb[k % 2][:, :nt], in_=pu[k % 2][:, :nt], func=AF.Relu))
                pu_free[k % 2] = t_r
                W(DVE, t_r)
                if fc == 0:
                    W(DVE, uT_free)
                t_q = s_dv.inc(DVE.tensor_tensor(out=uT[:, fc, :nt], in0=rb[k % 2][:, :nt], in1=rb[k % 2][:, :nt], op=ALU.mult))
                rb_free[k % 2] = t_q
                k += 1
            blk_done[s] = (s_pe, s_pe.n)
            W(PE, t_q)
            for ti in range(nt // 128):
                xs_ = xk % 2
                W(SP, xt_free[xs_])
                t_x = K.dma(SP, xt[xs_][:], x_src(q0 + ti * 128), xl[xs_])
                for half in range(2):
                    j = 2 * xk + half
                    W(PE, pw_free[j % 2])
                    for fc in range(32):
                        ins = PE.matmul(pw[j % 2][:, :], lhsT=uT[:, fc, ti * 128:(ti + 1) * 128], rhs=w2[:, fc, half * 512:(half + 1) * 512],
                                        start=(fc == 0), stop=(fc == 31))
                    t_p = s_pe.inc(ins)
                    W(DVE, t_p)
                    W(DVE, tm_free[j % 2])
                    t_m = s_dv.inc(DVE.tensor_tensor(out=tm2[j % 2][:], in0=pw[j % 2][:], in1=g2[:, m, half * 512:(half + 1) * 512], op=ALU.mult))
                    pw_free[j % 2] = t_m
                    W(POOL, t_m)
                    W(POOL, t_x)
                    if half == 0:
                        W(POOL, xo_free[xs_])
                    t_a = s_pl.inc(POOL.tensor_tensor(out=xo[xs_][:, half * 512:(half + 1) * 512], in0=tm2[j % 2][:], in1=xt[xs_][:, half * 512:(half + 1) * 512], op=ALU.add))
                    tm_free[j % 2] = t_a
                xt_free[xs_] = t_a
                t_fin = t_a
                if final_g is not None:
                    W(ACT, t_a)
                    t1 = s_ac.inc(ACT.activation(out=junk[:], in_=xo[xs_][:], func=AF.Square, accum_out=st4[:, 0:1]))
                    W(DVE, t1)
                    t2 = s_dv.inc(DVE.tensor_scalar(out=st4[:, 1:2], in0=st4[:, 0:1], scalar1=1.0 / D, scalar2=EPS, op0=ALU.mult, op1=ALU.add))
                    W(ACT, t2)
                    t3 = s_ac.inc(ACT.activation(out=st4[:, 2:3], in_=st4[:, 1:2], func=AF.Sqrt))
                    W(DVE, t3)
                    t4 = s_dv.inc(DVE.reciprocal(out=st4[:, 3:4], in_=st4[:, 2:3]))
                    W(DVE, t4)
                    t_fin = s_dv.inc(DVE.scalar_tensor_tensor(out=xo[xs_][:], in0=xo[xs_][:], scalar=st4[:, 3:4], in1=fg[:], op0=ALU.mult, op1=ALU.mult))
                W(POOL, t_fin)
                xo_free[xs_] = K.dma(POOL, x_dst(q0 + ti * 128), xo[xs_][:], stx[xs_])
                xk += 1
            uT_free = (s_pe, s_pe.n)
            if after_block is not None:
                after_block(bi, [xo_free[0], xo_free[1]])
        K.barrier([(s_, s_.n) for s_ in stx])


def phase_halo(K, pfx):
    nc = K.nc
    PE, ACT, DVE, POOL, SP = K.PE, K.ACT, K.DVE, K.POOL, K.SP
    dr = K.dram
    xg, x1, xna, sel = K.xg_at, K.x1_at, dr["x_na2"], dr["sel"]
    with ExitStack() as es:
        sb, ps = mk_alloc(nc, es, pfx)
        selb = sb("sel", [128, 8])
        cand = [sb(f"c{i}", [128, 1024]) for i in range(4)]
        acc = [sb(f"a{i}", [128, 1024]) for i in range(2)]
        ld = [K.S("ld0"), K.S("ld1"), K.S("ld3"), K.S("ld4")]
        lc = K.S("ld2")
        s_dv = K.S("dv")
        st = [K.S("st0"), K.S("st1")]
        so = K.S("st2")
        t_c = K.dma(SP, selb[:], sel.partition_broadcast(128), lc)
        for q in range(0, T, 128):
            t_own = K.dma(SP, xna[256 + q:256 + q + 128, :], x1(q), so)
        W(DVE, t_c)
        jobs = []
        for u in range(2):
            jobs.append((128 * u, [xg(2048 * r + 1792 + 128 * u) for r in range(4)], 0))
        for u in range(2):
            jobs.append((256 + T + 128 * u, [xg(2048 * r + 128 * u) for r in range(4)], 4))
        dv_prev = None
        st_t = [None, None]
        for n, (row0, srcs, c0) in enumerate(jobs):
            W(SP, dv_prev)
            lts = [K.dma(SP, cand[r][:], srcs[r], ld[r]) for r in range(4)]
            W(DVE, lts)
            W(DVE, st_t[n % 2])
            t = s_dv.inc(DVE.tensor_scalar(out=acc[n % 2][:], in0=cand[0][:], scalar1=selb[:, c0:c0 + 1], scalar2=0.0, op0=ALU.mult, op1=ALU.add))
            for r in range(1, 4):
                W(DVE, t)
                t = s_dv.inc(DVE.scalar_tensor_tensor(out=acc[n % 2][:], in0=cand[r][:], scalar=selb[:, c0 + r:c0 + r + 1], in1=acc[n % 2][:],
                                                      op0=ALU.mult, op1=ALU.add))
            dv_prev = t
            W(SP, t)
            st_t[n % 2] = K.dma(SP, xna[row0:row0 + 128, :], acc[n % 2][:], st[n % 2])
        K.barrier([t_own, st_t[0], st_t[1]])


W_NAMES = ["w_mod", "b_mod", "norm1_g", "norm2_g", "w_in", "gqa_q_norm", "gqa_k_norm", "mla_kv_norm", "mla_w_uk", "mla_w_uv",
           "w_o_gqa", "w_o_na", "w_o_mla", "w_out", "w_mlp1", "w_mlp2"]
W_SHAPES = {"w_mod": [D, 6 * D], "b_mod": [6 * D], "norm1_g": [D], "norm2_g": [D], "w_in": [D, WIN], "gqa_q_norm": [64], "gqa_k_norm": [64],
            "mla_kv_norm": [256], "mla_w_uk": [256, 512], "mla_w_uv": [256, 512], "w_o_gqa": [512, D], "w_o_na": [512, D], "w_o_mla": [512, D],
            "w_out": [D, D], "w_mlp1": [D, 4 * D], "w_mlp2": [4 * D, D]}
DEPTH = 2


def emit_layer(K, L, src, with_ctx, final, sfx):
    dr = K.dram
    NQ = T + C
    nc = K.nc
    with ExitStack() as esw:
        sbw, _ = mk_alloc(nc, esw, "pw" + sfx)
        pre = load_proj_weights(K, L, sbw)
        phase_mod(K, L, "md" + sfx)
        phase_norm(K, [(src["x_all"], S, dr["hT_all"][:, 0:S], 0, 0, 1), (src["ctx_in"], C, dr["hT_all"][:, S:SK], 1, 0, 1),
                       (src["x_na"], NAT, dr["hT_na"], 0, 0, 1)], "n1" + sfx)
        phase_proj(K, L, "pj" + sfx, with_ctx, pre=pre)
    esm = ExitStack()
    sbm, _ = mk_alloc(nc, esm, "mw" + sfx)
    pre_m = load_merge_weights(K, L, sbm)
    qbl = [(512 * i, 512) for i in range(4)]
    heads = []

    def wide_blocks(Y, h):
        blocks = [dict(q0=q0, nq=1024, tiles=[(k, None) for k in range(66)],
                       yts=[Y[64 * h:64 * h + 64, q0:q0 + 512], Y[64 * h:64 * h + 64, q0 + 512:q0 + 1024]]) for q0 in (0, 1024)]
        if with_ctx:
            blocks.append(dict(q0=T, nq=C, tiles=[(64, None), (65, None)], yts=[Y[64 * h:64 * h + 64, T:NQ]]))
        return blocks

    for h in range(8):
        heads.append(dict(kt=dr["GKT"][h // 4], v=dr["GV"][:, h // 4, :], qt=dr["GQT"][h], dk=64, scale=0.125, nk=SK, blocks=wide_blocks(dr["YAT"], h)))
    for h in range(8):
        heads.append(dict(kt=dr["MKT"][h], v=dr["MV"][:, h, :], qt=dr["MQT"][h], dk=96, scale=96 ** -0.5, nk=SK, blocks=wide_blocks(dr["YCT"], h)))
    phase_attn(K, heads, "at" + sfx, SK)
    heads = []
    var = [0, 1, 1, 2]
    for h in range(8):
        blocks = []
        for i, (q0, nq) in enumerate(qbl):
            tiles = [(4 * i + m, src["nabias"][var[i], h, m]) for m in range(8)] + [(20, None), (21, None)]
            blocks.append(dict(q0=q0, nq=nq, tiles=tiles, yts=[dr["YBT"][64 * h:64 * h + 64, q0:q0 + nq]]))
        if with_ctx:
            blocks.append(dict(q0=T, nq=C, tiles=[(20, None), (21, None)], yts=[dr["YBT"][64 * h:64 * h + 64, T:NQ]]))
        heads.append(dict(kt=dr["NKT"][h], v=dr["NV"][:, h, :], qt=dr["NQT"][h], dk=64, scale=1.0, nk=NAK, blocks=blocks))
    phase_attn(K, heads, "na" + sfx, NAK)
    mblocks = [(dr["hT_na"][:, 256 + 512 * i:256 + 512 * (i + 1)], 512 * i, 512, 0) for i in range(4)]
    if with_ctx:
        mblocks.append((dr["hT_all"][:, S:SK], T, C, 1))

    def x_src(q):
        if q >= T:
            return src["ctx_in"][q - T:q - T + 128, :]
        return src["x_own"](q) if callable(src["x_own"]) else src["x_own"][q:q + 128, :]

    def xs1_at(q):
        return dr["xs1"][q:q + 128, :]

    phase_merge(K, L, "mg" + sfx, mblocks, x_src, xs1_at, pre=pre_m)
    esm.close()
    njobs = [(dr["xs1"][0:T, :], T, dr["h2T"][:, 0:T], 0, 3, 4)]
    if with_ctx:
        njobs.append((dr["xs1"][T:NQ, :], C, dr["h2T"][:, T:NQ], 1, 3, 4))
    phase_norm(K, njobs, "n2" + sfx)
    fblocks = [(256 * i, 256, 0) for i in range(8)]
    if with_ctx:
        fblocks.append((T, C, 1))
    phase_mlp(K, L, "ml" + sfx, fblocks, xs1_at, src["x_dst"], L.get("final_norm_g") if final else None, after_block=src.get("after_block"))


def build_fused():
    nc = bass.Bass("TRN2", target_bir_lowering=False)
    K = KB(nc)
    NQ = T + C
    dr = K.dram

    def inp(name, shape, dt=F32):
        dr[name] = nc.dram_tensor(name, shape, dt, kind="ExternalInput").ap()

    def internal(name, shape, dt=BF16):
        dr[name] = nc.dram_tensor(name, shape, dt).ap()

    inp("x_all", [S, D]); inp("x_own", [T, D]); inp("x_na", [NAT, D]); inp("ctx_in", [C, D]); inp("cvec", [2, D])
    Wst = {}
    for n in W_NAMES:
        Wst[n] = nc.dram_tensor(n, [DEPTH] + W_SHAPES[n], F32, kind="ExternalInput").ap()
    fng = nc.dram_tensor("final_norm_g", [D], F32, kind="ExternalInput").ap()
    for n in ("ident_f", "onesbd_f", "ones_f"):
        inp(n, [128, 128])
    for n in ("pt128", "pt96", "pt32", "ident_b"):
        inp(n, [128, 128], BF16)
    inp("cosA_all", [128, S]); inp("sinA_all", [128, S]); inp("cosA_own", [128, T]); inp("sinA_own", [128, T])
    inp("cosM_all", [32, S]); inp("sinM_all", [32, S]); inp("cosM_own", [96, T]); inp("sinM_own", [96, T])
    inp("nabias", [DEPTH, 3, 8, 8, 128, 512])
    inp("sel", [8])
    dr["xout"] = nc.dram_tensor("xout", [T, D], F32, kind="ExternalOutput").ap()
    internal("modv", [2, 6, D], F32)
    internal("hT_all", [D, SK]); internal("hT_na", [D, NAT])
    internal("GKT", [2, 64, SK]); internal("CKVT", [256, SK]); internal("MKT", [8, 96, SK])
    internal("MV", [SK, 8, 65]); internal("GV", [SK, 2, 65])
    internal("NKT", [8, 64, NAK]); internal("NV", [NAK, 8, 65])
    internal("GQT", [8, 64, NQ]); internal("NQT", [8, 64, NQ]); internal("MQT", [8, 96, NQ])
    internal("YAT", [512, NQ]); internal("YBT", [512, NQ]); internal("YCT", [512, NQ])
    internal("xs1", [NQ, D], F32); internal("h2T", [D, NQ])
    NCH = 8
    x1c = [nc.dram_tensor(f"x1c{k}", [256, D], F32) for k in range(NCH)]
    xgc = [nc.dram_tensor(f"xgc{k}", [4 * 256, D], F32) for k in range(NCH)]
    internal("c1loc", [C, D], F32); internal("x_na2", [NAT, D], F32)
    internal("rls", [4, 512], F32)

    def x1_at(q):
        return x1c[q // 256].ap()[q % 256:q % 256 + 128, :]

    def xg_at(t0):
        r, k, off = t0 // 2048, (t0 % 2048) // 256, t0 % 256
        return xgc[k].ap()[r * 256 + off:r * 256 + off + 128, :]

    K.x1_at, K.xg_at = x1_at, xg_at

    for l in range(DEPTH):
        L = {n: Wst[n][l] for n in W_NAMES}
        final = (l == DEPTH - 1)
        with_ctx = not final
        if final:
            L["final_norm_g"] = fng
        if l == 0:
            src = dict(x_all=dr["x_all"], x_own=dr["x_own"], x_na=dr["x_na"], ctx_in=dr["ctx_in"])
        else:
            src = dict(x_all=(lambda i: xg_at(128 * i)), x_own=x1_at, x_na=dr["x_na2"], ctx_in=dr["c1loc"])
        src["nabias"] = dr["nabias"][l]
        if final:
            src["x_dst"] = lambda q: dr["xout"][q:q + 128, :]
        else:
            src["x_dst"] = lambda q: (x1_at(q) if q < T else dr["c1loc"][q - T:q - T + 128, :])
        cc = K.S("cc")
        cc_tok = [None]
        if not final:
            def after_block(bi, store_toks):
                if bi >= NCH:
                    return
                W(K.POOL, store_toks)
                ins = K.POOL.collective_compute("AllGather", mybir.AluOpType.bypass, replica_groups=[[0, 1, 2, 3], [4, 5, 6, 7]],
                                                ins=[x1c[bi].ap().opt()], outs=[xgc[bi].ap().opt()])
                cc_tok[0] = cc.inc(ins)

            src["after_block"] = after_block
        emit_layer(K, L, src, with_ctx, final, f"{l}_")
        if not final:
            K.barrier([cc_tok[0]])
            phase_halo(K, f"hl{l}_")
    K.semcounts = {n: s.n for n, s in K.sems.items()}
    return nc, K


def _rope_tables():
    t = np.arange(S, dtype=np.int32)
    row = (t // 64).astype(np.float32)
    col = (t % 64).astype(np.float32)

    def tabs(rot_dim):
        half = rot_dim // 2
        inv = (10000.0 ** (-np.arange(0, half, 2, dtype=np.float32) / np.float32(half))).astype(np.float32)
        ar = (row[:, None] * inv).astype(np.float32)
        ac = (col[:, None] * inv).astype(np.float32)
        cos = np.concatenate([np.cos(ar), np.cos(ar), np.cos(ac), np.cos(ac)], axis=1).T.astype(np.float32)
        sin = np.concatenate([np.sin(ar), np.sin(ar), np.sin(ac), np.sin(ac)], axis=1).T.astype(np.float32)
        return np.ascontiguousarray(cos), np.ascontiguousarray(sin)

    return tabs(64), tabs(32)


def _rot_matrix(n):
    q = n // 4
    P = np.zeros((n, n), np.float32)
    for base in (0, 2 * q):
        for i in range(q):
            P[base + i, base + q + i] = -1.0
            P[base + q + i, base + i] = 1.0
    return P


def _consts():
    c = {}
    c["ident_f"] = np.eye(128, dtype=np.float32)
    c["ones_f"] = np.ones((128, 128), np.float32)
    bd = np.zeros((128, 128), np.float32)
    bd[:64, :64] = 1.0
    bd[64:, 64:] = 1.0
    c["onesbd_f"] = bd
    P64 = _rot_matrix(64)
    P32 = _rot_matrix(32)
    pt128 = np.zeros((128, 128), np.float32)
    pt128[:64, :64] = P64.T
    pt128[64:, 64:] = P64.T
    pt96 = np.zeros((128, 128), np.float32)
    pt96[64:96, 64:96] = P32.T
    pt32 = np.zeros((128, 128), np.float32)
    pt32[:32, :32] = P32.T
    c["ident_b"] = np.eye(128, dtype=np.float32).astype(ml_dtypes.bfloat16)
    c["pt128"] = pt128.astype(ml_dtypes.bfloat16)
    c["pt96"] = pt96.astype(ml_dtypes.bfloat16)
    c["pt32"] = pt32.astype(ml_dtypes.bfloat16)
    return c


def _na_bias_tables(rpb, j):
    out = np.empty((3, 8, 8, 128, 512), np.float32)
    kcol = np.arange(64)[None, :, None, None]
    qcol = np.arange(64)[None, None, None, :]
    m = np.arange(16)[:, None, None, None]
    a = np.arange(8)[None, None, :, None]
    cs = np.clip(qcol - 8, 0, 48)
    colok = (kcol >= cs) & (kcol < cs + 16)
    cidx = np.clip(kcol - qcol + 15, 0, 30)
    for v, i in enumerate((0, 1, 3)):
        r = 32 * j + 8 * i + a
        k = 32 * j + 8 * i - 4 + m
        rs = np.clip(r - 4, 0, 120)
        ok = (k >= 0) & (k < 128) & (k >= rs) & (k < rs + 8) & colok
        ridx = np.clip(k - r + 7, 0, 14)
        ridx_b = np.broadcast_to(ridx, ok.shape)
        cidx_b = np.broadcast_to(cidx, ok.shape)
        vals = rpb[:, ridx_b, cidx_b]
        tab = np.where(ok[None], vals, np.float32(NEG)).astype(np.float32)
        out[v] = tab.reshape(8, 8, 128, 512)
    return out


def _core_inputs(x_full, ctx_full, c, c_ctx, Wd, consts, ropes):
    (cosA, sinA), (cosM, sinM) = ropes
    maps = []
    cosA2 = np.ascontiguousarray(np.concatenate([cosA, cosA], 0))
    sinA2 = np.ascontiguousarray(np.concatenate([sinA, sinA], 0))
    shared = dict(consts)
    for n in W_NAMES:
        shared[n] = np.ascontiguousarray(Wd[n])
    shared["final_norm_g"] = np.ascontiguousarray(Wd["final_norm_g"])
    shared["cosA_all"] = cosA2
    shared["sinA_all"] = sinA2
    shared["cosM_all"] = cosM
    shared["sinM_all"] = sinM
    rpb = np.asarray(Wd["na_rpb"], np.float32)
    nab = [np.stack([_na_bias_tables(rpb[l], j) for l in range(rpb.shape[0])], 0) for j in range(4)]
    for core in range(8):
        b, j = core // 4, core % 4
        t0 = T * j
        d = dict(shared)
        d["x_all"] = np.ascontiguousarray(x_full[b])
        d["x_own"] = np.ascontiguousarray(x_full[b, t0:t0 + T])
        xna = np.zeros((NAT, D), np.float32)
        lo = (32 * j - 4) * 64
        hi = lo + NAT
        slo, shi = max(lo, 0), min(hi, S)
        xna[slo - lo:shi - lo] = x_full[b, slo:shi]
        d["x_na"] = xna
        d["ctx_in"] = np.ascontiguousarray(ctx_full[b])
        d["cvec"] = np.ascontiguousarray(np.stack([c[b], c_ctx]))
        d["cosA_own"] = np.ascontiguousarray(cosA2[:, t0:t0 + T])
        d["sinA_own"] = np.ascontiguousarray(sinA2[:, t0:t0 + T])
        d["cosM_own"] = np.ascontiguousarray(np.concatenate([np.ones((64, T), np.float32), cosM[:, t0:t0 + T]], 0))
        d["sinM_own"] = np.ascontiguousarray(np.concatenate([np.zeros((64, T), np.float32), sinM[:, t0:t0 + T]], 0))
        d["nabias"] = nab[j]
        sel = np.zeros(8, np.float32)
        if j > 0:
            sel[j - 1] = 1.0
        if j < 3:
            sel[4 + j + 1] = 1.0
        d["sel"] = sel
        maps.append(d)
    return maps


_PROG = []


def kernel(**inputs):
    Wd = {k: np.asarray(v, np.float32) for k, v in inputs.items()}
    if not _PROG:
        _PROG.append(build_fused()[0])
    nc = _PROG[0]
    maps = _core_inputs(Wd["x"], Wd["ctx"], Wd["c"], Wd["c_ctx"], Wd, _consts(), _rope_tables())
    res = run_bass_kernel_spmd(nc, maps, core_ids=list(range(8)))
    outs = [r["xout"] for r in res.results]
    x = np.stack([np.concatenate([outs[4 * b + j] for j in range(4)], 0) for b in range(2)], 0)
    return np.ascontiguousarray(x.astype(np.float32))
```

```python
from contextlib import ExitStack
import os
import numpy as np
import ml_dtypes
import concourse.bass as bass
import concourse.mybir as mybir
from concourse.bass_utils import run_bass_kernel_spmd

F32 = mybir.dt.float32
BF16 = mybir.dt.bfloat16
AF = mybir.ActivationFunctionType
ALU = mybir.AluOpType

D = 1024
S = 8192
C = 256
SK = S + C
T = 2048
NAT = 2560
NAK = NAT + C
EPS = 1e-6
NEG = -30000.0
O_GQ, O_GK, O_GV, O_NQ, O_NK, O_NV, O_MQ, O_CKV, O_KR, O_GATE = 0, 512, 640, 768, 1280, 1792, 2304, 3072, 3328, 3360
WIN = 6432


class Sem:
    def __init__(self, nc, name):
        self.h = nc.alloc_semaphore(name)
        self.n = 0

    def inc(self, ins, k=1):
        ins.then_inc(self.h, k)
        self.n += k
        return (self, self.n)


def W(eng, tok):
    if tok is None:
        return
    if isinstance(tok, list):
        for t in tok:
            W(eng, t)
        return
    s, v = tok
    if v > 0:
        eng.wait_ge(s.h, v)


class KB:
    def __init__(self, nc):
        self.nc = nc
        self.PE, self.ACT, self.DVE, self.POOL, self.SP = nc.tensor, nc.scalar, nc.vector, nc.gpsimd, nc.sync
        self.sems = {}
        self.dram = {}

    def S(self, name):
        if name not in self.sems:
            self.sems[name] = Sem(self.nc, name)
        return self.sems[name]

    def dma(self, eng, out, in_, sem, slow=False):
        if slow:
            ins = eng.dma_start(out=out, in_=in_, allow_slow_non_contiguous=True)
        else:
            ins = eng.dma_start(out=out, in_=in_)
        return sem.inc(ins, 16)

    def barrier(self, toks):
        for e in (self.PE, self.ACT, self.DVE, self.POOL, self.SP):
            W(e, toks)


def mk_alloc(nc, es, pfx):
    def sb(name, shape, dt=F32):
        return es.enter_context(nc.sbuf_tensor(pfx + name, shape, dt))

    def ps(name, shape, dt=F32):
        return es.enter_context(nc.psum_tensor(pfx + name, shape, dt))

    return sb, ps


def phase_mod(K, L, pfx):
    nc = K.nc
    PE, ACT, DVE, POOL, SP = K.PE, K.ACT, K.DVE, K.POOL, K.SP
    with ExitStack() as es:
        sb, ps = mk_alloc(nc, es, pfx)
        cT = sb("cT", [128, 8, 2])
        sT = sb("sT", [128, 8, 2])
        NWB = 4
        wm = [sb(f"w{i}", [128, 8, 512]) for i in range(NWB)]
        bm = sb("b", [2, 6144])
        mrow = sb("m", [2, 6144])
        ng = sb("ng", [2, 2, 1024])
        mv = sb("mv", [2, 6, 1024])
        pm = [ps(f"p{i}", [2, 512]) for i in range(2)]
        ld = K.S("ld0")
        wl = [K.S("ld1"), K.S("ld2"), K.S("ld3"), K.S("ld4")]
        s_pe, s_ac, s_dv, st = K.S("pe"), K.S("ac"), K.S("dv"), K.S("st0")
        for m in range(2):
            K.dma(SP, cT[:, :, m], K.dram["cvec"][m].rearrange("(c p) -> p c", p=128), ld, slow=True)
        K.dma(SP, bm[:], L["b_mod"].partition_broadcast(2), ld)
        K.dma(SP, ng[:, 0, :], L["norm1_g"].partition_broadcast(2), ld)
        t_ld = K.dma(SP, ng[:, 1, :], L["norm2_g"].partition_broadcast(2), ld)
        W(ACT, t_ld)
        t_s = s_ac.inc(ACT.activation(out=sT[:].rearrange("p c m -> p (c m)"), in_=cT[:].rearrange("p c m -> p (c m)"), func=AF.Silu))
        W(PE, t_s)
        pe_t = [None] * 12
        dv_t = [None] * 12
        wsrc = L["w_mod"]
        w_t = [None] * 12

        def load_w(g):
            if g >= NWB:
                W(SP, pe_t[g - NWB])
            w_t[g] = K.dma(SP, wm[g % NWB][:], wsrc[:, g * 512:(g + 1) * 512].rearrange("(c p) n -> p c n", p=128), wl[g % NWB])

        for g in range(min(NWB - 1, 12)):
            load_w(g)
        for g in range(12):
            if g + NWB - 1 < 12:
                load_w(g + NWB - 1)
            W(PE, w_t[g])
            if g >= 2:
                W(PE, dv_t[g - 2])
            for c in range(8):
                ins = PE.matmul(pm[g % 2][:], lhsT=sT[:, c, :], rhs=wm[g % NWB][:, c, :], start=(c == 0), stop=(c == 7))
            pe_t[g] = s_pe.inc(ins)
            W(DVE, pe_t[g])
            if g == 0:
                W(DVE, t_ld)
            dv_t[g] = s_dv.inc(DVE.tensor_tensor(out=mrow[:, g * 512:(g + 1) * 512], in0=pm[g % 2][:], in1=bm[:, g * 512:(g + 1) * 512], op=ALU.add))
        W(DVE, dv_t[11])
        sl = lambda i: mrow[:, i * 1024:(i + 1) * 1024]
        DVE.scalar_tensor_tensor(out=mv[:, 0, :], in0=sl(1), scalar=1.0, in1=ng[:, 0, :], op0=ALU.add, op1=ALU.mult)
        DVE.tensor_copy(out=mv[:, 1, :], in_=sl(0))
        DVE.tensor_copy(out=mv[:, 2, :], in_=sl(2))
        DVE.scalar_tensor_tensor(out=mv[:, 3, :], in0=sl(4), scalar=1.0, in1=ng[:, 1, :], op0=ALU.add, op1=ALU.mult)
        DVE.tensor_copy(out=mv[:, 4, :], in_=sl(3))
        t_f = s_dv.inc(DVE.tensor_copy(out=mv[:, 5, :], in_=sl(5)))
        W(SP, t_f)
        t_st = K.dma(SP, K.dram["modv"], mv[:], st)
        K.barrier([t_st])


def phase_norm(K, jobs, pfx):
    nc = K.nc
    PE, ACT, DVE, POOL, SP = K.PE, K.ACT, K.DVE, K.POOL, K.SP
    tiles = []
    for ji, (src, ntok, dst, m, ia, ish) in enumerate(jobs):
        for i in range(ntok // 128):
            tiles.append((ji, i))
    NTI = len(tiles)
    with ExitStack() as es:
        sb, ps = mk_alloc(nc, es, pfx)
        NX = 6
        xt = [sb(f"xt{i}", [128, 1024]) for i in range(NX)]
        junk = sb("junk", [128, 1024])
        ss = sb("ss", [128, NTI])
        r1 = sb("r1", [128, NTI])
        r2 = sb("r2", [128, NTI])
        rstd = sb("rstd", [128, NTI])
        NXN = 3
        xn = [sb(f"xn{i}", [128, 1024]) for i in range(NXN)]
        hb = [sb(f"hb{i}", [128, 8, 512], BF16) for i in range(2)]
        ident = sb("ident", [128, 128])
        acol = sb("acol", [128, 2, 2, 8])
        pT = [ps(f"pT{i}", [128, 8, 128]) for i in range(2)]
        lds = [K.S("ld0"), K.S("ld1"), K.S("ld3"), K.S("ld4"), K.S("ld5"), K.S("ld6")]
        ldc = K.S("ld2")
        s_pe, s_ac, s_dv = K.S("pe"), K.S("ac"), K.S("dv")
        sts = [K.S("gs0"), K.S("gs1")]
        t_c = K.dma(SP, ident[:], K.dram["ident_f"], ldc)
        mods = sorted(set((j[3], j[4], j[5]) for j in jobs))
        assert len(set(m for m, _, _ in mods)) == len(mods)
        for (m, ia, ish) in mods:
            K.dma(SP, acol[:, m, 0, :], K.dram["modv"][m, ia].rearrange("(c p) -> p c", p=128), ldc, slow=True)
            t_c = K.dma(SP, acol[:, m, 1, :], K.dram["modv"][m, ish].rearrange("(c p) -> p c", p=128), ldc, slow=True)
        act_t = [None] * NTI
        for n, (ji, i) in enumerate(tiles):
            src = jobs[ji][0]
            if n >= NX:
                W(SP, act_t[n - NX])
            t_l = K.dma(SP, xt[n % NX][:], src(i) if callable(src) else src[i * 128:(i + 1) * 128, :], lds[n % NX])
            W(ACT, t_l)
            act_t[n] = s_ac.inc(ACT.activation(out=junk[:], in_=xt[n % NX][:], func=AF.Square, accum_out=ss[:, n:n + 1]))
        W(DVE, act_t[NTI - 1])
        t1 = s_dv.inc(DVE.tensor_scalar(out=r1[:], in0=ss[:], scalar1=1.0 / D, scalar2=EPS, op0=ALU.mult, op1=ALU.add))
        W(ACT, t1)
        t2 = s_ac.inc(ACT.activation(out=r2[:], in_=r1[:], func=AF.Sqrt))
        W(DVE, t2)
        t3 = s_dv.inc(DVE.reciprocal(out=rstd[:], in_=r2[:]))
        W(ACT, t3)
        W(SP, t2)
        W(PE, t_c)
        W(DVE, t_c)
        a_t = [None] * NTI
        p_t = [None] * NTI
        v_t = [None] * NTI
        st_t = {}
        blk = -1
        blk_of = []
        prev_key = None
        for n, (ji, i) in enumerate(tiles):
            key = (ji, i // 4)
            if key != prev_key:
                blk += 1
                prev_key = key
            blk_of.append(blk)
        for n, (ji, i) in enumerate(tiles):
            src, ntok, dst, m, ia, ish = jobs[ji]
            b = blk_of[n]
            if n >= NX:
                W(SP, a_t[n - NX])
            t_l = K.dma(SP, xt[n % NX][:], src(i) if callable(src) else src[i * 128:(i + 1) * 128, :], lds[n % NX])
            W(ACT, t_l)
            if n >= NXN:
                W(ACT, p_t[n - NXN])
            a_t[n] = s_ac.inc(ACT.activation(out=xn[n % NXN][:], in_=xt[n % NXN if False else n % NX][:], func=AF.Copy, scale=rstd[:, n:n + 1]))
            W(PE, a_t[n])
            if n >= 2:
                W(PE, v_t[n - 2])
            for c in range(8):
                ins = PE.transpose(out=pT[n % 2][:, c, :], in_=xn[n % NXN][:, c * 128:(c + 1) * 128], identity=ident[:])
            p_t[n] = s_pe.inc(ins)
            W(DVE, p_t[n])
            if (i % 4 == 0) and (b - 2) in st_t:
                W(DVE, st_t[b - 2])
            for c in range(8):
                ins = DVE.tensor_scalar(out=hb[b % 2][:, c, (i % 4) * 128:(i % 4 + 1) * 128], in0=pT[n % 2][:, c, :],
                                        scalar1=acol[:, m, 0, c:c + 1], scalar2=acol[:, m, 1, c:c + 1], op0=ALU.mult, op1=ALU.add)
            v_t[n] = s_dv.inc(ins)
            last_in_blk = (n + 1 == NTI) or (blk_of[n + 1] != b)
            if last_in_blk:
                nt = (i % 4 + 1) * 128
                t0 = (i // 4) * 512
                W(POOL, v_t[n])
                st_t[b] = K.dma(POOL, dst[:, t0:t0 + nt].rearrange("(c p) t -> p c t", p=128), hb[b % 2][:, :, 0:nt], sts[b % 2])
        K.barrier([st_t[blk], st_t.get(blk - 1)])


def load_proj_weights(K, L, sb):
    win = sb("win", [128, 8, O_GATE], BF16)
    wuk = sb("wuk", [128, 2, 512], BF16)
    wuv = sb("wuv", [128, 2, 512], BF16)
    gw = K.S("gw")
    for c in range(8):
        K.dma(K.POOL, win[:, c, :], L["w_in"][c * 128:(c + 1) * 128, 0:O_GATE], gw)
    K.dma(K.POOL, wuk[:], L["mla_w_uk"].rearrange("(r p) n -> p r n", p=128), gw)
    t_gw = K.dma(K.POOL, wuv[:], L["mla_w_uv"].rearrange("(r p) n -> p r n", p=128), gw)
    return win, wuk, wuv, t_gw


def load_merge_weights(K, L, sb):
    wg = sb("wg", [128, 8, 3072], BF16)
    wo = [sb(f"wo{i}", [128, 4, 1024], BF16) for i in range(3)]
    wout = sb("wout", [128, 8, 1024], BF16)
    gw = K.S("gw")
    for c in range(8):
        K.dma(K.POOL, wg[:, c, :], L["w_in"][c * 128:(c + 1) * 128, O_GATE:WIN], gw)
    for r, nm in enumerate(("w_o_gqa", "w_o_na", "w_o_mla")):
        K.dma(K.POOL, wo[r][:], L[nm].rearrange("(c p) n -> p c n", p=128), gw)
    t_gw = K.dma(K.POOL, wout[:], L["w_out"].rearrange("(c p) n -> p c n", p=128), gw)
    return wg, wo, wout, t_gw


def phase_proj(K, L, pfx, with_ctx, pre=None):
    nc = K.nc
    PE, ACT, DVE, POOL, SP = K.PE, K.ACT, K.DVE, K.POOL, K.SP
    dr = K.dram
    with ExitStack() as es:
        sb, ps = mk_alloc(nc, es, pfx)
        win, wuk, wuv, t_gw = pre if pre is not None else load_proj_weights(K, L, sb)
        onesbd = sb("onesbd", [128, 128])
        ones = sb("ones", [128, 128])
        pt128 = sb("pt128", [128, 128], BF16)
        pt96 = sb("pt96", [128, 128], BF16)
        pt32 = sb("pt32", [128, 128], BF16)
        gq = sb("gq", [128, 1])
        gk = sb("gk", [128, 1])
        kvg = sb("kvg", [128, 2])
        hblk = [sb(f"h{i}", [128, 8, 512], BF16) for i in range(2)]
        ckvn = sb("ckvn", [128, 2, 512], BF16)
        sqf = sb("sqf", [128, 512], BF16); sqf2 = sb("sqf2", [128, 512], BF16)
        onesbd_b = sb("onesbd_b", [128, 128], BF16); ones_b = sb("ones_b", [128, 128], BF16)
        qf = sb("qf", [128, 512]); qf2 = sb("qf2", [128, 512])
        sd = sb("sd", [128, 512]); rs = sb("rs", [128, 512]); qn = sb("qn", [128, 512])
        t1b = sb("t1", [128, 512]); t2b = sb("t2", [128, 512])
        cosb = [sb(f"cos{i}", [128, 512]) for i in range(2)]
        sinb = [sb(f"sin{i}", [128, 512]) for i in range(2)]
        qb = sb("qb", [128, 512], BF16)
        outb = [sb(f"ob{i}", [128, 512], BF16) for i in range(2)]
        vout = [sb(f"vo{i}", [128, 8, 65], BF16) for i in range(2)]
        acc = [ps(f"acc{i}", [128, 512]) for i in range(2)]
        acc2 = ps("acc2", [128, 512])
        pss = ps("pss", [128, 512])
        prot = ps("prot", [128, 512])
        ptm = [ps(f"ptm{i}", [128, 512]) for i in range(2)]
        wl = K.S("ld2")
        gw = K.S("gw")
        hl = [K.S("ld0"), K.S("ld1")]
        tl = [K.S("ld3"), K.S("ld4")]
        s_pe, s_ac, s_dv, s_pl = K.S("pe"), K.S("ac"), K.S("dv"), K.S("pl")
        sto = [K.S("gs0"), K.S("gs1")]
        stv = [K.S("gs2"), K.S("gs3")]
        stc = K.S("gs4")
        K.dma(SP, onesbd[:], dr["onesbd_f"], wl)
        K.dma(SP, ones[:], dr["ones_f"], wl)
        K.dma(SP, pt128[:], dr["pt128"], wl)
        K.dma(SP, pt96[:], dr["pt96"], wl)
        K.dma(SP, pt32[:], dr["pt32"], wl)
        for hh in range(2):
            K.dma(SP, gq[hh * 64:(hh + 1) * 64, :], L["gqa_q_norm"].rearrange("(p o) -> p o", o=1), wl)
            K.dma(SP, gk[hh * 64:(hh + 1) * 64, :], L["gqa_k_norm"].rearrange("(p o) -> p o", o=1), wl)
        t_w = K.dma(SP, kvg[:], L["mla_kv_norm"].rearrange("(r p) -> p r", p=128), wl, slow=True)
        for i in range(2):
            DVE.memset(vout[i][:], 1.0)
        W(DVE, t_w)
        DVE.tensor_copy(out=onesbd_b[:], in_=onesbd[:])
        DVE.tensor_copy(out=ones_b[:], in_=ones[:])
        t_ms = s_dv.inc(DVE.memset(qn[:], 0.0))
        for e in (PE, ACT, DVE, POOL):
            W(e, t_w)
            W(e, t_gw)
        W(ACT, t_ms)
        W(PE, t_ms)

        st = {"k": 0, "rk": 0, "vk": 0, "acc_free": [None, None], "ob_free": [None, None], "tab_free": [None, None],
              "vo_free": [None, None], "ptm_free": [None, None], "hb_tok": None}

        def store(eng, dst, src, sem):
            return K.dma(eng, dst, src, sem)

        import os
        LIMIT = int(os.environ.get("PROJ_LIMIT", "1000000"))
        units = [0]

        def over():
            units[0] += 1
            return units[0] > LIMIT

        def fm_job(chunks, M, nt, norm_g, rope, dsts, oscale=1.0):
            if over():
                return
            k = st["k"]; st["k"] += 1
            a = acc[k % 2]
            W(PE, st["acc_free"][k % 2])
            W(PE, st["hb_tok"])
            for ci, (lt, rh) in enumerate(chunks):
                ins = PE.matmul(a[:M, :nt], lhsT=lt, rhs=rh, start=(ci == 0), stop=(ci == len(chunks) - 1))
            t_main = s_pe.inc(ins)
            ob = outb[k % 2]
            if norm_g is None and rope is None:
                W(ACT, t_main)
                W(ACT, st["ob_free"][k % 2])
                t_out = s_ac.inc(ACT.activation(out=ob[:M, :nt], in_=a[:M, :nt], func=AF.Copy, scale=float(oscale)))
                st["acc_free"][k % 2] = t_out
            else:
                if rope is not None:
                    r = st["rk"]; st["rk"] += 1
                    PT, cos_ap, sin_ap = rope
                    W(SP, st["tab_free"][r % 2])
                    K.dma(SP, cosb[r % 2][:M, :nt], cos_ap, tl[r % 2])
                    t_tab = K.dma(SP, sinb[r % 2][:M, :nt], sin_ap, tl[r % 2])
                W(DVE, t_main)
                t_qf = s_dv.inc(DVE.tensor_copy(out=qf[:M, :nt], in_=a[:M, :nt]))
                t_cur = t_qf
                cur = qf
                free_toks = [t_qf]
                if norm_g is not None:
                    W(ACT, t_qf)
                    t_sq = s_ac.inc(ACT.activation(out=sqf[:M, :nt], in_=qf[:M, :nt], func=AF.Square))
                    W(PE, t_sq)
                    t_ss = s_pe.inc(PE.matmul(pss[:M, :nt], lhsT=onesbd_b[:M, :M], rhs=sqf[:M, :nt], start=True, stop=True))
                    W(ACT, t_ss)
                    W(ACT, t_qf)
                    t_sd = s_ac.inc(ACT.activation(out=sd[:M, :nt], in_=pss[:M, :nt], func=AF.Sqrt, bias=EPS, scale=1.0 / 64))
                    W(DVE, t_sd)
                    t_rs = s_dv.inc(DVE.reciprocal(out=rs[:M, :nt], in_=sd[:M, :nt]))
                    W(DVE, t_rs)
                    if rope is None:
                        W(DVE, st["ob_free"][k % 2])
                        t_out = s_dv.inc(DVE.scalar_tensor_tensor(out=ob[:M, :nt], in0=qf[:M, :nt], scalar=norm_g, in1=rs[:M, :nt], op0=ALU.mult, op1=ALU.mult))
                    else:
                        t_cur = s_dv.inc(DVE.scalar_tensor_tensor(out=qn[:M, :nt], in0=qf[:M, :nt], scalar=norm_g, in1=rs[:M, :nt], op0=ALU.mult, op1=ALU.mult))
                        cur = qn
                st["acc_free"][k % 2] = free_toks
                if rope is not None:
                    W(ACT, t_cur)
                    t_qb = s_ac.inc(ACT.activation(out=qb[:M, :nt], in_=cur[:M, :nt], func=AF.Copy))
                    W(PE, t_qb)
                    t_rot = s_pe.inc(PE.matmul(prot[:M, :nt], lhsT=PT[:M, :M], rhs=qb[:M, :nt], start=True, stop=True))
                    W(POOL, t_cur)
                    W(POOL, t_tab)
                    t_t1 = s_pl.inc(POOL.tensor_tensor(out=t1b[:M, :nt], in0=cur[:M, :nt], in1=cosb[r % 2][:M, :nt], op=ALU.mult))
                    W(DVE, t_rot)
                    W(DVE, t_tab)
                    t_t2 = s_dv.inc(DVE.tensor_tensor(out=t2b[:M, :nt], in0=prot[:M, :nt], in1=sinb[r % 2][:M, :nt], op=ALU.mult))
                    W(DVE, t_t1)
                    W(DVE, t_t2)
                    W(DVE, st["ob_free"][k % 2])
                    t_out = s_dv.inc(DVE.tensor_tensor(out=ob[:M, :nt], in0=t1b[:M, :nt], in1=t2b[:M, :nt], op=ALU.add))
                    st["tab_free"][r % 2] = t_out
            W(POOL, t_out)
            for (dst, r0, r1) in dsts:
                t_st = store(POOL, dst, ob[r0:r1, :nt], sto[k % 2])
            st["ob_free"][k % 2] = t_st

        def ckv_job(hs, nt, dst_ckvt):
            if over():
                st["ckvn_tok"] = None
                return
            k = st["k"]; st["k"] += 1
            a = acc[k % 2]
            W(PE, st["acc_free"][k % 2])
            W(PE, st["hb_tok"])
            W(PE, st.get("acc2_free"))
            for g, aa in enumerate((a, acc2)):
                for c in range(8):
                    ins = PE.matmul(aa[:, :nt], lhsT=win[:, c, O_CKV + g * 128:O_CKV + (g + 1) * 128], rhs=hblk[hs][:, c, :nt], start=(c == 0), stop=(c == 7))
            t_main = s_pe.inc(ins)
            CUT = int(os.environ.get("CKV_CUT", "99"))
            st["ckvn_tok"] = None
            if CUT <= 1:
                return
            W(DVE, t_main)
            DVE.tensor_copy(out=qf[:, :nt], in_=a[:, :nt])
            t_qf = s_dv.inc(DVE.tensor_copy(out=qf2[:, :nt], in_=acc2[:, :nt]))
            W(ACT, t_qf)
            ACT.activation(out=sqf[:, :nt], in_=qf[:, :nt], func=AF.Square)
            t_sq = s_ac.inc(ACT.activation(out=sqf2[:, :nt], in_=qf2[:, :nt], func=AF.Square))
            st["acc_free"][k % 2] = [t_qf]
            st["acc2_free"] = [t_qf]
            if CUT <= 2:
                return
            W(PE, t_sq)
            PE.matmul(pss[:, :nt], lhsT=ones_b[:], rhs=sqf[:, :nt], start=True, stop=False)
            t_ss = s_pe.inc(PE.matmul(pss[:, :nt], lhsT=ones_b[:], rhs=sqf2[:, :nt], start=False, stop=True))
            if CUT <= 3:
                return
            W(ACT, t_ss)
            W(ACT, t_qf)
            t_sd = s_ac.inc(ACT.activation(out=sd[:, :nt], in_=pss[:, :nt], func=AF.Sqrt, bias=EPS, scale=1.0 / 256))
            W(DVE, t_sd)
            t_rs = s_dv.inc(DVE.reciprocal(out=rs[:, :nt], in_=sd[:, :nt]))
            if CUT <= 4:
                return
            W(DVE, t_rs)
            W(DVE, st.get("ckvn_free"))
            DVE.scalar_tensor_tensor(out=ckvn[:, 0, :nt], in0=qf[:, :nt], scalar=kvg[:, 0:1], in1=rs[:, :nt], op0=ALU.mult, op1=ALU.mult)
            t_out = s_dv.inc(DVE.scalar_tensor_tensor(out=ckvn[:, 1, :nt], in0=qf2[:, :nt], scalar=kvg[:, 1:2], in1=rs[:, :nt], op0=ALU.mult, op1=ALU.mult))
            if CUT <= 5:
                return
            W(POOL, t_out)
            t_st = store(POOL, dst_ckvt.rearrange("(r p) t -> p r t", p=128), ckvn[:, :, :nt], stc)
            st["ckvn_tok"] = t_out
            st["ckvn_st"] = t_st

        def tm_job(chunks, N, nh, dst):
            if over():
                return
            j = st["vk"]; st["vk"] += 1
            p = ptm[j % 2]
            W(PE, st["ptm_free"][j % 2])
            W(PE, st["hb_tok"])
            for ci, (lt, rh) in enumerate(chunks):
                ins = PE.matmul(p[:, :N], lhsT=lt, rhs=rh, start=(ci == 0), stop=(ci == len(chunks) - 1))
            t_main = s_pe.inc(ins)
            W(ACT, t_main)
            W(ACT, st["vo_free"][j % 2])
            t_o = s_ac.inc(ACT.activation(out=vout[j % 2][:, 0:nh, 0:64], in_=p[:, :N].rearrange("p (h d) -> p h d", d=64), func=AF.Copy))
            st["ptm_free"][j % 2] = t_o
            W(POOL, t_o)
            st["vo_free"][j % 2] = store(POOL, dst, vout[j % 2][:, 0:nh, :], stv[j % 2])

        nblk = [0]
        last_users = [None, None]

        hT_all, hT_na = dr["hT_all"], dr["hT_na"]
        bsrc = [(hT_all[:, tb * 512:tb * 512 + (256 if tb == 16 else 512)], 256 if tb == 16 else 512) for tb in range(17)]
        bsrc += [(hT_na[:, tb * 512:(tb + 1) * 512], 512) for tb in range(5)]
        bsrc += [(hT_na[:, 256 + tb * 512:256 + (tb + 1) * 512], 512) for tb in range(4)]
        btok = [None] * len(bsrc)
        issued = [0]

        def issue_upto(i):
            while issued[0] <= i and issued[0] < len(bsrc):
                b = issued[0]
                src_ap, nt_ = bsrc[b]
                W(SP, last_users[b % 2])
                btok[b] = K.dma(SP, hblk[b % 2][:, :, :nt_], src_ap.rearrange("(c p) t -> p c t", p=128), hl[b % 2])
                issued[0] += 1

        def load_block(src_ap, nt):
            b = nblk[0]; nblk[0] += 1
            issue_upto(b)
            st["hb_tok"] = btok[b]
            issue_upto(b + 1)
            return b % 2

        def done_block(hs):
            last_users[hs] = (s_pe, s_pe.n)

        def hch(hs, c0, M, nt):
            return [(win[:, c, c0:c0 + M], hblk[hs][:, c, :nt]) for c in range(8)]

        for tb in range(17):
            ctxb = (tb == 16)
            t0 = tb * 512
            nt = 256 if ctxb else 512
            hs = load_block(hT_all[:, t0:t0 + nt], nt)
            rope = None if ctxb else (pt128, dr["cosA_all"][:, t0:t0 + nt], dr["sinA_all"][:, t0:t0 + nt])
            fm_job(hch(hs, O_GK, 128, nt), 128, nt, gk[:, 0:1], rope,
                   [(dr["GKT"][0, :, t0:t0 + nt], 0, 64), (dr["GKT"][1, :, t0:t0 + nt], 64, 128)])
            ckv_job(hs, nt, dr["CKVT"][:, t0:t0 + nt])
            rope = None if ctxb else (pt32, dr["cosM_all"][:, t0:t0 + nt], dr["sinM_all"][:, t0:t0 + nt])
            fm_job(hch(hs, O_KR, 32, nt), 32, nt, None, rope, [(dr["MKT"][h, 64:96, t0:t0 + nt], 0, 32) for h in range(8)])
            W(PE, st["ckvn_tok"])
            for g in range(4):
                fm_job([(wuk[:, r, g * 128:(g + 1) * 128], ckvn[:, r, :nt]) for r in range(2)], 128, nt, None, None,
                       [(dr["MKT"][2 * g, 0:64, t0:t0 + nt], 0, 64), (dr["MKT"][2 * g + 1, 0:64, t0:t0 + nt], 64, 128)])
            for ti in range(nt // 128):
                tsl = slice(ti * 128, (ti + 1) * 128)
                r0 = t0 + ti * 128
                tm_job([(ckvn[:, r, tsl], wuv[:, r, :]) for r in range(2)], 512, 8, dr["MV"][r0:r0 + 128, :, :])
                tm_job([(hblk[hs][:, c, tsl], win[:, c, O_GV:O_GV + 128]) for c in range(8)], 128, 2, dr["GV"][r0:r0 + 128, :, :])
            st["ckvn_free"] = (s_pe, s_pe.n)
            if ctxb:
                for g in range(4):
                    fm_job(hch(hs, O_NK + g * 128, 128, nt), 128, nt, None, None,
                           [(dr["NKT"][2 * g, :, NAT:NAT + nt], 0, 64), (dr["NKT"][2 * g + 1, :, NAT:NAT + nt], 64, 128)])
                for ti in range(nt // 128):
                    tsl = slice(ti * 128, (ti + 1) * 128)
                    tm_job([(hblk[hs][:, c, tsl], win[:, c, O_NV:O_NV + 512]) for c in range(8)], 512, 8, dr["NV"][NAT + ti * 128:NAT + (ti + 1) * 128, :, :])
                if with_ctx:
                    q0 = T
                    for g in range(4):
                        fm_job(hch(hs, O_GQ + g * 128, 128, nt), 128, nt, gq[:, 0:1], None,
                               [(dr["GQT"][2 * g, :, q0:q0 + nt], 0, 64), (dr["GQT"][2 * g + 1, :, q0:q0 + nt], 64, 128)])
                        fm_job(hch(hs, O_NQ + g * 128, 128, nt), 128, nt, None, None,
                               [(dr["NQT"][2 * g, :, q0:q0 + nt], 0, 64), (dr["NQT"][2 * g + 1, :, q0:q0 + nt], 64, 128)], oscale=0.125)
                    for h in range(8):
                        fm_job(hch(hs, O_MQ + h * 96, 96, nt), 96, nt, None, None, [(dr["MQT"][h, :, q0:q0 + nt], 0, 96)])
            done_block(hs)
        for tb in range(5):
            t0 = tb * 512
            nt = 512
            hs = load_block(hT_na[:, t0:t0 + nt], nt)
            for g in range(4):
                fm_job(hch(hs, O_NK + g * 128, 128, nt), 128, nt, None, None,
                       [(dr["NKT"][2 * g, :, t0:t0 + nt], 0, 64), (dr["NKT"][2 * g + 1, :, t0:t0 + nt], 64, 128)])
            for ti in range(4):
                tsl = slice(ti * 128, (ti + 1) * 128)
                tm_job([(hblk[hs][:, c, tsl], win[:, c, O_NV:O_NV + 512]) for c in range(8)], 512, 8, dr["NV"][t0 + ti * 128:t0 + (ti + 1) * 128, :, :])
            done_block(hs)
        for tb in range(4):
            q0 = tb * 512
            nt = 512
            hs = load_block(hT_na[:, 256 + q0:256 + q0 + nt], nt)
            for g in range(4):
                fm_job(hch(hs, O_GQ + g * 128, 128, nt), 128, nt, gq[:, 0:1], (pt128, dr["cosA_own"][:, q0:q0 + nt], dr["sinA_own"][:, q0:q0 + nt]),
                       [(dr["GQT"][2 * g, :, q0:q0 + nt], 0, 64), (dr["GQT"][2 * g + 1, :, q0:q0 + nt], 64, 128)])
                fm_job(hch(hs, O_NQ + g * 128, 128, nt), 128, nt, None, None,
                       [(dr["NQT"][2 * g, :, q0:q0 + nt], 0, 64), (dr["NQT"][2 * g + 1, :, q0:q0 + nt], 64, 128)], oscale=0.125)
            for h in range(8):
                fm_job(hch(hs, O_MQ + h * 96, 96, nt), 96, nt, None, (pt96, dr["cosM_own"][:, q0:q0 + nt], dr["sinM_own"][:, q0:q0 + nt]),
                       [(dr["MQT"][h, :, q0:q0 + nt], 0, 96)])
            done_block(hs)
        K.barrier([(s, s.n) for s in sto + stv + [stc]])


def phase_attn(K, heads, pfx, nkmax):
    nc = K.nc
    PE, ACT, DVE, POOL, SP = K.PE, K.ACT, K.DVE, K.POOL, K.SP
    NQ = T + C
    rls = K.dram["rls"]
    with ExitStack() as es:
        sb, ps = mk_alloc(nc, es, pfx)
        ktb = [sb(f"kt{i}", [128, nkmax], BF16) for i in range(2)]
        vb = [sb(f"v{i}", [128, nkmax // 128, 65], BF16) for i in range(2)]
        qb = [sb(f"q{i}", [128, NQ], BF16) for i in range(2)]
        pbuf = [sb(f"p{i}", [128, 1024], BF16) for i in range(3)]
        NBB, LB = 8, 6
        bb = [sb(f"bias{i}", [128, 512], BF16) for i in range(NBB)]
        identb = sb("identb", [128, 128], BF16)
        osb = [sb(f"osb{i}", [128, 512]) for i in range(2)]
        rl = [sb(f"rl{i}", [128, 512]) for i in range(2)]
        rbc = [sb(f"rbc{i}", [64, 512]) for i in range(2)]
        ysb = [[sb(f"y{i}_{w}", [64, 512], BF16) for w in range(2)] for i in range(2)]
        psb = [ps(f"s{i}", [128, 1024]) for i in range(3)]
        po = [ps(f"o{i}", [128, 512]) for i in range(2)]
        hl = [K.S("ld0"), K.S("ld1")]
        bl = [K.S(f"gb{i}") for i in range(NBB)]
        cl = K.S("ld5")
        rld = [K.S("ld3"), K.S("ld4")]
        rst = [K.S("st2"), K.S("st3")]
        s_pe, s_ac, s_dv = K.S("pe"), K.S("ac"), K.S("dv")
        sty = [K.S("st0"), K.S("st1")]
        t_c = K.dma(SP, identb[:], K.dram["ident_b"], cl)
        for i in range(2):
            DVE.memset(ktb[i][:], 0.0)
            DVE.memset(qb[i][:], 0.0)
            DVE.memset(rl[i][:], 1.0)
        t_m = s_dv.inc(DVE.memset(osb[0][:], 0.0))
        W(PE, t_c)
        W(PE, t_m)
        W(SP, t_m)
        steps = []
        for hi, h in enumerate(heads):
            for bi, b in enumerate(h["blocks"]):
                nt_ = len(b["tiles"])
                b["chunks"] = [(c0, min(512, b["nq"] - c0)) for c0 in range(0, b["nq"], 512)]
                for si, (kti, bias) in enumerate(b["tiles"]):
                    assert bias is None or len(b["chunks"]) == 1
                    steps.append(dict(hi=hi, b=b, kti=kti, bias=bias, first=(si == 0), last=(si == nt_ - 1),
                                      hfirst=(bi == 0 and si == 0), hlast=(bi == len(h["blocks"]) - 1 and si == nt_ - 1)))
        NS = len(steps)
        head_tok = [None] * len(heads)
        head_done = [None] * len(heads)

        def load_head(hi):
            h = heads[hi]
            s = hi % 2
            if hi >= 2:
                W(SP, head_done[hi - 2])
            dk, nk = h["dk"], h["nk"]
            K.dma(SP, ktb[s][:dk, :nk], h["kt"], hl[s])
            K.dma(SP, vb[s][:, :nk // 128, :], h["v"].rearrange("(t p) e -> p t e", p=128), hl[s])
            head_tok[hi] = K.dma(SP, qb[s][:dk, :], h["qt"], hl[s])

        tq = [None] * NS
        te = [None] * NS
        tv = [None] * NS
        bias_ld = [None] * NS
        nbias = [0]
        bidx = [None] * NS
        po_free = [None, None]
        y_free = [[None, None], [None, None]]
        pend_dv = {}
        state = {}

        def emit_qk(t):
            s = steps[t]
            h = heads[s["hi"]]
            hs = s["hi"] % 2
            if s["hfirst"]:
                W(PE, head_tok[s["hi"]])
            if t >= 3:
                W(PE, te[t - 3])
            b = s["b"]
            hasb = s["bias"] is not None
            for (c0, cn) in b["chunks"]:
                ins = PE.matmul(psb[t % 3][:, c0:c0 + cn], lhsT=ktb[hs][:, s["kti"] * 128:(s["kti"] + 1) * 128],
                                rhs=qb[hs][:, b["q0"] + c0:b["q0"] + c0 + cn], start=True, stop=not hasb)
            if hasb:
                W(PE, bias_ld[t])
                ins = PE.matmul(psb[t % 3][:, :b["nq"]], lhsT=identb[:], rhs=bb[bidx[t] % NBB][:, :b["nq"]], start=False, stop=True)
            tq[t] = s_pe.inc(ins)
            if hasb:
                state[("bfree", bidx[t] % NBB)] = tq[t]

        def emit_bias_load(t):
            s = steps[t]
            if s["bias"] is None:
                return
            n = nbias[0]; nbias[0] += 1
            bidx[t] = n
            W(POOL, state.get(("bfree", n % NBB)))
            bias_ld[t] = K.dma(POOL, bb[n % NBB][:, :s["b"]["nq"]], s["bias"], bl[n % NBB])

        if NS > 0:
            load_head(0)
        LA = 2
        for t in range(min(LB, NS)):
            emit_bias_load(t)
        for t in range(min(LA, NS)):
            emit_qk(t)
        cur_blk = -1
        for t in range(NS):
            s = steps[t]
            h = heads[s["hi"]]
            b = s["b"]
            nq = b["nq"]
            hs = s["hi"] % 2
            if s["hfirst"] and s["hi"] + 1 < len(heads):
                load_head(s["hi"] + 1)
            if s["first"]:
                cur_blk += 1
            if t + LB < NS:
                emit_bias_load(t + LB)
            if t + LA < NS:
                emit_qk(t + LA)
            W(ACT, tq[t])
            if t >= 3:
                W(ACT, tv[t - 3])
            te[t] = s_ac.inc(ACT.activation(out=pbuf[t % 3][:, :nq], in_=psb[t % 3][:, :nq], func=AF.Exp, scale=float(h["scale"])))
            W(PE, te[t])
            for w, (c0, cn) in enumerate(b["chunks"]):
                if s["first"]:
                    W(PE, po_free[w])
                ins = PE.matmul(po[w][:65, :cn], lhsT=vb[hs][:, s["kti"], :], rhs=pbuf[t % 3][:, c0:c0 + cn], start=s["first"], stop=s["last"])
            tv[t] = s_pe.inc(ins)
            if s["hlast"]:
                head_done[s["hi"]] = tv[t]
            for f in pend_dv.pop(t, []):
                f()
            if s["last"]:
                cb = cur_blk
                parts = []
                for w, (c0, cn) in enumerate(b["chunks"]):
                    W(DVE, tv[t])
                    t_o = s_dv.inc(DVE.tensor_copy(out=osb[w][:65, :cn], in_=po[w][:65, :cn]))
                    po_free[w] = t_o
                    W(DVE, t_o)
                    t_rl = s_dv.inc(DVE.reciprocal(out=rl[w][64:65, :cn], in_=osb[w][64:65, :cn]))
                    slot = (cb % 2) * 2 + w
                    W(SP, t_rl)
                    t_s = K.dma(SP, rls[slot:slot + 1, :cn], rl[w][64:65, :cn], rst[w])
                    W(SP, t_s)
                    t_b = K.dma(SP, rbc[w][:, :cn], rls[slot, :cn].partition_broadcast(64), rld[w])
                    parts.append((w, cn, t_b))

                def dv_part(parts=parts, cb=cb, yts=b["yts"]):
                    for (w, cn, t_b) in parts:
                        W(DVE, t_b)
                        W(DVE, y_free[cb % 2][w])
                        t_y = s_dv.inc(DVE.tensor_tensor(out=ysb[cb % 2][w][:, :cn], in0=osb[w][:64, :cn], in1=rbc[w][:, :cn], op=ALU.mult))
                        W(SP, t_y)
                        y_free[cb % 2][w] = K.dma(SP, yts[w], ysb[cb % 2][w][:, :cn], sty[w])

                if t + 1 < NS:
                    nxt_len = len(steps[t + 1]["b"]["tiles"])
                    d = max(1, min(2, nxt_len - 1))
                    pend_dv.setdefault(t + d, []).append(dv_part)
                else:
                    dv_part()
        assert not pend_dv
        K.barrier([(s_, s_.n) for s_ in sty])


def phase_merge(K, L, pfx, qblocks, x_src, x_dst, pre=None):
    nc = K.nc
    PE, ACT, DVE, POOL, SP = K.PE, K.ACT, K.DVE, K.POOL, K.SP
    dr = K.dram
    with ExitStack() as es:
        sb, ps = mk_alloc(nc, es, pfx)
        wg, wo, wout, t_gw = pre if pre is not None else load_merge_weights(K, L, sb)
        g1 = sb("g1", [128, 2, 1024])
        hblk = [sb(f"h{i}", [128, 8, 512], BF16) for i in range(2)]
        yb = [[sb(f"y{r}_{i}", [128, 4, 512], BF16) for r in range(3)] for i in range(2)]
        sg = [sb(f"sg{i}", [128, 512]) for i in range(2)]
        yacc = sb("yacc", [128, 512])
        tmp = sb("tmp", [128, 512])
        yT = sb("yT", [128, 8, 512], BF16)
        xt = [sb(f"xt{i}", [128, 1024]) for i in range(2)]
        xo = [sb(f"xo{i}", [128, 1024]) for i in range(2)]
        tm2 = [sb(f"tm{i}", [128, 512]) for i in range(2)]
        pg = [ps(f"pg{i}", [128, 512]) for i in range(2)]
        pbr = [ps(f"pb{i}", [128, 512]) for i in range(2)]
        pw = [ps(f"pw{i}", [128, 512]) for i in range(2)]
        wl = K.S("ld2")
        hl = [K.S("ld0"), K.S("ld1")]
        xl = [K.S("ld3"), K.S("ld4")]
        s_pe, s_ac, s_dv, s_pl = K.S("pe"), K.S("ac"), K.S("dv"), K.S("pl")
        stx = [K.S("gs0"), K.S("gs1")]
        K.dma(SP, g1[:, 0, :], dr["modv"][0, 2].partition_broadcast(128), wl)
        t_w = K.dma(SP, g1[:, 1, :], dr["modv"][1, 2].partition_broadcast(128), wl)
        for e in (PE, DVE, POOL):
            W(e, t_w)
            W(e, t_gw)
        ysrc = (dr["YAT"], dr["YBT"], dr["YCT"])
        blk_done = [None, None]
        k = 0
        xk = 0
        sg_free = [None, None]
        pg_free = [None, None]
        pbr_free = [None, None]
        pw_free = [None, None]
        xt_free = [None, None]
        xo_free = [None, None]
        tm_free = [None, None]
        yT_free = None
        for bi, (hT_ap, q0, nt, m) in enumerate(qblocks):
            s = bi % 2
            W(SP, blk_done[s])
            K.dma(SP, hblk[s][:, :, :nt], hT_ap.rearrange("(c p) t -> p c t", p=128), hl[s])
            for r in range(3):
                t_l = K.dma(SP, yb[s][r][:, :, :nt], ysrc[r][:, q0:q0 + nt].rearrange("(c p) t -> p c t", p=128), hl[s])
            W(PE, t_l)
            for oc in range(8):
                for r in range(3):
                    W(PE, pg_free[k % 2])
                    for c in range(8):
                        ins = PE.matmul(pg[k % 2][:, :nt], lhsT=wg[:, c, r * 1024 + oc * 128:r * 1024 + (oc + 1) * 128], rhs=hblk[s][:, c, :nt],
                                        start=(c == 0), stop=(c == 7))
                    t_g = s_pe.inc(ins)
                    W(PE, pbr_free[k % 2])
                    for c in range(4):
                        ins = PE.matmul(pbr[k % 2][:, :nt], lhsT=wo[r][:, c, oc * 128:(oc + 1) * 128], rhs=yb[s][r][:, c, :nt], start=(c == 0), stop=(c == 3))
                    t_b = s_pe.inc(ins)
                    W(ACT, t_g)
                    W(ACT, sg_free[k % 2])
                    t_s = s_ac.inc(ACT.activation(out=sg[k % 2][:, :nt], in_=pg[k % 2][:, :nt], func=AF.Sigmoid))
                    pg_free[k % 2] = t_s
                    W(DVE, t_s)
                    W(DVE, t_b)
                    if r == 0:
                        t_d = s_dv.inc(DVE.tensor_tensor(out=yacc[:, :nt], in0=sg[k % 2][:, :nt], in1=pbr[k % 2][:, :nt], op=ALU.mult))
                    else:
                        t_d = s_dv.inc(DVE.tensor_tensor(out=tmp[:, :nt], in0=sg[k % 2][:, :nt], in1=pbr[k % 2][:, :nt], op=ALU.mult))
                        W(DVE, t_d)
                        if r == 1:
                            t_d = s_dv.inc(DVE.tensor_tensor(out=yacc[:, :nt], in0=yacc[:, :nt], in1=tmp[:, :nt], op=ALU.add))
                        else:
                            if oc == 0:
                                W(DVE, yT_free)
                            t_d = s_dv.inc(DVE.tensor_tensor(out=yT[:, oc, :nt], in0=yacc[:, :nt], in1=tmp[:, :nt], op=ALU.add))
                    sg_free[k % 2] = t_d
                    pbr_free[k % 2] = t_d
                    k += 1
            blk_done[s] = (s_pe, s_pe.n)
            t_y = t_d
            W(PE, t_y)
            for ti in range(nt // 128):
                xs_ = xk % 2
                W(SP, xt_free[xs_])
                t_x = K.dma(SP, xt[xs_][:], x_src(q0 + ti * 128), xl[xs_])
                for half in range(2):
                    j = 2 * xk + half
                    W(PE, pw_free[j % 2])
                    for c in range(8):
                        ins = PE.matmul(pw[j % 2][:, :], lhsT=yT[:, c, ti * 128:(ti + 1) * 128], rhs=wout[:, c, half * 512:(half + 1) * 512],
                                        start=(c == 0), stop=(c == 7))
                    t_p = s_pe.inc(ins)
                    W(DVE, t_p)
                    W(DVE, tm_free[j % 2])
                    t_m = s_dv.inc(DVE.tensor_tensor(out=tm2[j % 2][:], in0=pw[j % 2][:], in1=g1[:, m, half * 512:(half + 1) * 512], op=ALU.mult))
                    pw_free[j % 2] = t_m
                    W(POOL, t_m)
                    W(POOL, t_x)
                    if half == 0:
                        W(POOL, xo_free[xs_])
                    t_a = s_pl.inc(POOL.tensor_tensor(out=xo[xs_][:, half * 512:(half + 1) * 512], in0=tm2[j % 2][:], in1=xt[xs_][:, half * 512:(half + 1) * 512], op=ALU.add))
                    tm_free[j % 2] = t_a
                xt_free[xs_] = t_a
                W(POOL, t_a)
                xo_free[xs_] = K.dma(POOL, x_dst(q0 + ti * 128), xo[xs_][:], stx[xs_])
                xk += 1
            yT_free = (s_pe, s_pe.n)
        K.barrier([(s_, s_.n) for s_ in stx])


def phase_mlp(K, L, pfx, qblocks, x_src, x_dst, final_g, after_block=None):
    nc = K.nc
    PE, ACT, DVE, POOL, SP = K.PE, K.ACT, K.DVE, K.POOL, K.SP
    dr = K.dram
    with ExitStack() as es:
        sb, ps = mk_alloc(nc, es, pfx)
        w1 = sb("w1", [128, 8, 4096], BF16)
        w2 = sb("w2", [128, 32, 1024], BF16)
        g2 = sb("g2", [128, 2, 1024])
        fg = sb("fg", [128, 1024])
        hblk = [sb(f"h{i}", [128, 8, 256], BF16) for i in range(2)]
        uT = sb("uT", [128, 32, 256], BF16)
        rb = [sb(f"r{i}", [128, 256]) for i in range(2)]
        xt = [sb(f"xt{i}", [128, 1024]) for i in range(2)]
        xo = [sb(f"xo{i}", [128, 1024]) for i in range(2)]
        tm2 = [sb(f"tm{i}", [128, 512]) for i in range(2)]
        junk = sb("junk", [128, 1024])
        st4 = sb("st4", [128, 4])
        pu = [ps(f"pu{i}", [128, 512]) for i in range(2)]
        pw = [ps(f"pw{i}", [128, 512]) for i in range(2)]
        wl = K.S("ld2")
        hl = [K.S("ld0"), K.S("ld1")]
        xl = [K.S("ld3"), K.S("ld4")]
        s_pe, s_ac, s_dv, s_pl = K.S("pe"), K.S("ac"), K.S("dv"), K.S("pl")
        stx = [K.S("gs0"), K.S("gs1")]
        gw = K.S("gw")
        for c in range(8):
            K.dma(POOL, w1[:, c, :], L["w_mlp1"][c * 128:(c + 1) * 128, :], gw)
        for c4 in range(4):
            t_gw = K.dma(POOL, w2[:, c4 * 8:(c4 + 1) * 8, :], L["w_mlp2"][c4 * 1024:(c4 + 1) * 1024, :].rearrange("(c p) n -> p c n", p=128), gw)
        K.dma(SP, g2[:, 0, :], dr["modv"][0, 5].partition_broadcast(128), wl)
        if final_g is not None:
            K.dma(SP, fg[:], final_g.partition_broadcast(128), wl)
        t_w = K.dma(SP, g2[:, 1, :], dr["modv"][1, 5].partition_broadcast(128), wl)
        for e in (PE, DVE, POOL, ACT):
            W(e, t_w)
            W(e, t_gw)
        h2T = dr["h2T"]
        blk_done = [None, None]
        k = 0
        xk = 0
        pu_free = [None, None]
        rb_free = [None, None]
        pw_free = [None, None]
        xt_free = [None, None]
        xo_free = [None, None]
        tm_free = [None, None]
        uT_free = None
        for bi, (q0, nt, m) in enumerate(qblocks):
            s = bi % 2
            W(SP, blk_done[s])
            t_l = K.dma(SP, hblk[s][:, :, :nt], h2T[:, q0:q0 + nt].rearrange("(c p) t -> p c t", p=128), hl[s])
            W(PE, t_l)
            for fc in range(32):
                W(PE, pu_free[k % 2])
                for c in range(8):
                    ins = PE.matmul(pu[k % 2][:, :nt], lhsT=w1[:, c, fc * 128:(fc + 1) * 128], rhs=hblk[s][:, c, :nt], start=(c == 0), stop=(c == 7))
                t_u = s_pe.inc(ins)
                W(ACT, t_u)
                W(ACT, rb_free[k % 2])
                t_r = s_ac.inc(ACT.activation(out=rb[k % 2][:, :nt], in_=pu[k % 2][:, :nt], func=AF.Relu))
                pu_free[k % 2] = t_r
                W(DVE, t_r)
                if fc == 0:
                    W(DVE, uT_free)
                t_q = s_dv.inc(DVE.tensor_tensor(out=uT[:, fc, :nt], in0=rb[k % 2][:, :nt], in1=rb[k % 2][:, :nt], op=ALU.mult))
                rb_free[k % 2] = t_q
                k += 1
            blk_done[s] = (s_pe, s_pe.n)
            W(PE, t_q)
            for ti in range(nt // 128):
                xs_ = xk % 2
                W(SP, xt_free[xs_])
                t_x = K.dma(SP, xt[xs_][:], x_src(q0 + ti * 128), xl[xs_])
                for half in range(2):
                    j = 2 * xk + half
                    W(PE, pw_free[j % 2])
                    for fc in range(32):
                        ins = PE.matmul(pw[j % 2][:, :], lhsT=uT[:, fc, ti * 128:(ti + 1) * 128], rhs=w2[:, fc, half * 512:(half + 1) * 512],
                                        start=(fc == 0), stop=(fc == 31))
                    t_p = s_pe.inc(ins)
                    W(DVE, t_p)
                    W(DVE, tm_free[j % 2])
                    t_m = s_dv.inc(DVE.tensor_tensor(out=tm2[j % 2][:], in0=pw[j % 2][:], in1=g2[:, m, half * 512:(half + 1) * 512], op=ALU.mult))
                    pw_free[j % 2] = t_m
                    W(POOL, t_m)
                    W(POOL, t_x)
                    if half == 0:
                        W(POOL, xo_free[xs_])
                    t_a = s_pl.inc(POOL.tensor_tensor(out=xo[xs_][:, half * 512:(half + 1) * 512], in0=tm2[j % 2][:], in1=xt[xs_][:, half * 512:(half + 1) * 512], op=ALU.add))
                    tm_free[j % 2] = t_a
                xt_free[xs_] = t_a
                t_fin = t_a
                if final_g is not None:
                    W(ACT, t_a)
                    t1 = s_ac.inc(ACT.activation(out=junk[:], in_=xo[xs_][:], func=AF.Square, accum_out=st4[:, 0:1]))
                    W(DVE, t1)
                    t2 = s_dv.inc(DVE.tensor_scalar(out=st4[:, 1:2], in0=st4[:, 0:1], scalar1=1.0 / D, scalar2=EPS, op0=ALU.mult, op1=ALU.add))
                    W(ACT, t2)
                    t3 = s_ac.inc(ACT.activation(out=st4[:, 2:3], in_=st4[:, 1:2], func=AF.Sqrt))
                    W(DVE, t3)
                    t4 = s_dv.inc(DVE.reciprocal(out=st4[:, 3:4], in_=st4[:, 2:3]))
                    W(DVE, t4)
                    t_fin = s_dv.inc(DVE.scalar_tensor_tensor(out=xo[xs_][:], in0=xo[xs_][:], scalar=st4[:, 3:4], in1=fg[:], op0=ALU.mult, op1=ALU.mult))
                W(POOL, t_fin)
                xo_free[xs_] = K.dma(POOL, x_dst(q0 + ti * 128), xo[xs_][:], stx[xs_])
                xk += 1
            uT_free = (s_pe, s_pe.n)
            if after_block is not None:
                after_block(bi, [xo_free[0], xo_free[1]])
        K.barrier([(s_, s_.n) for s_ in stx])


def phase_halo(K, pfx):
    nc = K.nc
    PE, ACT, DVE, POOL, SP = K.PE, K.ACT, K.DVE, K.POOL, K.SP
    dr = K.dram
    xg, x1, xna, sel = K.xg_at, K.x1_at, dr["x_na2"], dr["sel"]
    with ExitStack() as es:
        sb, ps = mk_alloc(nc, es, pfx)
        selb = sb("sel", [128, 8])
        cand = [sb(f"c{i}", [128, 1024]) for i in range(4)]
        acc = [sb(f"a{i}", [128, 1024]) for i in range(2)]
        ld = [K.S("ld0"), K.S("ld1"), K.S("ld3"), K.S("ld4")]
        lc = K.S("ld2")
        s_dv = K.S("dv")
        st = [K.S("st0"), K.S("st1")]
        so = K.S("st2")
        t_c = K.dma(SP, selb[:], sel.partition_broadcast(128), lc)
        for q in range(0, T, 128):
            t_own = K.dma(SP, xna[256 + q:256 + q + 128, :], x1(q), so)
        W(DVE, t_c)
        jobs = []
        for u in range(2):
            jobs.append((128 * u, [xg(2048 * r + 1792 + 128 * u) for r in range(4)], 0))
        for u in range(2):
            jobs.append((256 + T + 128 * u, [xg(2048 * r + 128 * u) for r in range(4)], 4))
        dv_prev = None
        st_t = [None, None]
        for n, (row0, srcs, c0) in enumerate(jobs):
            W(SP, dv_prev)
            lts = [K.dma(SP, cand[r][:], srcs[r], ld[r]) for r in range(4)]
            W(DVE, lts)
            W(DVE, st_t[n % 2])
            t = s_dv.inc(DVE.tensor_scalar(out=acc[n % 2][:], in0=cand[0][:], scalar1=selb[:, c0:c0 + 1], scalar2=0.0, op0=ALU.mult, op1=ALU.add))
            for r in range(1, 4):
                W(DVE, t)
                t = s_dv.inc(DVE.scalar_tensor_tensor(out=acc[n % 2][:], in0=cand[r][:], scalar=selb[:, c0 + r:c0 + r + 1], in1=acc[n % 2][:],
                                                      op0=ALU.mult, op1=ALU.add))
            dv_prev = t
            W(SP, t)
            st_t[n % 2] = K.dma(SP, xna[row0:row0 + 128, :], acc[n % 2][:], st[n % 2])
        K.barrier([t_own, st_t[0], st_t[1]])


W_NAMES = ["w_mod", "b_mod", "norm1_g", "norm2_g", "w_in", "gqa_q_norm", "gqa_k_norm", "mla_kv_norm", "mla_w_uk", "mla_w_uv",
           "w_o_gqa", "w_o_na", "w_o_mla", "w_out", "w_mlp1", "w_mlp2"]
W_SHAPES = {"w_mod": [D, 6 * D], "b_mod": [6 * D], "norm1_g": [D], "norm2_g": [D], "w_in": [D, WIN], "gqa_q_norm": [64], "gqa_k_norm": [64],
            "mla_kv_norm": [256], "mla_w_uk": [256, 512], "mla_w_uv": [256, 512], "w_o_gqa": [512, D], "w_o_na": [512, D], "w_o_mla": [512, D],
            "w_out": [D, D], "w_mlp1": [D, 4 * D], "w_mlp2": [4 * D, D]}
DEPTH = 2


def emit_layer(K, L, src, with_ctx, final, sfx):
    dr = K.dram
    NQ = T + C
    nc = K.nc
    with ExitStack() as esw:
        sbw, _ = mk_alloc(nc, esw, "pw" + sfx)
        pre = load_proj_weights(K, L, sbw)
        phase_mod(K, L, "md" + sfx)
        phase_norm(K, [(src["x_all"], S, dr["hT_all"][:, 0:S], 0, 0, 1), (src["ctx_in"], C, dr["hT_all"][:, S:SK], 1, 0, 1),
                       (src["x_na"], NAT, dr["hT_na"], 0, 0, 1)], "n1" + sfx)
        phase_proj(K, L, "pj" + sfx, with_ctx, pre=pre)
    esm = ExitStack()
    sbm, _ = mk_alloc(nc, esm, "mw" + sfx)
    pre_m = load_merge_weights(K, L, sbm)
    qbl = [(512 * i, 512) for i in range(4)]
    heads = []

    def wide_blocks(Y, h):
        blocks = [dict(q0=q0, nq=1024, tiles=[(k, None) for k in range(66)],
                       yts=[Y[64 * h:64 * h + 64, q0:q0 + 512], Y[64 * h:64 * h + 64, q0 + 512:q0 + 1024]]) for q0 in (0, 1024)]
        if with_ctx:
            blocks.append(dict(q0=T, nq=C, tiles=[(64, None), (65, None)], yts=[Y[64 * h:64 * h + 64, T:NQ]]))
        return blocks

    for h in range(8):
        heads.append(dict(kt=dr["GKT"][h // 4], v=dr["GV"][:, h // 4, :], qt=dr["GQT"][h], dk=64, scale=0.125, nk=SK, blocks=wide_blocks(dr["YAT"], h)))
    for h in range(8):
        heads.append(dict(kt=dr["MKT"][h], v=dr["MV"][:, h, :], qt=dr["MQT"][h], dk=96, scale=96 ** -0.5, nk=SK, blocks=wide_blocks(dr["YCT"], h)))
    phase_attn(K, heads, "at" + sfx, SK)
    heads = []
    var = [0, 1, 1, 2]
    for h in range(8):
        blocks = []
        for i, (q0, nq) in enumerate(qbl):
            tiles = [(4 * i + m, src["nabias"][var[i], h, m]) for m in range(8)] + [(20, None), (21, None)]
            blocks.append(dict(q0=q0, nq=nq, tiles=tiles, yts=[dr["YBT"][64 * h:64 * h + 64, q0:q0 + nq]]))
        if with_ctx:
            blocks.append(dict(q0=T, nq=C, tiles=[(20, None), (21, None)], yts=[dr["YBT"][64 * h:64 * h + 64, T:NQ]]))
        heads.append(dict(kt=dr["NKT"][h], v=dr["NV"][:, h, :], qt=dr["NQT"][h], dk=64, scale=1.0, nk=NAK, blocks=blocks))
    phase_attn(K, heads, "na" + sfx, NAK)
    mblocks = [(dr["hT_na"][:, 256 + 512 * i:256 + 512 * (i + 1)], 512 * i, 512, 0) for i in range(4)]
    if with_ctx:
        mblocks.append((dr["hT_all"][:, S:SK], T, C, 1))

    def x_src(q):
        if q >= T:
            return src["ctx_in"][q - T:q - T + 128, :]
        return src["x_own"](q) if callable(src["x_own"]) else src["x_own"][q:q + 128, :]

    def xs1_at(q):
        return dr["xs1"][q:q + 128, :]

    phase_merge(K, L, "mg" + sfx, mblocks, x_src, xs1_at, pre=pre_m)
    esm.close()
    njobs = [(dr["xs1"][0:T, :], T, dr["h2T"][:, 0:T], 0, 3, 4)]
    if with_ctx:
        njobs.append((dr["xs1"][T:NQ, :], C, dr["h2T"][:, T:NQ], 1, 3, 4))
    phase_norm(K, njobs, "n2" + sfx)
    fblocks = [(256 * i, 256, 0) for i in range(8)]
    if with_ctx:
        fblocks.append((T, C, 1))
    phase_mlp(K, L, "ml" + sfx, fblocks, xs1_at, src["x_dst"], L.get("final_norm_g") if final else None, after_block=src.get("after_block"))


def build_fused():
    nc = bass.Bass("TRN2", target_bir_lowering=False)
    K = KB(nc)
    NQ = T + C
    dr = K.dram

    def inp(name, shape, dt=F32):
        dr[name] = nc.dram_tensor(name, shape, dt, kind="ExternalInput").ap()

    def internal(name, shape, dt=BF16):
        dr[name] = nc.dram_tensor(name, shape, dt).ap()

    inp("x_all", [S, D]); inp("x_own", [T, D]); inp("x_na", [NAT, D]); inp("ctx_in", [C, D]); inp("cvec", [2, D])
    Wst = {}
    for n in W_NAMES:
        Wst[n] = nc.dram_tensor(n, [DEPTH] + W_SHAPES[n], F32, kind="ExternalInput").ap()
    fng = nc.dram_tensor("final_norm_g", [D], F32, kind="ExternalInput").ap()
    for n in ("ident_f", "onesbd_f", "ones_f"):
        inp(n, [128, 128])
    for n in ("pt128", "pt96", "pt32", "ident_b"):
        inp(n, [128, 128], BF16)
    inp("cosA_all", [128, S]); inp("sinA_all", [128, S]); inp("cosA_own", [128, T]); inp("sinA_own", [128, T])
    inp("cosM_all", [32, S]); inp("sinM_all", [32, S]); inp("cosM_own", [96, T]); inp("sinM_own", [96, T])
    inp("nabias", [DEPTH, 3, 8, 8, 128, 512])
    inp("sel", [8])
    dr["xout"] = nc.dram_tensor("xout", [T, D], F32, kind="ExternalOutput").ap()
    internal("modv", [2, 6, D], F32)
    internal("hT_all", [D, SK]); internal("hT_na", [D, NAT])
    internal("GKT", [2, 64, SK]); internal("CKVT", [256, SK]); internal("MKT", [8, 96, SK])
    internal("MV", [SK, 8, 65]); internal("GV", [SK, 2, 65])
    internal("NKT", [8, 64, NAK]); internal("NV", [NAK, 8, 65])
    internal("GQT", [8, 64, NQ]); internal("NQT", [8, 64, NQ]); internal("MQT", [8, 96, NQ])
    internal("YAT", [512, NQ]); internal("YBT", [512, NQ]); internal("YCT", [512, NQ])
    internal("xs1", [NQ, D], F32); internal("h2T", [D, NQ])
    NCH = 8
    x1c = [nc.dram_tensor(f"x1c{k}", [256, D], F32) for k in range(NCH)]
    xgc = [nc.dram_tensor(f"xgc{k}", [4 * 256, D], F32) for k in range(NCH)]
    internal("c1loc", [C, D], F32); internal("x_na2", [NAT, D], F32)
    internal("rls", [4, 512], F32)

    def x1_at(q):
        return x1c[q // 256].ap()[q % 256:q % 256 + 128, :]

    def xg_at(t0):
        r, k, off = t0 // 2048, (t0 % 2048) // 256, t0 % 256
        return xgc[k].ap()[r * 256 + off:r * 256 + off + 128, :]

    K.x1_at, K.xg_at = x1_at, xg_at

    for l in range(DEPTH):
        L = {n: Wst[n][l] for n in W_NAMES}
        final = (l == DEPTH - 1)
        with_ctx = not final
        if final:
            L["final_norm_g"] = fng
        if l == 0:
            src = dict(x_all=dr["x_all"], x_own=dr["x_own"], x_na=dr["x_na"], ctx_in=dr["ctx_in"])
        else:
            src = dict(x_all=(lambda i: xg_at(128 * i)), x_own=x1_at, x_na=dr["x_na2"], ctx_in=dr["c1loc"])
        src["nabias"] = dr["nabias"][l]
        if final:
            src["x_dst"] = lambda q: dr["xout"][q:q + 128, :]
        else:
            src["x_dst"] = lambda q: (x1_at(q) if q < T else dr["c1loc"][q - T:q - T + 128, :])
        cc = K.S("cc")
        cc_tok = [None]
        if not final:
            def after_block(bi, store_toks):
                if bi >= NCH:
                    return
                W(K.POOL, store_toks)
                ins = K.POOL.collective_compute("AllGather", mybir.AluOpType.bypass, replica_groups=[[0, 1, 2, 3], [4, 5, 6, 7]],
                                                ins=[x1c[bi].ap().opt()], outs=[xgc[bi].ap().opt()])
                cc_tok[0] = cc.inc(ins)

            src["after_block"] = after_block
        emit_layer(K, L, src, with_ctx, final, f"{l}_")
        if not final:
            K.barrier([cc_tok[0]])
            phase_halo(K, f"hl{l}_")
    K.semcounts = {n: s.n for n, s in K.sems.items()}
    return nc, K


def _rope_tables():
    t = np.arange(S, dtype=np.int32)
    row = (t // 64).astype(np.float32)
    col = (t % 64).astype(np.float32)

    def tabs(rot_dim):
        half = rot_dim // 2
        inv = (10000.0 ** (-np.arange(0, half, 2, dtype=np.float32) / np.float32(half))).astype(np.float32)
        ar = (row[:, None] * inv).astype(np.float32)
        ac = (col[:, None] * inv).astype(np.float32)
        cos = np.concatenate([np.cos(ar), np.cos(ar), np.cos(ac), np.cos(ac)], axis=1).T.astype(np.float32)
        sin = np.concatenate([np.sin(ar), np.sin(ar), np.sin(ac), np.sin(ac)], axis=1).T.astype(np.float32)
        return np.ascontiguousarray(cos), np.ascontiguousarray(sin)

    return tabs(64), tabs(32)


def _rot_matrix(n):
    q = n // 4
    P = np.zeros((n, n), np.float32)
    for base in (0, 2 * q):
        for i in range(q):
            P[base + i, base + q + i] = -1.0
            P[base + q + i, base + i] = 1.0
    return P


def _consts():
    c = {}
    c["ident_f"] = np.eye(128, dtype=np.float32)
    c["ones_f"] = np.ones((128, 128), np.float32)
    bd = np.zeros((128, 128), np.float32)
    bd[:64, :64] = 1.0
    bd[64:, 64:] = 1.0
    c["onesbd_f"] = bd
    P64 = _rot_matrix(64)
    P32 = _rot_matrix(32)
    pt128 = np.zeros((128, 128), np.float32)
    pt128[:64, :64] = P64.T
    pt128[64:, 64:] = P64.T
    pt96 = np.zeros((128, 128), np.float32)
    pt96[64:96, 64:96] = P32.T
    pt32 = np.zeros((128, 128), np.float32)
    pt32[:32, :32] = P32.T
    c["ident_b"] = np.eye(128, dtype=np.float32).astype(ml_dtypes.bfloat16)
    c["pt128"] = pt128.astype(ml_dtypes.bfloat16)
    c["pt96"] = pt96.astype(ml_dtypes.bfloat16)
    c["pt32"] = pt32.astype(ml_dtypes.bfloat16)
    return c


def _na_bias_tables(rpb, j):
    out = np.empty((3, 8, 8, 128, 512), np.float32)
    kcol = np.arange(64)[None, :, None, None]
    qcol = np.arange(64)[None, None, None, :]
    m = np.arange(16)[:, None, None, None]
    a = np.arange(8)[None, None, :, None]
    cs = np.clip(qcol - 8, 0, 48)
    colok = (kcol >= cs) & (kcol < cs + 16)
    cidx = np.clip(kcol - qcol + 15, 0, 30)
    for v, i in enumerate((0, 1, 3)):
        r = 32 * j + 8 * i + a
        k = 32 * j + 8 * i - 4 + m
        rs = np.clip(r - 4, 0, 120)
        ok = (k >= 0) & (k < 128) & (k >= rs) & (k < rs + 8) & colok
        ridx = np.clip(k - r + 7, 0, 14)
        ridx_b = np.broadcast_to(ridx, ok.shape)
        cidx_b = np.broadcast_to(cidx, ok.shape)
        vals = rpb[:, ridx_b, cidx_b]
        tab = np.where(ok[None], vals, np.float32(NEG)).astype(np.float32)
        out[v] = tab.reshape(8, 8, 128, 512)
    return out


def _core_inputs(x_full, ctx_full, c, c_ctx, Wd, consts, ropes):
    (cosA, sinA), (cosM, sinM) = ropes
    maps = []
    cosA2 = np.ascontiguousarray(np.concatenate([cosA, cosA], 0))
    sinA2 = np.ascontiguousarray(np.concatenate([sinA, sinA], 0))
    shared = dict(consts)
    for n in W_NAMES:
        shared[n] = np.ascontiguousarray(Wd[n])
    shared["final_norm_g"] = np.ascontiguousarray(Wd["final_norm_g"])
    shared["cosA_all"] = cosA2
    shared["sinA_all"] = sinA2
    shared["cosM_all"] = cosM
    shared["sinM_all"] = sinM
    rpb = np.asarray(Wd["na_rpb"], np.float32)
    nab = [np.stack([_na_bias_tables(rpb[l], j) for l in range(rpb.shape[0])], 0) for j in range(4)]
    for core in range(8):
        b, j = core // 4, core % 4
        t0 = T * j
        d = dict(shared)
        d["x_all"] = np.ascontiguousarray(x_full[b])
        d["x_own"] = np.ascontiguousarray(x_full[b, t0:t0 + T])
        xna = np.zeros((NAT, D), np.float32)
        lo = (32 * j - 4) * 64
        hi = lo + NAT
        slo, shi = max(lo, 0), min(hi, S)
        xna[slo - lo:shi - lo] = x_full[b, slo:shi]
        d["x_na"] = xna
        d["ctx_in"] = np.ascontiguousarray(ctx_full[b])
        d["cvec"] = np.ascontiguousarray(np.stack([c[b], c_ctx]))
        d["cosA_own"] = np.ascontiguousarray(cosA2[:, t0:t0 + T])
        d["sinA_own"] = np.ascontiguousarray(sinA2[:, t0:t0 + T])
        d["cosM_own"] = np.ascontiguousarray(np.concatenate([np.ones((64, T), np.float32), cosM[:, t0:t0 + T]], 0))
        d["sinM_own"] = np.ascontiguousarray(np.concatenate([np.zeros((64, T), np.float32), sinM[:, t0:t0 + T]], 0))
        d["nabias"] = nab[j]
        sel = np.zeros(8, np.float32)
        if j > 0:
            sel[j - 1] = 1.0
        if j < 3:
            sel[4 + j + 1] = 1.0
        d["sel"] = sel
        maps.append(d)
    return maps


_PROG = []


def kernel(**inputs):
    Wd = {k: np.asarray(v, np.float32) for k, v in inputs.items()}
    if not _PROG:
        _PROG.append(build_fused()[0])
    nc = _PROG[0]
    maps = _core_inputs(Wd["x"], Wd["ctx"], Wd["c"], Wd["c_ctx"], Wd, _consts(), _rope_tables())
    res = run_bass_kernel_spmd(nc, maps, core_ids=list(range(8)))
    outs = [r["xout"] for r in res.results]
    x = np.stack([np.concatenate([outs[4 * b + j] for j in range(4)], 0) for b in range(2)], 0)
    return np.ascontiguousarray(x.astype(np.float32))
```
